# Optimizing a Trainium2 kernel written in Bass

```python
import jax, jax.numpy as jnp
from jax import lax
import numpy as np

D_MODEL = 1024
BATCH = 2
SEQ = 8192
DEPTH = 1
DEC_BATCH = 1
DEC_SEQ = 16384
PAST_LEN = 128

HG_HEADS = 4
HG_KDIM = 128
HG_VDIM = 128
HG_WIDTH = HG_HEADS * HG_KDIM
HG_CHUNK = 64
ATT_HEADS = 8
ATT_HDIM = 64
ATT_WIDTH = ATT_HEADS * ATT_HDIM
DILATION_PATTERNS = ((128, 1), (512, 4), (2048, 16))
ROPE_THETA = 10000.0
FFN_HIDDEN = ((8 * D_MODEL // 3 + 255) // 256) * 256
NORM_EPS = 1e-6
IN_SIZES = (HG_WIDTH, HG_WIDTH, HG_WIDTH, HG_WIDTH, HG_WIDTH,
            ATT_WIDTH, ATT_WIDTH, ATT_WIDTH, D_MODEL, D_MODEL)
IN_WIDTH = 5 * HG_WIDTH + 3 * ATT_WIDTH + 2 * D_MODEL

kernel_name = "hgrn2_dilated_attn_adaln_encoder"


def _split_points():
    pts, acc = [], 0
    for s in IN_SIZES[:-1]:
        acc += s
        pts.append(acc)
    return pts


def rms_norm(x, gain):
    xf = x.astype(jnp.float32)
    y = xf * lax.rsqrt(jnp.mean(xf * xf, axis=-1, keepdims=True) + NORM_EPS)
    return (y * gain.astype(jnp.float32)).astype(x.dtype)


def rope(x, pos):
    half = x.shape[-1] // 2
    inv = ROPE_THETA ** (-jnp.arange(half, dtype=jnp.float32) / half)
    ang = pos[:, None] * inv[None, :]
    cos = jnp.cos(ang)[None, :, None, :]
    sin = jnp.sin(ang)[None, :, None, :]
    x1, x2 = x[..., :half], x[..., half:]
    return jnp.concatenate([x1 * cos - x2 * sin, x2 * cos + x1 * sin], axis=-1)


def hgrn2_scan(q, k, v, log_f):
    B, L, H, dk = q.shape
    dv = v.shape[-1]
    n = L // HG_CHUNK

    def to_chunks(t):
        return t.reshape(B, n, HG_CHUNK, H, t.shape[-1]).transpose(1, 0, 3, 2, 4)

    qc, kc, vc, gc = to_chunks(q), to_chunks(k), to_chunks(v), to_chunks(log_f)
    causal = jnp.tril(jnp.ones((HG_CHUNK, HG_CHUNK), dtype=bool))[:, :, None]

    def step(S, inp):
        qb, kb, vb, gb = inp
        b = jnp.cumsum(gb, axis=2)
        diff = b[:, :, :, None, :] - b[:, :, None, :, :]
        decay = jnp.exp(jnp.where(causal, diff, -jnp.inf))
        scores = jnp.einsum('bhtk,bhsk,bhtsk->bhts', qb, kb, decay)
        o_intra = jnp.einsum('bhts,bhsv->bhtv', scores, vb)
        o_inter = jnp.einsum('bhtk,bhkv->bhtv', qb * jnp.exp(b), S)
        b_last = b[:, :, -1:, :]
        S_new = S * jnp.exp(b_last[:, :, 0, :])[..., None] + jnp.einsum(
            'bhsk,bhsv->bhkv', kb * jnp.exp(b_last - b), vb)
        return S_new, o_intra + o_inter

    S0 = jnp.zeros((B, H, dk, dv), jnp.float32)
    _, o = lax.scan(step, S0, (qc, kc, vc, gc))
    return o.transpose(1, 0, 3, 2, 4).reshape(B, L, H, dv)


def hgrn2_mixer(q_raw, f_fwd_raw, f_bwd_raw, i_raw, g_raw, lb_fwd, lb_bwd, gn_gain):
    B, L, _ = q_raw.shape

    def heads(t, d):
        return t.reshape(B, L, HG_HEADS, d)

    q = heads(jax.nn.silu(q_raw.astype(jnp.float32)) * HG_KDIM ** -0.5, HG_KDIM)
    v = heads(i_raw.astype(jnp.float32), HG_VDIM)

    def gates(f_raw, lb):
        lb = lb.astype(jnp.float32)
        f = lb + (1.0 - lb) * jax.nn.sigmoid(f_raw.astype(jnp.float32))
        return heads(jnp.log(f), HG_KDIM), heads(1.0 - f, HG_KDIM)

    logf_f, k_f = gates(f_fwd_raw, lb_fwd)
    logf_b, k_b = gates(f_bwd_raw, lb_bwd)
    o_fwd = hgrn2_scan(q, k_f, v, logf_f)
    flip = lambda t: jnp.flip(t, axis=1)
    o_bwd = flip(hgrn2_scan(flip(q), flip(k_b), flip(v), flip(logf_b)))
    o = o_fwd + o_bwd
    o = o * lax.rsqrt(jnp.mean(o * o, axis=-1, keepdims=True) + NORM_EPS)
    o = o.reshape(B, L, HG_WIDTH) * gn_gain.astype(jnp.float32)
    return o * jax.nn.silu(g_raw.astype(jnp.float32))


def dilated_branch(q, k, v, window, dilation):
    B, L, H, D = q.shape
    r = dilation
    radius = window // (2 * r)
    blk = radius
    Lr = L // r
    nb = -(-Lr // blk)
    pad = nb * blk - Lr

    def sub(t):
        return t.reshape(B, Lr, r, H, D).transpose(0, 2, 1, 3, 4)

    qs = jnp.pad(sub(q), ((0, 0), (0, 0), (0, pad), (0, 0), (0, 0))).reshape(B, r, nb, blk, H, D)

    def band(t):
        tp = jnp.pad(sub(t), ((0, 0), (0, 0), (blk, pad + blk), (0, 0), (0, 0)))
        tp = tp.reshape(B, r, nb + 2, blk, H, D)
        return jnp.concatenate([tp[:, :, :-2], tp[:, :, 1:-1], tp[:, :, 2:]], axis=3)

    kb, vb = band(k), band(v)
    i = jnp.arange(blk)[:, None]
    j = jnp.arange(3 * blk)[None, :]
    kpos = jnp.arange(nb)[:, None, None] * blk - blk + j[None]
    mask = (jnp.abs(i - (j - blk)) <= radius)[None] & (kpos >= 0) & (kpos < Lr)
    s = jnp.einsum('brnqhd,brnkhd->brnhqk', qs, kb)
    s = jnp.where(mask[:, None], s, -jnp.inf)
    m = jnp.max(s, axis=-1)
    m = jnp.where(jnp.isfinite(m), m, 0.0)
    p = jnp.exp(s - m[..., None])
    den = jnp.sum(p, axis=-1)
    num = jnp.einsum('brnhqk,brnkhd->brnqhd', p, vb)

    num = num.reshape(B, r, nb * blk, H, D)[:, :, :Lr].transpose(0, 2, 1, 3, 4).reshape(B, L, H, D)

    def back(t):
        t = t.transpose(0, 1, 2, 4, 3).reshape(B, r, nb * blk, H)[:, :, :Lr]
        return t.transpose(0, 2, 1, 3).reshape(B, L, H)

    return num, back(m), back(den)


def dilated_attention(q_raw, k_raw, v_raw):
    B, L, _ = q_raw.shape
    pos = jnp.arange(L, dtype=jnp.float32)
    heads = lambda t: t.astype(jnp.float32).reshape(B, L, ATT_HEADS, ATT_HDIM)
    q = rope(heads(q_raw), pos) * ATT_HDIM ** -0.5
    k = rope(heads(k_raw), pos)
    v = heads(v_raw)
    parts = [dilated_branch(q, k, v, w, r) for (w, r) in DILATION_PATTERNS]
    M = jnp.max(jnp.stack([pm for (_, pm, _) in parts]), axis=0)
    num = jnp.zeros_like(q)
    den = jnp.zeros_like(M)
    for (n_i, m_i, d_i) in parts:
        wgt = jnp.exp(m_i - M)
        num = num + n_i * wgt[..., None]
        den = den + d_i * wgt
    return (num / den[..., None]).reshape(B, L, ATT_WIDTH)


def encoder_layer(x, c, w_ada, b_ada, norm1_g, w_in, b_in, lb_fwd, lb_bwd, hg_norm_g,
                  w_branch_a, w_branch_b, w_out, norm2_g, w_ffn_in, w_ffn_out):
    mod = jax.nn.silu(c) @ w_ada + b_ada
    shift1, scale1, gate1, shift2, scale2, gate2 = jnp.split(mod[:, None, :], 6, axis=-1)
    h = rms_norm(x, norm1_g) * (1.0 + scale1) + shift1
    proj = h @ w_in + b_in
    (q_hg, f_fw, f_bw, i_hg, g_hg, q_at, k_at, v_at, g_a, g_b) = jnp.split(proj, _split_points(), axis=-1)
    o_a = hgrn2_mixer(q_hg, f_fw, f_bw, i_hg, g_hg, lb_fwd, lb_bwd, hg_norm_g).astype(x.dtype)
    o_b = dilated_attention(q_at, k_at, v_at).astype(x.dtype)
    merged = jax.nn.sigmoid(g_a) * (o_a @ w_branch_a) + jax.nn.sigmoid(g_b) * (o_b @ w_branch_b)
    x = x + gate1 * (merged @ w_out)
    h = rms_norm(x, norm2_g) * (1.0 + scale2) + shift2
    gt, up = jnp.split(h @ w_ffn_in, 2, axis=-1)
    x = x + gate2 * ((jax.nn.silu(gt) * up) @ w_ffn_out)
    return x


def trunk(x, c, w_ada, b_ada, norm1_g, w_in, b_in, lb_logits, hg_norm_g, w_branch_a, w_branch_b,
          w_out, norm2_g, w_ffn_in, w_ffn_out, final_norm_g):
    lb_all = jnp.cumsum(jax.nn.softmax(lb_logits.astype(jnp.float32), axis=0), axis=0)
    for l in range(DEPTH):
        x = encoder_layer(x, c, w_ada[l], b_ada[l], norm1_g[l], w_in[l], b_in[l],
                          lb_all[l, 0], lb_all[l, 1], hg_norm_g[l], w_branch_a[l], w_branch_b[l],
                          w_out[l], norm2_g[l], w_ffn_in[l], w_ffn_out[l])
    return rms_norm(x, final_norm_g)


def setup_inputs(seed: int = 0) -> dict:
    key = jax.random.key(seed)
    ks = jax.random.split(key, 20)
    nrm = lambda k, shape, s: jax.random.normal(k, shape, jnp.float32) * s
    D = D_MODEL
    return {
        "x_prompt": nrm(ks[0], (BATCH, SEQ, D), 1.0),
        "x_sample": nrm(ks[1], (DEC_BATCH, DEC_SEQ, D), 1.0),
        "c_prompt": nrm(ks[2], (BATCH, D), 1.0),
        "c_sample": nrm(ks[3], (DEC_BATCH, D), 1.0),
        "w_ada": nrm(ks[4], (DEPTH, D, 6 * D), 0.5 * D ** -0.5),
        "b_ada": nrm(ks[5], (DEPTH, 6 * D), 0.01),
        "norm1_g": 1.0 + nrm(ks[6], (DEPTH, D), 0.02),
        "w_in": nrm(ks[7], (DEPTH, D, IN_WIDTH), D ** -0.5),
        "b_in": nrm(ks[8], (DEPTH, IN_WIDTH), 0.01),
        "lb_logits": nrm(ks[9], (DEPTH + 1, 2, HG_WIDTH), 0.1),
        "hg_norm_g": 1.0 + nrm(ks[10], (DEPTH, HG_WIDTH), 0.02),
        "w_branch_a": nrm(ks[11], (DEPTH, HG_WIDTH, D), HG_WIDTH ** -0.5),
        "w_branch_b": nrm(ks[12], (DEPTH, ATT_WIDTH, D), ATT_WIDTH ** -0.5),
        "w_out": nrm(ks[13], (DEPTH, D, D), D ** -0.5),
        "norm2_g": 1.0 + nrm(ks[14], (DEPTH, D), 0.02),
        "w_ffn_in": nrm(ks[15], (DEPTH, D, 2 * FFN_HIDDEN), D ** -0.5),
        "w_ffn_out": nrm(ks[16], (DEPTH, FFN_HIDDEN, D), FFN_HIDDEN ** -0.5),
        "final_norm_g": 1.0 + nrm(ks[17], (D,), 0.02),
    }


def reference(x_prompt, x_sample, c_prompt, c_sample, w_ada, b_ada, norm1_g, w_in, b_in, lb_logits,
              hg_norm_g, w_branch_a, w_branch_b, w_out, norm2_g, w_ffn_in, w_ffn_out, final_norm_g):
    y_prompt = trunk(x_prompt, c_prompt, w_ada, b_ada, norm1_g, w_in, b_in, lb_logits, hg_norm_g,
                     w_branch_a, w_branch_b, w_out, norm2_g, w_ffn_in, w_ffn_out, final_norm_g)
    y_sample = trunk(x_sample, c_sample, w_ada, b_ada, norm1_g, w_in, b_in, lb_logits, hg_norm_g,
                     w_branch_a, w_branch_b, w_out, norm2_g, w_ffn_in, w_ffn_out, final_norm_g)
    return (y_prompt, y_sample)
```

```python
import contextlib
import numpy as np
import concourse.bass as bass
import concourse.mybir as mybir
from concourse.alu_op_type import AluOpType as ALU
from concourse.bass_utils import run_bass_kernel_spmd

F32 = mybir.dt.float32
BF16 = mybir.dt.bfloat16
AF = mybir.ActivationFunctionType

D = 1024
NCORES = 8
OWN = 4096
HALO = 1024
WIN = OWN + 2 * HALO
HH = 512
FFN = 2816
NJ = FFN // 128
EPS = 1e-6
ROPE_THETA = 10000.0

PATTERNS = (1, 4, 16)


def att_vtiles():
    out = []
    for r in PATTERNS:
        jq0 = 1024 // r
        nb = (2048 // r) // 128
        for rho in range(r):
            for m in range(nb + 1):
                start = rho + r * (jq0 - 64 + 128 * m)
                out.append((r, rho, m, start, nb))
    return out


class Buf:
    __slots__ = ("name", "w", "r", "dsem", "ssem")

    def __init__(self, name):
        self.name = name
        self.w = {}
        self.r = {}
        self.dsem = None
        self.ssem = None


class Prog:
    ENG = ("pe", "act", "dve", "pool", "sp")

    def __init__(self, nc):
        self.nc = nc
        self.es = contextlib.ExitStack()
        self.eng = {"pe": nc.tensor, "act": nc.scalar, "dve": nc.vector, "pool": nc.gpsimd, "sp": nc.sync}
        self.sems = {}
        self.cnt = {}
        for e in self.ENG:
            self.sems[e] = self.es.enter_context(nc.semaphore("sem_" + e))
            self.cnt[e] = 0
        self.ndma = 12
        for i in range(self.ndma):
            k = "d%d" % i
            self.sems[k] = self.es.enter_context(nc.semaphore("sem_" + k))
            self.cnt[k] = 0
        self.nsw = 6
        for i in range(self.nsw):
            k = "w%d" % i
            self.sems[k] = self.es.enter_context(nc.semaphore("sem_" + k))
            self.cnt[k] = 0
        self.dma_next = 0
        self.sw_next = 0
        self.seen = {e: {} for e in self.ENG}
        self.nins = 0
        self.defer = True
        self.q = []
        self.window = 40
        self.act_tbl = None
        self.est_time = 0.0

    def buf(self, name, dma=False):
        b = Buf(name)
        if dma:
            assert self.dma_next < self.ndma, "out of dma semaphores"
            b.dsem = "d%d" % self.dma_next
            self.dma_next += 1
        return b

    def _deps(self, e, reads, writes):
        deps = {}
        for b in reads:
            for k, v in b.w.items():
                if deps.get(k, 0) < v:
                    deps[k] = v
        for b in writes:
            for k, v in b.w.items():
                if deps.get(k, 0) < v:
                    deps[k] = v
            for k, v in b.r.items():
                if deps.get(k, 0) < v:
                    deps[k] = v
        seen = self.seen[e]
        for k, v in deps.items():
            if k == e and e == "pe":
                continue
            if (k[0] == "d" and k != "dve") or k[0] == "w":
                v = self.cnt[k]
            if seen.get(k, 0) < v:
                self.eng[e].wait_ge(self.sems[k], v)
                seen[k] = v

    DEFAULT_COST = {"pe": 0.22, "act": 0.55, "dve": 0.55, "pool": 1.2, "sp": 0.15}

    def op(self, e, reads, writes, ins_fn, c=None, n=512, tb=None):
        if self.defer:
            if c is None:
                if e == "pe":
                    c = 0.015 + n * 0.0004
                elif e == "act":
                    c = 0.22 + n / 1400.0
                elif e == "dve":
                    c = 0.12 + n * 0.0011
                else:
                    c = 0.15 + n * 0.0022
            self.q.append((0, e, list(reads), list(writes), ins_fn, c, tb))
            return
        self._op_now(e, reads, writes, ins_fn)

    def dma(self, q, out, in_, reads, writes, semb, c=4.0):
        if self.defer:
            self.q.append((1, q, list(reads), list(writes), (out, in_, semb), c))
            return
        self._dma_now(q, out, in_, reads, writes, semb)

    def semkey(self, q, semb):
        if q == "sp":
            assert semb.dsem is not None
            return semb.dsem
        if semb.ssem is None:
            assert self.sw_next < self.nsw, "out of sw dma semaphores"
            semb.ssem = "w%d" % self.sw_next
            self.sw_next += 1
        return semb.ssem

    def _op_now(self, e, reads, writes, ins_fn):
        self._deps(e, reads, writes)
        ins = ins_fn(self.eng[e])
        self.cnt[e] += 1
        ins.then_inc(self.sems[e], 1)
        ev = self.cnt[e]
        for b in reads:
            b.r[e] = ev
        for b in writes:
            b.w = {e: ev}
            b.r = {}
        self.nins += 1

    def _dma_now(self, q, out, in_, reads, writes, semb):
        self._deps(q, reads, writes)
        k = self.semkey(q, semb)
        self.cnt[k] += 16
        self.eng[q].dma_start(out=out, in_=in_).then_inc(self.sems[k], 16)
        ev = self.cnt[k]
        for b in reads:
            b.r[k] = ev
        for b in writes:
            b.w = {k: ev}
            b.r = {}
        self.nins += 1

    def flush(self):
        ops = self.q
        self.q = []
        n = len(ops)
        if n == 0:
            return
        last_w = {}
        readers = {}
        deps = [None] * n
        for i, o in enumerate(ops):
            d = set()
            reads, writes = o[2], o[3]
            if o[0] == 1:
                kk_ = ("sem", self.semkey(o[1], o[4][2]))
                reads = reads + [kk_]
                writes = writes + [kk_]
            for b in reads:
                w = last_w.get(id(b) if not isinstance(b, tuple) else b)
                if w is not None:
                    d.add(w)
            for b in writes:
                kb = id(b) if not isinstance(b, tuple) else b
                w = last_w.get(kb)
                if w is not None:
                    d.add(w)
                for r in readers.get(kb, ()):
                    d.add(r)
            for b in reads:
                kb = id(b) if not isinstance(b, tuple) else b
                readers.setdefault(kb, []).append(i)
            for b in writes:
                kb = id(b) if not isinstance(b, tuple) else b
                last_w[kb] = i
                readers[kb] = []
            d.discard(i)
            deps[i] = d
        queues = {}
        for i, o in enumerate(ops):
            queues.setdefault(o[1], []).append(i)
        heads = {e: 0 for e in queues}
        done = [False] * n
        finish = [0.0] * n
        eng_free = {e: 0.0 for e in queues}
        W = self.window
        LAT = 0.08
        TBL = 1.3
        nsched = 0
        while nsched < n:
            best = None
            for e, ql in queues.items():
                h = heads[e]
                while h < len(ql) and done[ql[h]]:
                    h += 1
                heads[e] = h
                cnt = 0
                j = h
                ef = eng_free[e]
                while j < len(ql) and cnt < W:
                    i = ql[j]
                    j += 1
                    if done[i]:
                        continue
                    cnt += 1
                    rt = 0.0
                    ok = True
                    for dd in deps[i]:
                        if not done[dd]:
                            ok = False
                            break
                        f = finish[dd]
                        if f > rt:
                            rt = f
                    if not ok:
                        continue
                    st = rt + LAT if rt + LAT > ef else ef
                    if e == "act":
                        tb_ = ops[i][6] if ops[i][0] == 0 else None
                        if tb_ is not None and tb_ != self.act_tbl:
                            st += TBL
                    if best is None or st < best[0] or (st == best[0] and i < best[1]):
                        best = (st, i, e)
                    if st <= ef:
                        break
            st, i, e = best
            o = ops[i]
            if o[0] == 1:
                eng_free[e] = st + 0.15
                finish[i] = st + o[5]
            else:
                eng_free[e] = st + o[5]
                finish[i] = st + o[5]
                if e == "act" and o[6] is not None:
                    self.act_tbl = o[6]
            done[i] = True
            nsched += 1
            if o[0] == 1:
                self._dma_now(o[1], o[4][0], o[4][1], o[2], o[3], o[4][2])
            else:
                self._op_now(o[1], o[2], o[3], o[4])
        self.est_time += max(finish) if n else 0.0

    def barrier(self):
        self.flush()
        for e in self.ENG:
            seen = self.seen[e]
            for k, v in self.cnt.items():
                if v > 0 and seen.get(k, 0) < v:
                    self.eng[e].wait_ge(self.sems[k], v)
                    seen[k] = v
        self.dma_next = 0
        self.sw_next = 0


class Pools:
    def __init__(self, items):
        self.items = items
        self.i = 0

    def get(self):
        it = self.items[self.i % len(self.items)]
        self.i += 1
        return it


def build_program(debug=False, upto=4):
    nc = bass.Bass("TRN2", target_bir_lowering=False)
    P = Prog(nc)
    es = P.es

    def din(name, shape, dt=F32):
        return nc.dram_tensor(name, list(shape), dt, kind="ExternalInput").ap()

    xw = din("xw", [WIN, D])
    ccol = din("ccol", [128, 8])
    w_ada = din("w_ada", [D, 6 * D])
    b_ada = din("b_ada", [6 * D])
    n1g = din("n1g", [128, 8])
    n2g = din("n2g", [128, 8])
    w_in = din("w_in", [D, 6144])
    w_qksw = din("w_qksw", [D, 1024])
    bin_col = din("bin_col", [128, 48])
    bsw_col = din("bsw_col", [128, 8])
    b_in_row = din("b_in_row", [6144])
    lbl = din("lbl", [128, 16])
    gng = din("gng", [128, 4])
    w_bra = din("w_bra", [512, D])
    w_brb = din("w_brb", [512, D])
    w_out = din("w_out", [D, D])
    w_fi = din("w_fi", [D, 2 * FFN])
    w_fo = din("w_fo", [FFN, D])
    fng = din("fng", [D])
    cosT = din("cosT", [128, WIN])
    sinT = din("sinT", [128, WIN])
    vmA = din("vmA", [128, 64])
    vmB = din("vmB", [128, 36])
    vmC = din("vmC", [128, 36])
    consts = din("consts", [128, 1280])

    okind = "ExternalOutput"
    y = nc.dram_tensor("y", [OWN, D], F32, kind=okind).ap()
    skind = okind if debug else "Internal"
    ob_s = nc.dram_tensor("ob_s", [512, OWN], BF16, kind=skind).ap()
    obw_s = nc.dram_tensor("obw_s", [OWN, 512], F32, kind=skind).ap()
    ofw_s = nc.dram_tensor("ofw_s", [OWN, 512], F32, kind=skind).ap()
    v_s = nc.dram_tensor("v_s", [4096, 8, 65], BF16, kind="Internal").ap()
    B_vs = P.buf("v_s")
    x1_s = nc.dram_tensor("x1_s", [OWN, D], F32, kind=skind).ap()

    uid = [0]

    def sb(name, shape, dt, stack=None):
        uid[0] += 1
        return (stack or es).enter_context(nc.sbuf_tensor("%s_u%d" % (name, uid[0]), list(shape), dt))

    def ps(name, stack=None):
        return (stack or es).enter_context(nc.psum_tensor(name, [128, 512], F32))

    cst = sb("cst", [128, 1280], F32)
    cstb = sb("cstb", [128, 1280], BF16)
    B_cst = P.buf("cst", dma=True)
    B_cstb = P.buf("cstb")
    ident_f = cst[:, 0:128]
    J_f = cst[:, 128:256]
    ident_b = cstb[:, 0:128]
    J_b = cstb[:, 128:256]
    tri_b = cstb[:, 256:384]
    mA_b = cstb[:, 384:512]
    mB_b = cstb[:, 512:640]
    cmask_f = cst[:, 640:1152]

    colv = sb("colv", [128, 160], F32)
    B_colv = P.buf("colv")
    C_G1, C_SH1, C_G2, C_SH2 = 0, 8, 16, 24
    C_LB, C_OML = 32, 40
    C_GN = 48
    C_BIN = 52
    C_BSW = 100
    C_TMP = 108
    g_s = nc.dram_tensor("g_s", [2, 128, D], F32, kind="Internal").ap()

    psb = [ps("psb%d" % i) for i in range(8)]
    PSR = Pools([(psb[i], P.buf("ps%d" % i)) for i in range(8)])

    def cv(c, n=1):
        return colv[:, c:c + n]

    with contextlib.ExitStack() as st:
        small = sb("p0small", [128, 64], F32, st)
        B_small = P.buf("p0small", dma=True)
        P.dma("sp", cst[:], consts, [], [B_cst], B_cst)
        P.op("dve", [B_cst], [B_cstb], lambda e: e.tensor_copy(out=cstb[:], in_=cst[:]))
        P.dma("sp", small[:, 0:8], ccol, [], [B_small], B_small)
        P.dma("sp", small[:, 24:40], lbl, [], [B_small], B_small)
        P.dma("sp", colv[:, C_GN:C_GN + 4], gng, [], [B_colv], B_small)
        P.dma("sp", colv[:, C_BIN:C_BIN + 48], bin_col, [], [B_colv], B_small)
        P.dma("sp", colv[:, C_BSW:C_BSW + 8], bsw_col, [], [B_colv], B_small)
        P.op("dve", [B_small], [B_small], lambda e: e.tensor_tensor(
            out=small[:, 40:48], in0=small[:, 24:32], in1=small[:, 32:40], op=ALU.subtract))
        P.op("act", [B_small], [B_colv], lambda e: e.activation(
            out=cv(C_LB, 8), in_=small[:, 40:48], func=AF.Sigmoid), tb='s')
        P.op("dve", [B_colv], [B_colv], lambda e: e.tensor_scalar(
            out=cv(C_OML, 8), in0=cv(C_LB, 8), scalar1=-1.0, scalar2=1.0, op0=ALU.mult, op1=ALU.add))
        scb = sb("scb", [128, 8, 128], F32, st)
        B_scb = P.buf("scb")
        P.op("act", [B_small], [B_small], lambda e: e.activation(
            out=small[:, 48:56], in_=small[:, 0:8], func=AF.Silu), tb='s')
        for kc in range(8):
            src = bass.AP(small.tensor if hasattr(small, "tensor") else small, 48 + kc, [[64, 128], [0, 128]])
            P.op("dve", [B_small], [B_scb], lambda e, kc=kc, src=src: e.tensor_copy(out=scb[:, kc, :], in_=src))
        badab = sb("badab", [128, 6 * D], F32, st)
        B_badab = P.buf("badab", dma=True)
        P.dma("sp", badab[:], b_ada.partition_broadcast(128), [], [B_badab], B_badab)
        modbc = sb("modbc", [128, 6 * D], F32, st)
        B_mod = P.buf("modbc")
        stg = [(sb("p0stg%d" % i, [128, 8, 512], F32, st), P.buf("p0stg%d" % i, dma=True)) for i in range(2)]
        wa_v = w_ada.rearrange("(kc p) n -> p kc n", p=128)
        for blk in range(12):
            t, B = stg[blk % 2]
            P.dma("sp", t[:], wa_v[:, :, blk * 512:(blk + 1) * 512], [], [B], B)
            pt, PB_ = PSR.get()
            for kc in range(8):
                P.op("pe", [B, B_scb], [PB_], lambda e, kc=kc, t=t, pt=pt: e.matmul(
                    pt[:], scb[:, kc, :], t[:, kc, :], start=(kc == 0), stop=(kc == 7)), c=0.9)
            P.op("dve", [PB_, B_badab], [B_mod], lambda e, blk=blk, pt=pt: e.tensor_tensor(
                out=modbc[:, blk * 512:(blk + 1) * 512], in0=pt[:], in1=badab[:, blk * 512:(blk + 1) * 512],
                op=ALU.add))
        pt, PB_ = PSR.get()
        for vi, base in enumerate((0, 1024, 3072, 4096)):
            for kc in range(8):
                P.op("pe", [B_mod, B_cst], [PB_], lambda e, vi=vi, base=base, kc=kc, pt=pt: e.matmul(
                    pt[:, vi * 8 + kc: vi * 8 + kc + 1], modbc[:, base + kc * 128: base + (kc + 1) * 128],
                    ident_f[:, 0:1], start=True, stop=True))
        P.op("dve", [PB_], [B_small], lambda e, pt=pt: e.tensor_copy(out=small[:, 0:32], in_=pt[:, 0:32]))
        sm2 = sb("p0sm2", [128, 16], F32, st)
        B_sm2 = P.buf("p0sm2", dma=True)
        P.dma("sp", sm2[:, 0:8], n1g, [], [B_sm2], B_sm2)
        P.dma("sp", sm2[:, 8:16], n2g, [], [B_sm2], B_sm2)
        P.op("dve", [B_small, B_sm2], [B_colv], lambda e: e.scalar_tensor_tensor(
            out=cv(C_G1, 8), in0=small[:, 8:16], scalar=1.0, in1=sm2[:, 0:8], op0=ALU.add, op1=ALU.mult))
        P.op("dve", [B_small, B_sm2], [B_colv], lambda e: e.scalar_tensor_tensor(
            out=cv(C_G2, 8), in0=small[:, 24:32], scalar=1.0, in1=sm2[:, 8:16], op0=ALU.add, op1=ALU.mult))
        P.op("dve", [B_small], [B_colv], lambda e: e.tensor_copy(out=cv(C_SH1, 8), in_=small[:, 0:8]))
        P.op("dve", [B_small], [B_colv], lambda e: e.tensor_copy(out=cv(C_SH2, 8), in_=small[:, 16:24]))
        B_gs = P.buf("p0gs", dma=True)
        P.dma("pool", g_s[0], modbc[:, 2048:3072], [B_mod], [], B_gs)
        P.dma("pool", g_s[1], modbc[:, 5120:6144], [B_mod], [], B_gs)
        P.barrier()

    def load_weight(st, dst, B_dst, src_rows_ap, nkc, ncols, gate_bc=None, B_gate=None, stage=None):
        v = src_rows_ap.rearrange("(kc p) n -> p kc n", p=128)
        cw = max(64, min(ncols, stage.cols // nkc))
        i = 0
        for c0 in range(0, ncols, cw):
            c1 = min(ncols, c0 + cw)
            t, B = stage.get()
            P.dma("sp", t[:, 0:nkc * (c1 - c0)].rearrange("p (k n) -> p k n", k=nkc), v[:, :, c0:c1], [], [B], B)
            tv = t[:, 0:nkc * (c1 - c0)].rearrange("p (k n) -> p k n", k=nkc)
            if gate_bc is None:
                h0 = nkc // 2
                for (eng, ka, kb) in (("pool", 0, max(1, nkc // 4)), ("dve", max(1, nkc // 4), max(2, (5 * nkc) // 8)),
                                      ("act", max(2, (5 * nkc) // 8), nkc)):
                    if kb <= ka:
                        continue
                    if eng == "act":
                        P.op("act", [B], [B_dst], lambda e, tv=tv, c0=c0, c1=c1, ka=ka, kb=kb: e.activation(
                            out=dst[:, ka:kb, c0:c1], in_=tv[:, ka:kb, :], func=AF.Copy))
                    else:
                        P.op(eng, [B], [B_dst], lambda e, tv=tv, c0=c0, c1=c1, ka=ka, kb=kb: e.tensor_copy(
                            out=dst[:, ka:kb, c0:c1], in_=tv[:, ka:kb, :]))
            else:
                for kc in range(nkc):
                    eng = "dve" if kc % 3 != 2 else "pool"
                    P.op(eng, [B, B_gate], [B_dst], lambda e, tv=tv, c0=c0, c1=c1, kc=kc: e.tensor_tensor(
                        out=dst[:, kc, c0:c1], in0=tv[:, kc, :], in1=gate_bc[:, c0:c1], op=ALU.mult))
            i += 1

    def mk_stage(st, n=2, cols=4096):
        pl = Pools([(sb("wstg%d" % i, [128, cols], F32, st), P.buf("wstg%d" % i, dma=True)) for i in range(n)])
        pl.cols = cols
        return pl

    class Front:
        def __init__(self, st, nxs, cG, cSH, tag, nxn=2):
            self.xs = Pools([(sb("xs%s%d" % (tag, i), [128, D], F32, st), P.buf("xs%s%d" % (tag, i), dma=True))
                             for i in range(nxs)])
            self.xn = Pools([(sb("xn%s%d" % (tag, i), [128, D], BF16, st), P.buf("xn%s%d" % (tag, i)))
                             for i in range(nxn)])
            self.junk = sb("junk" + tag, [128, D], BF16, st)
            self.B_junk = P.buf("junk" + tag)
            self.st4 = Pools([(sb("fst%s%d" % (tag, i), [128, 4], F32, st), P.buf("fst%s%d" % (tag, i)))
                              for i in range(2)])
            self.cG, self.cSH = cG, cSH
            self.pool_xn = False

        def stats(self, xt, B_x):
            s4, B_s = self.st4.get()
            junk, B_junk = self.junk, self.B_junk
            P.op("pool", [B_x], [B_junk], lambda e: e.tensor_tensor(out=junk[:], in0=xt[:], in1=xt[:], op=ALU.mult), c=1.9)
            P.op("dve", [B_junk], [B_s], lambda e: e.reduce_sum(out=s4[:, 0:1], in_=junk[:], axis=mybir.AxisListType.X), n=1024)
            P.op("dve", [B_s], [B_s], lambda e: e.tensor_scalar(
                out=s4[:, 1:2], in0=s4[:, 0:1], scalar1=1.0 / D, scalar2=EPS, op0=ALU.mult, op1=ALU.add), n=1)
            P.op("act", [B_s], [B_s], lambda e: e.activation(out=s4[:, 2:3], in_=s4[:, 1:2], func=AF.Ln), n=1, tb='e')
            P.op("act", [B_s], [B_s], lambda e: e.activation(out=s4[:, 3:4], in_=s4[:, 2:3], func=AF.Exp, scale=-0.5), n=1, tb='e')
            return s4, B_s

        def run(self, xt, B_x, hT_view, B_hT, flip):
            self.run_b(self.run_a(xt, B_x), hT_view, B_hT, flip)

        def run_a(self, xt, B_x):
            s4, B_s = self.stats(xt, B_x)
            xn, B_xn = self.xn.get()
            if self.pool_xn:
                P.op("pool", [B_x, B_s], [B_xn], lambda e: e.tensor_scalar(
                    out=xn[:], in0=xt[:], scalar1=s4[:, 3:4], scalar2=0.0, op0=ALU.mult, op1=ALU.add), c=1.1)
            else:
                P.op("act", [B_x, B_s], [B_xn], lambda e: e.activation(
                    out=xn[:], in_=xt[:], func=AF.Copy, scale=s4[:, 3:4]), n=1024)
            return xn, B_xn

        def run_b(self, a, hT_view, B_hT, flip):
            xn, B_xn = a
            mat = J_b if flip else ident_b
            for half in range(2):
                pt, PB_ = PSR.get()
                for k4 in range(4):
                    kc = half * 4 + k4
                    P.op("pe", [B_xn, B_cstb], [PB_], lambda e, kc=kc, k4=k4, pt=pt: e.matmul(
                        pt[:, k4 * 128:(k4 + 1) * 128], xn[:, kc * 128:(kc + 1) * 128], mat, start=True, stop=True), n=128)
                for k4 in range(4):
                    kc = half * 4 + k4
                    if k4 % 2 == 0:
                        P.op("dve", [PB_, B_colv], [B_hT], lambda e, kc=kc, k4=k4, pt=pt: e.tensor_scalar(
                            out=hT_view[:, kc, :], in0=pt[:, k4 * 128:(k4 + 1) * 128],
                            scalar1=cv(self.cG + kc), scalar2=cv(self.cSH + kc), op0=ALU.mult, op1=ALU.add), n=128)
                    else:
                        P.op("act", [PB_, B_colv], [B_hT], lambda e, kc=kc, k4=k4, pt=pt: e.activation(
                            out=hT_view[:, kc, :], in_=pt[:, k4 * 128:(k4 + 1) * 128], func=AF.Identity,
                            scale=cv(self.cG + kc), bias=cv(self.cSH + kc)), n=128)

    def front_tile(fr, loader, hTt, B_hTt, flip):
        pend = None
        for s4 in range(5):
            a = None
            if s4 < 4:
                xt, B_x = loader(s4)
                a = (fr.run_a(xt, B_x), s4)
            if pend is not None:
                pa_, ps_ = pend
                fr.run_b(pa_, hTt[:, :, ps_ * 128:(ps_ + 1) * 128], B_hTt, flip)
            pend = a

    def proj_fm(pt, PB_, W, B_W, col0, hT, B_hT, nkc=8, ncol=128):
        for kc in range(nkc):
            P.op("pe", [B_W, B_hT], [PB_], lambda e, kc=kc: e.matmul(
                pt[0:ncol, 0:hT.shape[2]], W[:, kc, col0:col0 + ncol], hT[:, kc, :], start=(kc == 0), stop=(kc == nkc - 1)))

    def proj_tm(pt_view, PB_, lhs_fn, B_h, W, B_W, col0, ncol, nkc=8):
        for kc in range(nkc):
            P.op("pe", [B_W, B_h], [PB_], lambda e, kc=kc: e.matmul(
                pt_view, lhs_fn(kc), W[:, kc, col0:col0 + ncol], start=(kc == 0), stop=(kc == nkc - 1)))

    NVT = len(att_vtiles())

    pa_st = contextlib.ExitStack()
    if upto >= 1:
        with contextlib.ExitStack() as s0:
            Wq = sb("a_Wq", [128, 8, 512], BF16, pa_st)
            Wqs = sb("a_Wqs", [128, 8, 512], BF16, pa_st)
            Wk = sb("a_Wk", [128, 8, 512], BF16, pa_st)
            Wks = sb("a_Wks", [128, 8, 512], BF16, pa_st)
            Wv = sb("a_Wv", [128, 8, 512], BF16, pa_st)
            stage = mk_stage(s0, 3, 4096)
            B_W = P.buf("a_W")
            load_weight(s0, Wq, B_W, w_in[:, 2560:3072], 8, 512, stage=stage)
            load_weight(s0, Wk, B_W, w_in[:, 3072:3584], 8, 512, stage=stage)
            load_weight(s0, Wqs, B_W, w_qksw[:, 0:512], 8, 512, stage=stage)
            load_weight(s0, Wks, B_W, w_qksw[:, 512:1024], 8, 512, stage=stage)
            load_weight(s0, Wv, B_W, w_in[:, 3584:4096], 8, 512, stage=stage)
            P.barrier()
    for hf in (range(2) if upto >= 1 else []):
        with contextlib.ExitStack() as st:
            kT = sb("a_kT", [128, 4, 4096], BF16, st)
            B_kT = P.buf("a_kT")
            qT = sb("a_qT", [128, 4, 2048], BF16, st)
            B_qT = P.buf("a_qT")
            r0 = 2048 * hf
            with contextlib.ExitStack() as s1:
                fr = Front(s1, 3, C_G1, C_SH1, "a")
                bvb = sb("a_bvb", [128, 512], F32, s1)
                B_bvb = P.buf("a_bvb", dma=True)
                P.dma("sp", bvb[:], b_in_row[3584:4096].partition_broadcast(128), [], [B_bvb], B_bvb)
                vm = sb("a_vm", [128, 32], F32, s1)
                B_vm = P.buf("a_vm", dma=True)
                P.dma("sp", vm[:], vmA[:, hf * 32:(hf + 1) * 32], [], [B_vm], B_vm)
                hTr = [(sb("a_hT%d" % i, [128, 8, 512], BF16, s1), P.buf("a_hT%d" % i)) for i in range(2)]
                csr = Pools([(sb("a_cs%d" % i, [128, 2, 512], F32, s1), P.buf("a_cs%d" % i, dma=True)) for i in range(2)])
                tmp = Pools([(sb("a_tmp%d" % i, [128, 512], F32, s1), P.buf("a_tmp%d" % i)) for i in range(6)])
                vrow = Pools([(sb("a_vrow%d" % i, [128, 8, 65], BF16, s1), P.buf("a_vrow%d" % i, dma=True)) for i in range(3)])
                for tile in range(8):
                    hTt, B_hTt = hTr[tile % 2]

                    def a_ld(s4, tile=tile):
                        xt, B_x = fr.xs.get()
                        rb = r0 + tile * 512 + s4 * 128
                        P.dma("sp", xt[:], xw[rb:rb + 128, :], [], [B_x], B_x)
                        return xt, B_x
                    front_tile(fr, a_ld, hTt, B_hTt, False)
                    cst_, B_cs = csr.get()
                    P.dma("sp", cst_[:, 0, :], cosT[:, r0 + tile * 512: r0 + (tile + 1) * 512], [], [B_cs], B_cs)
                    P.dma("sp", cst_[:, 1, :], sinT[:, r0 + tile * 512: r0 + (tile + 1) * 512], [], [B_cs], B_cs)
                    cs_v, sn_v = cst_[:, 0, :], cst_[:, 1, :]
                    jobs = [(Wk, Wks, kT, B_kT, tile * 512, 4, False)]
                    if 2 <= tile < 6:
                        jobs.append((Wq, Wqs, qT, B_qT, (tile - 2) * 512, 0, True))
                    for (W1, W2, dst, B_dst, d0, bofs, isq) in jobs:
                        for cc in range(4):
                            p1, PB1 = PSR.get()
                            proj_fm(p1, PB1, W1, B_W, cc * 128, hTt, B_hTt)
                            p2, PB2 = PSR.get()
                            proj_fm(p2, PB2, W2, B_W, cc * 128, hTt, B_hTt)
                            t1, B1 = tmp.get()
                            t2, B2 = tmp.get()
                            bc1 = cv(C_BIN + 20 + bofs + cc)
                            bc2 = cv(C_BSW + bofs + cc)
                            P.op("dve", [PB1, B_colv, B_cs], [B1], lambda e, p1=p1, t1=t1, bc1=bc1, cs_v=cs_v: e.scalar_tensor_tensor(
                                out=t1[:], in0=p1[:], scalar=bc1, in1=cs_v, op0=ALU.add, op1=ALU.mult))
                            P.op("dve", [PB2, B_colv, B_cs], [B2], lambda e, p2=p2, t2=t2, bc2=bc2, sn_v=sn_v: e.scalar_tensor_tensor(
                                out=t2[:], in0=p2[:], scalar=bc2, in1=sn_v, op0=ALU.add, op1=ALU.mult))
                            if isq:
                                P.op("pool", [B1, B2], [B1], lambda e, t1=t1, t2=t2: e.tensor_tensor(
                                    out=t1[:], in0=t1[:], in1=t2[:], op=ALU.add))
                                P.op("act", [B1], [B_dst], lambda e, t1=t1, dst=dst, cc=cc, d0=d0: e.activation(
                                    out=dst[:, cc, d0:d0 + 512], in_=t1[:], func=AF.Copy, scale=0.125))
                            else:
                                P.op("pool", [B1, B2], [B_dst], lambda e, t1=t1, t2=t2, dst=dst, cc=cc, d0=d0: e.tensor_tensor(
                                    out=dst[:, cc, d0:d0 + 512], in0=t1[:], in1=t2[:], op=ALU.add))
                    for s4 in range(4):
                        pv, PBv = PSR.get()
                        proj_tm(pv[:], PBv, lambda kc, s4=s4, hTt=hTt: hTt[:, kc, s4 * 128:(s4 + 1) * 128], B_hTt, Wv, B_W, 0, 512)
                        vt, B_vt = tmp.get()
                        P.op("dve", [PBv, B_bvb], [B_vt], lambda e, pv=pv, vt=vt: e.tensor_tensor(
                            out=vt[:], in0=pv[:], in1=bvb[:], op=ALU.add))
                        vr, B_vr = vrow.get()
                        vcol = tile * 4 + s4
                        P.op("act", [B_vt, B_vm], [B_vr], lambda e, vt=vt, vr=vr, vcol=vcol: e.activation(
                            out=vr[:, :, 0:64], in_=vt[:].rearrange("p (h d) -> p h d", d=64), func=AF.Copy,
                            scale=vm[:, vcol:vcol + 1]))
                        vsrc = bass.AP(vm.tensor if hasattr(vm, "tensor") else vm, vcol, [[32, 128], [0, 8], [1, 1]])
                        P.op("pool", [B_vm], [B_vr], lambda e, vr=vr, vsrc=vsrc: e.tensor_copy(
                            out=vr[:, :, 64:65], in_=vsrc), n=8)
                        tk = tile * 512 + s4 * 128
                        P.dma("pool", v_s[tk:tk + 128, :, :], vr[:], [B_vr], [], B_vr)
                P.barrier()
            with contextlib.ExitStack() as s2:
                mask4 = sb("a_mask4", [128, 512], BF16, s2)
                B_m4 = P.buf("a_mask4")
                for i in range(4):
                    src = mA_b if i % 2 == 0 else mB_b
                    P.op("dve", [B_cstb], [B_m4], lambda e, i=i, src=src: e.tensor_copy(
                        out=mask4[:, i * 128:(i + 1) * 128], in_=src), n=128)
                acc = sb("a_acc", [65, 8, 2048], F32, s2)
                B_acc = [P.buf("a_acc%d" % i) for i in range(2)]
                vext = Pools([(sb("a_vext%d" % i, [128, 8, 65], BF16, s2), P.buf("a_vext%d" % i, dma=True)) for i in range(6)])
                pt_ = Pools([(sb("a_P%d" % i, [128, 512], BF16, s2), P.buf("a_P%d" % i)) for i in range(10)])
                obst = Pools([(sb("a_obst%d" % i, [64, 512], BF16, s2), P.buf("a_obst%d" % i, dma=True)) for i in range(2)])
                vts = att_vtiles()
                prev = None
                pend2 = None
                for vi, (r, rho, m, start, nb) in enumerate(vts):
                    ve, B_ve = vext.get()
                    P.dma("sp", ve[:], v_s[start:start + 127 * r + 1:r, :, :], [], [B_ve], B_ve, c=2.5)
                    if m == 0:
                        prev = (ve, B_ve)
                        continue
                    bidx = m - 1
                    veA, B_veA = prev
                    veB, B_veB = ve, B_ve
                    prev = (ve, B_ve)
                    jq0 = 1024 // r
                    qst = rho + r * (jq0 + 128 * bidx)
                    qsl = slice(qst - 1024, qst - 1024 + 127 * r + 1, r)
                    kA = slice(qst - 64 * r, qst - 64 * r + 127 * r + 1, r)
                    kB = slice(qst + 64 * r, qst + 64 * r + 127 * r + 1, r)
                    Ps = {}
                    for hg in range(2):
                        pscs = [PSR.get(), PSR.get()]
                        for hh2 in range(2):
                            for ti, ksl in enumerate((kA, kB)):
                                for pair in range(2):
                                    psc, PBs = pscs[pair]
                                    h = hg * 4 + hh2 * 2 + pair
                                    ch, pb = h // 2, (h % 2) * 64
                                    c0 = hh2 * 256 + ti * 128
                                    P.op("pe", [B_kT, B_qT], [PBs], lambda e, psc=psc, c0=c0, ch=ch, pb=pb, ksl=ksl, qsl=qsl: e.matmul(
                                        psc[:, c0:c0 + 128], kT[pb:pb + 64, ch, ksl], qT[pb:pb + 64, ch, qsl],
                                        start=True, stop=True), n=300)
                        for pair in range(2):
                            psc, PBs = pscs[pair]
                            pp, B_pp = pt_.get()
                            P.op("act", [PBs], [B_pp], lambda e, psc=psc, pp=pp: e.activation(
                                out=pp[:], in_=psc[:], func=AF.Exp), tb='e')
                            P.op("dve", [B_pp, B_m4], [B_pp], lambda e, pp=pp: e.tensor_tensor(
                                out=pp[:], in0=pp[:], in1=mask4[:], op=ALU.mult))
                            for hh2 in range(2):
                                Ps[hg * 4 + hh2 * 2 + pair] = (pp, B_pp, hh2 * 256)

                    def stage2(Ps=Ps, veA=veA, B_veA=B_veA, veB=veB, B_veB=B_veB, qsl=qsl, r=r):
                        for hg in range(2):
                            po, PBo = PSR.get()
                            for hh in range(4):
                                h = hg * 4 + hh
                                pp, B_pp, c0 = Ps[h]
                                P.op("pe", [B_pp, B_veA], [PBo], lambda e, po=po, hh=hh, h=h, pp=pp, c0=c0: e.matmul(
                                    po[0:65, hh * 128:(hh + 1) * 128], veA[:, h, :], pp[:, c0:c0 + 128], start=True, stop=False), n=128)
                                P.op("pe", [B_pp, B_veB], [PBo], lambda e, po=po, hh=hh, h=h, pp=pp, c0=c0: e.matmul(
                                    po[0:65, hh * 128:(hh + 1) * 128], veB[:, h, :], pp[:, c0 + 128:c0 + 256], start=False, stop=True), n=128)
                            accv = acc[:, hg * 4:(hg + 1) * 4, qsl]
                            pov = po[0:65, :].rearrange("p (h q) -> p h q", h=4)
                            if r == 1:
                                P.op("dve", [PBo], [B_acc[hg]], lambda e, accv=accv, pov=pov: e.tensor_copy(out=accv, in_=pov))
                            else:
                                P.op("dve", [PBo, B_acc[hg]], [B_acc[hg]], lambda e, accv=accv, pov=pov: e.tensor_tensor(
                                    out=accv, in0=accv, in1=pov, op=ALU.add))
                    if pend2 is not None:
                        pend2()
                    pend2 = stage2
                if pend2 is not None:
                    pend2()
                    pend2 = None
                for hg in range(2):
                    hs = slice(hg * 4, (hg + 1) * 4)
                    P.op("act", [B_acc[hg]], [B_acc[hg]], lambda e, hs=hs: e.activation(
                        out=acc[64:65, hs, :], in_=acc[64:65, hs, :], func=AF.Ln), n=8192, tb='e')
                    P.op("act", [B_acc[hg]], [B_acc[hg]], lambda e, hs=hs: e.activation(
                        out=acc[64:65, hs, :], in_=acc[64:65, hs, :], func=AF.Exp, scale=-1.0), n=8192, tb='e')
                    for hh in range(4):
                        h = hg * 4 + hh
                        for q4 in range(4):
                            pd, PBd = PSR.get()
                            P.op("pe", [B_acc[hg], B_cst], [PBd], lambda e, pd=pd, h=h, q4=q4: e.matmul(
                                pd[0:64, :], cst[64:65, 1152:1216], acc[64:65, h, q4 * 512:(q4 + 1) * 512],
                                start=True, stop=True))
                            ot, B_ot = obst.get()
                            P.op("dve", [PBd, B_acc[hg]], [B_ot], lambda e, pd=pd, ot=ot, h=h, q4=q4: e.tensor_tensor(
                                out=ot[:], in0=acc[0:64, h, q4 * 512:(q4 + 1) * 512], in1=pd[0:64, :], op=ALU.mult))
                            c0 = hf * 2048 + q4 * 512
                            P.dma("pool", ob_s[h * 64:(h + 1) * 64, c0:c0 + 512], ot[:], [B_ot], [], B_ot)
                P.barrier()

    pa_st.close()

    class Scan:
        def __init__(self, st, tag):
            f = lambda n, dt=F32, sh=(128, 512), k=1: Pools([(sb("%s_%s%d" % (tag, n, i), list(sh), dt, st), P.buf("%s_%s%d" % (tag, n, i)))
                                                              for i in range(k)])
            self.t_sg, self.t_f, self.t_g, self.t_b = f("sg", k=4), f("f", k=2), f("g", k=2), f("b", k=2)
            self.t_eb, self.t_enb, self.t_kk, self.t_qs = f("eb", k=2), f("enb", k=2), f("kk", k=2), f("qs", k=2)
            NS = 2
            self.qT = [[sb("%s_qT%d_%d" % (tag, z, h), [128, 512], BF16, st) for h in range(4)] for z in range(NS)]
            self.kTt = [[sb("%s_kT%d_%d" % (tag, z, h), [128, 512], BF16, st) for h in range(4)] for z in range(NS)]
            self.ebk = [[sb("%s_ebk%d_%d" % (tag, z, h), [128, 8], F32, st) for h in range(4)] for z in range(NS)]
            self.B_q = [[P.buf("q") for h in range(4)] for z in range(NS)]
            self.B_k = [[P.buf("k") for h in range(4)] for z in range(NS)]
            self.B_e = [[P.buf("e") for h in range(4)] for z in range(NS)]
            self.vtok = [sb("%s_vtok%d" % (tag, z), [128, 4, 512], BF16, st) for z in range(NS)]
            self.ktok = [sb("%s_ktok%d" % (tag, z), [128, 4, 512], BF16, st) for z in range(NS)]
            self.B_vtok = [[P.buf("vtok") for i in range(4)] for z in range(NS)]
            self.B_ktok = [[P.buf("ktok") for i in range(4)] for z in range(NS)]
            self.vtmp = f("vtmp", k=2)
            self.A = f("A", BF16, k=3)
            self.S = [sb("%s_S%d" % (tag, h), [128, 128], F32, st) for h in range(4)]
            self.S1 = [sb("%s_S1%d" % (tag, h), [128, 128], F32, st) for h in range(4)]
            self.Sb = [sb("%s_Sb%d" % (tag, h), [128, 128], BF16, st) for h in range(4)]
            self.B_S = [P.buf("S") for h in range(4)]
            self.B_S1 = [P.buf("S1") for h in range(4)]
            self.B_Sb = [P.buf("Sb") for h in range(4)]
            for h in range(4):
                P.op("dve", [], [self.B_S[h]], lambda e, h=h: e.memset(self.S[h][:], 0.0))
                P.op("pool", [], [self.B_Sb[h]], lambda e, h=h: e.memset(self.Sb[h][:], 0.0))
            self.bvb = sb(tag + "_bvb", [128, 512], F32, st)
            self.B_bvb = P.buf(tag + "_bvb", dma=True)
            P.dma("sp", self.bvb[:], b_in_row[1536:2048].partition_broadcast(128), [], [self.B_bvb], self.B_bvb)

        def prepH(self, z, h, hTt, B_hTt, W, B_W, d):
            pq, PBq = PSR.get()
            proj_fm(pq, PBq, W, B_W, h * 128, hTt, B_hTt)
            yield
            pf, PBf = PSR.get()
            proj_fm(pf, PBf, W, B_W, 512 + h * 128, hTt, B_hTt)
            sg, B_sg = self.t_sg.get()
            bq = cv(C_BIN + 0 + h)
            bf = cv(C_BIN + 4 * (1 + d) + h)
            P.op("act", [PBq, B_colv], [B_sg], lambda e: e.activation(out=sg[:], in_=pq[:], func=AF.Sigmoid, bias=bq), tb='s')
            sf, B_sf = self.t_sg.get()
            P.op("act", [PBf, B_colv], [B_sf], lambda e: e.activation(out=sf[:], in_=pf[:], func=AF.Sigmoid, bias=bf), tb='s')
            yield
            qs, B_qs = self.t_qs.get()
            P.op("dve", [PBq, B_sg, B_colv], [B_qs], lambda e: e.scalar_tensor_tensor(
                out=qs[:], in0=pq[:], scalar=bq, in1=sg[:], op0=ALU.add, op1=ALU.mult))
            ff, B_ff = self.t_f.get()
            P.op("dve", [B_sf, B_colv], [B_ff], lambda e: e.tensor_scalar(
                out=ff[:], in0=sf[:], scalar1=cv(C_OML + d * 4 + h), scalar2=cv(C_LB + d * 4 + h),
                op0=ALU.mult, op1=ALU.add))
            yield
            g, B_g = self.t_g.get()
            P.op("act", [B_ff], [B_g], lambda e: e.activation(out=g[:], in_=ff[:], func=AF.Ln), tb='e')
            kk, B_kk = self.t_kk.get()
            P.op("pool", [B_ff], [B_kk], lambda e: e.tensor_scalar(
                out=kk[:], in0=ff[:], scalar1=-1.0, scalar2=1.0, op0=ALU.mult, op1=ALU.add), c=0.62)
            yield
            b, B_b = self.t_b.get()
            P.op("dve", [B_g, B_cst], [B_b], lambda e: e.tensor_tensor_scan(
                out=b[:], data0=cmask_f, data1=g[:], initial=0.0, op0=ALU.mult, op1=ALU.add), n=1024)
            yield
            eb, B_eb = self.t_eb.get()
            P.op("act", [B_b], [B_eb], lambda e: e.activation(out=eb[:], in_=b[:], func=AF.Exp), tb='e')
            enb, B_enb = self.t_enb.get()
            P.op("act", [B_b], [B_enb], lambda e: e.activation(out=enb[:], in_=b[:], func=AF.Exp, scale=-1.0), tb='e')
            yield
            P.op("pool", [B_qs, B_eb], [self.B_q[z][h]], lambda e: e.tensor_tensor(
                out=self.qT[z][h][:], in0=qs[:], in1=eb[:], op=ALU.mult))
            P.op("pool", [B_kk, B_enb], [self.B_k[z][h]], lambda e: e.tensor_tensor(
                out=self.kTt[z][h][:], in0=kk[:], in1=enb[:], op=ALU.mult))
            P.op("pool", [B_eb], [self.B_e[z][h]], lambda e: e.tensor_copy(out=self.ebk[z][h][:], in_=eb[:, 63:512:64]), n=8)

        def prepV(self, z, sub, hTt, B_hTt, W, B_W, vm_t, B_vm, vmcol):
            pv, PBv = PSR.get()
            proj_tm(pv[:], PBv, lambda kc: hTt[:, kc, sub * 128:(sub + 1) * 128], B_hTt, W, B_W, 1024, 512)
            vt, B_vt = self.vtmp.get()
            P.op("dve", [PBv, self.B_bvb], [B_vt], lambda e: e.tensor_tensor(out=vt[:], in0=pv[:], in1=self.bvb[:], op=ALU.add))
            P.op("act", [B_vt, B_vm], [self.B_vtok[z][sub]], lambda e: e.activation(
                out=self.vtok[z][:, sub, :], in_=vt[:], func=AF.Copy, scale=vm_t[:, vmcol:vmcol + 1]))

        def prepK(self, z):
            for sub in range(4):
                pk, PBk = PSR.get()
                for h in range(4):
                    P.op("pe", [self.B_k[z][h], B_cstb], [PBk], lambda e, h=h, pk=pk, sub=sub: e.matmul(
                        pk[:, h * 128:(h + 1) * 128], self.kTt[z][h][:, sub * 128:(sub + 1) * 128], ident_b,
                        start=True, stop=True), n=128)
                P.op("act", [PBk], [self.B_ktok[z][sub]], lambda e, pk=pk, sub=sub: e.activation(
                    out=self.ktok[z][:, sub, :], in_=pk[:], func=AF.Copy))

        def scan_sub(self, z, sub, res):
            qT, kTt, ebk, vtok, ktok = self.qT[z], self.kTt[z], self.ebk[z], self.vtok[z], self.ktok[z]
            B_q, B_k, B_e, B_vtok, B_ktok = self.B_q[z], self.B_k[z], self.B_e[z], self.B_vtok[z], self.B_ktok[z]
            psc, PBs = PSR.get()
            for h in range(4):
                P.op("pe", [B_k[h], B_q[h]], [PBs], lambda e, h=h: e.matmul(
                    psc[:, h * 128:(h + 1) * 128], kTt[h][:, sub * 128:(sub + 1) * 128],
                    qT[h][:, sub * 128:(sub + 1) * 128], start=True, stop=True), n=128)
            pus = [PSR.get(), PSR.get()]
            for h in range(4):
                for c in range(2):
                    pu, PBu = pus[c]
                    rows = slice(c * 64, (c + 1) * 64)
                    P.op("pe", [B_ktok[sub], B_vtok[sub]], [PBu], lambda e, pu=pu, h=h, rows=rows: e.matmul(
                        pu[:, h * 128:(h + 1) * 128], ktok[rows, sub, h * 128:(h + 1) * 128],
                        vtok[rows, sub, h * 128:(h + 1) * 128], start=True, stop=True), n=128)
            yield
            A, B_A = self.A.get()
            tri4 = bass.AP(cstb.tensor if hasattr(cstb, "tensor") else cstb, 256, [[1280, 128], [0, 4], [1, 128]])
            P.op("dve", [PBs, B_cstb], [B_A], lambda e: e.tensor_tensor(
                out=A[:].rearrange("p (h t) -> p h t", h=4), in0=psc[:].rearrange("p (h t) -> p h t", h=4),
                in1=tri4, op=ALU.mult))
            yield
            po, PBo = PSR.get()
            res.append((po, PBo))
            for h in range(4):
                P.op("pe", [B_A, B_vtok[sub]], [PBo], lambda e, h=h: e.matmul(
                    po[:, h * 128:(h + 1) * 128], A[:, h * 128:(h + 1) * 128], vtok[:, sub, h * 128:(h + 1) * 128],
                    start=(h == 0), stop=False, skip_group_check=True), n=128)
            for c in range(2):
                pu, PBu = pus[c]
                rows = slice(c * 64, (c + 1) * 64)
                toks = slice(sub * 128 + c * 64, sub * 128 + (c + 1) * 64)
                for h in range(4):
                    last = (c == 1 and h == 3)
                    P.op("pe", [B_q[h], self.B_Sb[h]], [PBo], lambda e, h=h, rows=rows, toks=toks, last=last: e.matmul(
                        po[rows, h * 128:(h + 1) * 128], qT[h][:, toks], self.Sb[h][:],
                        start=False, stop=last, skip_group_check=True), n=128)
                ci = sub * 2 + c
                for h in range(4):
                    e_ap = ebk[h][:, ci:ci + 1]
                    P.op("pool", [self.B_S[h], B_e[h]], [self.B_S1[h]], lambda e, h=h, e_ap=e_ap: e.tensor_scalar(
                        out=self.S1[h][:], in0=self.S[h][:], scalar1=e_ap, scalar2=0.0, op0=ALU.mult, op1=ALU.add), c=0.34)
                yield
                for h in range(4):
                    e_ap = ebk[h][:, ci:ci + 1]
                    P.op("dve", [PBu, self.B_S1[h], B_e[h]], [self.B_Sb[h]], lambda e, pu=pu, h=h, e_ap=e_ap: e.scalar_tensor_tensor(
                        out=self.Sb[h][:], in0=pu[:, h * 128:(h + 1) * 128], scalar=e_ap, in1=self.S1[h][:],
                        op0=ALU.mult, op1=ALU.add), n=128)
                    P.op("dve", [PBu, self.B_S1[h], B_e[h]], [self.B_S[h]], lambda e, pu=pu, h=h, e_ap=e_ap: e.scalar_tensor_tensor(
                        out=self.S[h][:], in0=pu[:, h * 128:(h + 1) * 128], scalar=e_ap, in1=self.S1[h][:],
                        op0=ALU.mult, op1=ALU.add), n=128)
                yield

    def scan_phase(d):
        with contextlib.ExitStack() as st:
            W = sb("s_W%d" % d, [128, 8, 1536], BF16, st)
            B_W = P.buf("s_W")
            with contextlib.ExitStack() as s1:
                stage = mk_stage(s1, 4)
                fcol = 1024 if d == 1 else 512
                load_weight(s1, W[:, :, 0:512], B_W, w_in[:, 0:512], 8, 512, stage=stage)
                load_weight(s1, W[:, :, 512:1024], B_W, w_in[:, fcol:fcol + 512], 8, 512, stage=stage)
                load_weight(s1, W[:, :, 1024:1536], B_W, w_in[:, 1536:2048], 8, 512, stage=stage)
                P.barrier()
            fr = Front(st, 3, C_G1, C_SH1, "s%d" % d)
            fr.pool_xn = True
            sc = Scan(st, "s%d" % d)
            vm = sb("s_vm%d" % d, [128, 36], F32, st)
            B_vm = P.buf("s_vm", dma=True)
            P.dma("sp", vm[:], vmB if d == 1 else vmC, [], [B_vm], B_vm)
            hTr = [(sb("s_hT%d_%d" % (d, i), [128, 8, 512], BF16, st), P.buf("s_hT%d" % i)) for i in range(2)]
            ost = Pools([(sb("s_ost%d_%d" % (d, i), [128, 512], F32, st), P.buf("s_ost%d" % i, dma=True)) for i in range(2)])
            top = 1024 + OWN + HH
            base = 1024 - HH
            flip = (d == 1)
            o_s = obw_s if d == 1 else ofw_s
            NT = 9

            class FrontStream:
                def __init__(self):
                    self.pend = None

                def step(self, t, s4):
                    hTt, B_hTt = hTr[t % 2]
                    i = t * 4 + s4
                    rb = (top - 128 * (i + 1)) if flip else (base + 128 * i)
                    xt, B_x = fr.xs.get()
                    P.dma("sp", xt[:], xw[rb:rb + 128, :], [], [B_x], B_x)
                    a = (fr.run_a(xt, B_x), t, s4)
                    self.flush()
                    self.pend = a

                def flush(self):
                    if self.pend is not None:
                        pa_, pt, ps4 = self.pend
                        hTt, B_hTt = hTr[pt % 2]
                        fr.run_b(pa_, hTt[:, :, ps4 * 128:(ps4 + 1) * 128], B_hTt, flip)
                        self.pend = None

            fs = FrontStream()

            def drain(*gens):
                gens = [g for g in gens if g is not None]
                while gens:
                    for g in list(gens):
                        try:
                            next(g)
                        except StopIteration:
                            gens.remove(g)

            def gen_front(t, s4):
                fs.step(t, s4)
                yield

            def gen_prepV(z, s4, hn, B_hn, col):
                yield
                yield
                sc.prepV(z, s4, hn, B_hn, W, B_W, vm, B_vm, col)
                yield

            def gen_out(res, i):
                po, PBo = res[0]
                if i >= 4:
                    ot, B_ot = ost.get()
                    P.op("act", [PBo], [B_ot], lambda e: e.activation(
                        out=ot[:], in_=po[:], func=AF.Copy, scale=float(128 ** -0.5)))
                    row = (i - 4) * 128
                    P.dma("pool", o_s[row:row + 128, :], ot[:], [B_ot], [], B_ot)

            for s4 in range(4):
                fs.step(0, s4)
            fs.flush()
            for s4 in range(4):
                drain(sc.prepH(0, s4, hTr[0][0], hTr[0][1], W, B_W, d))
                sc.prepV(0, s4, hTr[0][0], hTr[0][1], W, B_W, vm, B_vm, s4)
            for s4 in range(4):
                fs.step(1, s4)
            fs.flush()
            for t in range(NT):
                z = t % 2
                sc.prepK(z)
                for s4 in range(4):
                    res = []
                    g_f = gen_front(t + 2, s4) if t + 2 < NT else None
                    g_p = g_v = None
                    if t + 1 < NT:
                        hn, B_hn = hTr[(t + 1) % 2]
                        g_p = sc.prepH(1 - z, s4, hn, B_hn, W, B_W, d)
                        g_v = gen_prepV(1 - z, s4, hn, B_hn, (t + 1) * 4 + s4)
                    drain(g_f, g_p, sc.scan_sub(z, s4, res), g_v)
                    gen_out(res, t * 4 + s4)
                fs.flush()
            P.barrier()

    if upto >= 2:
        scan_phase(1)
    if upto >= 3:
        scan_phase(0)

    if upto >= 3:
        with contextlib.ExitStack() as st:
            W = sb("c_W", [128, 8, 2560], BF16, st)
            Wa = sb("c_Wa", [128, 4, D], BF16, st)
            Wb = sb("c_Wb", [128, 4, D], BF16, st)
            Wo = sb("c_Wo", [128, 8, D], BF16, st)
            B_W = P.buf("c_W")
            with contextlib.ExitStack() as s1:
                stage = mk_stage(s1, 4)
                load_weight(s1, W[:, :, 0:512], B_W, w_in[:, 2048:2560], 8, 512, stage=stage)
                load_weight(s1, W[:, :, 512:2560], B_W, w_in[:, 4096:6144], 8, 2048, stage=stage)
                load_weight(s1, Wa, B_W, w_bra, 4, D, stage=stage)
                load_weight(s1, Wb, B_W, w_brb, 4, D, stage=stage)
                gate1_bc = sb("gate1_bc", [128, D], F32, s1)
                B_g1 = P.buf("g1bc", dma=True)
                P.dma("sp", gate1_bc[:], g_s[0], [], [B_g1], B_g1)
                load_weight(s1, Wo, B_W, w_out, 8, D, gate_bc=gate1_bc, B_gate=B_g1, stage=stage)
                P.barrier()
            fr = Front(st, 3, C_G1, C_SH1, "c")
            hTr = [(sb("c_hT%d" % i, [128, 8, 512], BF16, st), P.buf("c_hT%d" % i)) for i in range(2)]
            sgT = sb("c_sgT", [128, 4, 512], BF16, st)
            B_sgT = P.buf("c_sgT")
            oaT = sb("c_oaT", [128, 4, 512], BF16, st)
            B_oaT = P.buf("c_oaT")
            obTr = Pools([(sb("c_obT%d" % i, [128, 4, 512], BF16, st), P.buf("c_obT%d" % i, dma=True)) for i in range(2)])
            mT = sb("c_mT", [128, 8, 512], BF16, st)
            B_mT = P.buf("c_mT")
            obw = Pools([(sb("c_obw%d" % i, [128, 512], F32, st), P.buf("c_obw%d" % i, dma=True)) for i in range(2)])
            ofw = Pools([(sb("c_ofw%d" % i, [128, 512], F32, st), P.buf("c_ofw%d" % i, dma=True)) for i in range(2)])
            osbr = Pools([(sb("c_osb%d" % i, [128, 512], F32, st), P.buf("c_osb%d" % i)) for i in range(2)])
            onbr = Pools([(sb("c_onb%d" % i, [128, 512], BF16, st), P.buf("c_onb%d" % i)) for i in range(2)])
            gstr = Pools([(sb("c_gst%d" % i, [128, 16], F32, st), P.buf("c_gst%d" % i)) for i in range(2)])
            gtmp = Pools([(sb("c_gt%d" % i, [128, 512], F32, st), P.buf("c_gt%d" % i)) for i in range(4)])
            x1st = Pools([(sb("c_x1%d" % i, [128, D], F32, st), P.buf("c_x1%d" % i, dma=True)) for i in range(2)])
            junk = sb("c_junk", [128, 512], BF16, st)
            B_junk = P.buf("c_junk")

            class FS:
                def __init__(self):
                    self.pend = None

                def step(self, t, s4):
                    rb = 1024 + t * 512 + s4 * 128
                    xt, B_x = fr.xs.get()
                    P.dma("sp", xt[:], xw[rb:rb + 128, :], [], [B_x], B_x)
                    a = (fr.run_a(xt, B_x), t, s4)
                    self.flush()
                    self.pend = a

                def flush(self):
                    if self.pend is not None:
                        pa_, pt, ps4 = self.pend
                        hTt, B_hTt = hTr[pt % 2]
                        fr.run_b(pa_, hTt[:, :, ps4 * 128:(ps4 + 1) * 128], B_hTt, False)
                        self.pend = None

            fs = FS()
            for s4 in range(4):
                fs.step(0, s4)
            fs.flush()
            for tile in range(8):
                hTt, B_hTt = hTr[tile % 2]
                t0 = tile * 512
                obT, B_obT = obTr.get()
                P.dma("sp", obT[:], ob_s.rearrange("(c p) t -> p c t", p=128)[:, :, t0:t0 + 512], [], [B_obT], B_obT)
                for h in range(4):
                    pg, PBg = PSR.get()
                    proj_fm(pg, PBg, W, B_W, h * 128, hTt, B_hTt)
                    g1, B1 = gtmp.get()
                    bg = cv(C_BIN + 16 + h)
                    P.op("act", [PBg, B_colv], [B1], lambda e, pg=pg, g1=g1, bg=bg: e.activation(
                        out=g1[:], in_=pg[:], func=AF.Sigmoid, bias=bg), tb='s')
                    P.op("dve", [PBg, B1, B_colv], [B_sgT], lambda e, pg=pg, g1=g1, bg=bg, h=h: e.scalar_tensor_tensor(
                        out=sgT[:, h, :], in0=pg[:], scalar=bg, in1=g1[:], op0=ALU.add, op1=ALU.mult))
                for s4 in range(4):
                    tt = t0 + s4 * 128
                    ow, B_ow = obw.get()
                    P.dma("sp", ow[:], obw_s[OWN - 128 - tt: OWN - tt, :], [], [B_ow], B_ow)
                    of, B_of = ofw.get()
                    P.dma("sp", of[:], ofw_s[tt:tt + 128, :], [], [B_of], B_of)
                    po, PBo = PSR.get()
                    P.op("pe", [B_ow, B_cst], [PBo], lambda e, po=po, ow=ow: e.matmul(po[:], J_f, ow[:], start=True, stop=True), n=2048)
                    osb, B_osb = osbr.get()
                    P.op("dve", [PBo, B_of], [B_osb], lambda e, po=po, of=of, osb=osb: e.tensor_tensor(
                        out=osb[:], in0=po[:], in1=of[:], op=ALU.add))
                    P.op("pool", [B_osb], [B_junk], lambda e, osb=osb: e.tensor_tensor(out=junk[:], in0=osb[:], in1=osb[:], op=ALU.mult))
                    gst, B_gst = gstr.get()
                    P.op("dve", [B_junk], [B_gst], lambda e, gst=gst: e.reduce_sum(
                        out=gst[:, 0:4], in_=junk[:].rearrange("p (h d) -> p h d", h=4), axis=mybir.AxisListType.X), n=512)
                    P.op("dve", [B_gst], [B_gst], lambda e, gst=gst: e.tensor_scalar(
                        out=gst[:, 4:8], in0=gst[:, 0:4], scalar1=1.0 / 128, scalar2=EPS, op0=ALU.mult, op1=ALU.add), n=4)
                    P.op("act", [B_gst], [B_gst], lambda e, gst=gst: e.activation(out=gst[:, 8:12], in_=gst[:, 4:8], func=AF.Ln), n=4, tb='e')
                    P.op("act", [B_gst], [B_gst], lambda e, gst=gst: e.activation(
                        out=gst[:, 12:16], in_=gst[:, 8:12], func=AF.Exp, scale=-0.5), n=4, tb='e')
                    onb, B_onb = onbr.get()
                    for h in range(4):
                        eng = "act" if h % 2 == 0 else "pool"
                        if eng == "act":
                            P.op("act", [B_osb, B_gst], [B_onb], lambda e, h=h, osb=osb, onb=onb, gst=gst: e.activation(
                                out=onb[:, h * 128:(h + 1) * 128], in_=osb[:, h * 128:(h + 1) * 128], func=AF.Copy,
                                scale=gst[:, 12 + h:13 + h]), n=128)
                        else:
                            P.op("pool", [B_osb, B_gst], [B_onb], lambda e, h=h, osb=osb, onb=onb, gst=gst: e.tensor_scalar(
                                out=onb[:, h * 128:(h + 1) * 128], in0=osb[:, h * 128:(h + 1) * 128],
                                scalar1=gst[:, 12 + h:13 + h], scalar2=0.0, op0=ALU.mult, op1=ALU.add), n=128)
                    ptp, PBt = PSR.get()
                    for h in range(4):
                        P.op("pe", [B_onb, B_cstb], [PBt], lambda e, ptp=ptp, h=h, onb=onb: e.matmul(
                            ptp[:, h * 128:(h + 1) * 128], onb[:, h * 128:(h + 1) * 128], ident_b, start=True, stop=True), n=128)
                    for h in range(4):
                        P.op("dve", [PBt, B_colv, B_sgT], [B_oaT], lambda e, ptp=ptp, h=h, s4=s4: e.scalar_tensor_tensor(
                            out=oaT[:, h, s4 * 128:(s4 + 1) * 128], in0=ptp[:, h * 128:(h + 1) * 128],
                            scalar=cv(C_GN + h), in1=sgT[:, h, s4 * 128:(s4 + 1) * 128], op0=ALU.mult, op1=ALU.mult), n=128)
                for cc in range(8):
                    pga, PBga = PSR.get()
                    proj_fm(pga, PBga, W, B_W, 512 + cc * 128, hTt, B_hTt)
                    pgb, PBgb = PSR.get()
                    proj_fm(pgb, PBgb, W, B_W, 1536 + cc * 128, hTt, B_hTt)
                    pa, PBa = PSR.get()
                    proj_fm(pa, PBa, Wa, B_W, cc * 128, oaT, B_oaT, nkc=4)
                    pb_, PBb = PSR.get()
                    proj_fm(pb_, PBb, Wb, B_W, cc * 128, obT, B_obT, nkc=4)
                    ga, B_ga = gtmp.get()
                    gb, B_gb = gtmp.get()
                    P.op("act", [PBga, B_colv], [B_ga], lambda e, pga=pga, ga=ga, cc=cc: e.activation(
                        out=ga[:], in_=pga[:], func=AF.Sigmoid, bias=cv(C_BIN + 32 + cc)), tb='s')
                    P.op("act", [PBgb, B_colv], [B_gb], lambda e, pgb=pgb, gb=gb, cc=cc: e.activation(
                        out=gb[:], in_=pgb[:], func=AF.Sigmoid, bias=cv(C_BIN + 40 + cc)), tb='s')
                    P.op("dve", [PBa, B_ga], [B_ga], lambda e, pa=pa, ga=ga: e.tensor_tensor(
                        out=ga[:], in0=pa[:], in1=ga[:], op=ALU.mult))
                    P.op("dve", [PBb, B_gb], [B_gb], lambda e, pb_=pb_, gb=gb: e.tensor_tensor(
                        out=gb[:], in0=pb_[:], in1=gb[:], op=ALU.mult))
                    P.op("pool", [B_ga, B_gb], [B_mT], lambda e, ga=ga, gb=gb, cc=cc: e.tensor_tensor(
                        out=mT[:, cc, :], in0=ga[:], in1=gb[:], op=ALU.add))
                    if tile + 1 < 8 and cc % 2 == 1:
                        fs.step(tile + 1, cc // 2)
                fs.flush()
                for s4 in range(4):
                    xo, B_xo = x1st.get()
                    rb = 1024 + t0 + s4 * 128
                    P.dma("sp", xo[:], xw[rb:rb + 128, :], [], [B_xo], B_xo)
                    for half in range(2):
                        px, PBx = PSR.get()
                        proj_tm(px[:], PBx, lambda kc, s4=s4: mT[:, kc, s4 * 128:(s4 + 1) * 128], B_mT, Wo, B_W, half * 512, 512)
                        P.op("dve", [PBx, B_xo], [B_xo], lambda e, px=px, xo=xo, half=half: e.tensor_tensor(
                            out=xo[:, half * 512:(half + 1) * 512], in0=px[:], in1=xo[:, half * 512:(half + 1) * 512],
                            op=ALU.add))
                    tt = t0 + s4 * 128
                    P.dma("pool", x1_s[tt:tt + 128, :], xo[:], [B_xo], [], B_xo)
            P.barrier()

    if upto >= 4:
        with contextlib.ExitStack() as st:
            Wi = sb("d_Wi", [128, 8, 2 * FFN], BF16, st)
            Wf = sb("d_Wf", [128, NJ, D], BF16, st)
            B_W = P.buf("d_W")
            with contextlib.ExitStack() as s1:
                stage = mk_stage(s1, 3)
                load_weight(s1, Wi, B_W, w_fi, 8, 2 * FFN, stage=stage)
                gate2_bc = sb("gate2_bc", [128, D], F32, s1)
                B_g2 = P.buf("g2bc", dma=True)
                P.dma("sp", gate2_bc[:], g_s[1], [], [B_g2], B_g2)
                load_weight(s1, Wf, B_W, w_fo, NJ, D, gate_bc=gate2_bc, B_gate=B_g2, stage=stage)
                P.barrier()
            fngb = sb("d_fng", [128, D], F32, st)
            B_fng = P.buf("d_fng", dma=True)
            P.dma("sp", fngb[:], fng.partition_broadcast(128), [], [B_fng], B_fng)
            fr = Front(st, 2, C_G2, C_SH2, "d", nxn=1)
            hTr = [(sb("d_hT%d" % i, [128, 8, 512], BF16, st), P.buf("d_hT%d" % i)) for i in range(2)]
            yT = sb("d_yT", [128, NJ, 512], BF16, st)
            B_yT = P.buf("d_yT")
            gtmp = Pools([(sb("d_gt%d" % i, [128, 512], F32, st), P.buf("d_gt%d" % i)) for i in range(2)])
            res = Pools([(sb("d_res%d" % i, [128, D], F32, st), P.buf("d_res%d" % i, dma=True)) for i in range(2)])

            def d_front(tile):
                hTt, B_hTt = hTr[tile % 2]

                def d_ld(s4):
                    t0 = tile * 512 + s4 * 128
                    xt, B_x = fr.xs.get()
                    P.dma("sp", xt[:], x1_s[t0:t0 + 128, :], [], [B_x], B_x)
                    return xt, B_x
                for s4 in range(4):
                    xt, B_x = d_ld(s4)
                    fr.run(xt, B_x, hTt[:, :, s4 * 128:(s4 + 1) * 128], B_hTt, False)

            d_front(0)
            for tile in range(8):
                hTt, B_hTt = hTr[tile % 2]
                for j in range(NJ):
                    pg, PBg = PSR.get()
                    proj_fm(pg, PBg, Wi, B_W, j * 128, hTt, B_hTt)
                    pu, PBu = PSR.get()
                    proj_fm(pu, PBu, Wi, B_W, FFN + j * 128, hTt, B_hTt)
                    g1, B1 = gtmp.get()
                    P.op("act", [PBg], [B1], lambda e, pg=pg, g1=g1: e.activation(out=g1[:], in_=pg[:], func=AF.Sigmoid), tb='s')
                    P.op("dve", [PBg, B1], [B1], lambda e, pg=pg, g1=g1: e.tensor_tensor(
                        out=g1[:], in0=pg[:], in1=g1[:], op=ALU.mult))
                    P.op("dve", [PBu, B1], [B_yT], lambda e, pu=pu, g1=g1, j=j: e.tensor_tensor(
                        out=yT[:, j, :], in0=pu[:], in1=g1[:], op=ALU.mult))
                if tile + 1 < 8:
                    d_front(tile + 1)
                for s4 in range(4):
                    xo, B_xo = res.get()
                    t0 = tile * 512 + s4 * 128
                    P.dma("sp", xo[:], x1_s[t0:t0 + 128, :], [], [B_xo], B_xo)
                    for half in range(2):
                        px, PBx = PSR.get()
                        proj_tm(px[:], PBx, lambda kc, s4=s4: yT[:, kc, s4 * 128:(s4 + 1) * 128], B_yT, Wf, B_W, half * 512, 512, nkc=NJ)
                        P.op("dve", [PBx, B_xo], [B_xo], lambda e, px=px, xo=xo, half=half: e.tensor_tensor(
                            out=xo[:, half * 512:(half + 1) * 512], in0=px[:], in1=xo[:, half * 512:(half + 1) * 512],
                            op=ALU.add))
                    s4t, B_s = fr.stats(xo, B_xo)
                    P.op("act", [B_xo, B_s], [B_xo], lambda e, xo=xo, s4t=s4t: e.activation(
                        out=xo[:], in_=xo[:], func=AF.Copy, scale=s4t[:, 3:4]), n=1024)
                    P.op("pool", [B_xo, B_fng], [B_xo], lambda e, xo=xo: e.tensor_tensor(
                        out=xo[:], in0=xo[:], in1=fngb[:], op=ALU.mult), c=1.9)
                    P.dma("pool", y[t0:t0 + 128, :], xo[:], [B_xo], [], B_xo)
            P.barrier()
    P.flush()
    P.es.close()
    return nc, P


def _col(v, n):
    return np.ascontiguousarray(np.asarray(v, np.float32).reshape(n, 128).T)


def _consts():
    c = np.zeros((128, 1280), np.float32)
    i = np.arange(128)
    c[:, 0:128] = np.eye(128, dtype=np.float32)
    c[:, 128:256] = np.eye(128, dtype=np.float32)[::-1]
    s = i[:, None]
    t = i[None, :]
    c[:, 256:384] = ((s // 64 == t // 64) & (t >= s)).astype(np.float32)
    c[:, 384:512] = (t <= s).astype(np.float32)
    c[:, 512:640] = (t >= s).astype(np.float32)
    cm = np.ones(512, np.float32)
    cm[::64] = 0.0
    c[:, 640:1152] = cm[None, :]
    c[:, 1152:1280] = 1.0
    return c


def _core_geom(c):
    if c < 4:
        return 0, c // 2, (c % 2) * OWN, 8192
    return 1, 0, (c - 4) * OWN, 16384


_NC_CACHE = {}


def kernel(x_prompt, x_sample, c_prompt, c_sample, w_ada, b_ada, norm1_g, w_in, b_in, lb_logits,
           hg_norm_g, w_branch_a, w_branch_b, w_out, norm2_g, w_ffn_in, w_ffn_out, final_norm_g,
           _debug=False):
    f32 = lambda a: np.ascontiguousarray(np.asarray(a, dtype=np.float32))
    x_prompt, x_sample = f32(x_prompt), f32(x_sample)
    c_prompt, c_sample = f32(c_prompt), f32(c_sample)
    w_in0 = f32(w_in)[0]
    b_in0 = f32(b_in)[0]
    perm = np.arange(1024).reshape(2, 8, 2, 32)[:, :, ::-1, :].reshape(-1)
    w_qksw = np.ascontiguousarray(w_in0[:, 2560:3584][:, perm])
    b_sw = b_in0[2560:3584][perm]
    lbl = np.concatenate([_col(f32(lb_logits)[l, d], 4) for l in range(2) for d in range(2)], axis=1)
    shared = {
        "w_ada": f32(w_ada)[0], "b_ada": f32(b_ada)[0], "n1g": _col(f32(norm1_g)[0], 8),
        "n2g": _col(f32(norm2_g)[0], 8), "w_in": w_in0, "w_qksw": w_qksw, "bin_col": _col(b_in0, 48),
        "bsw_col": _col(b_sw, 8), "b_in_row": b_in0, "lbl": np.ascontiguousarray(lbl),
        "gng": _col(f32(hg_norm_g)[0], 4), "w_bra": f32(w_branch_a)[0], "w_brb": f32(w_branch_b)[0],
        "w_out": f32(w_out)[0], "w_fi": f32(w_ffn_in)[0], "w_fo": f32(w_ffn_out)[0],
        "fng": f32(final_norm_g), "consts": _consts(),
    }
    half = 32
    inv = (np.float32(ROPE_THETA) ** (-np.arange(half, dtype=np.float32) / np.float32(half))).astype(np.float32)
    pidx = np.arange(128)
    fidx = pidx % 32
    sgn = np.where((pidx % 64) < 32, -1.0, 1.0).astype(np.float32)
    vts = att_vtiles()
    in_maps = []
    for c in range(NCORES):
        grp, b, own0, L = _core_geom(c)
        xs = x_prompt[b] if grp == 0 else x_sample[0]
        cvec = c_prompt[b] if grp == 0 else c_sample[0]
        w0 = own0 - HALO
        xw = np.zeros((WIN, D), np.float32)
        lo, hi = max(0, w0), min(L, w0 + WIN)
        xw[lo - w0:hi - w0] = xs[lo:hi]
        pos = (w0 + np.arange(WIN)).astype(np.float32)
        ang = (pos[None, :] * inv[fidx][:, None]).astype(np.float32)
        cosT = np.cos(ang).astype(np.float32)
        sinT = (np.sin(ang).astype(np.float32) * sgn[:, None]).astype(np.float32)
        valid = ((w0 + np.arange(WIN) >= 0) & (w0 + np.arange(WIN) < L)).astype(np.float32)
        vmA = np.zeros((128, 64), np.float32)
        for hf in range(2):
            for i in range(32):
                vmA[:, hf * 32 + i] = valid[2048 * hf + 128 * i + np.arange(128)]
        vmB = np.zeros((128, 36), np.float32)
        top = 1024 + OWN + HH
        for i in range(36):
            rb = top - 128 * (i + 1)
            vmB[:, i] = valid[rb + 127 - np.arange(128)]
        vmC = np.zeros((128, 36), np.float32)
        for i in range(36):
            rb = (1024 - HH) + 128 * i
            vmC[:, i] = valid[rb + np.arange(128)]
        m = dict(shared)
        m.update({"xw": xw, "ccol": _col(cvec, 8), "cosT": cosT, "sinT": sinT, "vmA": vmA, "vmB": vmB, "vmC": vmC})
        in_maps.append(m)
    import os as _os
    upto = int(_os.environ.get("K_UPTO", "4")) if _debug else 4
    key = (bool(_debug), upto)
    if key not in _NC_CACHE:
        _NC_CACHE[key] = build_program(debug=key[0], upto=upto)[0]
    nc = _NC_CACHE[key]
    res = run_bass_kernel_spmd(nc, in_maps, core_ids=list(range(NCORES)))
    outs = res.results
    y_prompt = np.zeros((2, 8192, D), np.float32)
    y_sample = np.zeros((1, 16384, D), np.float32)
    for c in range(NCORES):
        grp, b, own0, L = _core_geom(c)
        yc = np.asarray(outs[c]["y"], np.float32)
        if grp == 0:
            y_prompt[b, own0:own0 + OWN] = yc
        else:
            y_sample[0, own0:own0 + OWN] = yc
    if _debug:
        return (y_prompt, y_sample), outs
    return (y_prompt, y_sample)
```

```python
import contextlib
import numpy as np
import concourse.bass as bass
import concourse.mybir as mybir
from concourse.alu_op_type import AluOpType as ALU
from concourse.bass_utils import run_bass_kernel_spmd

F32 = mybir.dt.float32
BF16 = mybir.dt.bfloat16
AF = mybir.ActivationFunctionType

D = 1024
NCORES = 8
OWN = 4096
HALO = 1024
WIN = OWN + 2 * HALO
HH = 512
FFN = 2816
NJ = FFN // 128
EPS = 1e-6
ROPE_THETA = 10000.0

PATTERNS = (1, 4, 16)


def att_vtiles():
    out = []
    for r in PATTERNS:
        jq0 = 1024 // r
        nb = (2048 // r) // 128
        for rho in range(r):
            for m in range(nb + 1):
                start = rho + r * (jq0 - 64 + 128 * m)
                out.append((r, rho, m, start, nb))
    return out


class Buf:
    __slots__ = ("name", "w", "r", "dsem", "ssem")

    def __init__(self, name):
        self.name = name
        self.w = {}
        self.r = {}
        self.dsem = None
        self.ssem = None


class Prog:
    ENG = ("pe", "act", "dve", "pool", "sp")

    def __init__(self, nc):
        self.nc = nc
        self.es = contextlib.ExitStack()
        self.eng = {"pe": nc.tensor, "act": nc.scalar, "dve": nc.vector, "pool": nc.gpsimd, "sp": nc.sync}
        self.sems = {}
        self.cnt = {}
        for e in self.ENG:
            self.sems[e] = self.es.enter_context(nc.semaphore("sem_" + e))
            self.cnt[e] = 0
        self.ndma = 12
        for i in range(self.ndma):
            k = "d%d" % i
            self.sems[k] = self.es.enter_context(nc.semaphore("sem_" + k))
            self.cnt[k] = 0
        self.nsw = 6
        for i in range(self.nsw):
            k = "w%d" % i
            self.sems[k] = self.es.enter_context(nc.semaphore("sem_" + k))
            self.cnt[k] = 0
        self.dma_next = 0
        self.sw_next = 0
        self.seen = {e: {} for e in self.ENG}
        self.nins = 0
        self.defer = True
        self.q = []
        self.window = 40
        self.act_tbl = None
        self.est_time = 0.0

    def buf(self, name, dma=False):
        b = Buf(name)
        if dma:
            assert self.dma_next < self.ndma, "out of dma semaphores"
            b.dsem = "d%d" % self.dma_next
            self.dma_next += 1
        return b

    def _deps(self, e, reads, writes):
        deps = {}
        for b in reads:
            for k, v in b.w.items():
                if deps.get(k, 0) < v:
                    deps[k] = v
        for b in writes:
            for k, v in b.w.items():
                if deps.get(k, 0) < v:
                    deps[k] = v
            for k, v in b.r.items():
                if deps.get(k, 0) < v:
                    deps[k] = v
        seen = self.seen[e]
        for k, v in deps.items():
            if k == e and e == "pe":
                continue
            if (k[0] == "d" and k != "dve") or k[0] == "w":
                v = self.cnt[k]
            if seen.get(k, 0) < v:
                self.eng[e].wait_ge(self.sems[k], v)
                seen[k] = v

    DEFAULT_COST = {"pe": 0.22, "act": 0.55, "dve": 0.55, "pool": 1.2, "sp": 0.15}

    def op(self, e, reads, writes, ins_fn, c=None, n=512, tb=None):
        if self.defer:
            if c is None:
                if e == "pe":
                    c = 0.07 + n * 0.0004
                elif e == "act":
                    c = 0.22 + n / 1400.0
                elif e == "dve":
                    c = 0.12 + n * 0.0011
                else:
                    c = 0.15 + n * 0.0022
            self.q.append((0, e, list(reads), list(writes), ins_fn, c, tb))
            return
        self._op_now(e, reads, writes, ins_fn)

    def dma(self, q, out, in_, reads, writes, semb, c=4.0):
        if self.defer:
            self.q.append((1, q, list(reads), list(writes), (out, in_, semb), c))
            return
        self._dma_now(q, out, in_, reads, writes, semb)

    def semkey(self, q, semb):
        if q == "sp":
            assert semb.dsem is not None
            return semb.dsem
        if semb.ssem is None:
            assert self.sw_next < self.nsw, "out of sw dma semaphores"
            semb.ssem = "w%d" % self.sw_next
            self.sw_next += 1
        return semb.ssem

    def _op_now(self, e, reads, writes, ins_fn):
        self._deps(e, reads, writes)
        ins = ins_fn(self.eng[e])
        self.cnt[e] += 1
        ins.then_inc(self.sems[e], 1)
        ev = self.cnt[e]
        for b in reads:
            b.r[e] = ev
        for b in writes:
            b.w = {e: ev}
            b.r = {}
        self.nins += 1

    def _dma_now(self, q, out, in_, reads, writes, semb):
        self._deps(q, reads, writes)
        k = self.semkey(q, semb)
        self.cnt[k] += 16
        self.eng[q].dma_start(out=out, in_=in_).then_inc(self.sems[k], 16)
        ev = self.cnt[k]
        for b in reads:
            b.r[k] = ev
        for b in writes:
            b.w = {k: ev}
            b.r = {}
        self.nins += 1

    def flush(self):
        ops = self.q
        self.q = []
        n = len(ops)
        if n == 0:
            return
        last_w = {}
        readers = {}
        deps = [None] * n
        for i, o in enumerate(ops):
            d = set()
            reads, writes = o[2], o[3]
            if o[0] == 1:
                kk_ = ("sem", self.semkey(o[1], o[4][2]))
                reads = reads + [kk_]
                writes = writes + [kk_]
            for b in reads:
                w = last_w.get(id(b) if not isinstance(b, tuple) else b)
                if w is not None:
                    d.add(w)
            for b in writes:
                kb = id(b) if not isinstance(b, tuple) else b
                w = last_w.get(kb)
                if w is not None:
                    d.add(w)
                for r in readers.get(kb, ()):
                    d.add(r)
            for b in reads:
                kb = id(b) if not isinstance(b, tuple) else b
                readers.setdefault(kb, []).append(i)
            for b in writes:
                kb = id(b) if not isinstance(b, tuple) else b
                last_w[kb] = i
                readers[kb] = []
            d.discard(i)
            deps[i] = d
        queues = {}
        for i, o in enumerate(ops):
            queues.setdefault(o[1], []).append(i)
        heads = {e: 0 for e in queues}
        done = [False] * n
        finish = [0.0] * n
        eng_free = {e: 0.0 for e in queues}
        W = self.window
        LAT = 0.25
        TBL = 1.3
        nsched = 0
        while nsched < n:
            best = None
            for e, ql in queues.items():
                h = heads[e]
                while h < len(ql) and done[ql[h]]:
                    h += 1
                heads[e] = h
                cnt = 0
                j = h
                ef = eng_free[e]
                while j < len(ql) and cnt < W:
                    i = ql[j]
                    j += 1
                    if done[i]:
                        continue
                    cnt += 1
                    rt = 0.0
                    ok = True
                    for dd in deps[i]:
                        if not done[dd]:
                            ok = False
                            break
                        f = finish[dd]
                        if f > rt:
                            rt = f
                    if not ok:
                        continue
                    st = rt + LAT if rt + LAT > ef else ef
                    if e == "act":
                        tb_ = ops[i][6] if ops[i][0] == 0 else None
                        if tb_ is not None and tb_ != self.act_tbl:
                            st += TBL
                    if best is None or st < best[0] or (st == best[0] and i < best[1]):
                        best = (st, i, e)
                    if st <= ef:
                        break
            st, i, e = best
            o = ops[i]
            if o[0] == 1:
                eng_free[e] = st + 0.15
                finish[i] = st + o[5]
            else:
                eng_free[e] = st + o[5]
                finish[i] = st + o[5]
                if e == "act" and o[6] is not None:
                    self.act_tbl = o[6]
            done[i] = True
            nsched += 1
            if o[0] == 1:
                self._dma_now(o[1], o[4][0], o[4][1], o[2], o[3], o[4][2])
            else:
                self._op_now(o[1], o[2], o[3], o[4])
        self.est_time += max(finish) if n else 0.0

    def barrier(self):
        self.flush()
        for e in self.ENG:
            seen = self.seen[e]
            for k, v in self.cnt.items():
                if v > 0 and seen.get(k, 0) < v:
                    self.eng[e].wait_ge(self.sems[k], v)
                    seen[k] = v
        self.dma_next = 0
        self.sw_next = 0


class Pools:
    def __init__(self, items):
        self.items = items
        self.i = 0

    def get(self):
        it = self.items[self.i % len(self.items)]
        self.i += 1
        return it


def build_program(debug=False, upto=4):
    nc = bass.Bass("TRN2", target_bir_lowering=False)
    P = Prog(nc)
    es = P.es

    def din(name, shape, dt=F32):
        return nc.dram_tensor(name, list(shape), dt, kind="ExternalInput").ap()

    xw = din("xw", [WIN, D])
    ccol = din("ccol", [128, 8])
    w_ada = din("w_ada", [D, 6 * D])
    b_ada = din("b_ada", [6 * D])
    n1g = din("n1g", [128, 8])
    n2g = din("n2g", [128, 8])
    w_in = din("w_in", [D, 6144])
    w_qksw = din("w_qksw", [D, 1024])
    bin_col = din("bin_col", [128, 48])
    bsw_col = din("bsw_col", [128, 8])
    b_in_row = din("b_in_row", [6144])
    lbl = din("lbl", [128, 16])
    gng = din("gng", [128, 4])
    w_bra = din("w_bra", [512, D])
    w_brb = din("w_brb", [512, D])
    w_out = din("w_out", [D, D])
    w_fi = din("w_fi", [D, 2 * FFN])
    w_fo = din("w_fo", [FFN, D])
    fng = din("fng", [D])
    cosT = din("cosT", [128, WIN])
    sinT = din("sinT", [128, WIN])
    vmA = din("vmA", [128, 64])
    vmB = din("vmB", [128, 36])
    vmC = din("vmC", [128, 36])
    consts = din("consts", [128, 1280])

    okind = "ExternalOutput"
    y = nc.dram_tensor("y", [OWN, D], F32, kind=okind).ap()
    skind = okind if debug else "Internal"
    ob_s = nc.dram_tensor("ob_s", [512, OWN], BF16, kind=skind).ap()
    obw_s = nc.dram_tensor("obw_s", [OWN, 512], F32, kind=skind).ap()
    ofw_s = nc.dram_tensor("ofw_s", [OWN, 512], F32, kind=skind).ap()
    v_s = nc.dram_tensor("v_s", [4096, 8, 65], BF16, kind="Internal").ap()
    B_vs = P.buf("v_s")
    x1_s = nc.dram_tensor("x1_s", [OWN, D], F32, kind=skind).ap()

    uid = [0]

    def sb(name, shape, dt, stack=None):
        uid[0] += 1
        return (stack or es).enter_context(nc.sbuf_tensor("%s_u%d" % (name, uid[0]), list(shape), dt))

    def ps(name, stack=None):
        return (stack or es).enter_context(nc.psum_tensor(name, [128, 512], F32))

    cst = sb("cst", [128, 1280], F32)
    cstb = sb("cstb", [128, 1280], BF16)
    B_cst = P.buf("cst", dma=True)
    B_cstb = P.buf("cstb")
    ident_f = cst[:, 0:128]
    J_f = cst[:, 128:256]
    ident_b = cstb[:, 0:128]
    J_b = cstb[:, 128:256]
    tri_b = cstb[:, 256:384]
    mA_b = cstb[:, 384:512]
    mB_b = cstb[:, 512:640]
    cmask_f = cst[:, 640:1152]

    colv = sb("colv", [128, 160], F32)
    B_colv = P.buf("colv")
    C_G1, C_SH1, C_G2, C_SH2 = 0, 8, 16, 24
    C_LB, C_OML = 32, 40
    C_GN = 48
    C_BIN = 52
    C_BSW = 100
    C_TMP = 108
    g_s = nc.dram_tensor("g_s", [2, 128, D], F32, kind="Internal").ap()

    psb = [ps("psb%d" % i) for i in range(8)]
    PSR = Pools([(psb[i], P.buf("ps%d" % i)) for i in range(8)])

    def cv(c, n=1):
        return colv[:, c:c + n]

    with contextlib.ExitStack() as st:
        small = sb("p0small", [128, 64], F32, st)
        B_small = P.buf("p0small", dma=True)
        P.dma("sp", cst[:], consts, [], [B_cst], B_cst)
        P.op("dve", [B_cst], [B_cstb], lambda e: e.tensor_copy(out=cstb[:], in_=cst[:]))
        P.dma("sp", small[:, 0:8], ccol, [], [B_small], B_small)
        P.dma("sp", small[:, 24:40], lbl, [], [B_small], B_small)
        P.dma("sp", colv[:, C_GN:C_GN + 4], gng, [], [B_colv], B_small)
        P.dma("sp", colv[:, C_BIN:C_BIN + 48], bin_col, [], [B_colv], B_small)
        P.dma("sp", colv[:, C_BSW:C_BSW + 8], bsw_col, [], [B_colv], B_small)
        P.op("dve", [B_small], [B_small], lambda e: e.tensor_tensor(
            out=small[:, 40:48], in0=small[:, 24:32], in1=small[:, 32:40], op=ALU.subtract))
        P.op("act", [B_small], [B_colv], lambda e: e.activation(
            out=cv(C_LB, 8), in_=small[:, 40:48], func=AF.Sigmoid), tb='s')
        P.op("dve", [B_colv], [B_colv], lambda e: e.tensor_scalar(
            out=cv(C_OML, 8), in0=cv(C_LB, 8), scalar1=-1.0, scalar2=1.0, op0=ALU.mult, op1=ALU.add))
        scb = sb("scb", [128, 8, 128], F32, st)
        B_scb = P.buf("scb")
        P.op("act", [B_small], [B_small], lambda e: e.activation(
            out=small[:, 48:56], in_=small[:, 0:8], func=AF.Silu), tb='s')
        for kc in range(8):
            src = bass.AP(small.tensor if hasattr(small, "tensor") else small, 48 + kc, [[64, 128], [0, 128]])
            P.op("dve", [B_small], [B_scb], lambda e, kc=kc, src=src: e.tensor_copy(out=scb[:, kc, :], in_=src))
        badab = sb("badab", [128, 6 * D], F32, st)
        B_badab = P.buf("badab", dma=True)
        P.dma("sp", badab[:], b_ada.partition_broadcast(128), [], [B_badab], B_badab)
        modbc = sb("modbc", [128, 6 * D], F32, st)
        B_mod = P.buf("modbc")
        stg = [(sb("p0stg%d" % i, [128, 8, 512], F32, st), P.buf("p0stg%d" % i, dma=True)) for i in range(2)]
        wa_v = w_ada.rearrange("(kc p) n -> p kc n", p=128)
        for blk in range(12):
            t, B = stg[blk % 2]
            P.dma("sp", t[:], wa_v[:, :, blk * 512:(blk + 1) * 512], [], [B], B)
            pt, PB_ = PSR.get()
            for kc in range(8):
                P.op("pe", [B, B_scb], [PB_], lambda e, kc=kc, t=t, pt=pt: e.matmul(
                    pt[:], scb[:, kc, :], t[:, kc, :], start=(kc == 0), stop=(kc == 7)))
            P.op("dve", [PB_, B_badab], [B_mod], lambda e, blk=blk, pt=pt: e.tensor_tensor(
                out=modbc[:, blk * 512:(blk + 1) * 512], in0=pt[:], in1=badab[:, blk * 512:(blk + 1) * 512],
                op=ALU.add))
        pt, PB_ = PSR.get()
        for vi, base in enumerate((0, 1024, 3072, 4096)):
            for kc in range(8):
                P.op("pe", [B_mod, B_cst], [PB_], lambda e, vi=vi, base=base, kc=kc, pt=pt: e.matmul(
                    pt[:, vi * 8 + kc: vi * 8 + kc + 1], modbc[:, base + kc * 128: base + (kc + 1) * 128],
                    ident_f[:, 0:1], start=True, stop=True))
        P.op("dve", [PB_], [B_small], lambda e, pt=pt: e.tensor_copy(out=small[:, 0:32], in_=pt[:, 0:32]))
        sm2 = sb("p0sm2", [128, 16], F32, st)
        B_sm2 = P.buf("p0sm2", dma=True)
        P.dma("sp", sm2[:, 0:8], n1g, [], [B_sm2], B_sm2)
        P.dma("sp", sm2[:, 8:16], n2g, [], [B_sm2], B_sm2)
        P.op("dve", [B_small, B_sm2], [B_colv], lambda e: e.scalar_tensor_tensor(
            out=cv(C_G1, 8), in0=small[:, 8:16], scalar=1.0, in1=sm2[:, 0:8], op0=ALU.add, op1=ALU.mult))
        P.op("dve", [B_small, B_sm2], [B_colv], lambda e: e.scalar_tensor_tensor(
            out=cv(C_G2, 8), in0=small[:, 24:32], scalar=1.0, in1=sm2[:, 8:16], op0=ALU.add, op1=ALU.mult))
        P.op("dve", [B_small], [B_colv], lambda e: e.tensor_copy(out=cv(C_SH1, 8), in_=small[:, 0:8]))
        P.op("dve", [B_small], [B_colv], lambda e: e.tensor_copy(out=cv(C_SH2, 8), in_=small[:, 16:24]))
        B_gs = P.buf("p0gs", dma=True)
        P.dma("pool", g_s[0], modbc[:, 2048:3072], [B_mod], [], B_gs)
        P.dma("pool", g_s[1], modbc[:, 5120:6144], [B_mod], [], B_gs)
        P.barrier()

    def load_weight(st, dst, B_dst, src_rows_ap, nkc, ncols, gate_bc=None, B_gate=None, stage=None):
        v = src_rows_ap.rearrange("(kc p) n -> p kc n", p=128)
        cw = max(64, min(ncols, stage.cols // nkc))
        i = 0
        for c0 in range(0, ncols, cw):
            c1 = min(ncols, c0 + cw)
            t, B = stage.get()
            P.dma("sp", t[:, 0:nkc * (c1 - c0)].rearrange("p (k n) -> p k n", k=nkc), v[:, :, c0:c1], [], [B], B)
            tv = t[:, 0:nkc * (c1 - c0)].rearrange("p (k n) -> p k n", k=nkc)
            if gate_bc is None:
                h0 = nkc // 2
                for (eng, ka, kb) in (("pool", 0, max(1, nkc // 4)), ("dve", max(1, nkc // 4), max(2, (5 * nkc) // 8)),
                                      ("act", max(2, (5 * nkc) // 8), nkc)):
                    if kb <= ka:
                        continue
                    if eng == "act":
                        P.op("act", [B], [B_dst], lambda e, tv=tv, c0=c0, c1=c1, ka=ka, kb=kb: e.activation(
                            out=dst[:, ka:kb, c0:c1], in_=tv[:, ka:kb, :], func=AF.Copy))
                    else:
                        P.op(eng, [B], [B_dst], lambda e, tv=tv, c0=c0, c1=c1, ka=ka, kb=kb: e.tensor_copy(
                            out=dst[:, ka:kb, c0:c1], in_=tv[:, ka:kb, :]))
            else:
                for kc in range(nkc):
                    eng = "dve" if kc % 3 != 2 else "pool"
                    P.op(eng, [B, B_gate], [B_dst], lambda e, tv=tv, c0=c0, c1=c1, kc=kc: e.tensor_tensor(
                        out=dst[:, kc, c0:c1], in0=tv[:, kc, :], in1=gate_bc[:, c0:c1], op=ALU.mult))
            i += 1

    def mk_stage(st, n=2, cols=4096):
        pl = Pools([(sb("wstg%d" % i, [128, cols], F32, st), P.buf("wstg%d" % i, dma=True)) for i in range(n)])
        pl.cols = cols
        return pl

    class Front:
        def __init__(self, st, nxs, cG, cSH, tag, nxn=2):
            self.xs = Pools([(sb("xs%s%d" % (tag, i), [128, D], F32, st), P.buf("xs%s%d" % (tag, i), dma=True))
                             for i in range(nxs)])
            self.xn = Pools([(sb("xn%s%d" % (tag, i), [128, D], BF16, st), P.buf("xn%s%d" % (tag, i)))
                             for i in range(nxn)])
            self.junk = sb("junk" + tag, [128, D], BF16, st)
            self.B_junk = P.buf("junk" + tag)
            self.st4 = Pools([(sb("fst%s%d" % (tag, i), [128, 4], F32, st), P.buf("fst%s%d" % (tag, i)))
                              for i in range(2)])
            self.cG, self.cSH = cG, cSH
            self.pool_xn = False

        def stats(self, xt, B_x):
            s4, B_s = self.st4.get()
            junk, B_junk = self.junk, self.B_junk
            P.op("pool", [B_x], [B_junk], lambda e: e.tensor_tensor(out=junk[:], in0=xt[:], in1=xt[:], op=ALU.mult), c=1.9)
            P.op("dve", [B_junk], [B_s], lambda e: e.reduce_sum(out=s4[:, 0:1], in_=junk[:], axis=mybir.AxisListType.X), n=1024)
            P.op("dve", [B_s], [B_s], lambda e: e.tensor_scalar(
                out=s4[:, 1:2], in0=s4[:, 0:1], scalar1=1.0 / D, scalar2=EPS, op0=ALU.mult, op1=ALU.add), n=1)
            P.op("act", [B_s], [B_s], lambda e: e.activation(out=s4[:, 2:3], in_=s4[:, 1:2], func=AF.Ln), n=1, tb='e')
            P.op("act", [B_s], [B_s], lambda e: e.activation(out=s4[:, 3:4], in_=s4[:, 2:3], func=AF.Exp, scale=-0.5), n=1, tb='e')
            return s4, B_s

        def run(self, xt, B_x, hT_view, B_hT, flip):
            self.run_b(self.run_a(xt, B_x), hT_view, B_hT, flip)

        def run_a(self, xt, B_x):
            s4, B_s = self.stats(xt, B_x)
            xn, B_xn = self.xn.get()
            if self.pool_xn:
                P.op("pool", [B_x, B_s], [B_xn], lambda e: e.tensor_scalar(
                    out=xn[:], in0=xt[:], scalar1=s4[:, 3:4], scalar2=0.0, op0=ALU.mult, op1=ALU.add), c=1.1)
            else:
                P.op("act", [B_x, B_s], [B_xn], lambda e: e.activation(
                    out=xn[:], in_=xt[:], func=AF.Copy, scale=s4[:, 3:4]), n=1024)
            return xn, B_xn

        def run_b(self, a, hT_view, B_hT, flip):
            xn, B_xn = a
            mat = J_b if flip else ident_b
            for half in range(2):
                pt, PB_ = PSR.get()
                for k4 in range(4):
                    kc = half * 4 + k4
                    P.op("pe", [B_xn, B_cstb], [PB_], lambda e, kc=kc, k4=k4, pt=pt: e.matmul(
                        pt[:, k4 * 128:(k4 + 1) * 128], xn[:, kc * 128:(kc + 1) * 128], mat, start=True, stop=True), n=128)
                for k4 in range(4):
                    kc = half * 4 + k4
                    if k4 % 2 == 0:
                        P.op("dve", [PB_, B_colv], [B_hT], lambda e, kc=kc, k4=k4, pt=pt: e.tensor_scalar(
                            out=hT_view[:, kc, :], in0=pt[:, k4 * 128:(k4 + 1) * 128],
                            scalar1=cv(self.cG + kc), scalar2=cv(self.cSH + kc), op0=ALU.mult, op1=ALU.add), n=128)
                    else:
                        P.op("act", [PB_, B_colv], [B_hT], lambda e, kc=kc, k4=k4, pt=pt: e.activation(
                            out=hT_view[:, kc, :], in_=pt[:, k4 * 128:(k4 + 1) * 128], func=AF.Identity,
                            scale=cv(self.cG + kc), bias=cv(self.cSH + kc)), n=128)

    def front_tile(fr, loader, hTt, B_hTt, flip):
        pend = None
        for s4 in range(5):
            a = None
            if s4 < 4:
                xt, B_x = loader(s4)
                a = (fr.run_a(xt, B_x), s4)
            if pend is not None:
                pa_, ps_ = pend
                fr.run_b(pa_, hTt[:, :, ps_ * 128:(ps_ + 1) * 128], B_hTt, flip)
            pend = a

    def proj_fm(pt, PB_, W, B_W, col0, hT, B_hT, nkc=8, ncol=128):
        for kc in range(nkc):
            P.op("pe", [B_W, B_hT], [PB_], lambda e, kc=kc: e.matmul(
                pt[0:ncol, 0:hT.shape[2]], W[:, kc, col0:col0 + ncol], hT[:, kc, :], start=(kc == 0), stop=(kc == nkc - 1)))

    def proj_tm(pt_view, PB_, lhs_fn, B_h, W, B_W, col0, ncol, nkc=8):
        for kc in range(nkc):
            P.op("pe", [B_W, B_h], [PB_], lambda e, kc=kc: e.matmul(
                pt_view, lhs_fn(kc), W[:, kc, col0:col0 + ncol], start=(kc == 0), stop=(kc == nkc - 1)))

    NVT = len(att_vtiles())

    pa_st = contextlib.ExitStack()
    if upto >= 1:
        with contextlib.ExitStack() as s0:
            Wq = sb("a_Wq", [128, 8, 512], BF16, pa_st)
            Wqs = sb("a_Wqs", [128, 8, 512], BF16, pa_st)
            Wk = sb("a_Wk", [128, 8, 512], BF16, pa_st)
            Wks = sb("a_Wks", [128, 8, 512], BF16, pa_st)
            Wv = sb("a_Wv", [128, 8, 512], BF16, pa_st)
            stage = mk_stage(s0, 3, 4096)
            B_W = P.buf("a_W")
            load_weight(s0, Wq, B_W, w_in[:, 2560:3072], 8, 512, stage=stage)
            load_weight(s0, Wk, B_W, w_in[:, 3072:3584], 8, 512, stage=stage)
            load_weight(s0, Wqs, B_W, w_qksw[:, 0:512], 8, 512, stage=stage)
            load_weight(s0, Wks, B_W, w_qksw[:, 512:1024], 8, 512, stage=stage)
            load_weight(s0, Wv, B_W, w_in[:, 3584:4096], 8, 512, stage=stage)
            P.barrier()
    for hf in (range(2) if upto >= 1 else []):
        with contextlib.ExitStack() as st:
            kT = sb("a_kT", [128, 4, 4096], BF16, st)
            B_kT = P.buf("a_kT")
            qT = sb("a_qT", [128, 4, 2048], BF16, st)
            B_qT = P.buf("a_qT")
            r0 = 2048 * hf
            with contextlib.ExitStack() as s1:
                fr = Front(s1, 3, C_G1, C_SH1, "a")
                bvb = sb("a_bvb", [128, 512], F32, s1)
                B_bvb = P.buf("a_bvb", dma=True)
                P.dma("sp", bvb[:], b_in_row[3584:4096].partition_broadcast(128), [], [B_bvb], B_bvb)
                vm = sb("a_vm", [128, 32], F32, s1)
                B_vm = P.buf("a_vm", dma=True)
                P.dma("sp", vm[:], vmA[:, hf * 32:(hf + 1) * 32], [], [B_vm], B_vm)
                hTr = [(sb("a_hT%d" % i, [128, 8, 512], BF16, s1), P.buf("a_hT%d" % i)) for i in range(2)]
                csr = Pools([(sb("a_cs%d" % i, [128, 2, 512], F32, s1), P.buf("a_cs%d" % i, dma=True)) for i in range(2)])
                tmp = Pools([(sb("a_tmp%d" % i, [128, 512], F32, s1), P.buf("a_tmp%d" % i)) for i in range(6)])
                vrow = Pools([(sb("a_vrow%d" % i, [128, 8, 65], BF16, s1), P.buf("a_vrow%d" % i, dma=True)) for i in range(3)])
                for tile in range(8):
                    hTt, B_hTt = hTr[tile % 2]

                    def a_ld(s4, tile=tile):
                        xt, B_x = fr.xs.get()
                        rb = r0 + tile * 512 + s4 * 128
                        P.dma("sp", xt[:], xw[rb:rb + 128, :], [], [B_x], B_x)
                        return xt, B_x
                    front_tile(fr, a_ld, hTt, B_hTt, False)
                    cst_, B_cs = csr.get()
                    P.dma("sp", cst_[:, 0, :], cosT[:, r0 + tile * 512: r0 + (tile + 1) * 512], [], [B_cs], B_cs)
                    P.dma("sp", cst_[:, 1, :], sinT[:, r0 + tile * 512: r0 + (tile + 1) * 512], [], [B_cs], B_cs)
                    cs_v, sn_v = cst_[:, 0, :], cst_[:, 1, :]
                    jobs = [(Wk, Wks, kT, B_kT, tile * 512, 4, False)]
                    if 2 <= tile < 6:
                        jobs.append((Wq, Wqs, qT, B_qT, (tile - 2) * 512, 0, True))
                    for (W1, W2, dst, B_dst, d0, bofs, isq) in jobs:
                        for cc in range(4):
                            p1, PB1 = PSR.get()
                            proj_fm(p1, PB1, W1, B_W, cc * 128, hTt, B_hTt)
                            p2, PB2 = PSR.get()
                            proj_fm(p2, PB2, W2, B_W, cc * 128, hTt, B_hTt)
                            t1, B1 = tmp.get()
                            t2, B2 = tmp.get()
                            bc1 = cv(C_BIN + 20 + bofs + cc)
                            bc2 = cv(C_BSW + bofs + cc)
                            P.op("dve", [PB1, B_colv, B_cs], [B1], lambda e, p1=p1, t1=t1, bc1=bc1, cs_v=cs_v: e.scalar_tensor_tensor(
                                out=t1[:], in0=p1[:], scalar=bc1, in1=cs_v, op0=ALU.add, op1=ALU.mult))
                            P.op("dve", [PB2, B_colv, B_cs], [B2], lambda e, p2=p2, t2=t2, bc2=bc2, sn_v=sn_v: e.scalar_tensor_tensor(
                                out=t2[:], in0=p2[:], scalar=bc2, in1=sn_v, op0=ALU.add, op1=ALU.mult))
                            if isq:
                                P.op("pool", [B1, B2], [B1], lambda e, t1=t1, t2=t2: e.tensor_tensor(
                                    out=t1[:], in0=t1[:], in1=t2[:], op=ALU.add))
                                P.op("act", [B1], [B_dst], lambda e, t1=t1, dst=dst, cc=cc, d0=d0: e.activation(
                                    out=dst[:, cc, d0:d0 + 512], in_=t1[:], func=AF.Copy, scale=0.125))
                            else:
                                P.op("pool", [B1, B2], [B_dst], lambda e, t1=t1, t2=t2, dst=dst, cc=cc, d0=d0: e.tensor_tensor(
                                    out=dst[:, cc, d0:d0 + 512], in0=t1[:], in1=t2[:], op=ALU.add))
                    for s4 in range(4):
                        pv, PBv = PSR.get()
                        proj_tm(pv[:], PBv, lambda kc, s4=s4, hTt=hTt: hTt[:, kc, s4 * 128:(s4 + 1) * 128], B_hTt, Wv, B_W, 0, 512)
                        vt, B_vt = tmp.get()
                        P.op("dve", [PBv, B_bvb], [B_vt], lambda e, pv=pv, vt=vt: e.tensor_tensor(
                            out=vt[:], in0=pv[:], in1=bvb[:], op=ALU.add))
                        vr, B_vr = vrow.get()
                        vcol = tile * 4 + s4
                        P.op("act", [B_vt, B_vm], [B_vr], lambda e, vt=vt, vr=vr, vcol=vcol: e.activation(
                            out=vr[:, :, 0:64], in_=vt[:].rearrange("p (h d) -> p h d", d=64), func=AF.Copy,
                            scale=vm[:, vcol:vcol + 1]))
                        vsrc = bass.AP(vm.tensor if hasattr(vm, "tensor") else vm, vcol, [[32, 128], [0, 8], [1, 1]])
                        P.op("pool", [B_vm], [B_vr], lambda e, vr=vr, vsrc=vsrc: e.tensor_copy(
                            out=vr[:, :, 64:65], in_=vsrc), n=8)
                        tk = tile * 512 + s4 * 128
                        P.dma("pool", v_s[tk:tk + 128, :, :], vr[:], [B_vr], [], B_vr)
                P.barrier()
            with contextlib.ExitStack() as s2:
                mask4 = sb("a_mask4", [128, 512], BF16, s2)
                B_m4 = P.buf("a_mask4")
                for i in range(4):
                    src = mA_b if i % 2 == 0 else mB_b
                    P.op("dve", [B_cstb], [B_m4], lambda e, i=i, src=src: e.tensor_copy(
                        out=mask4[:, i * 128:(i + 1) * 128], in_=src), n=128)
                acc = sb("a_acc", [65, 8, 2048], F32, s2)
                B_acc = [P.buf("a_acc%d" % i) for i in range(2)]
                vext = Pools([(sb("a_vext%d" % i, [128, 8, 65], BF16, s2), P.buf("a_vext%d" % i, dma=True)) for i in range(6)])
                pt_ = Pools([(sb("a_P%d" % i, [128, 512], BF16, s2), P.buf("a_P%d" % i)) for i in range(10)])
                obst = Pools([(sb("a_obst%d" % i, [64, 512], BF16, s2), P.buf("a_obst%d" % i, dma=True)) for i in range(2)])
                vts = att_vtiles()
                prev = None
                pend2 = None
                for vi, (r, rho, m, start, nb) in enumerate(vts):
                    ve, B_ve = vext.get()
                    P.dma("sp", ve[:], v_s[start:start + 127 * r + 1:r, :, :], [], [B_ve], B_ve, c=2.5)
                    if m == 0:
                        prev = (ve, B_ve)
                        continue
                    bidx = m - 1
                    veA, B_veA = prev
                    veB, B_veB = ve, B_ve
                    prev = (ve, B_ve)
                    jq0 = 1024 // r
                    qst = rho + r * (jq0 + 128 * bidx)
                    qsl = slice(qst - 1024, qst - 1024 + 127 * r + 1, r)
                    kA = slice(qst - 64 * r, qst - 64 * r + 127 * r + 1, r)
                    kB = slice(qst + 64 * r, qst + 64 * r + 127 * r + 1, r)
                    Ps = {}
                    for hg in range(2):
                        pscs = [PSR.get(), PSR.get()]
                        for hh2 in range(2):
                            for ti, ksl in enumerate((kA, kB)):
                                for pair in range(2):
                                    psc, PBs = pscs[pair]
                                    h = hg * 4 + hh2 * 2 + pair
                                    ch, pb = h // 2, (h % 2) * 64
                                    c0 = hh2 * 256 + ti * 128
                                    P.op("pe", [B_kT, B_qT], [PBs], lambda e, psc=psc, c0=c0, ch=ch, pb=pb, ksl=ksl, qsl=qsl: e.matmul(
                                        psc[:, c0:c0 + 128], kT[pb:pb + 64, ch, ksl], qT[pb:pb + 64, ch, qsl],
                                        start=True, stop=True), n=300)
                        for pair in range(2):
                            psc, PBs = pscs[pair]
                            pp, B_pp = pt_.get()
                            P.op("act", [PBs], [B_pp], lambda e, psc=psc, pp=pp: e.activation(
                                out=pp[:], in_=psc[:], func=AF.Exp), tb='e')
                            P.op("dve", [B_pp, B_m4], [B_pp], lambda e, pp=pp: e.tensor_tensor(
                                out=pp[:], in0=pp[:], in1=mask4[:], op=ALU.mult))
                            for hh2 in range(2):
                                Ps[hg * 4 + hh2 * 2 + pair] = (pp, B_pp, hh2 * 256)

                    def stage2(Ps=Ps, veA=veA, B_veA=B_veA, veB=veB, B_veB=B_veB, qsl=qsl, r=r):
                        for hg in range(2):
                            po, PBo = PSR.get()
                            for hh in range(4):
                                h = hg * 4 + hh
                                pp, B_pp, c0 = Ps[h]
                                P.op("pe", [B_pp, B_veA], [PBo], lambda e, po=po, hh=hh, h=h, pp=pp, c0=c0: e.matmul(
                                    po[0:65, hh * 128:(hh + 1) * 128], veA[:, h, :], pp[:, c0:c0 + 128], start=True, stop=False), n=128)
                                P.op("pe", [B_pp, B_veB], [PBo], lambda e, po=po, hh=hh, h=h, pp=pp, c0=c0: e.matmul(
                                    po[0:65, hh * 128:(hh + 1) * 128], veB[:, h, :], pp[:, c0 + 128:c0 + 256], start=False, stop=True), n=128)
                            accv = acc[:, hg * 4:(hg + 1) * 4, qsl]
                            pov = po[0:65, :].rearrange("p (h q) -> p h q", h=4)
                            if r == 1:
                                P.op("dve", [PBo], [B_acc[hg]], lambda e, accv=accv, pov=pov: e.tensor_copy(out=accv, in_=pov))
                            else:
                                P.op("dve", [PBo, B_acc[hg]], [B_acc[hg]], lambda e, accv=accv, pov=pov: e.tensor_tensor(
                                    out=accv, in0=accv, in1=pov, op=ALU.add))
                    if pend2 is not None:
                        pend2()
                    pend2 = stage2
                if pend2 is not None:
                    pend2()
                    pend2 = None
                for hg in range(2):
                    hs = slice(hg * 4, (hg + 1) * 4)
                    P.op("act", [B_acc[hg]], [B_acc[hg]], lambda e, hs=hs: e.activation(
                        out=acc[64:65, hs, :], in_=acc[64:65, hs, :], func=AF.Ln), n=8192, tb='e')
                    P.op("act", [B_acc[hg]], [B_acc[hg]], lambda e, hs=hs: e.activation(
                        out=acc[64:65, hs, :], in_=acc[64:65, hs, :], func=AF.Exp, scale=-1.0), n=8192, tb='e')
                    for hh in range(4):
                        h = hg * 4 + hh
                        for q4 in range(4):
                            pd, PBd = PSR.get()
                            P.op("pe", [B_acc[hg], B_cst], [PBd], lambda e, pd=pd, h=h, q4=q4: e.matmul(
                                pd[0:64, :], cst[64:65, 1152:1216], acc[64:65, h, q4 * 512:(q4 + 1) * 512],
                                start=True, stop=True))
                            ot, B_ot = obst.get()
                            P.op("dve", [PBd, B_acc[hg]], [B_ot], lambda e, pd=pd, ot=ot, h=h, q4=q4: e.tensor_tensor(
                                out=ot[:], in0=acc[0:64, h, q4 * 512:(q4 + 1) * 512], in1=pd[0:64, :], op=ALU.mult))
                            c0 = hf * 2048 + q4 * 512
                            P.dma("pool", ob_s[h * 64:(h + 1) * 64, c0:c0 + 512], ot[:], [B_ot], [], B_ot)
                P.barrier()

    pa_st.close()

    class Scan:
        def __init__(self, st, tag):
            f = lambda n, dt=F32, sh=(128, 512), k=1: Pools([(sb("%s_%s%d" % (tag, n, i), list(sh), dt, st), P.buf("%s_%s%d" % (tag, n, i)))
                                                              for i in range(k)])
            self.t_sg, self.t_f, self.t_g, self.t_b = f("sg", k=4), f("f", k=2), f("g", k=2), f("b", k=2)
            self.t_eb, self.t_enb, self.t_kk, self.t_qs = f("eb", k=2), f("enb", k=2), f("kk", k=2), f("qs", k=2)
            NS = 2
            self.qT = [[sb("%s_qT%d_%d" % (tag, z, h), [128, 512], BF16, st) for h in range(4)] for z in range(NS)]
            self.kTt = [[sb("%s_kT%d_%d" % (tag, z, h), [128, 512], BF16, st) for h in range(4)] for z in range(NS)]
            self.ebk = [[sb("%s_ebk%d_%d" % (tag, z, h), [128, 8], F32, st) for h in range(4)] for z in range(NS)]
            self.B_q = [[P.buf("q") for h in range(4)] for z in range(NS)]
            self.B_k = [[P.buf("k") for h in range(4)] for z in range(NS)]
            self.B_e = [[P.buf("e") for h in range(4)] for z in range(NS)]
            self.vtok = [sb("%s_vtok%d" % (tag, z), [128, 4, 512], BF16, st) for z in range(NS)]
            self.ktok = [sb("%s_ktok%d" % (tag, z), [128, 4, 512], BF16, st) for z in range(NS)]
            self.B_vtok = [[P.buf("vtok") for i in range(4)] for z in range(NS)]
            self.B_ktok = [[P.buf("ktok") for i in range(4)] for z in range(NS)]
            self.vtmp = f("vtmp", k=2)
            self.A = f("A", BF16, k=3)
            self.S = [sb("%s_S%d" % (tag, h), [128, 128], F32, st) for h in range(4)]
            self.S1 = [sb("%s_S1%d" % (tag, h), [128, 128], F32, st) for h in range(4)]
            self.Sb = [sb("%s_Sb%d" % (tag, h), [128, 128], BF16, st) for h in range(4)]
            self.B_S = [P.buf("S") for h in range(4)]
            self.B_S1 = [P.buf("S1") for h in range(4)]
            self.B_Sb = [P.buf("Sb") for h in range(4)]
            for h in range(4):
                P.op("dve", [], [self.B_S[h]], lambda e, h=h: e.memset(self.S[h][:], 0.0))
                P.op("pool", [], [self.B_Sb[h]], lambda e, h=h: e.memset(self.Sb[h][:], 0.0))
            self.bvb = sb(tag + "_bvb", [128, 512], F32, st)
            self.B_bvb = P.buf(tag + "_bvb", dma=True)
            P.dma("sp", self.bvb[:], b_in_row[1536:2048].partition_broadcast(128), [], [self.B_bvb], self.B_bvb)

        def prepH(self, z, h, hTt, B_hTt, W, B_W, d):
            pq, PBq = PSR.get()
            proj_fm(pq, PBq, W, B_W, h * 128, hTt, B_hTt)
            yield
            pf, PBf = PSR.get()
            proj_fm(pf, PBf, W, B_W, 512 + h * 128, hTt, B_hTt)
            sg, B_sg = self.t_sg.get()
            bq = cv(C_BIN + 0 + h)
            bf = cv(C_BIN + 4 * (1 + d) + h)
            P.op("act", [PBq, B_colv], [B_sg], lambda e: e.activation(out=sg[:], in_=pq[:], func=AF.Sigmoid, bias=bq), tb='s')
            sf, B_sf = self.t_sg.get()
            P.op("act", [PBf, B_colv], [B_sf], lambda e: e.activation(out=sf[:], in_=pf[:], func=AF.Sigmoid, bias=bf), tb='s')
            yield
            qs, B_qs = self.t_qs.get()
            P.op("dve", [PBq, B_sg, B_colv], [B_qs], lambda e: e.scalar_tensor_tensor(
                out=qs[:], in0=pq[:], scalar=bq, in1=sg[:], op0=ALU.add, op1=ALU.mult))
            ff, B_ff = self.t_f.get()
            P.op("dve", [B_sf, B_colv], [B_ff], lambda e: e.tensor_scalar(
                out=ff[:], in0=sf[:], scalar1=cv(C_OML + d * 4 + h), scalar2=cv(C_LB + d * 4 + h),
                op0=ALU.mult, op1=ALU.add))
            yield
            g, B_g = self.t_g.get()
            P.op("act", [B_ff], [B_g], lambda e: e.activation(out=g[:], in_=ff[:], func=AF.Ln), tb='e')
            kk, B_kk = self.t_kk.get()
            P.op("pool", [B_ff], [B_kk], lambda e: e.tensor_scalar(
                out=kk[:], in0=ff[:], scalar1=-1.0, scalar2=1.0, op0=ALU.mult, op1=ALU.add), c=0.62)
            yield
            b, B_b = self.t_b.get()
            P.op("dve", [B_g, B_cst], [B_b], lambda e: e.tensor_tensor_scan(
                out=b[:], data0=cmask_f, data1=g[:], initial=0.0, op0=ALU.mult, op1=ALU.add), n=1024)
            yield
            eb, B_eb = self.t_eb.get()
            P.op("act", [B_b], [B_eb], lambda e: e.activation(out=eb[:], in_=b[:], func=AF.Exp), tb='e')
            enb, B_enb = self.t_enb.get()
            P.op("act", [B_b], [B_enb], lambda e: e.activation(out=enb[:], in_=b[:], func=AF.Exp, scale=-1.0), tb='e')
            yield
            P.op("pool", [B_qs, B_eb], [self.B_q[z][h]], lambda e: e.tensor_tensor(
                out=self.qT[z][h][:], in0=qs[:], in1=eb[:], op=ALU.mult))
            P.op("pool", [B_kk, B_enb], [self.B_k[z][h]], lambda e: e.tensor_tensor(
                out=self.kTt[z][h][:], in0=kk[:], in1=enb[:], op=ALU.mult))
            P.op("pool", [B_eb], [self.B_e[z][h]], lambda e: e.tensor_copy(out=self.ebk[z][h][:], in_=eb[:, 63:512:64]), n=8)

        def prepV(self, z, sub, hTt, B_hTt, W, B_W, vm_t, B_vm, vmcol):
            pv, PBv = PSR.get()
            proj_tm(pv[:], PBv, lambda kc: hTt[:, kc, sub * 128:(sub + 1) * 128], B_hTt, W, B_W, 1024, 512)
            vt, B_vt = self.vtmp.get()
            P.op("dve", [PBv, self.B_bvb], [B_vt], lambda e: e.tensor_tensor(out=vt[:], in0=pv[:], in1=self.bvb[:], op=ALU.add))
            P.op("act", [B_vt, B_vm], [self.B_vtok[z][sub]], lambda e: e.activation(
                out=self.vtok[z][:, sub, :], in_=vt[:], func=AF.Copy, scale=vm_t[:, vmcol:vmcol + 1]))

        def prepK(self, z):
            for sub in range(4):
                pk, PBk = PSR.get()
                for h in range(4):
                    P.op("pe", [self.B_k[z][h], B_cstb], [PBk], lambda e, h=h, pk=pk, sub=sub: e.matmul(
                        pk[:, h * 128:(h + 1) * 128], self.kTt[z][h][:, sub * 128:(sub + 1) * 128], ident_b,
                        start=True, stop=True), n=128)
                P.op("act", [PBk], [self.B_ktok[z][sub]], lambda e, pk=pk, sub=sub: e.activation(
                    out=self.ktok[z][:, sub, :], in_=pk[:], func=AF.Copy))

        def scan_sub(self, z, sub, res):
            qT, kTt, ebk, vtok, ktok = self.qT[z], self.kTt[z], self.ebk[z], self.vtok[z], self.ktok[z]
            B_q, B_k, B_e, B_vtok, B_ktok = self.B_q[z], self.B_k[z], self.B_e[z], self.B_vtok[z], self.B_ktok[z]
            psc, PBs = PSR.get()
            for h in range(4):
                P.op("pe", [B_k[h], B_q[h]], [PBs], lambda e, h=h: e.matmul(
                    psc[:, h * 128:(h + 1) * 128], kTt[h][:, sub * 128:(sub + 1) * 128],
                    qT[h][:, sub * 128:(sub + 1) * 128], start=True, stop=True), n=128)
            pus = [PSR.get(), PSR.get()]
            for h in range(4):
                for c in range(2):
                    pu, PBu = pus[c]
                    rows = slice(c * 64, (c + 1) * 64)
                    P.op("pe", [B_ktok[sub], B_vtok[sub]], [PBu], lambda e, pu=pu, h=h, rows=rows: e.matmul(
                        pu[:, h * 128:(h + 1) * 128], ktok[rows, sub, h * 128:(h + 1) * 128],
                        vtok[rows, sub, h * 128:(h + 1) * 128], start=True, stop=True), n=128)
            yield
            A, B_A = self.A.get()
            tri4 = bass.AP(cstb.tensor if hasattr(cstb, "tensor") else cstb, 256, [[1280, 128], [0, 4], [1, 128]])
            P.op("dve", [PBs, B_cstb], [B_A], lambda e: e.tensor_tensor(
                out=A[:].rearrange("p (h t) -> p h t", h=4), in0=psc[:].rearrange("p (h t) -> p h t", h=4),
                in1=tri4, op=ALU.mult))
            yield
            po, PBo = PSR.get()
            res.append((po, PBo))
            for h in range(4):
                P.op("pe", [B_A, B_vtok[sub]], [PBo], lambda e, h=h: e.matmul(
                    po[:, h * 128:(h + 1) * 128], A[:, h * 128:(h + 1) * 128], vtok[:, sub, h * 128:(h + 1) * 128],
                    start=(h == 0), stop=False, skip_group_check=True), n=128)
            for c in range(2):
                pu, PBu = pus[c]
                rows = slice(c * 64, (c + 1) * 64)
                toks = slice(sub * 128 + c * 64, sub * 128 + (c + 1) * 64)
                for h in range(4):
                    last = (c == 1 and h == 3)
                    P.op("pe", [B_q[h], self.B_Sb[h]], [PBo], lambda e, h=h, rows=rows, toks=toks, last=last: e.matmul(
                        po[rows, h * 128:(h + 1) * 128], qT[h][:, toks], self.Sb[h][:],
                        start=False, stop=last, skip_group_check=True), n=128)
                ci = sub * 2 + c
                for h in range(4):
                    e_ap = ebk[h][:, ci:ci + 1]
                    P.op("pool", [self.B_S[h], B_e[h]], [self.B_S1[h]], lambda e, h=h, e_ap=e_ap: e.tensor_scalar(
                        out=self.S1[h][:], in0=self.S[h][:], scalar1=e_ap, scalar2=0.0, op0=ALU.mult, op1=ALU.add), c=0.34)
                yield
                for h in range(4):
                    e_ap = ebk[h][:, ci:ci + 1]
                    P.op("dve", [PBu, self.B_S1[h], B_e[h]], [self.B_Sb[h]], lambda e, pu=pu, h=h, e_ap=e_ap: e.scalar_tensor_tensor(
                        out=self.Sb[h][:], in0=pu[:, h * 128:(h + 1) * 128], scalar=e_ap, in1=self.S1[h][:],
                        op0=ALU.mult, op1=ALU.add), n=128)
                    P.op("dve", [PBu, self.B_S1[h], B_e[h]], [self.B_S[h]], lambda e, pu=pu, h=h, e_ap=e_ap: e.scalar_tensor_tensor(
                        out=self.S[h][:], in0=pu[:, h * 128:(h + 1) * 128], scalar=e_ap, in1=self.S1[h][:],
                        op0=ALU.mult, op1=ALU.add), n=128)
                yield

    def scan_phase(d):
        with contextlib.ExitStack() as st:
            W = sb("s_W%d" % d, [128, 8, 1536], BF16, st)
            B_W = P.buf("s_W")
            with contextlib.ExitStack() as s1:
                stage = mk_stage(s1, 4)
                fcol = 1024 if d == 1 else 512
                load_weight(s1, W[:, :, 0:512], B_W, w_in[:, 0:512], 8, 512, stage=stage)
                load_weight(s1, W[:, :, 512:1024], B_W, w_in[:, fcol:fcol + 512], 8, 512, stage=stage)
                load_weight(s1, W[:, :, 1024:1536], B_W, w_in[:, 1536:2048], 8, 512, stage=stage)
                P.barrier()
            fr = Front(st, 3, C_G1, C_SH1, "s%d" % d)
            fr.pool_xn = True
            sc = Scan(st, "s%d" % d)
            vm = sb("s_vm%d" % d, [128, 36], F32, st)
            B_vm = P.buf("s_vm", dma=True)
            P.dma("sp", vm[:], vmB if d == 1 else vmC, [], [B_vm], B_vm)
            hTr = [(sb("s_hT%d_%d" % (d, i), [128, 8, 512], BF16, st), P.buf("s_hT%d" % i)) for i in range(2)]
            ost = Pools([(sb("s_ost%d_%d" % (d, i), [128, 512], F32, st), P.buf("s_ost%d" % i, dma=True)) for i in range(2)])
            top = 1024 + OWN + HH
            base = 1024 - HH
            flip = (d == 1)
            o_s = obw_s if d == 1 else ofw_s
            NT = 9

            class FrontStream:
                def __init__(self):
                    self.pend = None

                def step(self, t, s4):
                    hTt, B_hTt = hTr[t % 2]
                    i = t * 4 + s4
                    rb = (top - 128 * (i + 1)) if flip else (base + 128 * i)
                    xt, B_x = fr.xs.get()
                    P.dma("sp", xt[:], xw[rb:rb + 128, :], [], [B_x], B_x)
                    a = (fr.run_a(xt, B_x), t, s4)
                    self.flush()
                    self.pend = a

                def flush(self):
                    if self.pend is not None:
                        pa_, pt, ps4 = self.pend
                        hTt, B_hTt = hTr[pt % 2]
                        fr.run_b(pa_, hTt[:, :, ps4 * 128:(ps4 + 1) * 128], B_hTt, flip)
                        self.pend = None

            fs = FrontStream()

            def drain(*gens):
                gens = [g for g in gens if g is not None]
                while gens:
                    for g in list(gens):
                        try:
                            next(g)
                        except StopIteration:
                            gens.remove(g)

            def gen_front(t, s4):
                fs.step(t, s4)
                yield

            def gen_prepV(z, s4, hn, B_hn, col):
                yield
                yield
                sc.prepV(z, s4, hn, B_hn, W, B_W, vm, B_vm, col)
                yield

            def gen_out(res, i):
                po, PBo = res[0]
                if i >= 4:
                    ot, B_ot = ost.get()
                    P.op("act", [PBo], [B_ot], lambda e: e.activation(
                        out=ot[:], in_=po[:], func=AF.Copy, scale=float(128 ** -0.5)))
                    row = (i - 4) * 128
                    P.dma("pool", o_s[row:row + 128, :], ot[:], [B_ot], [], B_ot)

            for s4 in range(4):
                fs.step(0, s4)
            fs.flush()
            for s4 in range(4):
                drain(sc.prepH(0, s4, hTr[0][0], hTr[0][1], W, B_W, d))
                sc.prepV(0, s4, hTr[0][0], hTr[0][1], W, B_W, vm, B_vm, s4)
            for s4 in range(4):
                fs.step(1, s4)
            fs.flush()
            for t in range(NT):
                z = t % 2
                sc.prepK(z)
                for s4 in range(4):
                    res = []
                    g_f = gen_front(t + 2, s4) if t + 2 < NT else None
                    g_p = g_v = None
                    if t + 1 < NT:
                        hn, B_hn = hTr[(t + 1) % 2]
                        g_p = sc.prepH(1 - z, s4, hn, B_hn, W, B_W, d)
                        g_v = gen_prepV(1 - z, s4, hn, B_hn, (t + 1) * 4 + s4)
                    drain(g_f, g_p, sc.scan_sub(z, s4, res), g_v)
                    gen_out(res, t * 4 + s4)
                fs.flush()
            P.barrier()

    if upto >= 2:
        scan_phase(1)
    if upto >= 3:
        scan_phase(0)

    if upto >= 3:
        with contextlib.ExitStack() as st:
            W = sb("c_W", [128, 8, 2560], BF16, st)
            Wa = sb("c_Wa", [128, 4, D], BF16, st)
            Wb = sb("c_Wb", [128, 4, D], BF16, st)
            Wo = sb("c_Wo", [128, 8, D], BF16, st)
            B_W = P.buf("c_W")
            with contextlib.ExitStack() as s1:
                stage = mk_stage(s1, 4)
                load_weight(s1, W[:, :, 0:512], B_W, w_in[:, 2048:2560], 8, 512, stage=stage)
                load_weight(s1, W[:, :, 512:2560], B_W, w_in[:, 4096:6144], 8, 2048, stage=stage)
                load_weight(s1, Wa, B_W, w_bra, 4, D, stage=stage)
                load_weight(s1, Wb, B_W, w_brb, 4, D, stage=stage)
                gate1_bc = sb("gate1_bc", [128, D], F32, s1)
                B_g1 = P.buf("g1bc", dma=True)
                P.dma("sp", gate1_bc[:], g_s[0], [], [B_g1], B_g1)
                load_weight(s1, Wo, B_W, w_out, 8, D, gate_bc=gate1_bc, B_gate=B_g1, stage=stage)
                P.barrier()
            fr = Front(st, 3, C_G1, C_SH1, "c")
            hTr = [(sb("c_hT%d" % i, [128, 8, 512], BF16, st), P.buf("c_hT%d" % i)) for i in range(2)]
            sgT = sb("c_sgT", [128, 4, 512], BF16, st)
            B_sgT = P.buf("c_sgT")
            oaT = sb("c_oaT", [128, 4, 512], BF16, st)
            B_oaT = P.buf("c_oaT")
            obTr = Pools([(sb("c_obT%d" % i, [128, 4, 512], BF16, st), P.buf("c_obT%d" % i, dma=True)) for i in range(2)])
            mT = sb("c_mT", [128, 8, 512], BF16, st)
            B_mT = P.buf("c_mT")
            obw = Pools([(sb("c_obw%d" % i, [128, 512], F32, st), P.buf("c_obw%d" % i, dma=True)) for i in range(2)])
            ofw = Pools([(sb("c_ofw%d" % i, [128, 512], F32, st), P.buf("c_ofw%d" % i, dma=True)) for i in range(2)])
            osbr = Pools([(sb("c_osb%d" % i, [128, 512], F32, st), P.buf("c_osb%d" % i)) for i in range(2)])
            onbr = Pools([(sb("c_onb%d" % i, [128, 512], BF16, st), P.buf("c_onb%d" % i)) for i in range(2)])
            gstr = Pools([(sb("c_gst%d" % i, [128, 16], F32, st), P.buf("c_gst%d" % i)) for i in range(2)])
            gtmp = Pools([(sb("c_gt%d" % i, [128, 512], F32, st), P.buf("c_gt%d" % i)) for i in range(4)])
            x1st = Pools([(sb("c_x1%d" % i, [128, D], F32, st), P.buf("c_x1%d" % i, dma=True)) for i in range(2)])
            junk = sb("c_junk", [128, 512], BF16, st)
            B_junk = P.buf("c_junk")

            class FS:
                def __init__(self):
                    self.pend = None

                def step(self, t, s4):
                    rb = 1024 + t * 512 + s4 * 128
                    xt, B_x = fr.xs.get()
                    P.dma("sp", xt[:], xw[rb:rb + 128, :], [], [B_x], B_x)
                    a = (fr.run_a(xt, B_x), t, s4)
                    self.flush()
                    self.pend = a

                def flush(self):
                    if self.pend is not None:
                        pa_, pt, ps4 = self.pend
                        hTt, B_hTt = hTr[pt % 2]
                        fr.run_b(pa_, hTt[:, :, ps4 * 128:(ps4 + 1) * 128], B_hTt, False)
                        self.pend = None

            fs = FS()
            for s4 in range(4):
                fs.step(0, s4)
            fs.flush()
            for tile in range(8):
                hTt, B_hTt = hTr[tile % 2]
                t0 = tile * 512
                obT, B_obT = obTr.get()
                P.dma("sp", obT[:], ob_s.rearrange("(c p) t -> p c t", p=128)[:, :, t0:t0 + 512], [], [B_obT], B_obT)
                for h in range(4):
                    pg, PBg = PSR.get()
                    proj_fm(pg, PBg, W, B_W, h * 128, hTt, B_hTt)
                    g1, B1 = gtmp.get()
                    bg = cv(C_BIN + 16 + h)
                    P.op("act", [PBg, B_colv], [B1], lambda e, pg=pg, g1=g1, bg=bg: e.activation(
                        out=g1[:], in_=pg[:], func=AF.Sigmoid, bias=bg), tb='s')
                    P.op("dve", [PBg, B1, B_colv], [B_sgT], lambda e, pg=pg, g1=g1, bg=bg, h=h: e.scalar_tensor_tensor(
                        out=sgT[:, h, :], in0=pg[:], scalar=bg, in1=g1[:], op0=ALU.add, op1=ALU.mult))
                for s4 in range(4):
                    tt = t0 + s4 * 128
                    ow, B_ow = obw.get()
                    P.dma("sp", ow[:], obw_s[OWN - 128 - tt: OWN - tt, :], [], [B_ow], B_ow)
                    of, B_of = ofw.get()
                    P.dma("sp", of[:], ofw_s[tt:tt + 128, :], [], [B_of], B_of)
                    po, PBo = PSR.get()
                    P.op("pe", [B_ow, B_cst], [PBo], lambda e, po=po, ow=ow: e.matmul(po[:], J_f, ow[:], start=True, stop=True), n=2048)
                    osb, B_osb = osbr.get()
                    P.op("dve", [PBo, B_of], [B_osb], lambda e, po=po, of=of, osb=osb: e.tensor_tensor(
                        out=osb[:], in0=po[:], in1=of[:], op=ALU.add))
                    P.op("pool", [B_osb], [B_junk], lambda e, osb=osb: e.tensor_tensor(out=junk[:], in0=osb[:], in1=osb[:], op=ALU.mult))
                    gst, B_gst = gstr.get()
                    P.op("dve", [B_junk], [B_gst], lambda e, gst=gst: e.reduce_sum(
                        out=gst[:, 0:4], in_=junk[:].rearrange("p (h d) -> p h d", h=4), axis=mybir.AxisListType.X), n=512)
                    P.op("dve", [B_gst], [B_gst], lambda e, gst=gst: e.tensor_scalar(
                        out=gst[:, 4:8], in0=gst[:, 0:4], scalar1=1.0 / 128, scalar2=EPS, op0=ALU.mult, op1=ALU.add), n=4)
                    P.op("act", [B_gst], [B_gst], lambda e, gst=gst: e.activation(out=gst[:, 8:12], in_=gst[:, 4:8], func=AF.Ln), n=4, tb='e')
                    P.op("act", [B_gst], [B_gst], lambda e, gst=gst: e.activation(
                        out=gst[:, 12:16], in_=gst[:, 8:12], func=AF.Exp, scale=-0.5), n=4, tb='e')
                    onb, B_onb = onbr.get()
                    for h in range(4):
                        eng = "act" if h % 2 == 0 else "pool"
                        if eng == "act":
                            P.op("act", [B_osb, B_gst], [B_onb], lambda e, h=h, osb=osb, onb=onb, gst=gst: e.activation(
                                out=onb[:, h * 128:(h + 1) * 128], in_=osb[:, h * 128:(h + 1) * 128], func=AF.Copy,
                                scale=gst[:, 12 + h:13 + h]), n=128)
                        else:
                            P.op("pool", [B_osb, B_gst], [B_onb], lambda e, h=h, osb=osb, onb=onb, gst=gst: e.tensor_scalar(
                                out=onb[:, h * 128:(h + 1) * 128], in0=osb[:, h * 128:(h + 1) * 128],
                                scalar1=gst[:, 12 + h:13 + h], scalar2=0.0, op0=ALU.mult, op1=ALU.add), n=128)
                    ptp, PBt = PSR.get()
                    for h in range(4):
                        P.op("pe", [B_onb, B_cstb], [PBt], lambda e, ptp=ptp, h=h, onb=onb: e.matmul(
                            ptp[:, h * 128:(h + 1) * 128], onb[:, h * 128:(h + 1) * 128], ident_b, start=True, stop=True), n=128)
                    for h in range(4):
                        P.op("dve", [PBt, B_colv, B_sgT], [B_oaT], lambda e, ptp=ptp, h=h, s4=s4: e.scalar_tensor_tensor(
                            out=oaT[:, h, s4 * 128:(s4 + 1) * 128], in0=ptp[:, h * 128:(h + 1) * 128],
                            scalar=cv(C_GN + h), in1=sgT[:, h, s4 * 128:(s4 + 1) * 128], op0=ALU.mult, op1=ALU.mult), n=128)
                for cc in range(8):
                    pga, PBga = PSR.get()
                    proj_fm(pga, PBga, W, B_W, 512 + cc * 128, hTt, B_hTt)
                    pgb, PBgb = PSR.get()
                    proj_fm(pgb, PBgb, W, B_W, 1536 + cc * 128, hTt, B_hTt)
                    pa, PBa = PSR.get()
                    proj_fm(pa, PBa, Wa, B_W, cc * 128, oaT, B_oaT, nkc=4)
                    pb_, PBb = PSR.get()
                    proj_fm(pb_, PBb, Wb, B_W, cc * 128, obT, B_obT, nkc=4)
                    ga, B_ga = gtmp.get()
                    gb, B_gb = gtmp.get()
                    P.op("act", [PBga, B_colv], [B_ga], lambda e, pga=pga, ga=ga, cc=cc: e.activation(
                        out=ga[:], in_=pga[:], func=AF.Sigmoid, bias=cv(C_BIN + 32 + cc)), tb='s')
                    P.op("act", [PBgb, B_colv], [B_gb], lambda e, pgb=pgb, gb=gb, cc=cc: e.activation(
                        out=gb[:], in_=pgb[:], func=AF.Sigmoid, bias=cv(C_BIN + 40 + cc)), tb='s')
                    P.op("dve", [PBa, B_ga], [B_ga], lambda e, pa=pa, ga=ga: e.tensor_tensor(
                        out=ga[:], in0=pa[:], in1=ga[:], op=ALU.mult))
                    P.op("dve", [PBb, B_gb], [B_gb], lambda e, pb_=pb_, gb=gb: e.tensor_tensor(
                        out=gb[:], in0=pb_[:], in1=gb[:], op=ALU.mult))
                    P.op("pool", [B_ga, B_gb], [B_mT], lambda e, ga=ga, gb=gb, cc=cc: e.tensor_tensor(
                        out=mT[:, cc, :], in0=ga[:], in1=gb[:], op=ALU.add))
                    if tile + 1 < 8 and cc % 2 == 1:
                        fs.step(tile + 1, cc // 2)
                fs.flush()
                for s4 in range(4):
                    xo, B_xo = x1st.get()
                    rb = 1024 + t0 + s4 * 128
                    P.dma("sp", xo[:], xw[rb:rb + 128, :], [], [B_xo], B_xo)
                    for half in range(2):
                        px, PBx = PSR.get()
                        proj_tm(px[:], PBx, lambda kc, s4=s4: mT[:, kc, s4 * 128:(s4 + 1) * 128], B_mT, Wo, B_W, half * 512, 512)
                        P.op("dve", [PBx, B_xo], [B_xo], lambda e, px=px, xo=xo, half=half: e.tensor_tensor(
                            out=xo[:, half * 512:(half + 1) * 512], in0=px[:], in1=xo[:, half * 512:(half + 1) * 512],
                            op=ALU.add))
                    tt = t0 + s4 * 128
                    P.dma("pool", x1_s[tt:tt + 128, :], xo[:], [B_xo], [], B_xo)
            P.barrier()

    if upto >= 4:
        with contextlib.ExitStack() as st:
            Wi = sb("d_Wi", [128, 8, 2 * FFN], BF16, st)
            Wf = sb("d_Wf", [128, NJ, D], BF16, st)
            B_W = P.buf("d_W")
            with contextlib.ExitStack() as s1:
                stage = mk_stage(s1, 3)
                load_weight(s1, Wi, B_W, w_fi, 8, 2 * FFN, stage=stage)
                gate2_bc = sb("gate2_bc", [128, D], F32, s1)
                B_g2 = P.buf("g2bc", dma=True)
                P.dma("sp", gate2_bc[:], g_s[1], [], [B_g2], B_g2)
                load_weight(s1, Wf, B_W, w_fo, NJ, D, gate_bc=gate2_bc, B_gate=B_g2, stage=stage)
                P.barrier()
            fngb = sb("d_fng", [128, D], F32, st)
            B_fng = P.buf("d_fng", dma=True)
            P.dma("sp", fngb[:], fng.partition_broadcast(128), [], [B_fng], B_fng)
            fr = Front(st, 2, C_G2, C_SH2, "d", nxn=1)
            hTr = [(sb("d_hT%d" % i, [128, 8, 512], BF16, st), P.buf("d_hT%d" % i)) for i in range(2)]
            yT = sb("d_yT", [128, NJ, 512], BF16, st)
            B_yT = P.buf("d_yT")
            gtmp = Pools([(sb("d_gt%d" % i, [128, 512], F32, st), P.buf("d_gt%d" % i)) for i in range(2)])
            res = Pools([(sb("d_res%d" % i, [128, D], F32, st), P.buf("d_res%d" % i, dma=True)) for i in range(2)])

            def d_front(tile):
                hTt, B_hTt = hTr[tile % 2]

                def d_ld(s4):
                    t0 = tile * 512 + s4 * 128
                    xt, B_x = fr.xs.get()
                    P.dma("sp", xt[:], x1_s[t0:t0 + 128, :], [], [B_x], B_x)
                    return xt, B_x
                for s4 in range(4):
                    xt, B_x = d_ld(s4)
                    fr.run(xt, B_x, hTt[:, :, s4 * 128:(s4 + 1) * 128], B_hTt, False)

            d_front(0)
            for tile in range(8):
                hTt, B_hTt = hTr[tile % 2]
                for j in range(NJ):
                    pg, PBg = PSR.get()
                    proj_fm(pg, PBg, Wi, B_W, j * 128, hTt, B_hTt)
                    pu, PBu = PSR.get()
                    proj_fm(pu, PBu, Wi, B_W, FFN + j * 128, hTt, B_hTt)
                    g1, B1 = gtmp.get()
                    P.op("act", [PBg], [B1], lambda e, pg=pg, g1=g1: e.activation(out=g1[:], in_=pg[:], func=AF.Sigmoid), tb='s')
                    P.op("dve", [PBg, B1], [B1], lambda e, pg=pg, g1=g1: e.tensor_tensor(
                        out=g1[:], in0=pg[:], in1=g1[:], op=ALU.mult))
                    P.op("dve", [PBu, B1], [B_yT], lambda e, pu=pu, g1=g1, j=j: e.tensor_tensor(
                        out=yT[:, j, :], in0=pu[:], in1=g1[:], op=ALU.mult))
                if tile + 1 < 8:
                    d_front(tile + 1)
                for s4 in range(4):
                    xo, B_xo = res.get()
                    t0 = tile * 512 + s4 * 128
                    P.dma("sp", xo[:], x1_s[t0:t0 + 128, :], [], [B_xo], B_xo)
                    for half in range(2):
                        px, PBx = PSR.get()
                        proj_tm(px[:], PBx, lambda kc, s4=s4: yT[:, kc, s4 * 128:(s4 + 1) * 128], B_yT, Wf, B_W, half * 512, 512, nkc=NJ)
                        P.op("dve", [PBx, B_xo], [B_xo], lambda e, px=px, xo=xo, half=half: e.tensor_tensor(
                            out=xo[:, half * 512:(half + 1) * 512], in0=px[:], in1=xo[:, half * 512:(half + 1) * 512],
                            op=ALU.add))
                    s4t, B_s = fr.stats(xo, B_xo)
                    P.op("act", [B_xo, B_s], [B_xo], lambda e, xo=xo, s4t=s4t: e.activation(
                        out=xo[:], in_=xo[:], func=AF.Copy, scale=s4t[:, 3:4]), n=1024)
                    P.op("pool", [B_xo, B_fng], [B_xo], lambda e, xo=xo: e.tensor_tensor(
                        out=xo[:], in0=xo[:], in1=fngb[:], op=ALU.mult), c=1.9)
                    P.dma("pool", y[t0:t0 + 128, :], xo[:], [B_xo], [], B_xo)
            P.barrier()
    P.flush()
    P.es.close()
    return nc, P


def _col(v, n):
    return np.ascontiguousarray(np.asarray(v, np.float32).reshape(n, 128).T)


def _consts():
    c = np.zeros((128, 1280), np.float32)
    i = np.arange(128)
    c[:, 0:128] = np.eye(128, dtype=np.float32)
    c[:, 128:256] = np.eye(128, dtype=np.float32)[::-1]
    s = i[:, None]
    t = i[None, :]
    c[:, 256:384] = ((s // 64 == t // 64) & (t >= s)).astype(np.float32)
    c[:, 384:512] = (t <= s).astype(np.float32)
    c[:, 512:640] = (t >= s).astype(np.float32)
    cm = np.ones(512, np.float32)
    cm[::64] = 0.0
    c[:, 640:1152] = cm[None, :]
    c[:, 1152:1280] = 1.0
    return c


def _core_geom(c):
    if c < 4:
        return 0, c // 2, (c % 2) * OWN, 8192
    return 1, 0, (c - 4) * OWN, 16384


_NC_CACHE = {}


def kernel(x_prompt, x_sample, c_prompt, c_sample, w_ada, b_ada, norm1_g, w_in, b_in, lb_logits,
           hg_norm_g, w_branch_a, w_branch_b, w_out, norm2_g, w_ffn_in, w_ffn_out, final_norm_g,
           _debug=False):
    f32 = lambda a: np.ascontiguousarray(np.asarray(a, dtype=np.float32))
    x_prompt, x_sample = f32(x_prompt), f32(x_sample)
    c_prompt, c_sample = f32(c_prompt), f32(c_sample)
    w_in0 = f32(w_in)[0]
    b_in0 = f32(b_in)[0]
    perm = np.arange(1024).reshape(2, 8, 2, 32)[:, :, ::-1, :].reshape(-1)
    w_qksw = np.ascontiguousarray(w_in0[:, 2560:3584][:, perm])
    b_sw = b_in0[2560:3584][perm]
    lbl = np.concatenate([_col(f32(lb_logits)[l, d], 4) for l in range(2) for d in range(2)], axis=1)
    shared = {
        "w_ada": f32(w_ada)[0], "b_ada": f32(b_ada)[0], "n1g": _col(f32(norm1_g)[0], 8),
        "n2g": _col(f32(norm2_g)[0], 8), "w_in": w_in0, "w_qksw": w_qksw, "bin_col": _col(b_in0, 48),
        "bsw_col": _col(b_sw, 8), "b_in_row": b_in0, "lbl": np.ascontiguousarray(lbl),
        "gng": _col(f32(hg_norm_g)[0], 4), "w_bra": f32(w_branch_a)[0], "w_brb": f32(w_branch_b)[0],
        "w_out": f32(w_out)[0], "w_fi": f32(w_ffn_in)[0], "w_fo": f32(w_ffn_out)[0],
        "fng": f32(final_norm_g), "consts": _consts(),
    }
    half = 32
    inv = (np.float32(ROPE_THETA) ** (-np.arange(half, dtype=np.float32) / np.float32(half))).astype(np.float32)
    pidx = np.arange(128)
    fidx = pidx % 32
    sgn = np.where((pidx % 64) < 32, -1.0, 1.0).astype(np.float32)
    vts = att_vtiles()
    in_maps = []
    for c in range(NCORES):
        grp, b, own0, L = _core_geom(c)
        xs = x_prompt[b] if grp == 0 else x_sample[0]
        cvec = c_prompt[b] if grp == 0 else c_sample[0]
        w0 = own0 - HALO
        xw = np.zeros((WIN, D), np.float32)
        lo, hi = max(0, w0), min(L, w0 + WIN)
        xw[lo - w0:hi - w0] = xs[lo:hi]
        pos = (w0 + np.arange(WIN)).astype(np.float32)
        ang = (pos[None, :] * inv[fidx][:, None]).astype(np.float32)
        cosT = np.cos(ang).astype(np.float32)
        sinT = (np.sin(ang).astype(np.float32) * sgn[:, None]).astype(np.float32)
        valid = ((w0 + np.arange(WIN) >= 0) & (w0 + np.arange(WIN) < L)).astype(np.float32)
        vmA = np.zeros((128, 64), np.float32)
        for hf in range(2):
            for i in range(32):
                vmA[:, hf * 32 + i] = valid[2048 * hf + 128 * i + np.arange(128)]
        vmB = np.zeros((128, 36), np.float32)
        top = 1024 + OWN + HH
        for i in range(36):
            rb = top - 128 * (i + 1)
            vmB[:, i] = valid[rb + 127 - np.arange(128)]
        vmC = np.zeros((128, 36), np.float32)
        for i in range(36):
            rb = (1024 - HH) + 128 * i
            vmC[:, i] = valid[rb + np.arange(128)]
        m = dict(shared)
        m.update({"xw": xw, "ccol": _col(cvec, 8), "cosT": cosT, "sinT": sinT, "vmA": vmA, "vmB": vmB, "vmC": vmC})
        in_maps.append(m)
    import os as _os
    upto = int(_os.environ.get("K_UPTO", "4")) if _debug else 4
    key = (bool(_debug), upto)
    if key not in _NC_CACHE:
        _NC_CACHE[key] = build_program(debug=key[0], upto=upto)[0]
    nc = _NC_CACHE[key]
    res = run_bass_kernel_spmd(nc, in_maps, core_ids=list(range(NCORES)))
    outs = res.results
    y_prompt = np.zeros((2, 8192, D), np.float32)
    y_sample = np.zeros((1, 16384, D), np.float32)
    for c in range(NCORES):
        grp, b, own0, L = _core_geom(c)
        yc = np.asarray(outs[c]["y"], np.float32)
        if grp == 0:
            y_prompt[b, own0:own0 + OWN] = yc
        else:
            y_sample[0, own0:own0 + OWN] = yc
    if _debug:
        return (y_prompt, y_sample), outs
    return (y_prompt, y_sample)
```

```python
import contextlib
import numpy as np
import concourse.bass as bass
import concourse.mybir as mybir
from concourse.alu_op_type import AluOpType as ALU
from concourse.bass_utils import run_bass_kernel_spmd

F32 = mybir.dt.float32
BF16 = mybir.dt.bfloat16
AF = mybir.ActivationFunctionType

D = 1024
NCORES = 8
OWN = 4096
HALO = 1024
WIN = OWN + 2 * HALO
HH = 512
FFN = 2816
NJ = FFN // 128
EPS = 1e-6
ROPE_THETA = 10000.0

PATTERNS = (1, 4, 16)


def att_vtiles():
    out = []
    for r in PATTERNS:
        jq0 = 1024 // r
        nb = (2048 // r) // 128
        for rho in range(r):
            for m in range(nb + 1):
                start = rho + r * (jq0 - 64 + 128 * m)
                out.append((r, rho, m, start, nb))
    return out


class Buf:
    __slots__ = ("name", "w", "r", "dsem", "ssem")

    def __init__(self, name):
        self.name = name
        self.w = {}
        self.r = {}
        self.dsem = None
        self.ssem = None


class Prog:
    ENG = ("pe", "act", "dve", "pool", "sp")

    def __init__(self, nc):
        self.nc = nc
        self.es = contextlib.ExitStack()
        self.eng = {"pe": nc.tensor, "act": nc.scalar, "dve": nc.vector, "pool": nc.gpsimd, "sp": nc.sync}
        self.sems = {}
        self.cnt = {}
        for e in self.ENG:
            self.sems[e] = self.es.enter_context(nc.semaphore("sem_" + e))
            self.cnt[e] = 0
        self.ndma = 12
        for i in range(self.ndma):
            k = "d%d" % i
            self.sems[k] = self.es.enter_context(nc.semaphore("sem_" + k))
            self.cnt[k] = 0
        self.nsw = 6
        for i in range(self.nsw):
            k = "w%d" % i
            self.sems[k] = self.es.enter_context(nc.semaphore("sem_" + k))
            self.cnt[k] = 0
        self.dma_next = 0
        self.sw_next = 0
        self.seen = {e: {} for e in self.ENG}
        self.nins = 0
        self.defer = True
        self.q = []
        self.window = 40
        self.act_tbl = None
        self.est_time = 0.0

    def buf(self, name, dma=False):
        b = Buf(name)
        if dma:
            assert self.dma_next < self.ndma, "out of dma semaphores"
            b.dsem = "d%d" % self.dma_next
            self.dma_next += 1
        return b

    def _deps(self, e, reads, writes):
        deps = {}
        for b in reads:
            for k, v in b.w.items():
                if deps.get(k, 0) < v:
                    deps[k] = v
        for b in writes:
            for k, v in b.w.items():
                if deps.get(k, 0) < v:
                    deps[k] = v
            for k, v in b.r.items():
                if deps.get(k, 0) < v:
                    deps[k] = v
        seen = self.seen[e]
        for k, v in deps.items():
            if k == e and e == "pe":
                continue
            if (k[0] == "d" and k != "dve") or k[0] == "w":
                v = self.cnt[k]
            if seen.get(k, 0) < v:
                self.eng[e].wait_ge(self.sems[k], v)
                seen[k] = v

    DEFAULT_COST = {"pe": 0.22, "act": 0.55, "dve": 0.55, "pool": 1.2, "sp": 0.15}

    def op(self, e, reads, writes, ins_fn, c=None, n=512, tb=None):
        if self.defer:
            if c is None:
                if e == "pe":
                    c = 0.07 + n * 0.0004
                elif e == "act":
                    c = 0.22 + n / 1400.0
                elif e == "dve":
                    c = 0.12 + n * 0.0011
                else:
                    c = 0.15 + n * 0.0022
            self.q.append((0, e, list(reads), list(writes), ins_fn, c, tb))
            return
        self._op_now(e, reads, writes, ins_fn)

    def dma(self, q, out, in_, reads, writes, semb, c=4.0):
        if self.defer:
            self.q.append((1, q, list(reads), list(writes), (out, in_, semb), c))
            return
        self._dma_now(q, out, in_, reads, writes, semb)

    def semkey(self, q, semb):
        if q == "sp":
            assert semb.dsem is not None
            return semb.dsem
        if semb.ssem is None:
            assert self.sw_next < self.nsw, "out of sw dma semaphores"
            semb.ssem = "w%d" % self.sw_next
            self.sw_next += 1
        return semb.ssem

    def _op_now(self, e, reads, writes, ins_fn):
        self._deps(e, reads, writes)
        ins = ins_fn(self.eng[e])
        self.cnt[e] += 1
        ins.then_inc(self.sems[e], 1)
        ev = self.cnt[e]
        for b in reads:
            b.r[e] = ev
        for b in writes:
            b.w = {e: ev}
            b.r = {}
        self.nins += 1

    def _dma_now(self, q, out, in_, reads, writes, semb):
        self._deps(q, reads, writes)
        k = self.semkey(q, semb)
        self.cnt[k] += 16
        self.eng[q].dma_start(out=out, in_=in_).then_inc(self.sems[k], 16)
        ev = self.cnt[k]
        for b in reads:
            b.r[k] = ev
        for b in writes:
            b.w = {k: ev}
            b.r = {}
        self.nins += 1

    def flush(self):
        ops = self.q
        self.q = []
        n = len(ops)
        if n == 0:
            return
        last_w = {}
        readers = {}
        deps = [None] * n
        for i, o in enumerate(ops):
            d = set()
            reads, writes = o[2], o[3]
            if o[0] == 1:
                kk_ = ("sem", self.semkey(o[1], o[4][2]))
                reads = reads + [kk_]
                writes = writes + [kk_]
            for b in reads:
                w = last_w.get(id(b) if not isinstance(b, tuple) else b)
                if w is not None:
                    d.add(w)
            for b in writes:
                kb = id(b) if not isinstance(b, tuple) else b
                w = last_w.get(kb)
                if w is not None:
                    d.add(w)
                for r in readers.get(kb, ()):
                    d.add(r)
            for b in reads:
                kb = id(b) if not isinstance(b, tuple) else b
                readers.setdefault(kb, []).append(i)
            for b in writes:
                kb = id(b) if not isinstance(b, tuple) else b
                last_w[kb] = i
                readers[kb] = []
            d.discard(i)
            deps[i] = d
        queues = {}
        for i, o in enumerate(ops):
            queues.setdefault(o[1], []).append(i)
        heads = {e: 0 for e in queues}
        done = [False] * n
        finish = [0.0] * n
        eng_free = {e: 0.0 for e in queues}
        W = self.window
        LAT = 0.08
        TBL = 1.3
        nsched = 0
        while nsched < n:
            best = None
            for e, ql in queues.items():
                h = heads[e]
                while h < len(ql) and done[ql[h]]:
                    h += 1
                heads[e] = h
                cnt = 0
                j = h
                ef = eng_free[e]
                while j < len(ql) and cnt < W:
                    i = ql[j]
                    j += 1
                    if done[i]:
                        continue
                    cnt += 1
                    rt = 0.0
                    ok = True
                    for dd in deps[i]:
                        if not done[dd]:
                            ok = False
                            break
                        f = finish[dd]
                        if f > rt:
                            rt = f
                    if not ok:
                        continue
                    st = rt + LAT if rt + LAT > ef else ef
                    if e == "act":
                        tb_ = ops[i][6] if ops[i][0] == 0 else None
                        if tb_ is not None and tb_ != self.act_tbl:
                            st += TBL
                    if best is None or st < best[0] or (st == best[0] and i < best[1]):
                        best = (st, i, e)
                    if st <= ef:
                        break
            st, i, e = best
            o = ops[i]
            if o[0] == 1:
                eng_free[e] = st + 0.15
                finish[i] = st + o[5]
            else:
                eng_free[e] = st + o[5]
                finish[i] = st + o[5]
                if e == "act" and o[6] is not None:
                    self.act_tbl = o[6]
            done[i] = True
            nsched += 1
            if o[0] == 1:
                self._dma_now(o[1], o[4][0], o[4][1], o[2], o[3], o[4][2])
            else:
                self._op_now(o[1], o[2], o[3], o[4])
        self.est_time += max(finish) if n else 0.0

    def barrier(self):
        self.flush()
        for e in self.ENG:
            seen = self.seen[e]
            for k, v in self.cnt.items():
                if v > 0 and seen.get(k, 0) < v:
                    self.eng[e].wait_ge(self.sems[k], v)
                    seen[k] = v
        self.dma_next = 0
        self.sw_next = 0


class Pools:
    def __init__(self, items):
        self.items = items
        self.i = 0

    def get(self):
        it = self.items[self.i % len(self.items)]
        self.i += 1
        return it


def build_program(debug=False, upto=4):
    nc = bass.Bass("TRN2", target_bir_lowering=False)
    P = Prog(nc)
    es = P.es

    def din(name, shape, dt=F32):
        return nc.dram_tensor(name, list(shape), dt, kind="ExternalInput").ap()

    xw = din("xw", [WIN, D])
    ccol = din("ccol", [128, 8])
    w_ada = din("w_ada", [D, 6 * D])
    b_ada = din("b_ada", [6 * D])
    n1g = din("n1g", [128, 8])
    n2g = din("n2g", [128, 8])
    w_in = din("w_in", [D, 6144])
    w_qksw = din("w_qksw", [D, 1024])
    bin_col = din("bin_col", [128, 48])
    bsw_col = din("bsw_col", [128, 8])
    b_in_row = din("b_in_row", [6144])
    lbl = din("lbl", [128, 16])
    gng = din("gng", [128, 4])
    w_bra = din("w_bra", [512, D])
    w_brb = din("w_brb", [512, D])
    w_out = din("w_out", [D, D])
    w_fi = din("w_fi", [D, 2 * FFN])
    w_fo = din("w_fo", [FFN, D])
    fng = din("fng", [D])
    cosT = din("cosT", [128, WIN])
    sinT = din("sinT", [128, WIN])
    vmA = din("vmA", [128, 64])
    vmB = din("vmB", [128, 36])
    vmC = din("vmC", [128, 36])
    consts = din("consts", [128, 1280])

    okind = "ExternalOutput"
    y = nc.dram_tensor("y", [OWN, D], F32, kind=okind).ap()
    skind = okind if debug else "Internal"
    ob_s = nc.dram_tensor("ob_s", [512, OWN], BF16, kind=skind).ap()
    obw_s = nc.dram_tensor("obw_s", [OWN, 512], F32, kind=skind).ap()
    ofw_s = nc.dram_tensor("ofw_s", [OWN, 512], F32, kind=skind).ap()
    v_s = nc.dram_tensor("v_s", [4096, 8, 65], BF16, kind="Internal").ap()
    B_vs = P.buf("v_s")
    x1_s = nc.dram_tensor("x1_s", [OWN, D], F32, kind=skind).ap()

    uid = [0]

    def sb(name, shape, dt, stack=None):
        uid[0] += 1
        return (stack or es).enter_context(nc.sbuf_tensor("%s_u%d" % (name, uid[0]), list(shape), dt))

    def ps(name, stack=None):
        return (stack or es).enter_context(nc.psum_tensor(name, [128, 512], F32))

    cst = sb("cst", [128, 1280], F32)
    cstb = sb("cstb", [128, 1280], BF16)
    B_cst = P.buf("cst", dma=True)
    B_cstb = P.buf("cstb")
    ident_f = cst[:, 0:128]
    J_f = cst[:, 128:256]
    ident_b = cstb[:, 0:128]
    J_b = cstb[:, 128:256]
    tri_b = cstb[:, 256:384]
    mA_b = cstb[:, 384:512]
    mB_b = cstb[:, 512:640]
    cmask_f = cst[:, 640:1152]

    colv = sb("colv", [128, 160], F32)
    B_colv = P.buf("colv")
    C_G1, C_SH1, C_G2, C_SH2 = 0, 8, 16, 24
    C_LB, C_OML = 32, 40
    C_GN = 48
    C_BIN = 52
    C_BSW = 100
    C_TMP = 108
    g_s = nc.dram_tensor("g_s", [2, 128, D], F32, kind="Internal").ap()

    psb = [ps("psb%d" % i) for i in range(8)]
    PSR = Pools([(psb[i], P.buf("ps%d" % i)) for i in range(8)])

    def cv(c, n=1):
        return colv[:, c:c + n]

    def load_weight(st, dst, B_dst, src_rows_ap, nkc, ncols, gate_bc=None, B_gate=None, stage=None):
        v = src_rows_ap.rearrange("(kc p) n -> p kc n", p=128)
        cw = max(64, min(ncols, stage.cols // nkc))
        i = 0
        for c0 in range(0, ncols, cw):
            c1 = min(ncols, c0 + cw)
            t, B = stage.get()
            P.dma("sp", t[:, 0:nkc * (c1 - c0)].rearrange("p (k n) -> p k n", k=nkc), v[:, :, c0:c1], [], [B], B)
            tv = t[:, 0:nkc * (c1 - c0)].rearrange("p (k n) -> p k n", k=nkc)
            if gate_bc is None:
                h0 = nkc // 2
                for (eng, ka, kb) in (("pool", 0, max(1, nkc // 4)), ("dve", max(1, nkc // 4), max(2, (5 * nkc) // 8)),
                                      ("act", max(2, (5 * nkc) // 8), nkc)):
                    if kb <= ka:
                        continue
                    if eng == "act":
                        P.op("act", [B], [B_dst], lambda e, tv=tv, c0=c0, c1=c1, ka=ka, kb=kb: e.activation(
                            out=dst[:, ka:kb, c0:c1], in_=tv[:, ka:kb, :], func=AF.Copy))
                    else:
                        P.op(eng, [B], [B_dst], lambda e, tv=tv, c0=c0, c1=c1, ka=ka, kb=kb: e.tensor_copy(
                            out=dst[:, ka:kb, c0:c1], in_=tv[:, ka:kb, :]))
            else:
                for kc in range(nkc):
                    eng = "dve" if kc % 3 != 2 else "pool"
                    P.op(eng, [B, B_gate], [B_dst], lambda e, tv=tv, c0=c0, c1=c1, kc=kc: e.tensor_tensor(
                        out=dst[:, kc, c0:c1], in0=tv[:, kc, :], in1=gate_bc[:, c0:c1], op=ALU.mult))
            i += 1

    def mk_stage(st, n=2, cols=4096):
        pl = Pools([(sb("wstg%d" % i, [128, cols], F32, st), P.buf("wstg%d" % i, dma=True)) for i in range(n)])
        pl.cols = cols
        return pl

    pa_st = contextlib.ExitStack()
    if upto >= 1:
        Wq = sb("a_Wq", [128, 8, 512], BF16, pa_st)
        Wqs = sb("a_Wqs", [128, 8, 512], BF16, pa_st)
        Wk = sb("a_Wk", [128, 8, 512], BF16, pa_st)
        Wks = sb("a_Wks", [128, 8, 512], BF16, pa_st)
        Wv = sb("a_Wv", [128, 8, 512], BF16, pa_st)
        B_W = P.buf("a_W")

    with contextlib.ExitStack() as st:
        small = sb("p0small", [128, 64], F32, st)
        B_small = P.buf("p0small", dma=True)
        P.dma("sp", cst[:], consts, [], [B_cst], B_cst)
        P.op("dve", [B_cst], [B_cstb], lambda e: e.tensor_copy(out=cstb[:], in_=cst[:]))
        P.dma("sp", small[:, 0:8], ccol, [], [B_small], B_small)
        P.dma("sp", small[:, 24:40], lbl, [], [B_small], B_small)
        P.dma("sp", colv[:, C_GN:C_GN + 4], gng, [], [B_colv], B_small)
        P.dma("sp", colv[:, C_BIN:C_BIN + 48], bin_col, [], [B_colv], B_small)
        P.dma("sp", colv[:, C_BSW:C_BSW + 8], bsw_col, [], [B_colv], B_small)
        P.op("dve", [B_small], [B_small], lambda e: e.tensor_tensor(
            out=small[:, 40:48], in0=small[:, 24:32], in1=small[:, 32:40], op=ALU.subtract))
        P.op("act", [B_small], [B_colv], lambda e: e.activation(
            out=cv(C_LB, 8), in_=small[:, 40:48], func=AF.Sigmoid), tb='s')
        P.op("dve", [B_colv], [B_colv], lambda e: e.tensor_scalar(
            out=cv(C_OML, 8), in0=cv(C_LB, 8), scalar1=-1.0, scalar2=1.0, op0=ALU.mult, op1=ALU.add))
        scb = sb("scb", [128, 8, 128], F32, st)
        B_scb = P.buf("scb")
        P.op("act", [B_small], [B_small], lambda e: e.activation(
            out=small[:, 48:56], in_=small[:, 0:8], func=AF.Silu), tb='s')
        for kc in range(8):
            src = bass.AP(small.tensor if hasattr(small, "tensor") else small, 48 + kc, [[64, 128], [0, 128]])
            P.op("dve", [B_small], [B_scb], lambda e, kc=kc, src=src: e.tensor_copy(out=scb[:, kc, :], in_=src))
        badab = sb("badab", [128, 6 * D], F32, st)
        B_badab = P.buf("badab", dma=True)
        P.dma("sp", badab[:], b_ada.partition_broadcast(128), [], [B_badab], B_badab)
        modbc = sb("modbc", [128, 6 * D], F32, st)
        B_mod = P.buf("modbc")
        stg = [(sb("p0stg%d" % i, [128, 8, 512], F32, st), P.buf("p0stg%d" % i, dma=True)) for i in range(2)]
        wa_v = w_ada.rearrange("(kc p) n -> p kc n", p=128)
        for blk in range(12):
            t, B = stg[blk % 2]
            P.dma("sp", t[:], wa_v[:, :, blk * 512:(blk + 1) * 512], [], [B], B)
            pt, PB_ = PSR.get()
            for kc in range(8):
                P.op("pe", [B, B_scb], [PB_], lambda e, kc=kc, t=t, pt=pt: e.matmul(
                    pt[:], scb[:, kc, :], t[:, kc, :], start=(kc == 0), stop=(kc == 7)))
            P.op("dve", [PB_, B_badab], [B_mod], lambda e, blk=blk, pt=pt: e.tensor_tensor(
                out=modbc[:, blk * 512:(blk + 1) * 512], in0=pt[:], in1=badab[:, blk * 512:(blk + 1) * 512],
                op=ALU.add))
        pt, PB_ = PSR.get()
        for vi, base in enumerate((0, 1024, 3072, 4096)):
            for kc in range(8):
                P.op("pe", [B_mod, B_cst], [PB_], lambda e, vi=vi, base=base, kc=kc, pt=pt: e.matmul(
                    pt[:, vi * 8 + kc: vi * 8 + kc + 1], modbc[:, base + kc * 128: base + (kc + 1) * 128],
                    ident_f[:, 0:1], start=True, stop=True))
        P.op("dve", [PB_], [B_small], lambda e, pt=pt: e.tensor_copy(out=small[:, 0:32], in_=pt[:, 0:32]))
        sm2 = sb("p0sm2", [128, 16], F32, st)
        B_sm2 = P.buf("p0sm2", dma=True)
        P.dma("sp", sm2[:, 0:8], n1g, [], [B_sm2], B_sm2)
        P.dma("sp", sm2[:, 8:16], n2g, [], [B_sm2], B_sm2)
        P.op("dve", [B_small, B_sm2], [B_colv], lambda e: e.scalar_tensor_tensor(
            out=cv(C_G1, 8), in0=small[:, 8:16], scalar=1.0, in1=sm2[:, 0:8], op0=ALU.add, op1=ALU.mult))
        P.op("dve", [B_small, B_sm2], [B_colv], lambda e: e.scalar_tensor_tensor(
            out=cv(C_G2, 8), in0=small[:, 24:32], scalar=1.0, in1=sm2[:, 8:16], op0=ALU.add, op1=ALU.mult))
        P.op("dve", [B_small], [B_colv], lambda e: e.tensor_copy(out=cv(C_SH1, 8), in_=small[:, 0:8]))
        P.op("dve", [B_small], [B_colv], lambda e: e.tensor_copy(out=cv(C_SH2, 8), in_=small[:, 16:24]))
        B_gs = P.buf("p0gs", dma=True)
        P.dma("pool", g_s[0], modbc[:, 2048:3072], [B_mod], [], B_gs)
        P.dma("pool", g_s[1], modbc[:, 5120:6144], [B_mod], [], B_gs)
        if upto >= 1:
            stage = mk_stage(st, 3, 4096)
            load_weight(st, Wq, B_W, w_in[:, 2560:3072], 8, 512, stage=stage)
            load_weight(st, Wk, B_W, w_in[:, 3072:3584], 8, 512, stage=stage)
            load_weight(st, Wqs, B_W, w_qksw[:, 0:512], 8, 512, stage=stage)
            load_weight(st, Wks, B_W, w_qksw[:, 512:1024], 8, 512, stage=stage)
            load_weight(st, Wv, B_W, w_in[:, 3584:4096], 8, 512, stage=stage)
        P.barrier()

    class Front:
        def __init__(self, st, nxs, cG, cSH, tag, nxn=2):
            self.xs = Pools([(sb("xs%s%d" % (tag, i), [128, D], F32, st), P.buf("xs%s%d" % (tag, i), dma=True))
                             for i in range(nxs)])
            self.xn = Pools([(sb("xn%s%d" % (tag, i), [128, D], BF16, st), P.buf("xn%s%d" % (tag, i)))
                             for i in range(nxn)])
            self.junk = sb("junk" + tag, [128, D], BF16, st)
            self.B_junk = P.buf("junk" + tag)
            self.st4 = Pools([(sb("fst%s%d" % (tag, i), [128, 4], F32, st), P.buf("fst%s%d" % (tag, i)))
                              for i in range(2)])
            self.cG, self.cSH = cG, cSH
            self.pool_xn = False

        def stats(self, xt, B_x):
            s4, B_s = self.st4.get()
            junk, B_junk = self.junk, self.B_junk
            P.op("pool", [B_x], [B_junk], lambda e: e.tensor_tensor(out=junk[:], in0=xt[:], in1=xt[:], op=ALU.mult), c=1.9)
            P.op("dve", [B_junk], [B_s], lambda e: e.reduce_sum(out=s4[:, 0:1], in_=junk[:], axis=mybir.AxisListType.X), n=1024)
            P.op("dve", [B_s], [B_s], lambda e: e.tensor_scalar(
                out=s4[:, 1:2], in0=s4[:, 0:1], scalar1=1.0 / D, scalar2=EPS, op0=ALU.mult, op1=ALU.add), n=1)
            P.op("act", [B_s], [B_s], lambda e: e.activation(out=s4[:, 2:3], in_=s4[:, 1:2], func=AF.Ln), n=1, tb='e')
            P.op("act", [B_s], [B_s], lambda e: e.activation(out=s4[:, 3:4], in_=s4[:, 2:3], func=AF.Exp, scale=-0.5), n=1, tb='e')
            return s4, B_s

        def run(self, xt, B_x, hT_view, B_hT, flip):
            self.run_b(self.run_a(xt, B_x), hT_view, B_hT, flip)

        def run_a(self, xt, B_x):
            s4, B_s = self.stats(xt, B_x)
            xn, B_xn = self.xn.get()
            if self.pool_xn:
                P.op("pool", [B_x, B_s], [B_xn], lambda e: e.tensor_scalar(
                    out=xn[:], in0=xt[:], scalar1=s4[:, 3:4], scalar2=0.0, op0=ALU.mult, op1=ALU.add), c=1.1)
            else:
                P.op("act", [B_x, B_s], [B_xn], lambda e: e.activation(
                    out=xn[:], in_=xt[:], func=AF.Copy, scale=s4[:, 3:4]), n=1024)
            return xn, B_xn

        def run_b(self, a, hT_view, B_hT, flip):
            xn, B_xn = a
            mat = J_b if flip else ident_b
            for half in range(2):
                pt, PB_ = PSR.get()
                for k4 in range(4):
                    kc = half * 4 + k4
                    P.op("pe", [B_xn, B_cstb], [PB_], lambda e, kc=kc, k4=k4, pt=pt: e.matmul(
                        pt[:, k4 * 128:(k4 + 1) * 128], xn[:, kc * 128:(kc + 1) * 128], mat, start=True, stop=True), n=128)
                for k4 in range(4):
                    kc = half * 4 + k4
                    if k4 % 2 == 0:
                        P.op("dve", [PB_, B_colv], [B_hT], lambda e, kc=kc, k4=k4, pt=pt: e.tensor_scalar(
                            out=hT_view[:, kc, :], in0=pt[:, k4 * 128:(k4 + 1) * 128],
                            scalar1=cv(self.cG + kc), scalar2=cv(self.cSH + kc), op0=ALU.mult, op1=ALU.add), n=128)
                    else:
                        P.op("act", [PB_, B_colv], [B_hT], lambda e, kc=kc, k4=k4, pt=pt: e.activation(
                            out=hT_view[:, kc, :], in_=pt[:, k4 * 128:(k4 + 1) * 128], func=AF.Identity,
                            scale=cv(self.cG + kc), bias=cv(self.cSH + kc)), n=128)

    def front_tile(fr, loader, hTt, B_hTt, flip):
        pend = None
        for s4 in range(5):
            a = None
            if s4 < 4:
                xt, B_x = loader(s4)
                a = (fr.run_a(xt, B_x), s4)
            if pend is not None:
                pa_, ps_ = pend
                fr.run_b(pa_, hTt[:, :, ps_ * 128:(ps_ + 1) * 128], B_hTt, flip)
            pend = a

    def proj_fm(pt, PB_, W, B_W, col0, hT, B_hT, nkc=8, ncol=128):
        for kc in range(nkc):
            P.op("pe", [B_W, B_hT], [PB_], lambda e, kc=kc: e.matmul(
                pt[0:ncol, 0:hT.shape[2]], W[:, kc, col0:col0 + ncol], hT[:, kc, :], start=(kc == 0), stop=(kc == nkc - 1)))

    def proj_tm(pt_view, PB_, lhs_fn, B_h, W, B_W, col0, ncol, nkc=8):
        for kc in range(nkc):
            P.op("pe", [B_W, B_h], [PB_], lambda e, kc=kc: e.matmul(
                pt_view, lhs_fn(kc), W[:, kc, col0:col0 + ncol], start=(kc == 0), stop=(kc == nkc - 1)))

    NVT = len(att_vtiles())

    for hf in (range(2) if upto >= 1 else []):
        with contextlib.ExitStack() as st:
            kT = sb("a_kT", [128, 4, 4096], BF16, st)
            B_kT = P.buf("a_kT")
            qT = sb("a_qT", [128, 4, 2048], BF16, st)
            B_qT = P.buf("a_qT")
            r0 = 2048 * hf
            with contextlib.ExitStack() as s1:
                fr = Front(s1, 3, C_G1, C_SH1, "a")
                bvb = sb("a_bvb", [128, 512], F32, s1)
                B_bvb = P.buf("a_bvb", dma=True)
                P.dma("sp", bvb[:], b_in_row[3584:4096].partition_broadcast(128), [], [B_bvb], B_bvb)
                vm = sb("a_vm", [128, 32], F32, s1)
                B_vm = P.buf("a_vm", dma=True)
                P.dma("sp", vm[:], vmA[:, hf * 32:(hf + 1) * 32], [], [B_vm], B_vm)
                hTr = [(sb("a_hT%d" % i, [128, 8, 512], BF16, s1), P.buf("a_hT%d" % i)) for i in range(2)]
                csr = Pools([(sb("a_cs%d" % i, [128, 2, 512], F32, s1), P.buf("a_cs%d" % i, dma=True)) for i in range(2)])
                tmp = Pools([(sb("a_tmp%d" % i, [128, 512], F32, s1), P.buf("a_tmp%d" % i)) for i in range(6)])
                vrow = Pools([(sb("a_vrow%d" % i, [128, 8, 65], BF16, s1), P.buf("a_vrow%d" % i, dma=True)) for i in range(3)])
                for tile in range(8):
                    hTt, B_hTt = hTr[tile % 2]

                    def a_ld(s4, tile=tile):
                        xt, B_x = fr.xs.get()
                        rb = r0 + tile * 512 + s4 * 128
                        P.dma("sp", xt[:], xw[rb:rb + 128, :], [], [B_x], B_x)
                        return xt, B_x
                    front_tile(fr, a_ld, hTt, B_hTt, False)
                    cst_, B_cs = csr.get()
                    P.dma("sp", cst_[:, 0, :], cosT[:, r0 + tile * 512: r0 + (tile + 1) * 512], [], [B_cs], B_cs)
                    P.dma("sp", cst_[:, 1, :], sinT[:, r0 + tile * 512: r0 + (tile + 1) * 512], [], [B_cs], B_cs)
                    cs_v, sn_v = cst_[:, 0, :], cst_[:, 1, :]
                    jobs = [(Wk, Wks, kT, B_kT, tile * 512, 4, False)]
                    if 2 <= tile < 6:
                        jobs.append((Wq, Wqs, qT, B_qT, (tile - 2) * 512, 0, True))
                    for (W1, W2, dst, B_dst, d0, bofs, isq) in jobs:
                        for cc in range(4):
                            p1, PB1 = PSR.get()
                            proj_fm(p1, PB1, W1, B_W, cc * 128, hTt, B_hTt)
                            p2, PB2 = PSR.get()
                            proj_fm(p2, PB2, W2, B_W, cc * 128, hTt, B_hTt)
                            t1, B1 = tmp.get()
                            t2, B2 = tmp.get()
                            bc1 = cv(C_BIN + 20 + bofs + cc)
                            bc2 = cv(C_BSW + bofs + cc)
                            P.op("dve", [PB1, B_colv, B_cs], [B1], lambda e, p1=p1, t1=t1, bc1=bc1, cs_v=cs_v: e.scalar_tensor_tensor(
                                out=t1[:], in0=p1[:], scalar=bc1, in1=cs_v, op0=ALU.add, op1=ALU.mult))
                            P.op("dve", [PB2, B_colv, B_cs], [B2], lambda e, p2=p2, t2=t2, bc2=bc2, sn_v=sn_v: e.scalar_tensor_tensor(
                                out=t2[:], in0=p2[:], scalar=bc2, in1=sn_v, op0=ALU.add, op1=ALU.mult))
                            if isq:
                                P.op("pool", [B1, B2], [B1], lambda e, t1=t1, t2=t2: e.tensor_tensor(
                                    out=t1[:], in0=t1[:], in1=t2[:], op=ALU.add))
                                P.op("act", [B1], [B_dst], lambda e, t1=t1, dst=dst, cc=cc, d0=d0: e.activation(
                                    out=dst[:, cc, d0:d0 + 512], in_=t1[:], func=AF.Copy, scale=0.125))
                            else:
                                P.op("pool", [B1, B2], [B_dst], lambda e, t1=t1, t2=t2, dst=dst, cc=cc, d0=d0: e.tensor_tensor(
                                    out=dst[:, cc, d0:d0 + 512], in0=t1[:], in1=t2[:], op=ALU.add))
                    for s4 in range(4):
                        pv, PBv = PSR.get()
                        proj_tm(pv[:], PBv, lambda kc, s4=s4, hTt=hTt: hTt[:, kc, s4 * 128:(s4 + 1) * 128], B_hTt, Wv, B_W, 0, 512)
                        vt, B_vt = tmp.get()
                        P.op("dve", [PBv, B_bvb], [B_vt], lambda e, pv=pv, vt=vt: e.tensor_tensor(
                            out=vt[:], in0=pv[:], in1=bvb[:], op=ALU.add))
                        vr, B_vr = vrow.get()
                        vcol = tile * 4 + s4
                        P.op("act", [B_vt, B_vm], [B_vr], lambda e, vt=vt, vr=vr, vcol=vcol: e.activation(
                            out=vr[:, :, 0:64], in_=vt[:].rearrange("p (h d) -> p h d", d=64), func=AF.Copy,
                            scale=vm[:, vcol:vcol + 1]))
                        vsrc = bass.AP(vm.tensor if hasattr(vm, "tensor") else vm, vcol, [[32, 128], [0, 8], [1, 1]])
                        P.op("pool", [B_vm], [B_vr], lambda e, vr=vr, vsrc=vsrc: e.tensor_copy(
                            out=vr[:, :, 64:65], in_=vsrc), n=8)
                        tk = tile * 512 + s4 * 128
                        P.dma("pool", v_s[tk:tk + 128, :, :], vr[:], [B_vr], [], B_vr)
                P.barrier()
            with contextlib.ExitStack() as s2:
                mask4 = sb("a_mask4", [128, 512], BF16, s2)
                B_m4 = P.buf("a_mask4")
                for i in range(4):
                    src = mA_b if i % 2 == 0 else mB_b
                    P.op("dve", [B_cstb], [B_m4], lambda e, i=i, src=src: e.tensor_copy(
                        out=mask4[:, i * 128:(i + 1) * 128], in_=src), n=128)
                acc = sb("a_acc", [65, 8, 2048], F32, s2)
                B_acc = [P.buf("a_acc%d" % i) for i in range(2)]
                vext = Pools([(sb("a_vext%d" % i, [128, 8, 65], BF16, s2), P.buf("a_vext%d" % i, dma=True)) for i in range(6)])
                pt_ = Pools([(sb("a_P%d" % i, [128, 512], BF16, s2), P.buf("a_P%d" % i)) for i in range(10)])
                obst = Pools([(sb("a_obst%d" % i, [64, 512], BF16, s2), P.buf("a_obst%d" % i, dma=True)) for i in range(2)])
                vts = att_vtiles()
                prev = None
                pend2 = None
                for vi, (r, rho, m, start, nb) in enumerate(vts):
                    ve, B_ve = vext.get()
                    P.dma("sp", ve[:], v_s[start:start + 127 * r + 1:r, :, :], [], [B_ve], B_ve, c=2.5)
                    if m == 0:
                        prev = (ve, B_ve)
                        continue
                    bidx = m - 1
                    veA, B_veA = prev
                    veB, B_veB = ve, B_ve
                    prev = (ve, B_ve)
                    jq0 = 1024 // r
                    qst = rho + r * (jq0 + 128 * bidx)
                    qsl = slice(qst - 1024, qst - 1024 + 127 * r + 1, r)
                    kA = slice(qst - 64 * r, qst - 64 * r + 127 * r + 1, r)
                    kB = slice(qst + 64 * r, qst + 64 * r + 127 * r + 1, r)
                    Ps = {}
                    for hg in range(2):
                        pscs = [PSR.get(), PSR.get()]
                        for hh2 in range(2):
                            for ti, ksl in enumerate((kA, kB)):
                                for pair in range(2):
                                    psc, PBs = pscs[pair]
                                    h = hg * 4 + hh2 * 2 + pair
                                    ch, pb = h // 2, (h % 2) * 64
                                    c0 = hh2 * 256 + ti * 128
                                    P.op("pe", [B_kT, B_qT], [PBs], lambda e, psc=psc, c0=c0, ch=ch, pb=pb, ksl=ksl, qsl=qsl: e.matmul(
                                        psc[:, c0:c0 + 128], kT[pb:pb + 64, ch, ksl], qT[pb:pb + 64, ch, qsl],
                                        start=True, stop=True), n=300)
                        for pair in range(2):
                            psc, PBs = pscs[pair]
                            pp, B_pp = pt_.get()
                            P.op("act", [PBs], [B_pp], lambda e, psc=psc, pp=pp: e.activation(
                                out=pp[:], in_=psc[:], func=AF.Exp), tb='e')
                            P.op("dve", [B_pp, B_m4], [B_pp], lambda e, pp=pp: e.tensor_tensor(
                                out=pp[:], in0=pp[:], in1=mask4[:], op=ALU.mult))
                            for hh2 in range(2):
                                Ps[hg * 4 + hh2 * 2 + pair] = (pp, B_pp, hh2 * 256)

                    def stage2(Ps=Ps, veA=veA, B_veA=B_veA, veB=veB, B_veB=B_veB, qsl=qsl, r=r):
                        for hg in range(2):
                            po, PBo = PSR.get()
                            for hh in range(4):
                                h = hg * 4 + hh
                                pp, B_pp, c0 = Ps[h]
                                P.op("pe", [B_pp, B_veA], [PBo], lambda e, po=po, hh=hh, h=h, pp=pp, c0=c0: e.matmul(
                                    po[0:65, hh * 128:(hh + 1) * 128], veA[:, h, :], pp[:, c0:c0 + 128], start=True, stop=False), n=128)
                                P.op("pe", [B_pp, B_veB], [PBo], lambda e, po=po, hh=hh, h=h, pp=pp, c0=c0: e.matmul(
                                    po[0:65, hh * 128:(hh + 1) * 128], veB[:, h, :], pp[:, c0 + 128:c0 + 256], start=False, stop=True), n=128)
                            accv = acc[:, hg * 4:(hg + 1) * 4, qsl]
                            pov = po[0:65, :].rearrange("p (h q) -> p h q", h=4)
                            if r == 1:
                                P.op("dve", [PBo], [B_acc[hg]], lambda e, accv=accv, pov=pov: e.tensor_copy(out=accv, in_=pov))
                            else:
                                P.op("dve", [PBo, B_acc[hg]], [B_acc[hg]], lambda e, accv=accv, pov=pov: e.tensor_tensor(
                                    out=accv, in0=accv, in1=pov, op=ALU.add))
                    if pend2 is not None:
                        pend2()
                    pend2 = stage2
                if pend2 is not None:
                    pend2()
                    pend2 = None
                for hg in range(2):
                    hs = slice(hg * 4, (hg + 1) * 4)
                    P.op("act", [B_acc[hg]], [B_acc[hg]], lambda e, hs=hs: e.activation(
                        out=acc[64:65, hs, :], in_=acc[64:65, hs, :], func=AF.Ln), n=8192, tb='e')
                    P.op("act", [B_acc[hg]], [B_acc[hg]], lambda e, hs=hs: e.activation(
                        out=acc[64:65, hs, :], in_=acc[64:65, hs, :], func=AF.Exp, scale=-1.0), n=8192, tb='e')
                    for hh in range(4):
                        h = hg * 4 + hh
                        for q4 in range(4):
                            pd, PBd = PSR.get()
                            P.op("pe", [B_acc[hg], B_cst], [PBd], lambda e, pd=pd, h=h, q4=q4: e.matmul(
                                pd[0:64, :], cst[64:65, 1152:1216], acc[64:65, h, q4 * 512:(q4 + 1) * 512],
                                start=True, stop=True))
                            ot, B_ot = obst.get()
                            P.op("dve", [PBd, B_acc[hg]], [B_ot], lambda e, pd=pd, ot=ot, h=h, q4=q4: e.tensor_tensor(
                                out=ot[:], in0=acc[0:64, h, q4 * 512:(q4 + 1) * 512], in1=pd[0:64, :], op=ALU.mult))
                            c0 = hf * 2048 + q4 * 512
                            P.dma("pool", ob_s[h * 64:(h + 1) * 64, c0:c0 + 512], ot[:], [B_ot], [], B_ot)
                P.barrier()

    pa_st.close()

    class Scan:
        def __init__(self, st, tag):
            f = lambda n, dt=F32, sh=(128, 512), k=1: Pools([(sb("%s_%s%d" % (tag, n, i), list(sh), dt, st), P.buf("%s_%s%d" % (tag, n, i)))
                                                              for i in range(k)])
            self.t_sg, self.t_f, self.t_g, self.t_b = f("sg", k=4), f("f", k=2), f("g", k=2), f("b", k=2)
            self.t_eb, self.t_enb, self.t_kk, self.t_qs = f("eb", k=2), f("enb", k=2), f("kk", k=2), f("qs", k=2)
            NS = 2
            self.qT = [[sb("%s_qT%d_%d" % (tag, z, h), [128, 512], BF16, st) for h in range(4)] for z in range(NS)]
            self.kTt = [[sb("%s_kT%d_%d" % (tag, z, h), [128, 512], BF16, st) for h in range(4)] for z in range(NS)]
            self.ebk = [[sb("%s_ebk%d_%d" % (tag, z, h), [128, 8], F32, st) for h in range(4)] for z in range(NS)]
            self.B_q = [[P.buf("q") for h in range(4)] for z in range(NS)]
            self.B_k = [[P.buf("k") for h in range(4)] for z in range(NS)]
            self.B_e = [[P.buf("e") for h in range(4)] for z in range(NS)]
            self.vtok = [sb("%s_vtok%d" % (tag, z), [128, 4, 512], BF16, st) for z in range(NS)]
            self.ktok = [sb("%s_ktok%d" % (tag, z), [128, 4, 512], BF16, st) for z in range(NS)]
            self.B_vtok = [[P.buf("vtok") for i in range(4)] for z in range(NS)]
            self.B_ktok = [[P.buf("ktok") for i in range(4)] for z in range(NS)]
            self.vtmp = f("vtmp", k=2)
            self.A = f("A", BF16, k=3)
            self.S = [sb("%s_S%d" % (tag, h), [128, 128], F32, st) for h in range(4)]
            self.S1 = [sb("%s_S1%d" % (tag, h), [128, 128], F32, st) for h in range(4)]
            self.Sb = [sb("%s_Sb%d" % (tag, h), [128, 128], BF16, st) for h in range(4)]
            self.B_S = [P.buf("S") for h in range(4)]
            self.B_S1 = [P.buf("S1") for h in range(4)]
            self.B_Sb = [P.buf("Sb") for h in range(4)]
            for h in range(4):
                P.op("dve", [], [self.B_S[h]], lambda e, h=h: e.memset(self.S[h][:], 0.0))
                P.op("pool", [], [self.B_Sb[h]], lambda e, h=h: e.memset(self.Sb[h][:], 0.0))
            self.bvb = sb(tag + "_bvb", [128, 512], F32, st)
            self.B_bvb = P.buf(tag + "_bvb", dma=True)
            P.dma("sp", self.bvb[:], b_in_row[1536:2048].partition_broadcast(128), [], [self.B_bvb], self.B_bvb)

        def prepH(self, z, h, hTt, B_hTt, W, B_W, d):
            pq, PBq = PSR.get()
            proj_fm(pq, PBq, W, B_W, h * 128, hTt, B_hTt)
            yield
            pf, PBf = PSR.get()
            proj_fm(pf, PBf, W, B_W, 512 + h * 128, hTt, B_hTt)
            sg, B_sg = self.t_sg.get()
            bq = cv(C_BIN + 0 + h)
            bf = cv(C_BIN + 4 * (1 + d) + h)
            P.op("act", [PBq, B_colv], [B_sg], lambda e: e.activation(out=sg[:], in_=pq[:], func=AF.Sigmoid, bias=bq), tb='s')
            sf, B_sf = self.t_sg.get()
            P.op("act", [PBf, B_colv], [B_sf], lambda e: e.activation(out=sf[:], in_=pf[:], func=AF.Sigmoid, bias=bf), tb='s')
            yield
            qs, B_qs = self.t_qs.get()
            P.op("dve", [PBq, B_sg, B_colv], [B_qs], lambda e: e.scalar_tensor_tensor(
                out=qs[:], in0=pq[:], scalar=bq, in1=sg[:], op0=ALU.add, op1=ALU.mult))
            ff, B_ff = self.t_f.get()
            P.op("dve", [B_sf, B_colv], [B_ff], lambda e: e.tensor_scalar(
                out=ff[:], in0=sf[:], scalar1=cv(C_OML + d * 4 + h), scalar2=cv(C_LB + d * 4 + h),
                op0=ALU.mult, op1=ALU.add))
            yield
            g, B_g = self.t_g.get()
            P.op("act", [B_ff], [B_g], lambda e: e.activation(out=g[:], in_=ff[:], func=AF.Ln), tb='e')
            kk, B_kk = self.t_kk.get()
            P.op("pool", [B_ff], [B_kk], lambda e: e.tensor_scalar(
                out=kk[:], in0=ff[:], scalar1=-1.0, scalar2=1.0, op0=ALU.mult, op1=ALU.add), c=0.62)
            yield
            b, B_b = self.t_b.get()
            P.op("dve", [B_g, B_cst], [B_b], lambda e: e.tensor_tensor_scan(
                out=b[:], data0=cmask_f, data1=g[:], initial=0.0, op0=ALU.mult, op1=ALU.add), n=1024)
            yield
            eb, B_eb = self.t_eb.get()
            P.op("act", [B_b], [B_eb], lambda e: e.activation(out=eb[:], in_=b[:], func=AF.Exp), tb='e')
            enb, B_enb = self.t_enb.get()
            P.op("act", [B_b], [B_enb], lambda e: e.activation(out=enb[:], in_=b[:], func=AF.Exp, scale=-1.0), tb='e')
            yield
            P.op("pool", [B_qs, B_eb], [self.B_q[z][h]], lambda e: e.tensor_tensor(
                out=self.qT[z][h][:], in0=qs[:], in1=eb[:], op=ALU.mult))
            P.op("pool", [B_kk, B_enb], [self.B_k[z][h]], lambda e: e.tensor_tensor(
                out=self.kTt[z][h][:], in0=kk[:], in1=enb[:], op=ALU.mult))
            P.op("pool", [B_eb], [self.B_e[z][h]], lambda e: e.tensor_copy(out=self.ebk[z][h][:], in_=eb[:, 63:512:64]), n=8)

        def prepV(self, z, sub, hTt, B_hTt, W, B_W, vm_t, B_vm, vmcol):
            pv, PBv = PSR.get()
            proj_tm(pv[:], PBv, lambda kc: hTt[:, kc, sub * 128:(sub + 1) * 128], B_hTt, W, B_W, 1024, 512)
            vt, B_vt = self.vtmp.get()
            P.op("dve", [PBv, self.B_bvb], [B_vt], lambda e: e.tensor_tensor(out=vt[:], in0=pv[:], in1=self.bvb[:], op=ALU.add))
            P.op("act", [B_vt, B_vm], [self.B_vtok[z][sub]], lambda e: e.activation(
                out=self.vtok[z][:, sub, :], in_=vt[:], func=AF.Copy, scale=vm_t[:, vmcol:vmcol + 1]))

        def prepK(self, z):
            for sub in range(4):
                pk, PBk = PSR.get()
                for h in range(4):
                    P.op("pe", [self.B_k[z][h], B_cstb], [PBk], lambda e, h=h, pk=pk, sub=sub: e.matmul(
                        pk[:, h * 128:(h + 1) * 128], self.kTt[z][h][:, sub * 128:(sub + 1) * 128], ident_b,
                        start=True, stop=True), n=128)
                P.op("act", [PBk], [self.B_ktok[z][sub]], lambda e, pk=pk, sub=sub: e.activation(
                    out=self.ktok[z][:, sub, :], in_=pk[:], func=AF.Copy))

        def scan_sub(self, z, sub, res):
            qT, kTt, ebk, vtok, ktok = self.qT[z], self.kTt[z], self.ebk[z], self.vtok[z], self.ktok[z]
            B_q, B_k, B_e, B_vtok, B_ktok = self.B_q[z], self.B_k[z], self.B_e[z], self.B_vtok[z], self.B_ktok[z]
            psc, PBs = PSR.get()
            for h in range(4):
                P.op("pe", [B_k[h], B_q[h]], [PBs], lambda e, h=h: e.matmul(
                    psc[:, h * 128:(h + 1) * 128], kTt[h][:, sub * 128:(sub + 1) * 128],
                    qT[h][:, sub * 128:(sub + 1) * 128], start=True, stop=True), n=128)
            pus = [PSR.get(), PSR.get()]
            for h in range(4):
                for c in range(2):
                    pu, PBu = pus[c]
                    rows = slice(c * 64, (c + 1) * 64)
                    P.op("pe", [B_ktok[sub], B_vtok[sub]], [PBu], lambda e, pu=pu, h=h, rows=rows: e.matmul(
                        pu[:, h * 128:(h + 1) * 128], ktok[rows, sub, h * 128:(h + 1) * 128],
                        vtok[rows, sub, h * 128:(h + 1) * 128], start=True, stop=True), n=128)
            yield
            A, B_A = self.A.get()
            tri4 = bass.AP(cstb.tensor if hasattr(cstb, "tensor") else cstb, 256, [[1280, 128], [0, 4], [1, 128]])
            P.op("dve", [PBs, B_cstb], [B_A], lambda e: e.tensor_tensor(
                out=A[:].rearrange("p (h t) -> p h t", h=4), in0=psc[:].rearrange("p (h t) -> p h t", h=4),
                in1=tri4, op=ALU.mult))
            yield
            po, PBo = PSR.get()
            res.append((po, PBo))
            for h in range(4):
                P.op("pe", [B_A, B_vtok[sub]], [PBo], lambda e, h=h: e.matmul(
                    po[:, h * 128:(h + 1) * 128], A[:, h * 128:(h + 1) * 128], vtok[:, sub, h * 128:(h + 1) * 128],
                    start=(h == 0), stop=False, skip_group_check=True), n=128)
            for c in range(2):
                pu, PBu = pus[c]
                rows = slice(c * 64, (c + 1) * 64)
                toks = slice(sub * 128 + c * 64, sub * 128 + (c + 1) * 64)
                for h in range(4):
                    last = (c == 1 and h == 3)
                    P.op("pe", [B_q[h], self.B_Sb[h]], [PBo], lambda e, h=h, rows=rows, toks=toks, last=last: e.matmul(
                        po[rows, h * 128:(h + 1) * 128], qT[h][:, toks], self.Sb[h][:],
                        start=False, stop=last, skip_group_check=True), n=128)
                ci = sub * 2 + c
                for h in range(4):
                    e_ap = ebk[h][:, ci:ci + 1]
                    P.op("pool", [self.B_S[h], B_e[h]], [self.B_S1[h]], lambda e, h=h, e_ap=e_ap: e.tensor_scalar(
                        out=self.S1[h][:], in0=self.S[h][:], scalar1=e_ap, scalar2=0.0, op0=ALU.mult, op1=ALU.add), c=0.34)
                yield
                for h in range(4):
                    e_ap = ebk[h][:, ci:ci + 1]
                    P.op("dve", [PBu, self.B_S1[h], B_e[h]], [self.B_Sb[h]], lambda e, pu=pu, h=h, e_ap=e_ap: e.scalar_tensor_tensor(
                        out=self.Sb[h][:], in0=pu[:, h * 128:(h + 1) * 128], scalar=e_ap, in1=self.S1[h][:],
                        op0=ALU.mult, op1=ALU.add), n=128)
                    P.op("dve", [PBu, self.B_S1[h], B_e[h]], [self.B_S[h]], lambda e, pu=pu, h=h, e_ap=e_ap: e.scalar_tensor_tensor(
                        out=self.S[h][:], in0=pu[:, h * 128:(h + 1) * 128], scalar=e_ap, in1=self.S1[h][:],
                        op0=ALU.mult, op1=ALU.add), n=128)
                yield

    scan_W = {}
    sc_st = contextlib.ExitStack()
    if upto >= 2:
        for d_ in (1, 0):
            scan_W[d_] = (sb("s_W%d" % d_, [128, 8, 1536], BF16, sc_st), P.buf("s_W%d" % d_))
        with contextlib.ExitStack() as s1:
            stage = mk_stage(s1, 4)
            for d_ in (1, 0):
                W_, B_W_ = scan_W[d_]
                fcol = 1024 if d_ == 1 else 512
                load_weight(s1, W_[:, :, 0:512], B_W_, w_in[:, 0:512], 8, 512, stage=stage)
                load_weight(s1, W_[:, :, 512:1024], B_W_, w_in[:, fcol:fcol + 512], 8, 512, stage=stage)
                load_weight(s1, W_[:, :, 1024:1536], B_W_, w_in[:, 1536:2048], 8, 512, stage=stage)
            P.barrier()

    def scan_phase(d):
        with contextlib.ExitStack() as st:
            W, B_W = scan_W[d]
            fr = Front(st, 3, C_G1, C_SH1, "s%d" % d)
            fr.pool_xn = True
            sc = Scan(st, "s%d" % d)
            vm = sb("s_vm%d" % d, [128, 36], F32, st)
            B_vm = P.buf("s_vm", dma=True)
            P.dma("sp", vm[:], vmB if d == 1 else vmC, [], [B_vm], B_vm)
            hTr = [(sb("s_hT%d_%d" % (d, i), [128, 8, 512], BF16, st), P.buf("s_hT%d" % i)) for i in range(2)]
            ost = Pools([(sb("s_ost%d_%d" % (d, i), [128, 512], F32, st), P.buf("s_ost%d" % i, dma=True)) for i in range(2)])
            top = 1024 + OWN + HH
            base = 1024 - HH
            flip = (d == 1)
            o_s = obw_s if d == 1 else ofw_s
            NT = 9

            class FrontStream:
                def __init__(self):
                    self.pend = None

                def step(self, t, s4):
                    hTt, B_hTt = hTr[t % 2]
                    i = t * 4 + s4
                    rb = (top - 128 * (i + 1)) if flip else (base + 128 * i)
                    xt, B_x = fr.xs.get()
                    P.dma("sp", xt[:], xw[rb:rb + 128, :], [], [B_x], B_x)
                    a = (fr.run_a(xt, B_x), t, s4)
                    self.flush()
                    self.pend = a

                def flush(self):
                    if self.pend is not None:
                        pa_, pt, ps4 = self.pend
                        hTt, B_hTt = hTr[pt % 2]
                        fr.run_b(pa_, hTt[:, :, ps4 * 128:(ps4 + 1) * 128], B_hTt, flip)
                        self.pend = None

            fs = FrontStream()

            def drain(*gens):
                gens = [g for g in gens if g is not None]
                while gens:
                    for g in list(gens):
                        try:
                            next(g)
                        except StopIteration:
                            gens.remove(g)

            def gen_front(t, s4):
                fs.step(t, s4)
                yield

            def gen_prepV(z, s4, hn, B_hn, col):
                yield
                yield
                sc.prepV(z, s4, hn, B_hn, W, B_W, vm, B_vm, col)
                yield

            def gen_out(res, i):
                po, PBo = res[0]
                if i >= 4:
                    ot, B_ot = ost.get()
                    P.op("act", [PBo], [B_ot], lambda e: e.activation(
                        out=ot[:], in_=po[:], func=AF.Copy, scale=float(128 ** -0.5)))
                    row = (i - 4) * 128
                    P.dma("pool", o_s[row:row + 128, :], ot[:], [B_ot], [], B_ot)

            for s4 in range(4):
                fs.step(0, s4)
            fs.flush()
            for s4 in range(4):
                drain(sc.prepH(0, s4, hTr[0][0], hTr[0][1], W, B_W, d))
                sc.prepV(0, s4, hTr[0][0], hTr[0][1], W, B_W, vm, B_vm, s4)
            for s4 in range(4):
                fs.step(1, s4)
            fs.flush()
            for t in range(NT):
                z = t % 2
                sc.prepK(z)
                for s4 in range(4):
                    res = []
                    g_f = gen_front(t + 2, s4) if t + 2 < NT else None
                    g_p = g_v = None
                    if t + 1 < NT:
                        hn, B_hn = hTr[(t + 1) % 2]
                        g_p = sc.prepH(1 - z, s4, hn, B_hn, W, B_W, d)
                        g_v = gen_prepV(1 - z, s4, hn, B_hn, (t + 1) * 4 + s4)
                    drain(g_f, g_p, sc.scan_sub(z, s4, res), g_v)
                    gen_out(res, t * 4 + s4)
                fs.flush()
            P.barrier()

    if upto >= 2:
        scan_phase(1)
    if upto >= 3:
        scan_phase(0)
    sc_st.close()

    if upto >= 3:
        with contextlib.ExitStack() as st:
            W = sb("c_W", [128, 8, 2560], BF16, st)
            Wa = sb("c_Wa", [128, 4, D], BF16, st)
            Wb = sb("c_Wb", [128, 4, D], BF16, st)
            Wo = sb("c_Wo", [128, 8, D], BF16, st)
            B_W = P.buf("c_W")
            with contextlib.ExitStack() as s1:
                stage = mk_stage(s1, 4)
                load_weight(s1, W[:, :, 0:512], B_W, w_in[:, 2048:2560], 8, 512, stage=stage)
                load_weight(s1, W[:, :, 512:2560], B_W, w_in[:, 4096:6144], 8, 2048, stage=stage)
                load_weight(s1, Wa, B_W, w_bra, 4, D, stage=stage)
                load_weight(s1, Wb, B_W, w_brb, 4, D, stage=stage)
                gate1_bc = sb("gate1_bc", [128, D], F32, s1)
                B_g1 = P.buf("g1bc", dma=True)
                P.dma("sp", gate1_bc[:], g_s[0], [], [B_g1], B_g1)
                load_weight(s1, Wo, B_W, w_out, 8, D, gate_bc=gate1_bc, B_gate=B_g1, stage=stage)
                P.barrier()
            fr = Front(st, 3, C_G1, C_SH1, "c")
            hTr = [(sb("c_hT%d" % i, [128, 8, 512], BF16, st), P.buf("c_hT%d" % i)) for i in range(2)]
            sgT = sb("c_sgT", [128, 4, 512], BF16, st)
            B_sgT = P.buf("c_sgT")
            oaT = sb("c_oaT", [128, 4, 512], BF16, st)
            B_oaT = P.buf("c_oaT")
            obTr = Pools([(sb("c_obT%d" % i, [128, 4, 512], BF16, st), P.buf("c_obT%d" % i, dma=True)) for i in range(2)])
            mT = sb("c_mT", [128, 8, 512], BF16, st)
            B_mT = P.buf("c_mT")
            obw = Pools([(sb("c_obw%d" % i, [128, 512], F32, st), P.buf("c_obw%d" % i, dma=True)) for i in range(2)])
            ofw = Pools([(sb("c_ofw%d" % i, [128, 512], F32, st), P.buf("c_ofw%d" % i, dma=True)) for i in range(2)])
            osbr = Pools([(sb("c_osb%d" % i, [128, 512], F32, st), P.buf("c_osb%d" % i)) for i in range(2)])
            onbr = Pools([(sb("c_onb%d" % i, [128, 512], BF16, st), P.buf("c_onb%d" % i)) for i in range(2)])
            gstr = Pools([(sb("c_gst%d" % i, [128, 16], F32, st), P.buf("c_gst%d" % i)) for i in range(2)])
            gtmp = Pools([(sb("c_gt%d" % i, [128, 512], F32, st), P.buf("c_gt%d" % i)) for i in range(4)])
            x1st = Pools([(sb("c_x1%d" % i, [128, D], F32, st), P.buf("c_x1%d" % i, dma=True)) for i in range(2)])
            junk = sb("c_junk", [128, 512], BF16, st)
            B_junk = P.buf("c_junk")

            class FS:
                def __init__(self):
                    self.pend = None

                def step(self, t, s4):
                    rb = 1024 + t * 512 + s4 * 128
                    xt, B_x = fr.xs.get()
                    P.dma("sp", xt[:], xw[rb:rb + 128, :], [], [B_x], B_x)
                    a = (fr.run_a(xt, B_x), t, s4)
                    self.flush()
                    self.pend = a

                def flush(self):
                    if self.pend is not None:
                        pa_, pt, ps4 = self.pend
                        hTt, B_hTt = hTr[pt % 2]
                        fr.run_b(pa_, hTt[:, :, ps4 * 128:(ps4 + 1) * 128], B_hTt, False)
                        self.pend = None

            fs = FS()
            for s4 in range(4):
                fs.step(0, s4)
            fs.flush()
            for tile in range(8):
                hTt, B_hTt = hTr[tile % 2]
                t0 = tile * 512
                obT, B_obT = obTr.get()
                P.dma("sp", obT[:], ob_s.rearrange("(c p) t -> p c t", p=128)[:, :, t0:t0 + 512], [], [B_obT], B_obT)
                for h in range(4):
                    pg, PBg = PSR.get()
                    proj_fm(pg, PBg, W, B_W, h * 128, hTt, B_hTt)
                    g1, B1 = gtmp.get()
                    bg = cv(C_BIN + 16 + h)
                    P.op("act", [PBg, B_colv], [B1], lambda e, pg=pg, g1=g1, bg=bg: e.activation(
                        out=g1[:], in_=pg[:], func=AF.Sigmoid, bias=bg), tb='s')
                    P.op("dve", [PBg, B1, B_colv], [B_sgT], lambda e, pg=pg, g1=g1, bg=bg, h=h: e.scalar_tensor_tensor(
                        out=sgT[:, h, :], in0=pg[:], scalar=bg, in1=g1[:], op0=ALU.add, op1=ALU.mult))
                for s4 in range(4):
                    tt = t0 + s4 * 128
                    ow, B_ow = obw.get()
                    P.dma("sp", ow[:], obw_s[OWN - 128 - tt: OWN - tt, :], [], [B_ow], B_ow)
                    of, B_of = ofw.get()
                    P.dma("sp", of[:], ofw_s[tt:tt + 128, :], [], [B_of], B_of)
                    po, PBo = PSR.get()
                    P.op("pe", [B_ow, B_cst], [PBo], lambda e, po=po, ow=ow: e.matmul(po[:], J_f, ow[:], start=True, stop=True), n=2048)
                    osb, B_osb = osbr.get()
                    P.op("dve", [PBo, B_of], [B_osb], lambda e, po=po, of=of, osb=osb: e.tensor_tensor(
                        out=osb[:], in0=po[:], in1=of[:], op=ALU.add))
                    P.op("pool", [B_osb], [B_junk], lambda e, osb=osb: e.tensor_tensor(out=junk[:], in0=osb[:], in1=osb[:], op=ALU.mult))
                    gst, B_gst = gstr.get()
                    P.op("dve", [B_junk], [B_gst], lambda e, gst=gst: e.reduce_sum(
                        out=gst[:, 0:4], in_=junk[:].rearrange("p (h d) -> p h d", h=4), axis=mybir.AxisListType.X), n=512)
                    P.op("dve", [B_gst], [B_gst], lambda e, gst=gst: e.tensor_scalar(
                        out=gst[:, 4:8], in0=gst[:, 0:4], scalar1=1.0 / 128, scalar2=EPS, op0=ALU.mult, op1=ALU.add), n=4)
                    P.op("act", [B_gst], [B_gst], lambda e, gst=gst: e.activation(out=gst[:, 8:12], in_=gst[:, 4:8], func=AF.Ln), n=4, tb='e')
                    P.op("act", [B_gst], [B_gst], lambda e, gst=gst: e.activation(
                        out=gst[:, 12:16], in_=gst[:, 8:12], func=AF.Exp, scale=-0.5), n=4, tb='e')
                    onb, B_onb = onbr.get()
                    for h in range(4):
                        eng = "act" if h % 2 == 0 else "pool"
                        if eng == "act":
                            P.op("act", [B_osb, B_gst], [B_onb], lambda e, h=h, osb=osb, onb=onb, gst=gst: e.activation(
                                out=onb[:, h * 128:(h + 1) * 128], in_=osb[:, h * 128:(h + 1) * 128], func=AF.Copy,
                                scale=gst[:, 12 + h:13 + h]), n=128)
                        else:
                            P.op("pool", [B_osb, B_gst], [B_onb], lambda e, h=h, osb=osb, onb=onb, gst=gst: e.tensor_scalar(
                                out=onb[:, h * 128:(h + 1) * 128], in0=osb[:, h * 128:(h + 1) * 128],
                                scalar1=gst[:, 12 + h:13 + h], scalar2=0.0, op0=ALU.mult, op1=ALU.add), n=128)
                    ptp, PBt = PSR.get()
                    for h in range(4):
                        P.op("pe", [B_onb, B_cstb], [PBt], lambda e, ptp=ptp, h=h, onb=onb: e.matmul(
                            ptp[:, h * 128:(h + 1) * 128], onb[:, h * 128:(h + 1) * 128], ident_b, start=True, stop=True), n=128)
                    for h in range(4):
                        P.op("dve", [PBt, B_colv, B_sgT], [B_oaT], lambda e, ptp=ptp, h=h, s4=s4: e.scalar_tensor_tensor(
                            out=oaT[:, h, s4 * 128:(s4 + 1) * 128], in0=ptp[:, h * 128:(h + 1) * 128],
                            scalar=cv(C_GN + h), in1=sgT[:, h, s4 * 128:(s4 + 1) * 128], op0=ALU.mult, op1=ALU.mult), n=128)
                for cc in range(8):
                    pga, PBga = PSR.get()
                    proj_fm(pga, PBga, W, B_W, 512 + cc * 128, hTt, B_hTt)
                    pgb, PBgb = PSR.get()
                    proj_fm(pgb, PBgb, W, B_W, 1536 + cc * 128, hTt, B_hTt)
                    pa, PBa = PSR.get()
                    proj_fm(pa, PBa, Wa, B_W, cc * 128, oaT, B_oaT, nkc=4)
                    pb_, PBb = PSR.get()
                    proj_fm(pb_, PBb, Wb, B_W, cc * 128, obT, B_obT, nkc=4)
                    ga, B_ga = gtmp.get()
                    gb, B_gb = gtmp.get()
                    P.op("act", [PBga, B_colv], [B_ga], lambda e, pga=pga, ga=ga, cc=cc: e.activation(
                        out=ga[:], in_=pga[:], func=AF.Sigmoid, bias=cv(C_BIN + 32 + cc)), tb='s')
                    P.op("act", [PBgb, B_colv], [B_gb], lambda e, pgb=pgb, gb=gb, cc=cc: e.activation(
                        out=gb[:], in_=pgb[:], func=AF.Sigmoid, bias=cv(C_BIN + 40 + cc)), tb='s')
                    P.op("dve", [PBa, B_ga], [B_ga], lambda e, pa=pa, ga=ga: e.tensor_tensor(
                        out=ga[:], in0=pa[:], in1=ga[:], op=ALU.mult))
                    P.op("dve", [PBb, B_gb], [B_gb], lambda e, pb_=pb_, gb=gb: e.tensor_tensor(
                        out=gb[:], in0=pb_[:], in1=gb[:], op=ALU.mult))
                    P.op("pool", [B_ga, B_gb], [B_mT], lambda e, ga=ga, gb=gb, cc=cc: e.tensor_tensor(
                        out=mT[:, cc, :], in0=ga[:], in1=gb[:], op=ALU.add))
                    if tile + 1 < 8 and cc % 2 == 1:
                        fs.step(tile + 1, cc // 2)
                fs.flush()
                for s4 in range(4):
                    xo, B_xo = x1st.get()
                    rb = 1024 + t0 + s4 * 128
                    P.dma("sp", xo[:], xw[rb:rb + 128, :], [], [B_xo], B_xo)
                    for half in range(2):
                        px, PBx = PSR.get()
                        proj_tm(px[:], PBx, lambda kc, s4=s4: mT[:, kc, s4 * 128:(s4 + 1) * 128], B_mT, Wo, B_W, half * 512, 512)
                        P.op("dve", [PBx, B_xo], [B_xo], lambda e, px=px, xo=xo, half=half: e.tensor_tensor(
                            out=xo[:, half * 512:(half + 1) * 512], in0=px[:], in1=xo[:, half * 512:(half + 1) * 512],
                            op=ALU.add))
                    tt = t0 + s4 * 128
                    P.dma("pool", x1_s[tt:tt + 128, :], xo[:], [B_xo], [], B_xo)
            P.barrier()

    if upto >= 4:
        with contextlib.ExitStack() as st:
            Wi = sb("d_Wi", [128, 8, 2 * FFN], BF16, st)
            Wf = sb("d_Wf", [128, NJ, D], BF16, st)
            B_W = P.buf("d_W")
            with contextlib.ExitStack() as s1:
                stage = mk_stage(s1, 3)
                load_weight(s1, Wi, B_W, w_fi, 8, 2 * FFN, stage=stage)
                gate2_bc = sb("gate2_bc", [128, D], F32, s1)
                B_g2 = P.buf("g2bc", dma=True)
                P.dma("sp", gate2_bc[:], g_s[1], [], [B_g2], B_g2)
                load_weight(s1, Wf, B_W, w_fo, NJ, D, gate_bc=gate2_bc, B_gate=B_g2, stage=stage)
                P.barrier()
            fngb = sb("d_fng", [128, D], F32, st)
            B_fng = P.buf("d_fng", dma=True)
            P.dma("sp", fngb[:], fng.partition_broadcast(128), [], [B_fng], B_fng)
            fr = Front(st, 2, C_G2, C_SH2, "d", nxn=1)
            hTr = [(sb("d_hT%d" % i, [128, 8, 512], BF16, st), P.buf("d_hT%d" % i)) for i in range(2)]
            yT = sb("d_yT", [128, NJ, 512], BF16, st)
            B_yT = P.buf("d_yT")
            gtmp = Pools([(sb("d_gt%d" % i, [128, 512], F32, st), P.buf("d_gt%d" % i)) for i in range(2)])
            res = Pools([(sb("d_res%d" % i, [128, D], F32, st), P.buf("d_res%d" % i, dma=True)) for i in range(2)])

            def d_front(tile):
                hTt, B_hTt = hTr[tile % 2]

                def d_ld(s4):
                    t0 = tile * 512 + s4 * 128
                    xt, B_x = fr.xs.get()
                    P.dma("sp", xt[:], x1_s[t0:t0 + 128, :], [], [B_x], B_x)
                    return xt, B_x
                for s4 in range(4):
                    xt, B_x = d_ld(s4)
                    fr.run(xt, B_x, hTt[:, :, s4 * 128:(s4 + 1) * 128], B_hTt, False)

            d_front(0)
            for tile in range(8):
                hTt, B_hTt = hTr[tile % 2]
                for j in range(NJ):
                    pg, PBg = PSR.get()
                    proj_fm(pg, PBg, Wi, B_W, j * 128, hTt, B_hTt)
                    pu, PBu = PSR.get()
                    proj_fm(pu, PBu, Wi, B_W, FFN + j * 128, hTt, B_hTt)
                    g1, B1 = gtmp.get()
                    P.op("act", [PBg], [B1], lambda e, pg=pg, g1=g1: e.activation(out=g1[:], in_=pg[:], func=AF.Sigmoid), tb='s')
                    P.op("dve", [PBg, B1], [B1], lambda e, pg=pg, g1=g1: e.tensor_tensor(
                        out=g1[:], in0=pg[:], in1=g1[:], op=ALU.mult))
                    P.op("dve", [PBu, B1], [B_yT], lambda e, pu=pu, g1=g1, j=j: e.tensor_tensor(
                        out=yT[:, j, :], in0=pu[:], in1=g1[:], op=ALU.mult))
                if tile + 1 < 8:
                    d_front(tile + 1)
                for s4 in range(4):
                    xo, B_xo = res.get()
                    t0 = tile * 512 + s4 * 128
                    P.dma("sp", xo[:], x1_s[t0:t0 + 128, :], [], [B_xo], B_xo)
                    for half in range(2):
                        px, PBx = PSR.get()
                        proj_tm(px[:], PBx, lambda kc, s4=s4: yT[:, kc, s4 * 128:(s4 + 1) * 128], B_yT, Wf, B_W, half * 512, 512, nkc=NJ)
                        P.op("dve", [PBx, B_xo], [B_xo], lambda e, px=px, xo=xo, half=half: e.tensor_tensor(
                            out=xo[:, half * 512:(half + 1) * 512], in0=px[:], in1=xo[:, half * 512:(half + 1) * 512],
                            op=ALU.add))
                    s4t, B_s = fr.stats(xo, B_xo)
                    P.op("act", [B_xo, B_s], [B_xo], lambda e, xo=xo, s4t=s4t: e.activation(
                        out=xo[:], in_=xo[:], func=AF.Copy, scale=s4t[:, 3:4]), n=1024)
                    P.op("pool", [B_xo, B_fng], [B_xo], lambda e, xo=xo: e.tensor_tensor(
                        out=xo[:], in0=xo[:], in1=fngb[:], op=ALU.mult), c=1.9)
                    P.dma("pool", y[t0:t0 + 128, :], xo[:], [B_xo], [], B_xo)
            P.barrier()
    P.flush()
    P.es.close()
    return nc, P


def _col(v, n):
    return np.ascontiguousarray(np.asarray(v, np.float32).reshape(n, 128).T)


def _consts():
    c = np.zeros((128, 1280), np.float32)
    i = np.arange(128)
    c[:, 0:128] = np.eye(128, dtype=np.float32)
    c[:, 128:256] = np.eye(128, dtype=np.float32)[::-1]
    s = i[:, None]
    t = i[None, :]
    c[:, 256:384] = ((s // 64 == t // 64) & (t >= s)).astype(np.float32)
    c[:, 384:512] = (t <= s).astype(np.float32)
    c[:, 512:640] = (t >= s).astype(np.float32)
    cm = np.ones(512, np.float32)
    cm[::64] = 0.0
    c[:, 640:1152] = cm[None, :]
    c[:, 1152:1280] = 1.0
    return c


def _core_geom(c):
    if c < 4:
        return 0, c // 2, (c % 2) * OWN, 8192
    return 1, 0, (c - 4) * OWN, 16384


_NC_CACHE = {}


def kernel(x_prompt, x_sample, c_prompt, c_sample, w_ada, b_ada, norm1_g, w_in, b_in, lb_logits,
           hg_norm_g, w_branch_a, w_branch_b, w_out, norm2_g, w_ffn_in, w_ffn_out, final_norm_g,
           _debug=False):
    f32 = lambda a: np.ascontiguousarray(np.asarray(a, dtype=np.float32))
    x_prompt, x_sample = f32(x_prompt), f32(x_sample)
    c_prompt, c_sample = f32(c_prompt), f32(c_sample)
    w_in0 = f32(w_in)[0]
    b_in0 = f32(b_in)[0]
    perm = np.arange(1024).reshape(2, 8, 2, 32)[:, :, ::-1, :].reshape(-1)
    w_qksw = np.ascontiguousarray(w_in0[:, 2560:3584][:, perm])
    b_sw = b_in0[2560:3584][perm]
    lbl = np.concatenate([_col(f32(lb_logits)[l, d], 4) for l in range(2) for d in range(2)], axis=1)
    shared = {
        "w_ada": f32(w_ada)[0], "b_ada": f32(b_ada)[0], "n1g": _col(f32(norm1_g)[0], 8),
        "n2g": _col(f32(norm2_g)[0], 8), "w_in": w_in0, "w_qksw": w_qksw, "bin_col": _col(b_in0, 48),
        "bsw_col": _col(b_sw, 8), "b_in_row": b_in0, "lbl": np.ascontiguousarray(lbl),
        "gng": _col(f32(hg_norm_g)[0], 4), "w_bra": f32(w_branch_a)[0], "w_brb": f32(w_branch_b)[0],
        "w_out": f32(w_out)[0], "w_fi": f32(w_ffn_in)[0], "w_fo": f32(w_ffn_out)[0],
        "fng": f32(final_norm_g), "consts": _consts(),
    }
    half = 32
    inv = (np.float32(ROPE_THETA) ** (-np.arange(half, dtype=np.float32) / np.float32(half))).astype(np.float32)
    pidx = np.arange(128)
    fidx = pidx % 32
    sgn = np.where((pidx % 64) < 32, -1.0, 1.0).astype(np.float32)
    vts = att_vtiles()
    in_maps = []
    for c in range(NCORES):
        grp, b, own0, L = _core_geom(c)
        xs = x_prompt[b] if grp == 0 else x_sample[0]
        cvec = c_prompt[b] if grp == 0 else c_sample[0]
        w0 = own0 - HALO
        xw = np.zeros((WIN, D), np.float32)
        lo, hi = max(0, w0), min(L, w0 + WIN)
        xw[lo - w0:hi - w0] = xs[lo:hi]
        pos = (w0 + np.arange(WIN)).astype(np.float32)
        ang = (pos[None, :] * inv[fidx][:, None]).astype(np.float32)
        cosT = np.cos(ang).astype(np.float32)
        sinT = (np.sin(ang).astype(np.float32) * sgn[:, None]).astype(np.float32)
        valid = ((w0 + np.arange(WIN) >= 0) & (w0 + np.arange(WIN) < L)).astype(np.float32)
        vmA = np.zeros((128, 64), np.float32)
        for hf in range(2):
            for i in range(32):
                vmA[:, hf * 32 + i] = valid[2048 * hf + 128 * i + np.arange(128)]
        vmB = np.zeros((128, 36), np.float32)
        top = 1024 + OWN + HH
        for i in range(36):
            rb = top - 128 * (i + 1)
            vmB[:, i] = valid[rb + 127 - np.arange(128)]
        vmC = np.zeros((128, 36), np.float32)
        for i in range(36):
            rb = (1024 - HH) + 128 * i
            vmC[:, i] = valid[rb + np.arange(128)]
        m = dict(shared)
        m.update({"xw": xw, "ccol": _col(cvec, 8), "cosT": cosT, "sinT": sinT, "vmA": vmA, "vmB": vmB, "vmC": vmC})
        in_maps.append(m)
    import os as _os
    upto = int(_os.environ.get("K_UPTO", "4")) if _debug else 4
    key = (bool(_debug), upto)
    if key not in _NC_CACHE:
        _NC_CACHE[key] = build_program(debug=key[0], upto=upto)[0]
    nc = _NC_CACHE[key]
    res = run_bass_kernel_spmd(nc, in_maps, core_ids=list(range(NCORES)))
    outs = res.results
    y_prompt = np.zeros((2, 8192, D), np.float32)
    y_sample = np.zeros((1, 16384, D), np.float32)
    for c in range(NCORES):
        grp, b, own0, L = _core_geom(c)
        yc = np.asarray(outs[c]["y"], np.float32)
        if grp == 0:
            y_prompt[b, own0:own0 + OWN] = yc
        else:
            y_sample[0, own0:own0 + OWN] = yc
    if _debug:
        return (y_prompt, y_sample), outs
    return (y_prompt, y_sample)
```

```python
import contextlib
import numpy as np
import concourse.bass as bass
import concourse.mybir as mybir
from concourse.alu_op_type import AluOpType as ALU
from concourse.bass_utils import run_bass_kernel_spmd

F32 = mybir.dt.float32
BF16 = mybir.dt.bfloat16
AF = mybir.ActivationFunctionType

D = 1024
NCORES = 8
OWN = 4096
HALO = 1024
WIN = OWN + 2 * HALO
HH = 512
FFN = 2816
NJ = FFN // 128
EPS = 1e-6
ROPE_THETA = 10000.0

PATTERNS = (1, 4, 16)


def att_vtiles():
    out = []
    for r in PATTERNS:
        jq0 = 1024 // r
        nb = (2048 // r) // 128
        for rho in range(r):
            for m in range(nb + 1):
                start = rho + r * (jq0 - 64 + 128 * m)
                out.append((r, rho, m, start, nb))
    return out


class Buf:
    __slots__ = ("name", "w", "r", "dsem", "ssem")

    def __init__(self, name):
        self.name = name
        self.w = {}
        self.r = {}
        self.dsem = None
        self.ssem = None


class Prog:
    ENG = ("pe", "act", "dve", "pool", "sp")

    def __init__(self, nc):
        self.nc = nc
        self.es = contextlib.ExitStack()
        self.eng = {"pe": nc.tensor, "act": nc.scalar, "dve": nc.vector, "pool": nc.gpsimd, "sp": nc.sync}
        self.sems = {}
        self.cnt = {}
        for e in self.ENG:
            self.sems[e] = self.es.enter_context(nc.semaphore("sem_" + e))
            self.cnt[e] = 0
        self.ndma = 12
        for i in range(self.ndma):
            k = "d%d" % i
            self.sems[k] = self.es.enter_context(nc.semaphore("sem_" + k))
            self.cnt[k] = 0
        self.nsw = 6
        for i in range(self.nsw):
            k = "w%d" % i
            self.sems[k] = self.es.enter_context(nc.semaphore("sem_" + k))
            self.cnt[k] = 0
        self.dma_next = 0
        self.sw_next = 0
        self.seen = {e: {} for e in self.ENG}
        self.nins = 0
        self.defer = True
        self.q = []
        self.window = 40
        self.act_tbl = None
        self.est_time = 0.0

    def buf(self, name, dma=False):
        b = Buf(name)
        if dma:
            assert self.dma_next < self.ndma, "out of dma semaphores"
            b.dsem = "d%d" % self.dma_next
            self.dma_next += 1
        return b

    def _deps(self, e, reads, writes):
        deps = {}
        for b in reads:
            for k, v in b.w.items():
                if deps.get(k, 0) < v:
                    deps[k] = v
        for b in writes:
            for k, v in b.w.items():
                if deps.get(k, 0) < v:
                    deps[k] = v
            for k, v in b.r.items():
                if deps.get(k, 0) < v:
                    deps[k] = v
        seen = self.seen[e]
        for k, v in deps.items():
            if k == e and e == "pe":
                continue
            if (k[0] == "d" and k != "dve") or k[0] == "w":
                v = self.cnt[k]
            if seen.get(k, 0) < v:
                self.eng[e].wait_ge(self.sems[k], v)
                seen[k] = v

    DEFAULT_COST = {"pe": 0.22, "act": 0.55, "dve": 0.55, "pool": 1.2, "sp": 0.15}

    def op(self, e, reads, writes, ins_fn, c=None, n=512, tb=None):
        if self.defer:
            if c is None:
                if e == "pe":
                    c = 0.07 + n * 0.0004
                elif e == "act":
                    c = 0.22 + n / 1400.0
                elif e == "dve":
                    c = 0.12 + n * 0.0011
                else:
                    c = 0.15 + n * 0.0022
            self.q.append((0, e, list(reads), list(writes), ins_fn, c, tb))
            return
        self._op_now(e, reads, writes, ins_fn)

    def dma(self, q, out, in_, reads, writes, semb, c=4.0):
        if self.defer:
            self.q.append((1, q, list(reads), list(writes), (out, in_, semb), c))
            return
        self._dma_now(q, out, in_, reads, writes, semb)

    def semkey(self, q, semb):
        if q == "sp":
            assert semb.dsem is not None
            return semb.dsem
        if semb.ssem is None:
            assert self.sw_next < self.nsw, "out of sw dma semaphores"
            semb.ssem = "w%d" % self.sw_next
            self.sw_next += 1
        return semb.ssem

    def _op_now(self, e, reads, writes, ins_fn):
        self._deps(e, reads, writes)
        ins = ins_fn(self.eng[e])
        self.cnt[e] += 1
        ins.then_inc(self.sems[e], 1)
        ev = self.cnt[e]
        for b in reads:
            b.r[e] = ev
        for b in writes:
            b.w = {e: ev}
            b.r = {}
        self.nins += 1

    def _dma_now(self, q, out, in_, reads, writes, semb):
        self._deps(q, reads, writes)
        k = self.semkey(q, semb)
        self.cnt[k] += 16
        self.eng[q].dma_start(out=out, in_=in_).then_inc(self.sems[k], 16)
        ev = self.cnt[k]
        for b in reads:
            b.r[k] = ev
        for b in writes:
            b.w = {k: ev}
            b.r = {}
        self.nins += 1

    def flush(self):
        ops = self.q
        self.q = []
        n = len(ops)
        if n == 0:
            return
        last_w = {}
        readers = {}
        deps = [None] * n
        for i, o in enumerate(ops):
            d = set()
            reads, writes = o[2], o[3]
            if o[0] == 1:
                kk_ = ("sem", self.semkey(o[1], o[4][2]))
                reads = reads + [kk_]
                writes = writes + [kk_]
            for b in reads:
                w = last_w.get(id(b) if not isinstance(b, tuple) else b)
                if w is not None:
                    d.add(w)
            for b in writes:
                kb = id(b) if not isinstance(b, tuple) else b
                w = last_w.get(kb)
                if w is not None:
                    d.add(w)
                for r in readers.get(kb, ()):
                    d.add(r)
            for b in reads:
                kb = id(b) if not isinstance(b, tuple) else b
                readers.setdefault(kb, []).append(i)
            for b in writes:
                kb = id(b) if not isinstance(b, tuple) else b
                last_w[kb] = i
                readers[kb] = []
            d.discard(i)
            deps[i] = d
        queues = {}
        for i, o in enumerate(ops):
            queues.setdefault(o[1], []).append(i)
        heads = {e: 0 for e in queues}
        done = [False] * n
        finish = [0.0] * n
        eng_free = {e: 0.0 for e in queues}
        W = self.window
        LAT = 0.08
        TBL = 1.3
        nsched = 0
        while nsched < n:
            best = None
            for e, ql in queues.items():
                h = heads[e]
                while h < len(ql) and done[ql[h]]:
                    h += 1
                heads[e] = h
                cnt = 0
                j = h
                ef = eng_free[e]
                while j < len(ql) and cnt < W:
                    i = ql[j]
                    j += 1
                    if done[i]:
                        continue
                    cnt += 1
                    rt = 0.0
                    ok = True
                    for dd in deps[i]:
                        if not done[dd]:
                            ok = False
                            break
                        f = finish[dd]
                        if f > rt:
                            rt = f
                    if not ok:
                        continue
                    st = rt + LAT if rt + LAT > ef else ef
                    if e == "act":
                        tb_ = ops[i][6] if ops[i][0] == 0 else None
                        if tb_ is not None and tb_ != self.act_tbl:
                            st += TBL
                    if best is None or st < best[0] or (st == best[0] and i < best[1]):
                        best = (st, i, e)
                    if st <= ef:
                        break
            st, i, e = best
            o = ops[i]
            if o[0] == 1:
                eng_free[e] = st + 0.15
                finish[i] = st + o[5]
            else:
                eng_free[e] = st + o[5]
                finish[i] = st + o[5]
                if e == "act" and o[6] is not None:
                    self.act_tbl = o[6]
            done[i] = True
            nsched += 1
            if o[0] == 1:
                self._dma_now(o[1], o[4][0], o[4][1], o[2], o[3], o[4][2])
            else:
                self._op_now(o[1], o[2], o[3], o[4])
        self.est_time += max(finish) if n else 0.0

    def barrier(self):
        self.flush()
        for e in self.ENG:
            seen = self.seen[e]
            for k, v in self.cnt.items():
                if v > 0 and seen.get(k, 0) < v:
                    self.eng[e].wait_ge(self.sems[k], v)
                    seen[k] = v
        self.dma_next = 0
        self.sw_next = 0


class Pools:
    def __init__(self, items):
        self.items = items
        self.i = 0

    def get(self):
        it = self.items[self.i % len(self.items)]
        self.i += 1
        return it


def build_program(debug=False, upto=4):
    nc = bass.Bass("TRN2", target_bir_lowering=False)
    P = Prog(nc)
    es = P.es

    def din(name, shape, dt=F32):
        return nc.dram_tensor(name, list(shape), dt, kind="ExternalInput").ap()

    xw = din("xw", [WIN, D])
    ccol = din("ccol", [128, 8])
    w_ada = din("w_ada", [D, 6 * D])
    b_ada = din("b_ada", [6 * D])
    n1g = din("n1g", [128, 8])
    n2g = din("n2g", [128, 8])
    w_in = din("w_in", [D, 6144])
    w_qksw = din("w_qksw", [D, 1024])
    bin_col = din("bin_col", [128, 48])
    bsw_col = din("bsw_col", [128, 8])
    b_in_row = din("b_in_row", [6144])
    lbl = din("lbl", [128, 16])
    gng = din("gng", [128, 4])
    w_bra = din("w_bra", [512, D])
    w_brb = din("w_brb", [512, D])
    w_out = din("w_out", [D, D])
    w_fi = din("w_fi", [D, 2 * FFN])
    w_fo = din("w_fo", [FFN, D])
    fng = din("fng", [D])
    cosT = din("cosT", [128, WIN])
    sinT = din("sinT", [128, WIN])
    vmA = din("vmA", [128, 64])
    vmB = din("vmB", [128, 36])
    vmC = din("vmC", [128, 36])
    consts = din("consts", [128, 1280])

    okind = "ExternalOutput"
    y = nc.dram_tensor("y", [OWN, D], F32, kind=okind).ap()
    skind = okind if debug else "Internal"
    ob_s = nc.dram_tensor("ob_s", [512, OWN], BF16, kind=skind).ap()
    obw_s = nc.dram_tensor("obw_s", [OWN, 512], F32, kind=skind).ap()
    ofw_s = nc.dram_tensor("ofw_s", [OWN, 512], F32, kind=skind).ap()
    v_s = nc.dram_tensor("v_s", [4096, 8, 65], BF16, kind="Internal").ap()
    B_vs = P.buf("v_s")
    x1_s = nc.dram_tensor("x1_s", [OWN, D], F32, kind=skind).ap()

    uid = [0]

    def sb(name, shape, dt, stack=None):
        uid[0] += 1
        return (stack or es).enter_context(nc.sbuf_tensor("%s_u%d" % (name, uid[0]), list(shape), dt))

    def ps(name, stack=None):
        return (stack or es).enter_context(nc.psum_tensor(name, [128, 512], F32))

    cst = sb("cst", [128, 1280], F32)
    cstb = sb("cstb", [128, 1280], BF16)
    B_cst = P.buf("cst", dma=True)
    B_cstb = P.buf("cstb")
    ident_f = cst[:, 0:128]
    J_f = cst[:, 128:256]
    ident_b = cstb[:, 0:128]
    J_b = cstb[:, 128:256]
    tri_b = cstb[:, 256:384]
    mA_b = cstb[:, 384:512]
    mB_b = cstb[:, 512:640]
    cmask_f = cst[:, 640:1152]

    colv = sb("colv", [128, 160], F32)
    B_colv = P.buf("colv")
    C_G1, C_SH1, C_G2, C_SH2 = 0, 8, 16, 24
    C_LB, C_OML = 32, 40
    C_GN = 48
    C_BIN = 52
    C_BSW = 100
    C_TMP = 108
    g_s = nc.dram_tensor("g_s", [2, 128, D], F32, kind="Internal").ap()

    psb = [ps("psb%d" % i) for i in range(8)]
    PSR = Pools([(psb[i], P.buf("ps%d" % i)) for i in range(8)])

    def cv(c, n=1):
        return colv[:, c:c + n]

    def load_weight(st, dst, B_dst, src_rows_ap, nkc, ncols, gate_bc=None, B_gate=None, stage=None):
        v = src_rows_ap.rearrange("(kc p) n -> p kc n", p=128)
        cw = max(64, min(ncols, stage.cols // nkc))
        i = 0
        for c0 in range(0, ncols, cw):
            c1 = min(ncols, c0 + cw)
            t, B = stage.get()
            P.dma("sp", t[:, 0:nkc * (c1 - c0)].rearrange("p (k n) -> p k n", k=nkc), v[:, :, c0:c1], [], [B], B)
            tv = t[:, 0:nkc * (c1 - c0)].rearrange("p (k n) -> p k n", k=nkc)
            if gate_bc is None:
                h0 = nkc // 2
                for (eng, ka, kb) in (("pool", 0, max(1, nkc // 4)), ("dve", max(1, nkc // 4), max(2, (5 * nkc) // 8)),
                                      ("act", max(2, (5 * nkc) // 8), nkc)):
                    if kb <= ka:
                        continue
                    if eng == "act":
                        P.op("act", [B], [P.buf('wcast')], lambda e, tv=tv, c0=c0, c1=c1, ka=ka, kb=kb: e.activation(
                            out=dst[:, ka:kb, c0:c1], in_=tv[:, ka:kb, :], func=AF.Copy))
                    else:
                        P.op(eng, [B], [P.buf('wcast')], lambda e, tv=tv, c0=c0, c1=c1, ka=ka, kb=kb: e.tensor_copy(
                            out=dst[:, ka:kb, c0:c1], in_=tv[:, ka:kb, :]))
            else:
                for kc in range(nkc):
                    eng = "dve" if kc % 3 != 2 else "pool"
                    P.op(eng, [B, B_gate], [P.buf('wcast')], lambda e, tv=tv, c0=c0, c1=c1, kc=kc: e.tensor_tensor(
                        out=dst[:, kc, c0:c1], in0=tv[:, kc, :], in1=gate_bc[:, c0:c1], op=ALU.mult))
            i += 1

    def mk_stage(st, n=2, cols=4096):
        pl = Pools([(sb("wstg%d" % i, [128, cols], F32, st), P.buf("wstg%d" % i, dma=True)) for i in range(n)])
        pl.cols = cols
        return pl

    pa_st = contextlib.ExitStack()
    if upto >= 1:
        Wq = sb("a_Wq", [128, 8, 512], BF16, pa_st)
        Wqs = sb("a_Wqs", [128, 8, 512], BF16, pa_st)
        Wk = sb("a_Wk", [128, 8, 512], BF16, pa_st)
        Wks = sb("a_Wks", [128, 8, 512], BF16, pa_st)
        Wv = sb("a_Wv", [128, 8, 512], BF16, pa_st)
        B_W = P.buf("a_W")

    with contextlib.ExitStack() as st:
        small = sb("p0small", [128, 64], F32, st)
        B_small = P.buf("p0small", dma=True)
        P.dma("sp", cst[:], consts, [], [B_cst], B_cst)
        P.op("dve", [B_cst], [B_cstb], lambda e: e.tensor_copy(out=cstb[:], in_=cst[:]))
        P.dma("sp", small[:, 0:8], ccol, [], [B_small], B_small)
        P.dma("sp", small[:, 24:40], lbl, [], [B_small], B_small)
        P.dma("sp", colv[:, C_GN:C_GN + 4], gng, [], [B_colv], B_small)
        P.dma("sp", colv[:, C_BIN:C_BIN + 48], bin_col, [], [B_colv], B_small)
        P.dma("sp", colv[:, C_BSW:C_BSW + 8], bsw_col, [], [B_colv], B_small)
        P.op("dve", [B_small], [B_small], lambda e: e.tensor_tensor(
            out=small[:, 40:48], in0=small[:, 24:32], in1=small[:, 32:40], op=ALU.subtract))
        P.op("act", [B_small], [B_colv], lambda e: e.activation(
            out=cv(C_LB, 8), in_=small[:, 40:48], func=AF.Sigmoid), tb='s')
        P.op("dve", [B_colv], [B_colv], lambda e: e.tensor_scalar(
            out=cv(C_OML, 8), in0=cv(C_LB, 8), scalar1=-1.0, scalar2=1.0, op0=ALU.mult, op1=ALU.add))
        scb = sb("scb", [128, 8, 128], F32, st)
        B_scb = P.buf("scb")
        P.op("act", [B_small], [B_small], lambda e: e.activation(
            out=small[:, 48:56], in_=small[:, 0:8], func=AF.Silu), tb='s')
        for kc in range(8):
            src = bass.AP(small.tensor if hasattr(small, "tensor") else small, 48 + kc, [[64, 128], [0, 128]])
            P.op("dve", [B_small], [B_scb], lambda e, kc=kc, src=src: e.tensor_copy(out=scb[:, kc, :], in_=src))
        badab = sb("badab", [128, 6 * D], F32, st)
        B_badab = P.buf("badab", dma=True)
        P.dma("sp", badab[:], b_ada.partition_broadcast(128), [], [B_badab], B_badab)
        modbc = sb("modbc", [128, 6 * D], F32, st)
        B_mod = P.buf("modbc")
        stg = [(sb("p0stg%d" % i, [128, 8, 512], F32, st), P.buf("p0stg%d" % i, dma=True)) for i in range(2)]
        wa_v = w_ada.rearrange("(kc p) n -> p kc n", p=128)
        for blk in range(12):
            t, B = stg[blk % 2]
            P.dma("sp", t[:], wa_v[:, :, blk * 512:(blk + 1) * 512], [], [B], B)
            pt, PB_ = PSR.get()
            for kc in range(8):
                P.op("pe", [B, B_scb], [PB_], lambda e, kc=kc, t=t, pt=pt: e.matmul(
                    pt[:], scb[:, kc, :], t[:, kc, :], start=(kc == 0), stop=(kc == 7)))
            P.op("dve", [PB_, B_badab], [B_mod], lambda e, blk=blk, pt=pt: e.tensor_tensor(
                out=modbc[:, blk * 512:(blk + 1) * 512], in0=pt[:], in1=badab[:, blk * 512:(blk + 1) * 512],
                op=ALU.add))
        pt, PB_ = PSR.get()
        for vi, base in enumerate((0, 1024, 3072, 4096)):
            for kc in range(8):
                P.op("pe", [B_mod, B_cst], [PB_], lambda e, vi=vi, base=base, kc=kc, pt=pt: e.matmul(
                    pt[:, vi * 8 + kc: vi * 8 + kc + 1], modbc[:, base + kc * 128: base + (kc + 1) * 128],
                    ident_f[:, 0:1], start=True, stop=True))
        P.op("dve", [PB_], [B_small], lambda e, pt=pt: e.tensor_copy(out=small[:, 0:32], in_=pt[:, 0:32]))
        sm2 = sb("p0sm2", [128, 16], F32, st)
        B_sm2 = P.buf("p0sm2", dma=True)
        P.dma("sp", sm2[:, 0:8], n1g, [], [B_sm2], B_sm2)
        P.dma("sp", sm2[:, 8:16], n2g, [], [B_sm2], B_sm2)
        P.op("dve", [B_small, B_sm2], [B_colv], lambda e: e.scalar_tensor_tensor(
            out=cv(C_G1, 8), in0=small[:, 8:16], scalar=1.0, in1=sm2[:, 0:8], op0=ALU.add, op1=ALU.mult))
        P.op("dve", [B_small, B_sm2], [B_colv], lambda e: e.scalar_tensor_tensor(
            out=cv(C_G2, 8), in0=small[:, 24:32], scalar=1.0, in1=sm2[:, 8:16], op0=ALU.add, op1=ALU.mult))
        P.op("dve", [B_small], [B_colv], lambda e: e.tensor_copy(out=cv(C_SH1, 8), in_=small[:, 0:8]))
        P.op("dve", [B_small], [B_colv], lambda e: e.tensor_copy(out=cv(C_SH2, 8), in_=small[:, 16:24]))
        B_gs = P.buf("p0gs", dma=True)
        P.dma("pool", g_s[0], modbc[:, 2048:3072], [B_mod], [], B_gs)
        P.dma("pool", g_s[1], modbc[:, 5120:6144], [B_mod], [], B_gs)
        if upto >= 1:
            stage = mk_stage(st, 3, 4096)
            load_weight(st, Wq, B_W, w_in[:, 2560:3072], 8, 512, stage=stage)
            load_weight(st, Wk, B_W, w_in[:, 3072:3584], 8, 512, stage=stage)
            load_weight(st, Wqs, B_W, w_qksw[:, 0:512], 8, 512, stage=stage)
            load_weight(st, Wks, B_W, w_qksw[:, 512:1024], 8, 512, stage=stage)
            load_weight(st, Wv, B_W, w_in[:, 3584:4096], 8, 512, stage=stage)
        P.barrier()

    class Front:
        def __init__(self, st, nxs, cG, cSH, tag, nxn=2):
            self.xs = Pools([(sb("xs%s%d" % (tag, i), [128, D], F32, st), P.buf("xs%s%d" % (tag, i), dma=True))
                             for i in range(nxs)])
            self.xn = Pools([(sb("xn%s%d" % (tag, i), [128, D], BF16, st), P.buf("xn%s%d" % (tag, i)))
                             for i in range(nxn)])
            self.junk = sb("junk" + tag, [128, D], BF16, st)
            self.B_junk = P.buf("junk" + tag)
            self.st4 = Pools([(sb("fst%s%d" % (tag, i), [128, 4], F32, st), P.buf("fst%s%d" % (tag, i)))
                              for i in range(2)])
            self.cG, self.cSH = cG, cSH
            self.pool_xn = False

        def stats(self, xt, B_x):
            s4, B_s = self.st4.get()
            junk, B_junk = self.junk, self.B_junk
            P.op("pool", [B_x], [B_junk], lambda e: e.tensor_tensor(out=junk[:], in0=xt[:], in1=xt[:], op=ALU.mult), c=1.9)
            P.op("dve", [B_junk], [B_s], lambda e: e.reduce_sum(out=s4[:, 0:1], in_=junk[:], axis=mybir.AxisListType.X), n=1024)
            P.op("dve", [B_s], [B_s], lambda e: e.tensor_scalar(
                out=s4[:, 1:2], in0=s4[:, 0:1], scalar1=1.0 / D, scalar2=EPS, op0=ALU.mult, op1=ALU.add), n=1)
            P.op("act", [B_s], [B_s], lambda e: e.activation(out=s4[:, 2:3], in_=s4[:, 1:2], func=AF.Ln), n=1, tb='e')
            P.op("act", [B_s], [B_s], lambda e: e.activation(out=s4[:, 3:4], in_=s4[:, 2:3], func=AF.Exp, scale=-0.5), n=1, tb='e')
            return s4, B_s

        def run(self, xt, B_x, hT_view, B_hT, flip):
            self.run_b(self.run_a(xt, B_x), hT_view, B_hT, flip)

        def run_a(self, xt, B_x):
            s4, B_s = self.stats(xt, B_x)
            xn, B_xn = self.xn.get()
            if self.pool_xn:
                P.op("pool", [B_x, B_s], [B_xn], lambda e: e.tensor_scalar(
                    out=xn[:], in0=xt[:], scalar1=s4[:, 3:4], scalar2=0.0, op0=ALU.mult, op1=ALU.add), c=1.1)
            else:
                P.op("act", [B_x, B_s], [B_xn], lambda e: e.activation(
                    out=xn[:], in_=xt[:], func=AF.Copy, scale=s4[:, 3:4]), n=1024)
            return xn, B_xn

        def run_b(self, a, hT_view, B_hT, flip):
            xn, B_xn = a
            mat = J_b if flip else ident_b
            for half in range(2):
                pt, PB_ = PSR.get()
                for k4 in range(4):
                    kc = half * 4 + k4
                    P.op("pe", [B_xn, B_cstb], [PB_], lambda e, kc=kc, k4=k4, pt=pt: e.matmul(
                        pt[:, k4 * 128:(k4 + 1) * 128], xn[:, kc * 128:(kc + 1) * 128], mat, start=True, stop=True), n=128)
                for k4 in range(4):
                    kc = half * 4 + k4
                    if k4 % 2 == 0:
                        P.op("dve", [PB_, B_colv], [B_hT], lambda e, kc=kc, k4=k4, pt=pt: e.tensor_scalar(
                            out=hT_view[:, kc, :], in0=pt[:, k4 * 128:(k4 + 1) * 128],
                            scalar1=cv(self.cG + kc), scalar2=cv(self.cSH + kc), op0=ALU.mult, op1=ALU.add), n=128)
                    else:
                        P.op("act", [PB_, B_colv], [B_hT], lambda e, kc=kc, k4=k4, pt=pt: e.activation(
                            out=hT_view[:, kc, :], in_=pt[:, k4 * 128:(k4 + 1) * 128], func=AF.Identity,
                            scale=cv(self.cG + kc), bias=cv(self.cSH + kc)), n=128)

    def front_tile(fr, loader, hTt, B_hTt, flip):
        pend = None
        for s4 in range(5):
            a = None
            if s4 < 4:
                xt, B_x = loader(s4)
                a = (fr.run_a(xt, B_x), s4)
            if pend is not None:
                pa_, ps_ = pend
                fr.run_b(pa_, hTt[:, :, ps_ * 128:(ps_ + 1) * 128], B_hTt, flip)
            pend = a

    def proj_fm(pt, PB_, W, B_W, col0, hT, B_hT, nkc=8, ncol=128):
        for kc in range(nkc):
            P.op("pe", [B_W, B_hT], [PB_], lambda e, kc=kc: e.matmul(
                pt[0:ncol, 0:hT.shape[2]], W[:, kc, col0:col0 + ncol], hT[:, kc, :], start=(kc == 0), stop=(kc == nkc - 1)))

    def proj_tm(pt_view, PB_, lhs_fn, B_h, W, B_W, col0, ncol, nkc=8):
        for kc in range(nkc):
            P.op("pe", [B_W, B_h], [PB_], lambda e, kc=kc: e.matmul(
                pt_view, lhs_fn(kc), W[:, kc, col0:col0 + ncol], start=(kc == 0), stop=(kc == nkc - 1)))

    NVT = len(att_vtiles())

    for hf in (range(2) if upto >= 1 else []):
        with contextlib.ExitStack() as st:
            kT = sb("a_kT", [128, 4, 4096], BF16, st)
            B_kT = P.buf("a_kT")
            qT = sb("a_qT", [128, 4, 2048], BF16, st)
            B_qT = P.buf("a_qT")
            r0 = 2048 * hf
            with contextlib.ExitStack() as s1:
                fr = Front(s1, 3, C_G1, C_SH1, "a")
                bvb = sb("a_bvb", [128, 512], F32, s1)
                B_bvb = P.buf("a_bvb", dma=True)
                P.dma("sp", bvb[:], b_in_row[3584:4096].partition_broadcast(128), [], [B_bvb], B_bvb)
                vm = sb("a_vm", [128, 32], F32, s1)
                B_vm = P.buf("a_vm", dma=True)
                P.dma("sp", vm[:], vmA[:, hf * 32:(hf + 1) * 32], [], [B_vm], B_vm)
                hTr = [(sb("a_hT%d" % i, [128, 8, 512], BF16, s1), P.buf("a_hT%d" % i)) for i in range(2)]
                csr = Pools([(sb("a_cs%d" % i, [128, 2, 512], F32, s1), P.buf("a_cs%d" % i, dma=True)) for i in range(2)])
                tmp = Pools([(sb("a_tmp%d" % i, [128, 512], F32, s1), P.buf("a_tmp%d" % i)) for i in range(6)])
                vrow = Pools([(sb("a_vrow%d" % i, [128, 8, 65], BF16, s1), P.buf("a_vrow%d" % i, dma=True)) for i in range(3)])
                for tile in range(8):
                    hTt, B_hTt = hTr[tile % 2]

                    def a_ld(s4, tile=tile):
                        xt, B_x = fr.xs.get()
                        rb = r0 + tile * 512 + s4 * 128
                        P.dma("sp", xt[:], xw[rb:rb + 128, :], [], [B_x], B_x)
                        return xt, B_x
                    front_tile(fr, a_ld, hTt, B_hTt, False)
                    cst_, B_cs = csr.get()
                    P.dma("sp", cst_[:, 0, :], cosT[:, r0 + tile * 512: r0 + (tile + 1) * 512], [], [B_cs], B_cs)
                    P.dma("sp", cst_[:, 1, :], sinT[:, r0 + tile * 512: r0 + (tile + 1) * 512], [], [B_cs], B_cs)
                    cs_v, sn_v = cst_[:, 0, :], cst_[:, 1, :]
                    jobs = [(Wk, Wks, kT, B_kT, tile * 512, 4, False)]
                    if 2 <= tile < 6:
                        jobs.append((Wq, Wqs, qT, B_qT, (tile - 2) * 512, 0, True))
                    for (W1, W2, dst, B_dst, d0, bofs, isq) in jobs:
                        for cc in range(4):
                            p1, PB1 = PSR.get()
                            proj_fm(p1, PB1, W1, B_W, cc * 128, hTt, B_hTt)
                            p2, PB2 = PSR.get()
                            proj_fm(p2, PB2, W2, B_W, cc * 128, hTt, B_hTt)
                            t1, B1 = tmp.get()
                            t2, B2 = tmp.get()
                            bc1 = cv(C_BIN + 20 + bofs + cc)
                            bc2 = cv(C_BSW + bofs + cc)
                            P.op("dve", [PB1, B_colv, B_cs], [B1], lambda e, p1=p1, t1=t1, bc1=bc1, cs_v=cs_v: e.scalar_tensor_tensor(
                                out=t1[:], in0=p1[:], scalar=bc1, in1=cs_v, op0=ALU.add, op1=ALU.mult))
                            P.op("dve", [PB2, B_colv, B_cs], [B2], lambda e, p2=p2, t2=t2, bc2=bc2, sn_v=sn_v: e.scalar_tensor_tensor(
                                out=t2[:], in0=p2[:], scalar=bc2, in1=sn_v, op0=ALU.add, op1=ALU.mult))
                            if isq:
                                P.op("pool", [B1, B2], [B1], lambda e, t1=t1, t2=t2: e.tensor_tensor(
                                    out=t1[:], in0=t1[:], in1=t2[:], op=ALU.add))
                                P.op("act", [B1], [B_dst], lambda e, t1=t1, dst=dst, cc=cc, d0=d0: e.activation(
                                    out=dst[:, cc, d0:d0 + 512], in_=t1[:], func=AF.Copy, scale=0.125))
                            else:
                                P.op("pool", [B1, B2], [B_dst], lambda e, t1=t1, t2=t2, dst=dst, cc=cc, d0=d0: e.tensor_tensor(
                                    out=dst[:, cc, d0:d0 + 512], in0=t1[:], in1=t2[:], op=ALU.add))
                    for s4 in range(4):
                        pv, PBv = PSR.get()
                        proj_tm(pv[:], PBv, lambda kc, s4=s4, hTt=hTt: hTt[:, kc, s4 * 128:(s4 + 1) * 128], B_hTt, Wv, B_W, 0, 512)
                        vt, B_vt = tmp.get()
                        P.op("dve", [PBv, B_bvb], [B_vt], lambda e, pv=pv, vt=vt: e.tensor_tensor(
                            out=vt[:], in0=pv[:], in1=bvb[:], op=ALU.add))
                        vr, B_vr = vrow.get()
                        vcol = tile * 4 + s4
                        P.op("act", [B_vt, B_vm], [B_vr], lambda e, vt=vt, vr=vr, vcol=vcol: e.activation(
                            out=vr[:, :, 0:64], in_=vt[:].rearrange("p (h d) -> p h d", d=64), func=AF.Copy,
                            scale=vm[:, vcol:vcol + 1]))
                        vsrc = bass.AP(vm.tensor if hasattr(vm, "tensor") else vm, vcol, [[32, 128], [0, 8], [1, 1]])
                        P.op("pool", [B_vm], [B_vr], lambda e, vr=vr, vsrc=vsrc: e.tensor_copy(
                            out=vr[:, :, 64:65], in_=vsrc), n=8)
                        tk = tile * 512 + s4 * 128
                        P.dma("pool", v_s[tk:tk + 128, :, :], vr[:], [B_vr], [], B_vr)
                P.barrier()
            with contextlib.ExitStack() as s2:
                mask4 = sb("a_mask4", [128, 512], BF16, s2)
                B_m4 = P.buf("a_mask4")
                for i in range(4):
                    src = mA_b if i % 2 == 0 else mB_b
                    P.op("dve", [B_cstb], [B_m4], lambda e, i=i, src=src: e.tensor_copy(
                        out=mask4[:, i * 128:(i + 1) * 128], in_=src), n=128)
                acc = sb("a_acc", [65, 8, 2048], F32, s2)
                B_acc = [P.buf("a_acc%d" % i) for i in range(2)]
                vext = Pools([(sb("a_vext%d" % i, [128, 8, 65], BF16, s2), P.buf("a_vext%d" % i, dma=True)) for i in range(6)])
                pt_ = Pools([(sb("a_P%d" % i, [128, 512], BF16, s2), P.buf("a_P%d" % i)) for i in range(10)])
                obst = Pools([(sb("a_obst%d" % i, [64, 512], BF16, s2), P.buf("a_obst%d" % i, dma=True)) for i in range(2)])
                vts = att_vtiles()
                prev = None
                pend2 = None
                for vi, (r, rho, m, start, nb) in enumerate(vts):
                    ve, B_ve = vext.get()
                    P.dma("sp", ve[:], v_s[start:start + 127 * r + 1:r, :, :], [], [B_ve], B_ve, c=2.5)
                    if m == 0:
                        prev = (ve, B_ve)
                        continue
                    bidx = m - 1
                    veA, B_veA = prev
                    veB, B_veB = ve, B_ve
                    prev = (ve, B_ve)
                    jq0 = 1024 // r
                    qst = rho + r * (jq0 + 128 * bidx)
                    qsl = slice(qst - 1024, qst - 1024 + 127 * r + 1, r)
                    kA = slice(qst - 64 * r, qst - 64 * r + 127 * r + 1, r)
                    kB = slice(qst + 64 * r, qst + 64 * r + 127 * r + 1, r)
                    Ps = {}
                    for hg in range(2):
                        pscs = [PSR.get(), PSR.get()]
                        for hh2 in range(2):
                            for ti, ksl in enumerate((kA, kB)):
                                for pair in range(2):
                                    psc, PBs = pscs[pair]
                                    h = hg * 4 + hh2 * 2 + pair
                                    ch, pb = h // 2, (h % 2) * 64
                                    c0 = hh2 * 256 + ti * 128
                                    P.op("pe", [B_kT, B_qT], [PBs], lambda e, psc=psc, c0=c0, ch=ch, pb=pb, ksl=ksl, qsl=qsl: e.matmul(
                                        psc[:, c0:c0 + 128], kT[pb:pb + 64, ch, ksl], qT[pb:pb + 64, ch, qsl],
                                        start=True, stop=True), n=300)
                        for pair in range(2):
                            psc, PBs = pscs[pair]
                            pp, B_pp = pt_.get()
                            P.op("act", [PBs], [B_pp], lambda e, psc=psc, pp=pp: e.activation(
                                out=pp[:], in_=psc[:], func=AF.Exp), tb='e')
                            P.op("dve", [B_pp, B_m4], [B_pp], lambda e, pp=pp: e.tensor_tensor(
                                out=pp[:], in0=pp[:], in1=mask4[:], op=ALU.mult))
                            for hh2 in range(2):
                                Ps[hg * 4 + hh2 * 2 + pair] = (pp, B_pp, hh2 * 256)

                    def stage2(Ps=Ps, veA=veA, B_veA=B_veA, veB=veB, B_veB=B_veB, qsl=qsl, r=r):
                        for hg in range(2):
                            po, PBo = PSR.get()
                            for hh in range(4):
                                h = hg * 4 + hh
                                pp, B_pp, c0 = Ps[h]
                                P.op("pe", [B_pp, B_veA], [PBo], lambda e, po=po, hh=hh, h=h, pp=pp, c0=c0: e.matmul(
                                    po[0:65, hh * 128:(hh + 1) * 128], veA[:, h, :], pp[:, c0:c0 + 128], start=True, stop=False), n=128)
                                P.op("pe", [B_pp, B_veB], [PBo], lambda e, po=po, hh=hh, h=h, pp=pp, c0=c0: e.matmul(
                                    po[0:65, hh * 128:(hh + 1) * 128], veB[:, h, :], pp[:, c0 + 128:c0 + 256], start=False, stop=True), n=128)
                            accv = acc[:, hg * 4:(hg + 1) * 4, qsl]
                            pov = po[0:65, :].rearrange("p (h q) -> p h q", h=4)
                            if r == 1:
                                P.op("dve", [PBo], [B_acc[hg]], lambda e, accv=accv, pov=pov: e.tensor_copy(out=accv, in_=pov))
                            else:
                                P.op("dve", [PBo, B_acc[hg]], [B_acc[hg]], lambda e, accv=accv, pov=pov: e.tensor_tensor(
                                    out=accv, in0=accv, in1=pov, op=ALU.add))
                    if pend2 is not None:
                        pend2()
                    pend2 = stage2
                if pend2 is not None:
                    pend2()
                    pend2 = None
                for hg in range(2):
                    hs = slice(hg * 4, (hg + 1) * 4)
                    P.op("act", [B_acc[hg]], [B_acc[hg]], lambda e, hs=hs: e.activation(
                        out=acc[64:65, hs, :], in_=acc[64:65, hs, :], func=AF.Ln), n=8192, tb='e')
                    P.op("act", [B_acc[hg]], [B_acc[hg]], lambda e, hs=hs: e.activation(
                        out=acc[64:65, hs, :], in_=acc[64:65, hs, :], func=AF.Exp, scale=-1.0), n=8192, tb='e')
                    for hh in range(4):
                        h = hg * 4 + hh
                        for q4 in range(4):
                            pd, PBd = PSR.get()
                            P.op("pe", [B_acc[hg], B_cst], [PBd], lambda e, pd=pd, h=h, q4=q4: e.matmul(
                                pd[0:64, :], cst[64:65, 1152:1216], acc[64:65, h, q4 * 512:(q4 + 1) * 512],
                                start=True, stop=True))
                            ot, B_ot = obst.get()
                            P.op("dve", [PBd, B_acc[hg]], [B_ot], lambda e, pd=pd, ot=ot, h=h, q4=q4: e.tensor_tensor(
                                out=ot[:], in0=acc[0:64, h, q4 * 512:(q4 + 1) * 512], in1=pd[0:64, :], op=ALU.mult))
                            c0 = hf * 2048 + q4 * 512
                            P.dma("pool", ob_s[h * 64:(h + 1) * 64, c0:c0 + 512], ot[:], [B_ot], [], B_ot)
                P.barrier()

    pa_st.close()

    class Scan:
        def __init__(self, st, tag):
            f = lambda n, dt=F32, sh=(128, 512), k=1: Pools([(sb("%s_%s%d" % (tag, n, i), list(sh), dt, st), P.buf("%s_%s%d" % (tag, n, i)))
                                                              for i in range(k)])
            self.t_sg, self.t_f, self.t_g, self.t_b = f("sg", k=4), f("f", k=2), f("g", k=2), f("b", k=2)
            self.t_eb, self.t_enb, self.t_kk, self.t_qs = f("eb", k=2), f("enb", k=2), f("kk", k=2), f("qs", k=2)
            NS = 2
            self.qT = [[sb("%s_qT%d_%d" % (tag, z, h), [128, 512], BF16, st) for h in range(4)] for z in range(NS)]
            self.kTt = [[sb("%s_kT%d_%d" % (tag, z, h), [128, 512], BF16, st) for h in range(4)] for z in range(NS)]
            self.ebk = [[sb("%s_ebk%d_%d" % (tag, z, h), [128, 8], F32, st) for h in range(4)] for z in range(NS)]
            self.B_q = [[P.buf("q") for h in range(4)] for z in range(NS)]
            self.B_k = [[P.buf("k") for h in range(4)] for z in range(NS)]
            self.B_e = [[P.buf("e") for h in range(4)] for z in range(NS)]
            self.vtok = [sb("%s_vtok%d" % (tag, z), [128, 4, 512], BF16, st) for z in range(NS)]
            self.ktok = [sb("%s_ktok%d" % (tag, z), [128, 4, 512], BF16, st) for z in range(NS)]
            self.B_vtok = [[P.buf("vtok") for i in range(4)] for z in range(NS)]
            self.B_ktok = [[P.buf("ktok") for i in range(4)] for z in range(NS)]
            self.vtmp = f("vtmp", k=2)
            self.A = f("A", BF16, k=3)
            self.S = [sb("%s_S%d" % (tag, h), [128, 128], F32, st) for h in range(4)]
            self.S1 = [sb("%s_S1%d" % (tag, h), [128, 128], F32, st) for h in range(4)]
            self.Sb = [sb("%s_Sb%d" % (tag, h), [128, 128], BF16, st) for h in range(4)]
            self.B_S = [P.buf("S") for h in range(4)]
            self.B_S1 = [P.buf("S1") for h in range(4)]
            self.B_Sb = [P.buf("Sb") for h in range(4)]
            for h in range(4):
                P.op("dve", [], [self.B_S[h]], lambda e, h=h: e.memset(self.S[h][:], 0.0))
                P.op("pool", [], [self.B_Sb[h]], lambda e, h=h: e.memset(self.Sb[h][:], 0.0))
            self.bvb = sb(tag + "_bvb", [128, 512], F32, st)
            self.B_bvb = P.buf(tag + "_bvb", dma=True)
            P.dma("sp", self.bvb[:], b_in_row[1536:2048].partition_broadcast(128), [], [self.B_bvb], self.B_bvb)

        def prepH(self, z, h, hTt, B_hTt, W, B_W, d):
            pq, PBq = PSR.get()
            proj_fm(pq, PBq, W, B_W, h * 128, hTt, B_hTt)
            yield
            pf, PBf = PSR.get()
            proj_fm(pf, PBf, W, B_W, 512 + h * 128, hTt, B_hTt)
            sg, B_sg = self.t_sg.get()
            bq = cv(C_BIN + 0 + h)
            bf = cv(C_BIN + 4 * (1 + d) + h)
            P.op("act", [PBq, B_colv], [B_sg], lambda e: e.activation(out=sg[:], in_=pq[:], func=AF.Sigmoid, bias=bq), tb='s')
            sf, B_sf = self.t_sg.get()
            P.op("act", [PBf, B_colv], [B_sf], lambda e: e.activation(out=sf[:], in_=pf[:], func=AF.Sigmoid, bias=bf), tb='s')
            yield
            qs, B_qs = self.t_qs.get()
            P.op("dve", [PBq, B_sg, B_colv], [B_qs], lambda e: e.scalar_tensor_tensor(
                out=qs[:], in0=pq[:], scalar=bq, in1=sg[:], op0=ALU.add, op1=ALU.mult))
            ff, B_ff = self.t_f.get()
            P.op("dve", [B_sf, B_colv], [B_ff], lambda e: e.tensor_scalar(
                out=ff[:], in0=sf[:], scalar1=cv(C_OML + d * 4 + h), scalar2=cv(C_LB + d * 4 + h),
                op0=ALU.mult, op1=ALU.add))
            yield
            g, B_g = self.t_g.get()
            P.op("act", [B_ff], [B_g], lambda e: e.activation(out=g[:], in_=ff[:], func=AF.Ln), tb='e')
            kk, B_kk = self.t_kk.get()
            P.op("pool", [B_ff], [B_kk], lambda e: e.tensor_scalar(
                out=kk[:], in0=ff[:], scalar1=-1.0, scalar2=1.0, op0=ALU.mult, op1=ALU.add), c=0.62)
            yield
            b, B_b = self.t_b.get()
            P.op("dve", [B_g, B_cst], [B_b], lambda e: e.tensor_tensor_scan(
                out=b[:], data0=cmask_f, data1=g[:], initial=0.0, op0=ALU.mult, op1=ALU.add), n=1024)
            yield
            eb, B_eb = self.t_eb.get()
            P.op("act", [B_b], [B_eb], lambda e: e.activation(out=eb[:], in_=b[:], func=AF.Exp), tb='e')
            enb, B_enb = self.t_enb.get()
            P.op("act", [B_b], [B_enb], lambda e: e.activation(out=enb[:], in_=b[:], func=AF.Exp, scale=-1.0), tb='e')
            yield
            P.op("pool", [B_qs, B_eb], [self.B_q[z][h]], lambda e: e.tensor_tensor(
                out=self.qT[z][h][:], in0=qs[:], in1=eb[:], op=ALU.mult))
            P.op("pool", [B_kk, B_enb], [self.B_k[z][h]], lambda e: e.tensor_tensor(
                out=self.kTt[z][h][:], in0=kk[:], in1=enb[:], op=ALU.mult))
            P.op("pool", [B_eb], [self.B_e[z][h]], lambda e: e.tensor_copy(out=self.ebk[z][h][:], in_=eb[:, 63:512:64]), n=8)

        def prepV(self, z, sub, hTt, B_hTt, W, B_W, vm_t, B_vm, vmcol):
            pv, PBv = PSR.get()
            proj_tm(pv[:], PBv, lambda kc: hTt[:, kc, sub * 128:(sub + 1) * 128], B_hTt, W, B_W, 1024, 512)
            vt, B_vt = self.vtmp.get()
            P.op("dve", [PBv, self.B_bvb], [B_vt], lambda e: e.tensor_tensor(out=vt[:], in0=pv[:], in1=self.bvb[:], op=ALU.add))
            P.op("act", [B_vt, B_vm], [self.B_vtok[z][sub]], lambda e: e.activation(
                out=self.vtok[z][:, sub, :], in_=vt[:], func=AF.Copy, scale=vm_t[:, vmcol:vmcol + 1]))

        def prepK(self, z):
            for sub in range(4):
                pk, PBk = PSR.get()
                for h in range(4):
                    P.op("pe", [self.B_k[z][h], B_cstb], [PBk], lambda e, h=h, pk=pk, sub=sub: e.matmul(
                        pk[:, h * 128:(h + 1) * 128], self.kTt[z][h][:, sub * 128:(sub + 1) * 128], ident_b,
                        start=True, stop=True), n=128)
                P.op("act", [PBk], [self.B_ktok[z][sub]], lambda e, pk=pk, sub=sub: e.activation(
                    out=self.ktok[z][:, sub, :], in_=pk[:], func=AF.Copy))

        def scan_sub(self, z, sub, res):
            qT, kTt, ebk, vtok, ktok = self.qT[z], self.kTt[z], self.ebk[z], self.vtok[z], self.ktok[z]
            B_q, B_k, B_e, B_vtok, B_ktok = self.B_q[z], self.B_k[z], self.B_e[z], self.B_vtok[z], self.B_ktok[z]
            psc, PBs = PSR.get()
            for h in range(4):
                P.op("pe", [B_k[h], B_q[h]], [PBs], lambda e, h=h: e.matmul(
                    psc[:, h * 128:(h + 1) * 128], kTt[h][:, sub * 128:(sub + 1) * 128],
                    qT[h][:, sub * 128:(sub + 1) * 128], start=True, stop=True), n=128)
            pus = [PSR.get(), PSR.get()]
            for h in range(4):
                for c in range(2):
                    pu, PBu = pus[c]
                    rows = slice(c * 64, (c + 1) * 64)
                    P.op("pe", [B_ktok[sub], B_vtok[sub]], [PBu], lambda e, pu=pu, h=h, rows=rows: e.matmul(
                        pu[:, h * 128:(h + 1) * 128], ktok[rows, sub, h * 128:(h + 1) * 128],
                        vtok[rows, sub, h * 128:(h + 1) * 128], start=True, stop=True), n=128)
            yield
            A, B_A = self.A.get()
            tri4 = bass.AP(cstb.tensor if hasattr(cstb, "tensor") else cstb, 256, [[1280, 128], [0, 4], [1, 128]])
            P.op("dve", [PBs, B_cstb], [B_A], lambda e: e.tensor_tensor(
                out=A[:].rearrange("p (h t) -> p h t", h=4), in0=psc[:].rearrange("p (h t) -> p h t", h=4),
                in1=tri4, op=ALU.mult))
            yield
            po, PBo = PSR.get()
            res.append((po, PBo))
            for h in range(4):
                P.op("pe", [B_A, B_vtok[sub]], [PBo], lambda e, h=h: e.matmul(
                    po[:, h * 128:(h + 1) * 128], A[:, h * 128:(h + 1) * 128], vtok[:, sub, h * 128:(h + 1) * 128],
                    start=(h == 0), stop=False, skip_group_check=True), n=128)
            for c in range(2):
                pu, PBu = pus[c]
                rows = slice(c * 64, (c + 1) * 64)
                toks = slice(sub * 128 + c * 64, sub * 128 + (c + 1) * 64)
                for h in range(4):
                    last = (c == 1 and h == 3)
                    P.op("pe", [B_q[h], self.B_Sb[h]], [PBo], lambda e, h=h, rows=rows, toks=toks, last=last: e.matmul(
                        po[rows, h * 128:(h + 1) * 128], qT[h][:, toks], self.Sb[h][:],
                        start=False, stop=last, skip_group_check=True), n=128)
                ci = sub * 2 + c
                for h in range(4):
                    e_ap = ebk[h][:, ci:ci + 1]
                    P.op("pool", [self.B_S[h], B_e[h]], [self.B_S1[h]], lambda e, h=h, e_ap=e_ap: e.tensor_scalar(
                        out=self.S1[h][:], in0=self.S[h][:], scalar1=e_ap, scalar2=0.0, op0=ALU.mult, op1=ALU.add), c=0.34)
                yield
                for h in range(4):
                    e_ap = ebk[h][:, ci:ci + 1]
                    P.op("dve", [PBu, self.B_S1[h], B_e[h]], [self.B_Sb[h]], lambda e, pu=pu, h=h, e_ap=e_ap: e.scalar_tensor_tensor(
                        out=self.Sb[h][:], in0=pu[:, h * 128:(h + 1) * 128], scalar=e_ap, in1=self.S1[h][:],
                        op0=ALU.mult, op1=ALU.add), n=128)
                    P.op("dve", [PBu, self.B_S1[h], B_e[h]], [self.B_S[h]], lambda e, pu=pu, h=h, e_ap=e_ap: e.scalar_tensor_tensor(
                        out=self.S[h][:], in0=pu[:, h * 128:(h + 1) * 128], scalar=e_ap, in1=self.S1[h][:],
                        op0=ALU.mult, op1=ALU.add), n=128)
                yield

    scan_W = {}
    sc_st = contextlib.ExitStack()
    if upto >= 2:
        for d_ in (1, 0):
            scan_W[d_] = (sb("s_W%d" % d_, [128, 8, 1536], BF16, sc_st), P.buf("s_W%d" % d_))
        with contextlib.ExitStack() as s1:
            stage = mk_stage(s1, 4)
            for d_ in (1, 0):
                W_, B_W_ = scan_W[d_]
                fcol = 1024 if d_ == 1 else 512
                load_weight(s1, W_[:, :, 0:512], B_W_, w_in[:, 0:512], 8, 512, stage=stage)
                load_weight(s1, W_[:, :, 512:1024], B_W_, w_in[:, fcol:fcol + 512], 8, 512, stage=stage)
                load_weight(s1, W_[:, :, 1024:1536], B_W_, w_in[:, 1536:2048], 8, 512, stage=stage)
            P.barrier()

    def scan_phase(d):
        with contextlib.ExitStack() as st:
            W, B_W = scan_W[d]
            fr = Front(st, 3, C_G1, C_SH1, "s%d" % d)
            fr.pool_xn = True
            sc = Scan(st, "s%d" % d)
            vm = sb("s_vm%d" % d, [128, 36], F32, st)
            B_vm = P.buf("s_vm", dma=True)
            P.dma("sp", vm[:], vmB if d == 1 else vmC, [], [B_vm], B_vm)
            hTr = [(sb("s_hT%d_%d" % (d, i), [128, 8, 512], BF16, st), P.buf("s_hT%d" % i)) for i in range(2)]
            ost = Pools([(sb("s_ost%d_%d" % (d, i), [128, 512], F32, st), P.buf("s_ost%d" % i, dma=True)) for i in range(2)])
            top = 1024 + OWN + HH
            base = 1024 - HH
            flip = (d == 1)
            o_s = obw_s if d == 1 else ofw_s
            NT = 9

            class FrontStream:
                def __init__(self):
                    self.pend = None

                def step(self, t, s4):
                    hTt, B_hTt = hTr[t % 2]
                    i = t * 4 + s4
                    rb = (top - 128 * (i + 1)) if flip else (base + 128 * i)
                    xt, B_x = fr.xs.get()
                    P.dma("sp", xt[:], xw[rb:rb + 128, :], [], [B_x], B_x)
                    a = (fr.run_a(xt, B_x), t, s4)
                    self.flush()
                    self.pend = a

                def flush(self):
                    if self.pend is not None:
                        pa_, pt, ps4 = self.pend
                        hTt, B_hTt = hTr[pt % 2]
                        fr.run_b(pa_, hTt[:, :, ps4 * 128:(ps4 + 1) * 128], B_hTt, flip)
                        self.pend = None

            fs = FrontStream()

            def drain(*gens):
                gens = [g for g in gens if g is not None]
                while gens:
                    for g in list(gens):
                        try:
                            next(g)
                        except StopIteration:
                            gens.remove(g)

            def gen_front(t, s4):
                fs.step(t, s4)
                yield

            def gen_prepV(z, s4, hn, B_hn, col):
                yield
                yield
                sc.prepV(z, s4, hn, B_hn, W, B_W, vm, B_vm, col)
                yield

            def gen_out(res, i):
                po, PBo = res[0]
                if i >= 4:
                    ot, B_ot = ost.get()
                    P.op("act", [PBo], [B_ot], lambda e: e.activation(
                        out=ot[:], in_=po[:], func=AF.Copy, scale=float(128 ** -0.5)))
                    row = (i - 4) * 128
                    P.dma("pool", o_s[row:row + 128, :], ot[:], [B_ot], [], B_ot)

            for s4 in range(4):
                fs.step(0, s4)
            fs.flush()
            for s4 in range(4):
                drain(sc.prepH(0, s4, hTr[0][0], hTr[0][1], W, B_W, d))
                sc.prepV(0, s4, hTr[0][0], hTr[0][1], W, B_W, vm, B_vm, s4)
            for s4 in range(4):
                fs.step(1, s4)
            fs.flush()
            for t in range(NT):
                z = t % 2
                sc.prepK(z)
                for s4 in range(4):
                    res = []
                    g_f = gen_front(t + 2, s4) if t + 2 < NT else None
                    g_p = g_v = None
                    if t + 1 < NT:
                        hn, B_hn = hTr[(t + 1) % 2]
                        g_p = sc.prepH(1 - z, s4, hn, B_hn, W, B_W, d)
                        g_v = gen_prepV(1 - z, s4, hn, B_hn, (t + 1) * 4 + s4)
                    drain(g_f, g_p, sc.scan_sub(z, s4, res), g_v)
                    gen_out(res, t * 4 + s4)
                fs.flush()
            P.barrier()

    if upto >= 2:
        scan_phase(1)
    if upto >= 3:
        scan_phase(0)
    sc_st.close()

    if upto >= 3:
        with contextlib.ExitStack() as st:
            W = sb("c_W", [128, 8, 2560], BF16, st)
            Wa = sb("c_Wa", [128, 4, D], BF16, st)
            Wb = sb("c_Wb", [128, 4, D], BF16, st)
            Wo = sb("c_Wo", [128, 8, D], BF16, st)
            B_W = P.buf("c_W")
            with contextlib.ExitStack() as s1:
                stage = mk_stage(s1, 4)
                load_weight(s1, W[:, :, 0:512], B_W, w_in[:, 2048:2560], 8, 512, stage=stage)
                load_weight(s1, W[:, :, 512:2560], B_W, w_in[:, 4096:6144], 8, 2048, stage=stage)
                load_weight(s1, Wa, B_W, w_bra, 4, D, stage=stage)
                load_weight(s1, Wb, B_W, w_brb, 4, D, stage=stage)
                gate1_bc = sb("gate1_bc", [128, D], F32, s1)
                B_g1 = P.buf("g1bc", dma=True)
                P.dma("sp", gate1_bc[:], g_s[0], [], [B_g1], B_g1)
                load_weight(s1, Wo, B_W, w_out, 8, D, gate_bc=gate1_bc, B_gate=B_g1, stage=stage)
                P.barrier()
            fr = Front(st, 3, C_G1, C_SH1, "c")
            hTr = [(sb("c_hT%d" % i, [128, 8, 512], BF16, st), P.buf("c_hT%d" % i)) for i in range(2)]
            sgT = sb("c_sgT", [128, 4, 512], BF16, st)
            B_sgT = P.buf("c_sgT")
            oaT = sb("c_oaT", [128, 4, 512], BF16, st)
            B_oaT = P.buf("c_oaT")
            obTr = Pools([(sb("c_obT%d" % i, [128, 4, 512], BF16, st), P.buf("c_obT%d" % i, dma=True)) for i in range(2)])
            mT = sb("c_mT", [128, 8, 512], BF16, st)
            B_mT = P.buf("c_mT")
            obw = Pools([(sb("c_obw%d" % i, [128, 512], F32, st), P.buf("c_obw%d" % i, dma=True)) for i in range(2)])
            ofw = Pools([(sb("c_ofw%d" % i, [128, 512], F32, st), P.buf("c_ofw%d" % i, dma=True)) for i in range(2)])
            osbr = Pools([(sb("c_osb%d" % i, [128, 512], F32, st), P.buf("c_osb%d" % i)) for i in range(2)])
            onbr = Pools([(sb("c_onb%d" % i, [128, 512], BF16, st), P.buf("c_onb%d" % i)) for i in range(2)])
            gstr = Pools([(sb("c_gst%d" % i, [128, 16], F32, st), P.buf("c_gst%d" % i)) for i in range(2)])
            gtmp = Pools([(sb("c_gt%d" % i, [128, 512], F32, st), P.buf("c_gt%d" % i)) for i in range(4)])
            x1st = Pools([(sb("c_x1%d" % i, [128, D], F32, st), P.buf("c_x1%d" % i, dma=True)) for i in range(2)])
            junk = sb("c_junk", [128, 512], BF16, st)
            B_junk = P.buf("c_junk")

            class FS:
                def __init__(self):
                    self.pend = None

                def step(self, t, s4):
                    rb = 1024 + t * 512 + s4 * 128
                    xt, B_x = fr.xs.get()
                    P.dma("sp", xt[:], xw[rb:rb + 128, :], [], [B_x], B_x)
                    a = (fr.run_a(xt, B_x), t, s4)
                    self.flush()
                    self.pend = a

                def flush(self):
                    if self.pend is not None:
                        pa_, pt, ps4 = self.pend
                        hTt, B_hTt = hTr[pt % 2]
                        fr.run_b(pa_, hTt[:, :, ps4 * 128:(ps4 + 1) * 128], B_hTt, False)
                        self.pend = None

            fs = FS()
            for s4 in range(4):
                fs.step(0, s4)
            fs.flush()
            for tile in range(8):
                hTt, B_hTt = hTr[tile % 2]
                t0 = tile * 512
                obT, B_obT = obTr.get()
                P.dma("sp", obT[:], ob_s.rearrange("(c p) t -> p c t", p=128)[:, :, t0:t0 + 512], [], [B_obT], B_obT)
                for h in range(4):
                    pg, PBg = PSR.get()
                    proj_fm(pg, PBg, W, B_W, h * 128, hTt, B_hTt)
                    g1, B1 = gtmp.get()
                    bg = cv(C_BIN + 16 + h)
                    P.op("act", [PBg, B_colv], [B1], lambda e, pg=pg, g1=g1, bg=bg: e.activation(
                        out=g1[:], in_=pg[:], func=AF.Sigmoid, bias=bg), tb='s')
                    P.op("dve", [PBg, B1, B_colv], [B_sgT], lambda e, pg=pg, g1=g1, bg=bg, h=h: e.scalar_tensor_tensor(
                        out=sgT[:, h, :], in0=pg[:], scalar=bg, in1=g1[:], op0=ALU.add, op1=ALU.mult))
                for s4 in range(4):
                    tt = t0 + s4 * 128
                    ow, B_ow = obw.get()
                    P.dma("sp", ow[:], obw_s[OWN - 128 - tt: OWN - tt, :], [], [B_ow], B_ow)
                    of, B_of = ofw.get()
                    P.dma("sp", of[:], ofw_s[tt:tt + 128, :], [], [B_of], B_of)
                    po, PBo = PSR.get()
                    P.op("pe", [B_ow, B_cst], [PBo], lambda e, po=po, ow=ow: e.matmul(po[:], J_f, ow[:], start=True, stop=True), n=2048)
                    osb, B_osb = osbr.get()
                    P.op("dve", [PBo, B_of], [B_osb], lambda e, po=po, of=of, osb=osb: e.tensor_tensor(
                        out=osb[:], in0=po[:], in1=of[:], op=ALU.add))
                    P.op("pool", [B_osb], [B_junk], lambda e, osb=osb: e.tensor_tensor(out=junk[:], in0=osb[:], in1=osb[:], op=ALU.mult))
                    gst, B_gst = gstr.get()
                    P.op("dve", [B_junk], [B_gst], lambda e, gst=gst: e.reduce_sum(
                        out=gst[:, 0:4], in_=junk[:].rearrange("p (h d) -> p h d", h=4), axis=mybir.AxisListType.X), n=512)
                    P.op("dve", [B_gst], [B_gst], lambda e, gst=gst: e.tensor_scalar(
                        out=gst[:, 4:8], in0=gst[:, 0:4], scalar1=1.0 / 128, scalar2=EPS, op0=ALU.mult, op1=ALU.add), n=4)
                    P.op("act", [B_gst], [B_gst], lambda e, gst=gst: e.activation(out=gst[:, 8:12], in_=gst[:, 4:8], func=AF.Ln), n=4, tb='e')
                    P.op("act", [B_gst], [B_gst], lambda e, gst=gst: e.activation(
                        out=gst[:, 12:16], in_=gst[:, 8:12], func=AF.Exp, scale=-0.5), n=4, tb='e')
                    onb, B_onb = onbr.get()
                    for h in range(4):
                        eng = "act" if h % 2 == 0 else "pool"
                        if eng == "act":
                            P.op("act", [B_osb, B_gst], [B_onb], lambda e, h=h, osb=osb, onb=onb, gst=gst: e.activation(
                                out=onb[:, h * 128:(h + 1) * 128], in_=osb[:, h * 128:(h + 1) * 128], func=AF.Copy,
                                scale=gst[:, 12 + h:13 + h]), n=128)
                        else:
                            P.op("pool", [B_osb, B_gst], [B_onb], lambda e, h=h, osb=osb, onb=onb, gst=gst: e.tensor_scalar(
                                out=onb[:, h * 128:(h + 1) * 128], in0=osb[:, h * 128:(h + 1) * 128],
                                scalar1=gst[:, 12 + h:13 + h], scalar2=0.0, op0=ALU.mult, op1=ALU.add), n=128)
                    ptp, PBt = PSR.get()
                    for h in range(4):
                        P.op("pe", [B_onb, B_cstb], [PBt], lambda e, ptp=ptp, h=h, onb=onb: e.matmul(
                            ptp[:, h * 128:(h + 1) * 128], onb[:, h * 128:(h + 1) * 128], ident_b, start=True, stop=True), n=128)
                    for h in range(4):
                        P.op("dve", [PBt, B_colv, B_sgT], [B_oaT], lambda e, ptp=ptp, h=h, s4=s4: e.scalar_tensor_tensor(
                            out=oaT[:, h, s4 * 128:(s4 + 1) * 128], in0=ptp[:, h * 128:(h + 1) * 128],
                            scalar=cv(C_GN + h), in1=sgT[:, h, s4 * 128:(s4 + 1) * 128], op0=ALU.mult, op1=ALU.mult), n=128)
                for cc in range(8):
                    pga, PBga = PSR.get()
                    proj_fm(pga, PBga, W, B_W, 512 + cc * 128, hTt, B_hTt)
                    pgb, PBgb = PSR.get()
                    proj_fm(pgb, PBgb, W, B_W, 1536 + cc * 128, hTt, B_hTt)
                    pa, PBa = PSR.get()
                    proj_fm(pa, PBa, Wa, B_W, cc * 128, oaT, B_oaT, nkc=4)
                    pb_, PBb = PSR.get()
                    proj_fm(pb_, PBb, Wb, B_W, cc * 128, obT, B_obT, nkc=4)
                    ga, B_ga = gtmp.get()
                    gb, B_gb = gtmp.get()
                    P.op("act", [PBga, B_colv], [B_ga], lambda e, pga=pga, ga=ga, cc=cc: e.activation(
                        out=ga[:], in_=pga[:], func=AF.Sigmoid, bias=cv(C_BIN + 32 + cc)), tb='s')
                    P.op("act", [PBgb, B_colv], [B_gb], lambda e, pgb=pgb, gb=gb, cc=cc: e.activation(
                        out=gb[:], in_=pgb[:], func=AF.Sigmoid, bias=cv(C_BIN + 40 + cc)), tb='s')
                    P.op("dve", [PBa, B_ga], [B_ga], lambda e, pa=pa, ga=ga: e.tensor_tensor(
                        out=ga[:], in0=pa[:], in1=ga[:], op=ALU.mult))
                    P.op("dve", [PBb, B_gb], [B_gb], lambda e, pb_=pb_, gb=gb: e.tensor_tensor(
                        out=gb[:], in0=pb_[:], in1=gb[:], op=ALU.mult))
                    P.op("pool", [B_ga, B_gb], [B_mT], lambda e, ga=ga, gb=gb, cc=cc: e.tensor_tensor(
                        out=mT[:, cc, :], in0=ga[:], in1=gb[:], op=ALU.add))
                    if tile + 1 < 8 and cc % 2 == 1:
                        fs.step(tile + 1, cc // 2)
                fs.flush()
                for s4 in range(4):
                    xo, B_xo = x1st.get()
                    rb = 1024 + t0 + s4 * 128
                    P.dma("sp", xo[:], xw[rb:rb + 128, :], [], [B_xo], B_xo)
                    for half in range(2):
                        px, PBx = PSR.get()
                        proj_tm(px[:], PBx, lambda kc, s4=s4: mT[:, kc, s4 * 128:(s4 + 1) * 128], B_mT, Wo, B_W, half * 512, 512)
                        P.op("dve", [PBx, B_xo], [B_xo], lambda e, px=px, xo=xo, half=half: e.tensor_tensor(
                            out=xo[:, half * 512:(half + 1) * 512], in0=px[:], in1=xo[:, half * 512:(half + 1) * 512],
                            op=ALU.add))
                    tt = t0 + s4 * 128
                    P.dma("pool", x1_s[tt:tt + 128, :], xo[:], [B_xo], [], B_xo)
            P.barrier()

    if upto >= 4:
        with contextlib.ExitStack() as st:
            Wi = sb("d_Wi", [128, 8, 2 * FFN], BF16, st)
            Wf = sb("d_Wf", [128, NJ, D], BF16, st)
            B_W = P.buf("d_W")
            with contextlib.ExitStack() as s1:
                stage = mk_stage(s1, 3)
                load_weight(s1, Wi, B_W, w_fi, 8, 2 * FFN, stage=stage)
                gate2_bc = sb("gate2_bc", [128, D], F32, s1)
                B_g2 = P.buf("g2bc", dma=True)
                P.dma("sp", gate2_bc[:], g_s[1], [], [B_g2], B_g2)
                load_weight(s1, Wf, B_W, w_fo, NJ, D, gate_bc=gate2_bc, B_gate=B_g2, stage=stage)
                P.barrier()
            fngb = sb("d_fng", [128, D], F32, st)
            B_fng = P.buf("d_fng", dma=True)
            P.dma("sp", fngb[:], fng.partition_broadcast(128), [], [B_fng], B_fng)
            fr = Front(st, 2, C_G2, C_SH2, "d", nxn=1)
            hTr = [(sb("d_hT%d" % i, [128, 8, 512], BF16, st), P.buf("d_hT%d" % i)) for i in range(2)]
            yT = sb("d_yT", [128, NJ, 512], BF16, st)
            B_yT = P.buf("d_yT")
            gtmp = Pools([(sb("d_gt%d" % i, [128, 512], F32, st), P.buf("d_gt%d" % i)) for i in range(2)])
            res = Pools([(sb("d_res%d" % i, [128, D], F32, st), P.buf("d_res%d" % i, dma=True)) for i in range(2)])

            def d_front(tile):
                hTt, B_hTt = hTr[tile % 2]

                def d_ld(s4):
                    t0 = tile * 512 + s4 * 128
                    xt, B_x = fr.xs.get()
                    P.dma("sp", xt[:], x1_s[t0:t0 + 128, :], [], [B_x], B_x)
                    return xt, B_x
                for s4 in range(4):
                    xt, B_x = d_ld(s4)
                    fr.run(xt, B_x, hTt[:, :, s4 * 128:(s4 + 1) * 128], B_hTt, False)

            d_front(0)
            for tile in range(8):
                hTt, B_hTt = hTr[tile % 2]
                for j in range(NJ):
                    pg, PBg = PSR.get()
                    proj_fm(pg, PBg, Wi, B_W, j * 128, hTt, B_hTt)
                    pu, PBu = PSR.get()
                    proj_fm(pu, PBu, Wi, B_W, FFN + j * 128, hTt, B_hTt)
                    g1, B1 = gtmp.get()
                    P.op("act", [PBg], [B1], lambda e, pg=pg, g1=g1: e.activation(out=g1[:], in_=pg[:], func=AF.Sigmoid), tb='s')
                    P.op("dve", [PBg, B1], [B1], lambda e, pg=pg, g1=g1: e.tensor_tensor(
                        out=g1[:], in0=pg[:], in1=g1[:], op=ALU.mult))
                    P.op("dve", [PBu, B1], [B_yT], lambda e, pu=pu, g1=g1, j=j: e.tensor_tensor(
                        out=yT[:, j, :], in0=pu[:], in1=g1[:], op=ALU.mult))
                if tile + 1 < 8:
                    d_front(tile + 1)
                for s4 in range(4):
                    xo, B_xo = res.get()
                    t0 = tile * 512 + s4 * 128
                    P.dma("sp", xo[:], x1_s[t0:t0 + 128, :], [], [B_xo], B_xo)
                    for half in range(2):
                        px, PBx = PSR.get()
                        proj_tm(px[:], PBx, lambda kc, s4=s4: yT[:, kc, s4 * 128:(s4 + 1) * 128], B_yT, Wf, B_W, half * 512, 512, nkc=NJ)
                        P.op("dve", [PBx, B_xo], [B_xo], lambda e, px=px, xo=xo, half=half: e.tensor_tensor(
                            out=xo[:, half * 512:(half + 1) * 512], in0=px[:], in1=xo[:, half * 512:(half + 1) * 512],
                            op=ALU.add))
                    s4t, B_s = fr.stats(xo, B_xo)
                    P.op("act", [B_xo, B_s], [B_xo], lambda e, xo=xo, s4t=s4t: e.activation(
                        out=xo[:], in_=xo[:], func=AF.Copy, scale=s4t[:, 3:4]), n=1024)
                    P.op("pool", [B_xo, B_fng], [B_xo], lambda e, xo=xo: e.tensor_tensor(
                        out=xo[:], in0=xo[:], in1=fngb[:], op=ALU.mult), c=1.9)
                    P.dma("pool", y[t0:t0 + 128, :], xo[:], [B_xo], [], B_xo)
            P.barrier()
    P.flush()
    P.es.close()
    return nc, P


def _col(v, n):
    return np.ascontiguousarray(np.asarray(v, np.float32).reshape(n, 128).T)


def _consts():
    c = np.zeros((128, 1280), np.float32)
    i = np.arange(128)
    c[:, 0:128] = np.eye(128, dtype=np.float32)
    c[:, 128:256] = np.eye(128, dtype=np.float32)[::-1]
    s = i[:, None]
    t = i[None, :]
    c[:, 256:384] = ((s // 64 == t // 64) & (t >= s)).astype(np.float32)
    c[:, 384:512] = (t <= s).astype(np.float32)
    c[:, 512:640] = (t >= s).astype(np.float32)
    cm = np.ones(512, np.float32)
    cm[::64] = 0.0
    c[:, 640:1152] = cm[None, :]
    c[:, 1152:1280] = 1.0
    return c


def _core_geom(c):
    if c < 4:
        return 0, c // 2, (c % 2) * OWN, 8192
    return 1, 0, (c - 4) * OWN, 16384


_NC_CACHE = {}


def kernel(x_prompt, x_sample, c_prompt, c_sample, w_ada, b_ada, norm1_g, w_in, b_in, lb_logits,
           hg_norm_g, w_branch_a, w_branch_b, w_out, norm2_g, w_ffn_in, w_ffn_out, final_norm_g,
           _debug=False):
    f32 = lambda a: np.ascontiguousarray(np.asarray(a, dtype=np.float32))
    x_prompt, x_sample = f32(x_prompt), f32(x_sample)
    c_prompt, c_sample = f32(c_prompt), f32(c_sample)
    w_in0 = f32(w_in)[0]
    b_in0 = f32(b_in)[0]
    perm = np.arange(1024).reshape(2, 8, 2, 32)[:, :, ::-1, :].reshape(-1)
    w_qksw = np.ascontiguousarray(w_in0[:, 2560:3584][:, perm])
    b_sw = b_in0[2560:3584][perm]
    lbl = np.concatenate([_col(f32(lb_logits)[l, d], 4) for l in range(2) for d in range(2)], axis=1)
    shared = {
        "w_ada": f32(w_ada)[0], "b_ada": f32(b_ada)[0], "n1g": _col(f32(norm1_g)[0], 8),
        "n2g": _col(f32(norm2_g)[0], 8), "w_in": w_in0, "w_qksw": w_qksw, "bin_col": _col(b_in0, 48),
        "bsw_col": _col(b_sw, 8), "b_in_row": b_in0, "lbl": np.ascontiguousarray(lbl),
        "gng": _col(f32(hg_norm_g)[0], 4), "w_bra": f32(w_branch_a)[0], "w_brb": f32(w_branch_b)[0],
        "w_out": f32(w_out)[0], "w_fi": f32(w_ffn_in)[0], "w_fo": f32(w_ffn_out)[0],
        "fng": f32(final_norm_g), "consts": _consts(),
    }
    half = 32
    inv = (np.float32(ROPE_THETA) ** (-np.arange(half, dtype=np.float32) / np.float32(half))).astype(np.float32)
    pidx = np.arange(128)
    fidx = pidx % 32
    sgn = np.where((pidx % 64) < 32, -1.0, 1.0).astype(np.float32)
    vts = att_vtiles()
    in_maps = []
    for c in range(NCORES):
        grp, b, own0, L = _core_geom(c)
        xs = x_prompt[b] if grp == 0 else x_sample[0]
        cvec = c_prompt[b] if grp == 0 else c_sample[0]
        w0 = own0 - HALO
        xw = np.zeros((WIN, D), np.float32)
        lo, hi = max(0, w0), min(L, w0 + WIN)
        xw[lo - w0:hi - w0] = xs[lo:hi]
        pos = (w0 + np.arange(WIN)).astype(np.float32)
        ang = (pos[None, :] * inv[fidx][:, None]).astype(np.float32)
        cosT = np.cos(ang).astype(np.float32)
        sinT = (np.sin(ang).astype(np.float32) * sgn[:, None]).astype(np.float32)
        valid = ((w0 + np.arange(WIN) >= 0) & (w0 + np.arange(WIN) < L)).astype(np.float32)
        vmA = np.zeros((128, 64), np.float32)
        for hf in range(2):
            for i in range(32):
                vmA[:, hf * 32 + i] = valid[2048 * hf + 128 * i + np.arange(128)]
        vmB = np.zeros((128, 36), np.float32)
        top = 1024 + OWN + HH
        for i in range(36):
            rb = top - 128 * (i + 1)
            vmB[:, i] = valid[rb + 127 - np.arange(128)]
        vmC = np.zeros((128, 36), np.float32)
        for i in range(36):
            rb = (1024 - HH) + 128 * i
            vmC[:, i] = valid[rb + np.arange(128)]
        m = dict(shared)
        m.update({"xw": xw, "ccol": _col(cvec, 8), "cosT": cosT, "sinT": sinT, "vmA": vmA, "vmB": vmB, "vmC": vmC})
        in_maps.append(m)
    import os as _os
    upto = int(_os.environ.get("K_UPTO", "4")) if _debug else 4
    key = (bool(_debug), upto)
    if key not in _NC_CACHE:
        _NC_CACHE[key] = build_program(debug=key[0], upto=upto)[0]
    nc = _NC_CACHE[key]
    res = run_bass_kernel_spmd(nc, in_maps, core_ids=list(range(NCORES)))
    outs = res.results
    y_prompt = np.zeros((2, 8192, D), np.float32)
    y_sample = np.zeros((1, 16384, D), np.float32)
    for c in range(NCORES):
        grp, b, own0, L = _core_geom(c)
        yc = np.asarray(outs[c]["y"], np.float32)
        if grp == 0:
            y_prompt[b, own0:own0 + OWN] = yc
        else:
            y_sample[0, own0:own0 + OWN] = yc
    if _debug:
        return (y_prompt, y_sample), outs
    return (y_prompt, y_sample)
```

```python
import contextlib
import numpy as np
import concourse.bass as bass
import concourse.mybir as mybir
from concourse.alu_op_type import AluOpType as ALU
from concourse.bass_utils import run_bass_kernel_spmd

F32 = mybir.dt.float32
BF16 = mybir.dt.bfloat16
AF = mybir.ActivationFunctionType

D = 1024
NCORES = 8
OWN = 4096
HALO = 1024
WIN = OWN + 2 * HALO
HH = 512
FFN = 2816
NJ = FFN // 128
EPS = 1e-6
ROPE_THETA = 10000.0

PATTERNS = (1, 4, 16)


def att_vtiles():
    out = []
    for r in PATTERNS:
        jq0 = 1024 // r
        nb = (2048 // r) // 128
        for rho in range(r):
            for m in range(nb + 1):
                start = rho + r * (jq0 - 64 + 128 * m)
                out.append((r, rho, m, start, nb))
    return out


class Buf:
    __slots__ = ("name", "w", "r", "dsem", "ssem")

    def __init__(self, name):
        self.name = name
        self.w = {}
        self.r = {}
        self.dsem = None
        self.ssem = None


class Prog:
    ENG = ("pe", "act", "dve", "pool", "sp")

    def __init__(self, nc):
        self.nc = nc
        self.es = contextlib.ExitStack()
        self.eng = {"pe": nc.tensor, "act": nc.scalar, "dve": nc.vector, "pool": nc.gpsimd, "sp": nc.sync}
        self.sems = {}
        self.cnt = {}
        for e in self.ENG:
            self.sems[e] = self.es.enter_context(nc.semaphore("sem_" + e))
            self.cnt[e] = 0
        self.ndma = 12
        for i in range(self.ndma):
            k = "d%d" % i
            self.sems[k] = self.es.enter_context(nc.semaphore("sem_" + k))
            self.cnt[k] = 0
        self.nsw = 6
        for i in range(self.nsw):
            k = "w%d" % i
            self.sems[k] = self.es.enter_context(nc.semaphore("sem_" + k))
            self.cnt[k] = 0
        self.dma_next = 0
        self.sw_next = 0
        self.seen = {e: {} for e in self.ENG}
        self.nins = 0
        self.defer = True
        self.q = []
        self.window = 40
        self.act_tbl = None
        self.est_time = 0.0

    def buf(self, name, dma=False):
        b = Buf(name)
        if dma:
            assert self.dma_next < self.ndma, "out of dma semaphores"
            b.dsem = "d%d" % self.dma_next
            self.dma_next += 1
        return b

    def _deps(self, e, reads, writes):
        deps = {}
        for b in reads:
            for k, v in b.w.items():
                if deps.get(k, 0) < v:
                    deps[k] = v
        for b in writes:
            for k, v in b.w.items():
                if deps.get(k, 0) < v:
                    deps[k] = v
            for k, v in b.r.items():
                if deps.get(k, 0) < v:
                    deps[k] = v
        seen = self.seen[e]
        for k, v in deps.items():
            if k == e and e == "pe":
                continue
            if (k[0] == "d" and k != "dve") or k[0] == "w":
                v = self.cnt[k]
            if seen.get(k, 0) < v:
                self.eng[e].wait_ge(self.sems[k], v)
                seen[k] = v

    DEFAULT_COST = {"pe": 0.22, "act": 0.55, "dve": 0.55, "pool": 1.2, "sp": 0.15}

    @staticmethod
    def _flat(bl):
        out = []
        for b in bl:
            if isinstance(b, (tuple, list)):
                out.extend(b)
            else:
                out.append(b)
        return out

    def op(self, e, reads, writes, ins_fn, c=None, n=512, tb=None):
        reads, writes = self._flat(reads), self._flat(writes)
        if self.defer:
            if c is None:
                if e == "pe":
                    c = 0.07 + n * 0.0004
                elif e == "act":
                    c = 0.22 + n / 1400.0
                elif e == "dve":
                    c = 0.12 + n * 0.0011
                else:
                    c = 0.15 + n * 0.0022
            self.q.append((0, e, list(reads), list(writes), ins_fn, c, tb))
            return
        self._op_now(e, reads, writes, ins_fn)

    def dma(self, q, out, in_, reads, writes, semb, c=4.0):
        reads, writes = self._flat(reads), self._flat(writes)
        if self.defer:
            self.q.append((1, q, list(reads), list(writes), (out, in_, semb), c))
            return
        self._dma_now(q, out, in_, reads, writes, semb)

    def semkey(self, q, semb):
        if q == "sp":
            assert semb.dsem is not None
            return semb.dsem
        if semb.ssem is None:
            assert self.sw_next < self.nsw, "out of sw dma semaphores"
            semb.ssem = "w%d" % self.sw_next
            self.sw_next += 1
        return semb.ssem

    def _op_now(self, e, reads, writes, ins_fn):
        self._deps(e, reads, writes)
        ins = ins_fn(self.eng[e])
        self.cnt[e] += 1
        ins.then_inc(self.sems[e], 1)
        ev = self.cnt[e]
        for b in reads:
            b.r[e] = ev
        for b in writes:
            b.w = {e: ev}
            b.r = {}
        self.nins += 1

    def _dma_now(self, q, out, in_, reads, writes, semb):
        self._deps(q, reads, writes)
        k = self.semkey(q, semb)
        self.cnt[k] += 16
        self.eng[q].dma_start(out=out, in_=in_).then_inc(self.sems[k], 16)
        ev = self.cnt[k]
        for b in reads:
            b.r[k] = ev
        for b in writes:
            b.w = {k: ev}
            b.r = {}
        self.nins += 1

    def flush(self):
        ops = self.q
        self.q = []
        n = len(ops)
        if n == 0:
            return
        last_w = {}
        readers = {}
        deps = [None] * n
        for i, o in enumerate(ops):
            d = set()
            reads, writes = o[2], o[3]
            if o[0] == 1:
                kk_ = ("sem", self.semkey(o[1], o[4][2]))
                reads = reads + [kk_]
                writes = writes + [kk_]
            for b in reads:
                w = last_w.get(id(b) if not isinstance(b, tuple) else b)
                if w is not None:
                    d.add(w)
            for b in writes:
                kb = id(b) if not isinstance(b, tuple) else b
                w = last_w.get(kb)
                if w is not None:
                    d.add(w)
                for r in readers.get(kb, ()):
                    d.add(r)
            for b in reads:
                kb = id(b) if not isinstance(b, tuple) else b
                readers.setdefault(kb, []).append(i)
            for b in writes:
                kb = id(b) if not isinstance(b, tuple) else b
                last_w[kb] = i
                readers[kb] = []
            d.discard(i)
            deps[i] = d
        queues = {}
        for i, o in enumerate(ops):
            queues.setdefault(o[1], []).append(i)
        heads = {e: 0 for e in queues}
        done = [False] * n
        finish = [0.0] * n
        eng_free = {e: 0.0 for e in queues}
        W = self.window
        LAT = 0.08
        TBL = 1.3
        nsched = 0
        while nsched < n:
            best = None
            for e, ql in queues.items():
                h = heads[e]
                while h < len(ql) and done[ql[h]]:
                    h += 1
                heads[e] = h
                cnt = 0
                j = h
                ef = eng_free[e]
                while j < len(ql) and cnt < W:
                    i = ql[j]
                    j += 1
                    if done[i]:
                        continue
                    cnt += 1
                    rt = 0.0
                    ok = True
                    for dd in deps[i]:
                        if not done[dd]:
                            ok = False
                            break
                        f = finish[dd]
                        if f > rt:
                            rt = f
                    if not ok:
                        continue
                    st = rt + LAT if rt + LAT > ef else ef
                    if e == "act":
                        tb_ = ops[i][6] if ops[i][0] == 0 else None
                        if tb_ is not None and tb_ != self.act_tbl:
                            st += TBL
                    if best is None or st < best[0] or (st == best[0] and i < best[1]):
                        best = (st, i, e)
                    if st <= ef:
                        break
            st, i, e = best
            o = ops[i]
            if o[0] == 1:
                eng_free[e] = st + 0.15
                finish[i] = st + o[5]
            else:
                eng_free[e] = st + o[5]
                finish[i] = st + o[5]
                if e == "act" and o[6] is not None:
                    self.act_tbl = o[6]
            done[i] = True
            nsched += 1
            if o[0] == 1:
                self._dma_now(o[1], o[4][0], o[4][1], o[2], o[3], o[4][2])
            else:
                self._op_now(o[1], o[2], o[3], o[4])
        self.est_time += max(finish) if n else 0.0

    def barrier(self):
        self.flush()
        for e in self.ENG:
            seen = self.seen[e]
            for k, v in self.cnt.items():
                if v > 0 and seen.get(k, 0) < v:
                    self.eng[e].wait_ge(self.sems[k], v)
                    seen[k] = v
        self.dma_next = 0
        self.sw_next = 0


class Pools:
    def __init__(self, items):
        self.items = items
        self.i = 0

    def get(self):
        it = self.items[self.i % len(self.items)]
        self.i += 1
        return it


def build_program(debug=False, upto=4):
    nc = bass.Bass("TRN2", target_bir_lowering=False)
    P = Prog(nc)
    es = P.es

    def din(name, shape, dt=F32):
        return nc.dram_tensor(name, list(shape), dt, kind="ExternalInput").ap()

    xw = din("xw", [WIN, D])
    ccol = din("ccol", [128, 8])
    w_ada = din("w_ada", [D, 6 * D])
    b_ada = din("b_ada", [6 * D])
    n1g = din("n1g", [128, 8])
    n2g = din("n2g", [128, 8])
    w_in = din("w_in", [D, 6144])
    w_qksw = din("w_qksw", [D, 1024])
    bin_col = din("bin_col", [128, 48])
    bsw_col = din("bsw_col", [128, 8])
    b_in_row = din("b_in_row", [6144])
    lbl = din("lbl", [128, 16])
    gng = din("gng", [128, 4])
    w_bra = din("w_bra", [512, D])
    w_brb = din("w_brb", [512, D])
    w_out = din("w_out", [D, D])
    w_fi = din("w_fi", [D, 2 * FFN])
    w_fo = din("w_fo", [FFN, D])
    fng = din("fng", [D])
    cosT = din("cosT", [128, WIN])
    sinT = din("sinT", [128, WIN])
    vmA = din("vmA", [128, 64])
    vmB = din("vmB", [128, 36])
    vmC = din("vmC", [128, 36])
    consts = din("consts", [128, 1408])

    okind = "ExternalOutput"
    y = nc.dram_tensor("y", [OWN, D], F32, kind=okind).ap()
    skind = okind if debug else "Internal"
    ob_s = nc.dram_tensor("ob_s", [512, OWN], BF16, kind=skind).ap()
    obw_s = nc.dram_tensor("obw_s", [OWN, 512], F32, kind=skind).ap()
    ofw_s = nc.dram_tensor("ofw_s", [OWN, 512], F32, kind=skind).ap()
    v_s = nc.dram_tensor("v_s", [4096, 8, 65], BF16, kind="Internal").ap()
    B_vs = P.buf("v_s")
    x1_s = nc.dram_tensor("x1_s", [OWN, D], F32, kind=skind).ap()

    uid = [0]

    def sb(name, shape, dt, stack=None):
        uid[0] += 1
        return (stack or es).enter_context(nc.sbuf_tensor("%s_u%d" % (name, uid[0]), list(shape), dt))

    def ps(name, stack=None):
        return (stack or es).enter_context(nc.psum_tensor(name, [128, 512], F32))

    cst = sb("cst", [128, 1408], F32)
    cstb = sb("cstb", [128, 1408], BF16)
    B_cst = P.buf("cst", dma=True)
    B_cstb = P.buf("cstb")
    ident_f = cst[:, 0:128]
    J_f = cst[:, 128:256]
    ident_b = cstb[:, 0:128]
    J_b = cstb[:, 128:256]
    tri_b = cstb[:, 256:384]
    mA_b = cstb[:, 384:512]
    mB_b = cstb[:, 512:640]
    cmask_f = cst[:, 640:1152]
    Rsw_b = cstb[:, 1280:1408]

    colv = sb("colv", [128, 160], F32)
    B_colv = P.buf("colv")
    C_G1, C_SH1, C_G2, C_SH2 = 0, 8, 16, 24
    C_LB, C_OML = 32, 40
    C_GN = 48
    C_BIN = 52
    C_BSW = 100
    C_TMP = 108
    g_s = nc.dram_tensor("g_s", [2, 128, D], F32, kind="Internal").ap()

    psb = [ps("psb%d" % i) for i in range(8)]
    PSR = Pools([(psb[i], P.buf("ps%d" % i)) for i in range(8)])

    def cv(c, n=1):
        return colv[:, c:c + n]

    def load_weight(st, dst, B_dst, src_rows_ap, nkc, ncols, gate_bc=None, B_gate=None, stage=None):
        v = src_rows_ap.rearrange("(kc p) n -> p kc n", p=128)
        cw = max(64, min(ncols, stage.cols // nkc))
        i = 0
        for c0 in range(0, ncols, cw):
            c1 = min(ncols, c0 + cw)
            t, B = stage.get()
            P.dma("sp", t[:, 0:nkc * (c1 - c0)].rearrange("p (k n) -> p k n", k=nkc), v[:, :, c0:c1], [], [B], B)
            tv = t[:, 0:nkc * (c1 - c0)].rearrange("p (k n) -> p k n", k=nkc)
            if gate_bc is None:
                h0 = nkc // 2
                for (eng, ka, kb) in (("pool", 0, max(1, nkc // 4)), ("dve", max(1, nkc // 4), max(2, (5 * nkc) // 8)),
                                      ("act", max(2, (5 * nkc) // 8), nkc)):
                    if kb <= ka:
                        continue
                    if eng == "act":
                        P.op("act", [B], [P.buf('wcast')], lambda e, tv=tv, c0=c0, c1=c1, ka=ka, kb=kb: e.activation(
                            out=dst[:, ka:kb, c0:c1], in_=tv[:, ka:kb, :], func=AF.Copy))
                    else:
                        P.op(eng, [B], [P.buf('wcast')], lambda e, tv=tv, c0=c0, c1=c1, ka=ka, kb=kb: e.tensor_copy(
                            out=dst[:, ka:kb, c0:c1], in_=tv[:, ka:kb, :]))
            else:
                for kc in range(nkc):
                    eng = "dve" if kc % 3 != 2 else "pool"
                    P.op(eng, [B, B_gate], [P.buf('wcast')], lambda e, tv=tv, c0=c0, c1=c1, kc=kc: e.tensor_tensor(
                        out=dst[:, kc, c0:c1], in0=tv[:, kc, :], in1=gate_bc[:, c0:c1], op=ALU.mult))
            i += 1

    def mk_stage(st, n=2, cols=4096):
        pl = Pools([(sb("wstg%d" % i, [128, cols], F32, st), P.buf("wstg%d" % i, dma=True)) for i in range(n)])
        pl.cols = cols
        return pl

    pa_st = contextlib.ExitStack()
    if upto >= 1:
        Wq = sb("a_Wq", [128, 8, 512], BF16, pa_st)
        Wk = sb("a_Wk", [128, 8, 512], BF16, pa_st)
        Wv = sb("a_Wv", [128, 8, 512], BF16, pa_st)
        B_W = P.buf("a_W")

    with contextlib.ExitStack() as st:
        small = sb("p0small", [128, 64], F32, st)
        B_small = P.buf("p0small", dma=True)
        P.dma("sp", cst[:], consts, [], [B_cst], B_cst)
        P.op("dve", [B_cst], [B_cstb], lambda e: e.tensor_copy(out=cstb[:], in_=cst[:]))
        P.dma("sp", small[:, 0:8], ccol, [], [B_small], B_small)
        P.dma("sp", small[:, 24:40], lbl, [], [B_small], B_small)
        P.dma("sp", colv[:, C_GN:C_GN + 4], gng, [], [B_colv], B_small)
        P.dma("sp", colv[:, C_BIN:C_BIN + 48], bin_col, [], [B_colv], B_small)
        P.dma("sp", colv[:, C_BSW:C_BSW + 8], bsw_col, [], [B_colv], B_small)
        P.op("dve", [B_small], [B_small], lambda e: e.tensor_tensor(
            out=small[:, 40:48], in0=small[:, 24:32], in1=small[:, 32:40], op=ALU.subtract))
        P.op("act", [B_small], [B_colv], lambda e: e.activation(
            out=cv(C_LB, 8), in_=small[:, 40:48], func=AF.Sigmoid), tb='s')
        P.op("dve", [B_colv], [B_colv], lambda e: e.tensor_scalar(
            out=cv(C_OML, 8), in0=cv(C_LB, 8), scalar1=-1.0, scalar2=1.0, op0=ALU.mult, op1=ALU.add))
        scb = sb("scb", [128, 8, 128], F32, st)
        B_scb = P.buf("scb")
        P.op("act", [B_small], [B_small], lambda e: e.activation(
            out=small[:, 48:56], in_=small[:, 0:8], func=AF.Silu), tb='s')
        for kc in range(8):
            src = bass.AP(small.tensor if hasattr(small, "tensor") else small, 48 + kc, [[64, 128], [0, 128]])
            P.op("dve", [B_small], [B_scb], lambda e, kc=kc, src=src: e.tensor_copy(out=scb[:, kc, :], in_=src))
        badab = sb("badab", [128, 6 * D], F32, st)
        B_badab = P.buf("badab", dma=True)
        P.dma("sp", badab[:], b_ada.partition_broadcast(128), [], [B_badab], B_badab)
        modbc = sb("modbc", [128, 6 * D], F32, st)
        B_mod = P.buf("modbc")
        stg = [(sb("p0stg%d" % i, [128, 8, 512], F32, st), P.buf("p0stg%d" % i, dma=True)) for i in range(2)]
        wa_v = w_ada.rearrange("(kc p) n -> p kc n", p=128)
        for blk in range(12):
            t, B = stg[blk % 2]
            P.dma("sp", t[:], wa_v[:, :, blk * 512:(blk + 1) * 512], [], [B], B)
            pt, PB_ = PSR.get()
            for kc in range(8):
                P.op("pe", [B, B_scb], [PB_], lambda e, kc=kc, t=t, pt=pt: e.matmul(
                    pt[:], scb[:, kc, :], t[:, kc, :], start=(kc == 0), stop=(kc == 7)))
            P.op("dve", [PB_, B_badab], [B_mod], lambda e, blk=blk, pt=pt: e.tensor_tensor(
                out=modbc[:, blk * 512:(blk + 1) * 512], in0=pt[:], in1=badab[:, blk * 512:(blk + 1) * 512],
                op=ALU.add))
        pt, PB_ = PSR.get()
        for vi, base in enumerate((0, 1024, 3072, 4096)):
            for kc in range(8):
                P.op("pe", [B_mod, B_cst], [PB_], lambda e, vi=vi, base=base, kc=kc, pt=pt: e.matmul(
                    pt[:, vi * 8 + kc: vi * 8 + kc + 1], modbc[:, base + kc * 128: base + (kc + 1) * 128],
                    ident_f[:, 0:1], start=True, stop=True))
        P.op("dve", [PB_], [B_small], lambda e, pt=pt: e.tensor_copy(out=small[:, 0:32], in_=pt[:, 0:32]))
        sm2 = sb("p0sm2", [128, 16], F32, st)
        B_sm2 = P.buf("p0sm2", dma=True)
        P.dma("sp", sm2[:, 0:8], n1g, [], [B_sm2], B_sm2)
        P.dma("sp", sm2[:, 8:16], n2g, [], [B_sm2], B_sm2)
        P.op("dve", [B_small, B_sm2], [B_colv], lambda e: e.scalar_tensor_tensor(
            out=cv(C_G1, 8), in0=small[:, 8:16], scalar=1.0, in1=sm2[:, 0:8], op0=ALU.add, op1=ALU.mult))
        P.op("dve", [B_small, B_sm2], [B_colv], lambda e: e.scalar_tensor_tensor(
            out=cv(C_G2, 8), in0=small[:, 24:32], scalar=1.0, in1=sm2[:, 8:16], op0=ALU.add, op1=ALU.mult))
        P.op("dve", [B_small], [B_colv], lambda e: e.tensor_copy(out=cv(C_SH1, 8), in_=small[:, 0:8]))
        P.op("dve", [B_small], [B_colv], lambda e: e.tensor_copy(out=cv(C_SH2, 8), in_=small[:, 16:24]))
        B_gs = P.buf("p0gs", dma=True)
        P.dma("pool", g_s[0], modbc[:, 2048:3072], [B_mod], [], B_gs)
        P.dma("pool", g_s[1], modbc[:, 5120:6144], [B_mod], [], B_gs)
        if upto >= 1:
            stage = mk_stage(st, 3, 4096)
            load_weight(st, Wq, B_W, w_in[:, 2560:3072], 8, 512, stage=stage)
            load_weight(st, Wk, B_W, w_in[:, 3072:3584], 8, 512, stage=stage)
            load_weight(st, Wv, B_W, w_in[:, 3584:4096], 8, 512, stage=stage)
        P.barrier()

    class Front:
        def __init__(self, st, nxs, cG, cSH, tag, nxn=2):
            self.xs = Pools([(sb("xs%s%d" % (tag, i), [128, D], F32, st), P.buf("xs%s%d" % (tag, i), dma=True))
                             for i in range(nxs)])
            self.xn = Pools([(sb("xn%s%d" % (tag, i), [128, D], BF16, st), P.buf("xn%s%d" % (tag, i)))
                             for i in range(nxn)])
            self.junk = sb("junk" + tag, [128, D], BF16, st)
            self.B_junk = P.buf("junk" + tag)
            self.st4 = Pools([(sb("fst%s%d" % (tag, i), [128, 4], F32, st), P.buf("fst%s%d" % (tag, i)))
                              for i in range(2)])
            self.cG, self.cSH = cG, cSH
            self.pool_xn = False

        def stats(self, xt, B_x):
            s4, B_s = self.st4.get()
            junk, B_junk = self.junk, self.B_junk
            P.op("pool", [B_x], [B_junk], lambda e: e.tensor_tensor(out=junk[:], in0=xt[:], in1=xt[:], op=ALU.mult), c=1.9)
            P.op("dve", [B_junk], [B_s], lambda e: e.reduce_sum(out=s4[:, 0:1], in_=junk[:], axis=mybir.AxisListType.X), n=1024)
            P.op("dve", [B_s], [B_s], lambda e: e.tensor_scalar(
                out=s4[:, 1:2], in0=s4[:, 0:1], scalar1=1.0 / D, scalar2=EPS, op0=ALU.mult, op1=ALU.add), n=1)
            P.op("act", [B_s], [B_s], lambda e: e.activation(out=s4[:, 2:3], in_=s4[:, 1:2], func=AF.Ln), n=1, tb='e')
            P.op("act", [B_s], [B_s], lambda e: e.activation(out=s4[:, 3:4], in_=s4[:, 2:3], func=AF.Exp, scale=-0.5), n=1, tb='e')
            return s4, B_s

        def run(self, xt, B_x, hT_view, B_hT, flip):
            self.run_b(self.run_a(xt, B_x), hT_view, B_hT, flip)

        def run_a(self, xt, B_x):
            s4, B_s = self.stats(xt, B_x)
            xn, B_xn = self.xn.get()
            if self.pool_xn:
                P.op("pool", [B_x, B_s], [B_xn], lambda e: e.tensor_scalar(
                    out=xn[:], in0=xt[:], scalar1=s4[:, 3:4], scalar2=0.0, op0=ALU.mult, op1=ALU.add), c=1.1)
            else:
                P.op("act", [B_x, B_s], [B_xn], lambda e: e.activation(
                    out=xn[:], in_=xt[:], func=AF.Copy, scale=s4[:, 3:4]), n=1024)
            return xn, B_xn

        def run_b(self, a, hT_view, B_hT, flip):
            xn, B_xn = a
            mat = J_b if flip else ident_b
            for half in range(2):
                pt, PB_ = PSR.get()
                for k4 in range(4):
                    kc = half * 4 + k4
                    P.op("pe", [B_xn, B_cstb], [PB_], lambda e, kc=kc, k4=k4, pt=pt: e.matmul(
                        pt[:, k4 * 128:(k4 + 1) * 128], xn[:, kc * 128:(kc + 1) * 128], mat, start=True, stop=True), n=128)
                for k4 in range(4):
                    kc = half * 4 + k4
                    if half == 0:
                        P.op("dve", [PB_, B_colv], [B_hT[0]], lambda e, kc=kc, k4=k4, pt=pt: e.tensor_scalar(
                            out=hT_view[:, kc, :], in0=pt[:, k4 * 128:(k4 + 1) * 128],
                            scalar1=cv(self.cG + kc), scalar2=cv(self.cSH + kc), op0=ALU.mult, op1=ALU.add), n=128)
                    else:
                        P.op("act", [PB_, B_colv], [B_hT[1]], lambda e, kc=kc, k4=k4, pt=pt: e.activation(
                            out=hT_view[:, kc, :], in_=pt[:, k4 * 128:(k4 + 1) * 128], func=AF.Identity,
                            scale=cv(self.cG + kc), bias=cv(self.cSH + kc)), n=128)

    def front_tile(fr, loader, hTt, B_hTt, flip):
        pend = None
        for s4 in range(5):
            a = None
            if s4 < 4:
                xt, B_x = loader(s4)
                a = (fr.run_a(xt, B_x), s4)
            if pend is not None:
                pa_, ps_ = pend
                fr.run_b(pa_, hTt[:, :, ps_ * 128:(ps_ + 1) * 128], B_hTt, flip)
            pend = a

    def proj_fm(pt, PB_, W, B_W, col0, hT, B_hT, nkc=8, ncol=128):
        for kc in range(nkc):
            P.op("pe", [B_W, B_hT], [PB_], lambda e, kc=kc: e.matmul(
                pt[0:ncol, 0:hT.shape[2]], W[:, kc, col0:col0 + ncol], hT[:, kc, :], start=(kc == 0), stop=(kc == nkc - 1)))

    def proj_tm(pt_view, PB_, lhs_fn, B_h, W, B_W, col0, ncol, nkc=8):
        for kc in range(nkc):
            P.op("pe", [B_W, B_h], [PB_], lambda e, kc=kc: e.matmul(
                pt_view, lhs_fn(kc), W[:, kc, col0:col0 + ncol], start=(kc == 0), stop=(kc == nkc - 1)))

    NVT = len(att_vtiles())

    for hf in (range(2) if upto >= 1 else []):
        with contextlib.ExitStack() as st:
            kT = sb("a_kT", [128, 4, 4096], BF16, st)
            B_kT = P.buf("a_kT")
            qT = sb("a_qT", [128, 4, 2048], BF16, st)
            B_qT = P.buf("a_qT")
            r0 = 2048 * hf
            with contextlib.ExitStack() as s1:
                fr = Front(s1, 3, C_G1, C_SH1, "a")
                bvb = sb("a_bvb", [128, 512], F32, s1)
                B_bvb = P.buf("a_bvb", dma=True)
                P.dma("sp", bvb[:], b_in_row[3584:4096].partition_broadcast(128), [], [B_bvb], B_bvb)
                vm = sb("a_vm", [128, 32], F32, s1)
                B_vm = P.buf("a_vm", dma=True)
                P.dma("sp", vm[:], vmA[:, hf * 32:(hf + 1) * 32], [], [B_vm], B_vm)
                hTr = [(sb("a_hT%d" % i, [128, 8, 512], BF16, s1), (P.buf("a_hT%d" % i), P.buf("a_hTb%d" % i))) for i in range(2)]
                csr = Pools([(sb("a_cs%d" % i, [128, 2, 512], F32, s1), P.buf("a_cs%d" % i, dma=True)) for i in range(2)])
                tmp = Pools([(sb("a_tmp%d" % i, [128, 512], F32, s1), P.buf("a_tmp%d" % i)) for i in range(6)])
                vrow = Pools([(sb("a_vrow%d" % i, [128, 8, 65], BF16, s1), P.buf("a_vrow%d" % i, dma=True)) for i in range(3)])
                kbp = Pools([(sb("a_kb%d" % i, [128, 512], BF16, s1), P.buf("a_kb%d" % i)) for i in range(4)])
                for tile in range(8):
                    hTt, B_hTt = hTr[tile % 2]

                    def a_ld(s4, tile=tile):
                        xt, B_x = fr.xs.get()
                        rb = r0 + tile * 512 + s4 * 128
                        P.dma("sp", xt[:], xw[rb:rb + 128, :], [], [B_x], B_x)
                        return xt, B_x
                    front_tile(fr, a_ld, hTt, B_hTt, False)
                    cst_, B_cs = csr.get()
                    P.dma("sp", cst_[:, 0, :], cosT[:, r0 + tile * 512: r0 + (tile + 1) * 512], [], [B_cs], B_cs)
                    P.dma("sp", cst_[:, 1, :], sinT[:, r0 + tile * 512: r0 + (tile + 1) * 512], [], [B_cs], B_cs)
                    cs_v, sn_v = cst_[:, 0, :], cst_[:, 1, :]
                    jobs = [(Wk, kT, B_kT, tile * 512, 4, False)]
                    if 2 <= tile < 6:
                        jobs.append((Wq, qT, B_qT, (tile - 2) * 512, 0, True))
                    for (W1, dst, B_dst, d0, bofs, isq) in jobs:
                        for cc in range(4):
                            p1, PB1 = PSR.get()
                            proj_fm(p1, PB1, W1, B_W, cc * 128, hTt, B_hTt)
                            kb, B_kb = kbp.get()
                            bc1 = cv(C_BIN + 20 + bofs + cc)
                            P.op("act", [PB1, B_colv], [B_kb], lambda e, p1=p1, kb=kb, bc1=bc1: e.activation(
                                out=kb[:], in_=p1[:], func=AF.Identity, bias=bc1))
                            p2, PB2 = PSR.get()
                            P.op("pe", [B_kb, B_cstb], [PB2], lambda e, p2=p2, kb=kb: e.matmul(
                                p2[:], Rsw_b, kb[:], start=True, stop=True))
                            t1, B1 = tmp.get()
                            t2, B2 = tmp.get()
                            P.op("dve", [B_kb, B_cs], [B1], lambda e, kb=kb, t1=t1, cs_v=cs_v: e.tensor_tensor(
                                out=t1[:], in0=kb[:], in1=cs_v, op=ALU.mult))
                            P.op("dve", [PB2, B_cs], [B2], lambda e, p2=p2, t2=t2, sn_v=sn_v: e.tensor_tensor(
                                out=t2[:], in0=p2[:], in1=sn_v, op=ALU.mult))
                            if isq:
                                P.op("pool", [B1, B2], [B1], lambda e, t1=t1, t2=t2: e.tensor_tensor(
                                    out=t1[:], in0=t1[:], in1=t2[:], op=ALU.add))
                                P.op("act", [B1], [B_dst], lambda e, t1=t1, dst=dst, cc=cc, d0=d0: e.activation(
                                    out=dst[:, cc, d0:d0 + 512], in_=t1[:], func=AF.Copy, scale=0.125))
                            else:
                                P.op("pool", [B1, B2], [B_dst], lambda e, t1=t1, t2=t2, dst=dst, cc=cc, d0=d0: e.tensor_tensor(
                                    out=dst[:, cc, d0:d0 + 512], in0=t1[:], in1=t2[:], op=ALU.add))
                    for s4 in range(4):
                        pv, PBv = PSR.get()
                        proj_tm(pv[:], PBv, lambda kc, s4=s4, hTt=hTt: hTt[:, kc, s4 * 128:(s4 + 1) * 128], B_hTt, Wv, B_W, 0, 512)
                        vt, B_vt = tmp.get()
                        P.op("dve", [PBv, B_bvb], [B_vt], lambda e, pv=pv, vt=vt: e.tensor_tensor(
                            out=vt[:], in0=pv[:], in1=bvb[:], op=ALU.add))
                        vr, B_vr = vrow.get()
                        vcol = tile * 4 + s4
                        P.op("act", [B_vt, B_vm], [B_vr], lambda e, vt=vt, vr=vr, vcol=vcol: e.activation(
                            out=vr[:, :, 0:64], in_=vt[:].rearrange("p (h d) -> p h d", d=64), func=AF.Copy,
                            scale=vm[:, vcol:vcol + 1]))
                        vsrc = bass.AP(vm.tensor if hasattr(vm, "tensor") else vm, vcol, [[32, 128], [0, 8], [1, 1]])
                        P.op("pool", [B_vm], [B_vr], lambda e, vr=vr, vsrc=vsrc: e.tensor_copy(
                            out=vr[:, :, 64:65], in_=vsrc), n=8)
                        tk = tile * 512 + s4 * 128
                        P.dma("pool", v_s[tk:tk + 128, :, :], vr[:], [B_vr], [], B_vr)
                P.barrier()
            with contextlib.ExitStack() as s2:
                mask4 = sb("a_mask4", [128, 512], BF16, s2)
                B_m4 = P.buf("a_mask4")
                for i in range(4):
                    src = mA_b if i % 2 == 0 else mB_b
                    P.op("dve", [B_cstb], [B_m4], lambda e, i=i, src=src: e.tensor_copy(
                        out=mask4[:, i * 128:(i + 1) * 128], in_=src), n=128)
                acc = sb("a_acc", [65, 8, 2048], F32, s2)
                B_acc = [P.buf("a_acc%d" % i) for i in range(2)]
                vext = Pools([(sb("a_vext%d" % i, [128, 8, 65], BF16, s2), P.buf("a_vext%d" % i, dma=True)) for i in range(6)])
                pt_ = Pools([(sb("a_P%d" % i, [128, 512], BF16, s2), P.buf("a_P%d" % i)) for i in range(10)])
                obst = Pools([(sb("a_obst%d" % i, [64, 512], BF16, s2), P.buf("a_obst%d" % i, dma=True)) for i in range(2)])
                vts = att_vtiles()
                prev = None
                pend2 = None
                for vi, (r, rho, m, start, nb) in enumerate(vts):
                    ve, B_ve = vext.get()
                    P.dma("sp", ve[:], v_s[start:start + 127 * r + 1:r, :, :], [], [B_ve], B_ve, c=2.5)
                    if m == 0:
                        prev = (ve, B_ve)
                        continue
                    bidx = m - 1
                    veA, B_veA = prev
                    veB, B_veB = ve, B_ve
                    prev = (ve, B_ve)
                    jq0 = 1024 // r
                    qst = rho + r * (jq0 + 128 * bidx)
                    qsl = slice(qst - 1024, qst - 1024 + 127 * r + 1, r)
                    kA = slice(qst - 64 * r, qst - 64 * r + 127 * r + 1, r)
                    kB = slice(qst + 64 * r, qst + 64 * r + 127 * r + 1, r)
                    Ps = {}
                    for hg in range(2):
                        pscs = [PSR.get(), PSR.get()]
                        for hh2 in range(2):
                            for ti, ksl in enumerate((kA, kB)):
                                for pair in range(2):
                                    psc, PBs = pscs[pair]
                                    h = hg * 4 + hh2 * 2 + pair
                                    ch, pb = h // 2, (h % 2) * 64
                                    c0 = hh2 * 256 + ti * 128
                                    P.op("pe", [B_kT, B_qT], [PBs], lambda e, psc=psc, c0=c0, ch=ch, pb=pb, ksl=ksl, qsl=qsl: e.matmul(
                                        psc[:, c0:c0 + 128], kT[pb:pb + 64, ch, ksl], qT[pb:pb + 64, ch, qsl],
                                        start=True, stop=True), n=300)
                        for pair in range(2):
                            psc, PBs = pscs[pair]
                            pp, B_pp = pt_.get()
                            P.op("act", [PBs], [B_pp], lambda e, psc=psc, pp=pp: e.activation(
                                out=pp[:], in_=psc[:], func=AF.Exp), tb='e')
                            P.op("dve", [B_pp, B_m4], [B_pp], lambda e, pp=pp: e.tensor_tensor(
                                out=pp[:], in0=pp[:], in1=mask4[:], op=ALU.mult))
                            for hh2 in range(2):
                                Ps[hg * 4 + hh2 * 2 + pair] = (pp, B_pp, hh2 * 256)

                    def stage2(Ps=Ps, veA=veA, B_veA=B_veA, veB=veB, B_veB=B_veB, qsl=qsl, r=r):
                        for hg in range(2):
                            po, PBo = PSR.get()
                            for hh in range(4):
                                h = hg * 4 + hh
                                pp, B_pp, c0 = Ps[h]
                                P.op("pe", [B_pp, B_veA], [PBo], lambda e, po=po, hh=hh, h=h, pp=pp, c0=c0: e.matmul(
                                    po[0:65, hh * 128:(hh + 1) * 128], veA[:, h, :], pp[:, c0:c0 + 128], start=True, stop=False), n=128)
                                P.op("pe", [B_pp, B_veB], [PBo], lambda e, po=po, hh=hh, h=h, pp=pp, c0=c0: e.matmul(
                                    po[0:65, hh * 128:(hh + 1) * 128], veB[:, h, :], pp[:, c0 + 128:c0 + 256], start=False, stop=True), n=128)
                            accv = acc[:, hg * 4:(hg + 1) * 4, qsl]
                            pov = po[0:65, :].rearrange("p (h q) -> p h q", h=4)
                            if r == 1:
                                P.op("dve", [PBo], [B_acc[hg]], lambda e, accv=accv, pov=pov: e.tensor_copy(out=accv, in_=pov))
                            else:
                                P.op("dve", [PBo, B_acc[hg]], [B_acc[hg]], lambda e, accv=accv, pov=pov: e.tensor_tensor(
                                    out=accv, in0=accv, in1=pov, op=ALU.add))
                    if pend2 is not None:
                        pend2()
                    pend2 = stage2
                if pend2 is not None:
                    pend2()
                    pend2 = None
                for hg in range(2):
                    hs = slice(hg * 4, (hg + 1) * 4)
                    P.op("act", [B_acc[hg]], [B_acc[hg]], lambda e, hs=hs: e.activation(
                        out=acc[64:65, hs, :], in_=acc[64:65, hs, :], func=AF.Ln), n=8192, tb='e')
                    P.op("act", [B_acc[hg]], [B_acc[hg]], lambda e, hs=hs: e.activation(
                        out=acc[64:65, hs, :], in_=acc[64:65, hs, :], func=AF.Exp, scale=-1.0), n=8192, tb='e')
                    for hh in range(4):
                        h = hg * 4 + hh
                        for q4 in range(4):
                            pd, PBd = PSR.get()
                            P.op("pe", [B_acc[hg], B_cst], [PBd], lambda e, pd=pd, h=h, q4=q4: e.matmul(
                                pd[0:64, :], cst[64:65, 1152:1216], acc[64:65, h, q4 * 512:(q4 + 1) * 512],
                                start=True, stop=True))
                            ot, B_ot = obst.get()
                            P.op("dve", [PBd, B_acc[hg]], [B_ot], lambda e, pd=pd, ot=ot, h=h, q4=q4: e.tensor_tensor(
                                out=ot[:], in0=acc[0:64, h, q4 * 512:(q4 + 1) * 512], in1=pd[0:64, :], op=ALU.mult))
                            c0 = hf * 2048 + q4 * 512
                            P.dma("pool", ob_s[h * 64:(h + 1) * 64, c0:c0 + 512], ot[:], [B_ot], [], B_ot)
                P.barrier()

    pa_st.close()

    class Scan:
        def __init__(self, st, tag):
            f = lambda n, dt=F32, sh=(128, 512), k=1: Pools([(sb("%s_%s%d" % (tag, n, i), list(sh), dt, st), P.buf("%s_%s%d" % (tag, n, i)))
                                                              for i in range(k)])
            self.t_sg, self.t_f, self.t_g, self.t_b = f("sg", k=4), f("f", k=2), f("g", k=2), f("b", k=2)
            self.t_eb, self.t_enb, self.t_kk, self.t_qs = f("eb", k=2), f("enb", k=2), f("kk", k=2), f("qs", k=2)
            NS = 2
            self.qT = [[sb("%s_qT%d_%d" % (tag, z, h), [128, 512], BF16, st) for h in range(4)] for z in range(NS)]
            self.kTt = [[sb("%s_kT%d_%d" % (tag, z, h), [128, 512], BF16, st) for h in range(4)] for z in range(NS)]
            self.ebk = [[sb("%s_ebk%d_%d" % (tag, z, h), [128, 8], F32, st) for h in range(4)] for z in range(NS)]
            self.B_q = [[P.buf("q") for h in range(4)] for z in range(NS)]
            self.B_k = [[P.buf("k") for h in range(4)] for z in range(NS)]
            self.B_e = [[P.buf("e") for h in range(4)] for z in range(NS)]
            self.vtok = [sb("%s_vtok%d" % (tag, z), [128, 4, 512], BF16, st) for z in range(NS)]
            self.ktok = [sb("%s_ktok%d" % (tag, z), [128, 4, 512], BF16, st) for z in range(NS)]
            self.B_vtok = [[P.buf("vtok") for i in range(4)] for z in range(NS)]
            self.B_ktok = [[P.buf("ktok") for i in range(4)] for z in range(NS)]
            self.vtmp = f("vtmp", k=2)
            self.A = f("A", BF16, k=3)
            self.S = [sb("%s_S%d" % (tag, h), [128, 128], F32, st) for h in range(4)]
            self.S1 = [sb("%s_S1%d" % (tag, h), [128, 128], F32, st) for h in range(4)]
            self.Sb = [sb("%s_Sb%d" % (tag, h), [128, 128], BF16, st) for h in range(4)]
            self.B_S = [P.buf("S") for h in range(4)]
            self.B_S1 = [P.buf("S1") for h in range(4)]
            self.B_Sb = [P.buf("Sb") for h in range(4)]
            for h in range(4):
                P.op("dve", [], [self.B_S[h]], lambda e, h=h: e.memset(self.S[h][:], 0.0))
                P.op("pool", [], [self.B_Sb[h]], lambda e, h=h: e.memset(self.Sb[h][:], 0.0))
            self.bvb = sb(tag + "_bvb", [128, 512], F32, st)
            self.B_bvb = P.buf(tag + "_bvb", dma=True)
            P.dma("sp", self.bvb[:], b_in_row[1536:2048].partition_broadcast(128), [], [self.B_bvb], self.B_bvb)

        def prepH(self, z, h, hTt, B_hTt, W, B_W, d):
            pq, PBq = PSR.get()
            proj_fm(pq, PBq, W, B_W, h * 128, hTt, B_hTt)
            yield
            pf, PBf = PSR.get()
            proj_fm(pf, PBf, W, B_W, 512 + h * 128, hTt, B_hTt)
            sg, B_sg = self.t_sg.get()
            bq = cv(C_BIN + 0 + h)
            bf = cv(C_BIN + 4 * (1 + d) + h)
            P.op("act", [PBq, B_colv], [B_sg], lambda e: e.activation(out=sg[:], in_=pq[:], func=AF.Sigmoid, bias=bq), tb='s')
            sf, B_sf = self.t_sg.get()
            P.op("act", [PBf, B_colv], [B_sf], lambda e: e.activation(out=sf[:], in_=pf[:], func=AF.Sigmoid, bias=bf), tb='s')
            yield
            qs, B_qs = self.t_qs.get()
            P.op("dve", [PBq, B_sg, B_colv], [B_qs], lambda e: e.scalar_tensor_tensor(
                out=qs[:], in0=pq[:], scalar=bq, in1=sg[:], op0=ALU.add, op1=ALU.mult))
            ff, B_ff = self.t_f.get()
            P.op("dve", [B_sf, B_colv], [B_ff], lambda e: e.tensor_scalar(
                out=ff[:], in0=sf[:], scalar1=cv(C_OML + d * 4 + h), scalar2=cv(C_LB + d * 4 + h),
                op0=ALU.mult, op1=ALU.add))
            yield
            g, B_g = self.t_g.get()
            P.op("act", [B_ff], [B_g], lambda e: e.activation(out=g[:], in_=ff[:], func=AF.Ln), tb='e')
            kk, B_kk = self.t_kk.get()
            P.op("pool", [B_ff], [B_kk], lambda e: e.tensor_scalar(
                out=kk[:], in0=ff[:], scalar1=-1.0, scalar2=1.0, op0=ALU.mult, op1=ALU.add), c=0.62)
            yield
            b, B_b = self.t_b.get()
            P.op("dve", [B_g, B_cst], [B_b], lambda e: e.tensor_tensor_scan(
                out=b[:], data0=cmask_f, data1=g[:], initial=0.0, op0=ALU.mult, op1=ALU.add), n=1024)
            yield
            eb, B_eb = self.t_eb.get()
            P.op("act", [B_b], [B_eb], lambda e: e.activation(out=eb[:], in_=b[:], func=AF.Exp), tb='e')
            enb, B_enb = self.t_enb.get()
            P.op("act", [B_b], [B_enb], lambda e: e.activation(out=enb[:], in_=b[:], func=AF.Exp, scale=-1.0), tb='e')
            yield
            P.op("pool", [B_qs, B_eb], [self.B_q[z][h]], lambda e: e.tensor_tensor(
                out=self.qT[z][h][:], in0=qs[:], in1=eb[:], op=ALU.mult))
            P.op("pool", [B_kk, B_enb], [self.B_k[z][h]], lambda e: e.tensor_tensor(
                out=self.kTt[z][h][:], in0=kk[:], in1=enb[:], op=ALU.mult))
            P.op("pool", [B_eb], [self.B_e[z][h]], lambda e: e.tensor_copy(out=self.ebk[z][h][:], in_=eb[:, 63:512:64]), n=8)

        def prepV(self, z, sub, hTt, B_hTt, W, B_W, vm_t, B_vm, vmcol):
            pv, PBv = PSR.get()
            proj_tm(pv[:], PBv, lambda kc: hTt[:, kc, sub * 128:(sub + 1) * 128], B_hTt, W, B_W, 1024, 512)
            vt, B_vt = self.vtmp.get()
            P.op("dve", [PBv, self.B_bvb], [B_vt], lambda e: e.tensor_tensor(out=vt[:], in0=pv[:], in1=self.bvb[:], op=ALU.add))
            P.op("act", [B_vt, B_vm], [self.B_vtok[z][sub]], lambda e: e.activation(
                out=self.vtok[z][:, sub, :], in_=vt[:], func=AF.Copy, scale=vm_t[:, vmcol:vmcol + 1]))

        def prepK(self, z):
            for sub in range(4):
                pk, PBk = PSR.get()
                for h in range(4):
                    P.op("pe", [self.B_k[z][h], B_cstb], [PBk], lambda e, h=h, pk=pk, sub=sub: e.matmul(
                        pk[:, h * 128:(h + 1) * 128], self.kTt[z][h][:, sub * 128:(sub + 1) * 128], ident_b,
                        start=True, stop=True), n=128)
                P.op("act", [PBk], [self.B_ktok[z][sub]], lambda e, pk=pk, sub=sub: e.activation(
                    out=self.ktok[z][:, sub, :], in_=pk[:], func=AF.Copy))

        def scan_sub(self, z, sub, res):
            qT, kTt, ebk, vtok, ktok = self.qT[z], self.kTt[z], self.ebk[z], self.vtok[z], self.ktok[z]
            B_q, B_k, B_e, B_vtok, B_ktok = self.B_q[z], self.B_k[z], self.B_e[z], self.B_vtok[z], self.B_ktok[z]
            psc, PBs = PSR.get()
            for h in range(4):
                P.op("pe", [B_k[h], B_q[h]], [PBs], lambda e, h=h: e.matmul(
                    psc[:, h * 128:(h + 1) * 128], kTt[h][:, sub * 128:(sub + 1) * 128],
                    qT[h][:, sub * 128:(sub + 1) * 128], start=True, stop=True), n=128)
            pus = [PSR.get(), PSR.get()]
            for h in range(4):
                for c in range(2):
                    pu, PBu = pus[c]
                    rows = slice(c * 64, (c + 1) * 64)
                    P.op("pe", [B_ktok[sub], B_vtok[sub]], [PBu], lambda e, pu=pu, h=h, rows=rows: e.matmul(
                        pu[:, h * 128:(h + 1) * 128], ktok[rows, sub, h * 128:(h + 1) * 128],
                        vtok[rows, sub, h * 128:(h + 1) * 128], start=True, stop=True), n=128)
            yield
            A, B_A = self.A.get()
            tri4 = bass.AP(cstb.tensor if hasattr(cstb, "tensor") else cstb, 256, [[1408, 128], [0, 4], [1, 128]])
            P.op("dve", [PBs, B_cstb], [B_A], lambda e: e.tensor_tensor(
                out=A[:].rearrange("p (h t) -> p h t", h=4), in0=psc[:].rearrange("p (h t) -> p h t", h=4),
                in1=tri4, op=ALU.mult))
            yield
            po, PBo = PSR.get()
            res.append((po, PBo))
            for h in range(4):
                P.op("pe", [B_A, B_vtok[sub]], [PBo], lambda e, h=h: e.matmul(
                    po[:, h * 128:(h + 1) * 128], A[:, h * 128:(h + 1) * 128], vtok[:, sub, h * 128:(h + 1) * 128],
                    start=(h == 0), stop=False, skip_group_check=True), n=128)
            for c in range(2):
                pu, PBu = pus[c]
                rows = slice(c * 64, (c + 1) * 64)
                toks = slice(sub * 128 + c * 64, sub * 128 + (c + 1) * 64)
                for h in range(4):
                    last = (c == 1 and h == 3)
                    P.op("pe", [B_q[h], self.B_Sb[h]], [PBo], lambda e, h=h, rows=rows, toks=toks, last=last: e.matmul(
                        po[rows, h * 128:(h + 1) * 128], qT[h][:, toks], self.Sb[h][:],
                        start=False, stop=last, skip_group_check=True), n=128)
                ci = sub * 2 + c
                for h in range(4):
                    e_ap = ebk[h][:, ci:ci + 1]
                    P.op("pool", [self.B_S[h], B_e[h]], [self.B_S1[h]], lambda e, h=h, e_ap=e_ap: e.tensor_scalar(
                        out=self.S1[h][:], in0=self.S[h][:], scalar1=e_ap, scalar2=0.0, op0=ALU.mult, op1=ALU.add), c=0.34)
                yield
                for h in range(4):
                    e_ap = ebk[h][:, ci:ci + 1]
                    P.op("dve", [PBu, self.B_S1[h], B_e[h]], [self.B_Sb[h]], lambda e, pu=pu, h=h, e_ap=e_ap: e.scalar_tensor_tensor(
                        out=self.Sb[h][:], in0=pu[:, h * 128:(h + 1) * 128], scalar=e_ap, in1=self.S1[h][:],
                        op0=ALU.mult, op1=ALU.add), n=128)
                    P.op("dve", [PBu, self.B_S1[h], B_e[h]], [self.B_S[h]], lambda e, pu=pu, h=h, e_ap=e_ap: e.scalar_tensor_tensor(
                        out=self.S[h][:], in0=pu[:, h * 128:(h + 1) * 128], scalar=e_ap, in1=self.S1[h][:],
                        op0=ALU.mult, op1=ALU.add), n=128)
                yield

    scan_W = {}
    sc_st = contextlib.ExitStack()
    if upto >= 2:
        for d_ in (1, 0):
            scan_W[d_] = (sb("s_W%d" % d_, [128, 8, 1536], BF16, sc_st), P.buf("s_W%d" % d_))
        with contextlib.ExitStack() as s1:
            stage = mk_stage(s1, 4)
            for d_ in (1, 0):
                W_, B_W_ = scan_W[d_]
                fcol = 1024 if d_ == 1 else 512
                load_weight(s1, W_[:, :, 0:512], B_W_, w_in[:, 0:512], 8, 512, stage=stage)
                load_weight(s1, W_[:, :, 512:1024], B_W_, w_in[:, fcol:fcol + 512], 8, 512, stage=stage)
                load_weight(s1, W_[:, :, 1024:1536], B_W_, w_in[:, 1536:2048], 8, 512, stage=stage)
            P.barrier()

    def scan_phase(d):
        with contextlib.ExitStack() as st:
            W, B_W = scan_W[d]
            fr = Front(st, 3, C_G1, C_SH1, "s%d" % d)
            fr.pool_xn = True
            sc = Scan(st, "s%d" % d)
            vm = sb("s_vm%d" % d, [128, 36], F32, st)
            B_vm = P.buf("s_vm", dma=True)
            P.dma("sp", vm[:], vmB if d == 1 else vmC, [], [B_vm], B_vm)
            hTr = [(sb("s_hT%d_%d" % (d, i), [128, 8, 512], BF16, st), (P.buf("s_hT%d" % i), P.buf("s_hTb%d" % i))) for i in range(2)]
            ost = Pools([(sb("s_ost%d_%d" % (d, i), [128, 512], F32, st), P.buf("s_ost%d" % i, dma=True)) for i in range(2)])
            top = 1024 + OWN + HH
            base = 1024 - HH
            flip = (d == 1)
            o_s = obw_s if d == 1 else ofw_s
            NT = 9

            class FrontStream:
                def __init__(self):
                    self.pend = None

                def step(self, t, s4):
                    hTt, B_hTt = hTr[t % 2]
                    i = t * 4 + s4
                    rb = (top - 128 * (i + 1)) if flip else (base + 128 * i)
                    xt, B_x = fr.xs.get()
                    P.dma("sp", xt[:], xw[rb:rb + 128, :], [], [B_x], B_x)
                    a = (fr.run_a(xt, B_x), t, s4)
                    self.flush()
                    self.pend = a

                def flush(self):
                    if self.pend is not None:
                        pa_, pt, ps4 = self.pend
                        hTt, B_hTt = hTr[pt % 2]
                        fr.run_b(pa_, hTt[:, :, ps4 * 128:(ps4 + 1) * 128], B_hTt, flip)
                        self.pend = None

            fs = FrontStream()

            def drain(*gens):
                gens = [g for g in gens if g is not None]
                while gens:
                    for g in list(gens):
                        try:
                            next(g)
                        except StopIteration:
                            gens.remove(g)

            def gen_front(t, s4):
                fs.step(t, s4)
                yield

            def gen_prepV(z, s4, hn, B_hn, col):
                yield
                yield
                sc.prepV(z, s4, hn, B_hn, W, B_W, vm, B_vm, col)
                yield

            def gen_out(res, i):
                po, PBo = res[0]
                if i >= 4:
                    ot, B_ot = ost.get()
                    P.op("act", [PBo], [B_ot], lambda e: e.activation(
                        out=ot[:], in_=po[:], func=AF.Copy, scale=float(128 ** -0.5)))
                    row = (i - 4) * 128
                    P.dma("pool", o_s[row:row + 128, :], ot[:], [B_ot], [], B_ot)

            for s4 in range(4):
                fs.step(0, s4)
            fs.flush()
            for s4 in range(4):
                drain(sc.prepH(0, s4, hTr[0][0], hTr[0][1], W, B_W, d))
                sc.prepV(0, s4, hTr[0][0], hTr[0][1], W, B_W, vm, B_vm, s4)
            for s4 in range(4):
                fs.step(1, s4)
            fs.flush()
            for t in range(NT):
                z = t % 2
                sc.prepK(z)
                for s4 in range(4):
                    res = []
                    g_f = gen_front(t + 2, s4) if t + 2 < NT else None
                    g_p = g_v = None
                    if t + 1 < NT:
                        hn, B_hn = hTr[(t + 1) % 2]
                        g_p = sc.prepH(1 - z, s4, hn, B_hn, W, B_W, d)
                        g_v = gen_prepV(1 - z, s4, hn, B_hn, (t + 1) * 4 + s4)
                    drain(g_f, g_p, sc.scan_sub(z, s4, res), g_v)
                    gen_out(res, t * 4 + s4)
                fs.flush()
            P.barrier()

    if upto >= 2:
        scan_phase(1)
    if upto >= 3:
        scan_phase(0)
    sc_st.close()

    if upto >= 3:
        with contextlib.ExitStack() as st:
            W = sb("c_W", [128, 8, 2560], BF16, st)
            Wa = sb("c_Wa", [128, 4, D], BF16, st)
            Wb = sb("c_Wb", [128, 4, D], BF16, st)
            Wo = sb("c_Wo", [128, 8, D], BF16, st)
            B_W = P.buf("c_W")
            with contextlib.ExitStack() as s1:
                stage = mk_stage(s1, 4)
                load_weight(s1, W[:, :, 0:512], B_W, w_in[:, 2048:2560], 8, 512, stage=stage)
                load_weight(s1, W[:, :, 512:2560], B_W, w_in[:, 4096:6144], 8, 2048, stage=stage)
                load_weight(s1, Wa, B_W, w_bra, 4, D, stage=stage)
                load_weight(s1, Wb, B_W, w_brb, 4, D, stage=stage)
                gate1_bc = sb("gate1_bc", [128, D], F32, s1)
                B_g1 = P.buf("g1bc", dma=True)
                P.dma("sp", gate1_bc[:], g_s[0], [], [B_g1], B_g1)
                load_weight(s1, Wo, B_W, w_out, 8, D, gate_bc=gate1_bc, B_gate=B_g1, stage=stage)
                P.barrier()
            fr = Front(st, 3, C_G1, C_SH1, "c")
            hTr = [(sb("c_hT%d" % i, [128, 8, 512], BF16, st), (P.buf("c_hT%d" % i), P.buf("c_hTb%d" % i))) for i in range(2)]
            sgT = sb("c_sgT", [128, 4, 512], BF16, st)
            B_sgT = P.buf("c_sgT")
            oaT = sb("c_oaT", [128, 4, 512], BF16, st)
            B_oaT = P.buf("c_oaT")
            obTr = Pools([(sb("c_obT%d" % i, [128, 4, 512], BF16, st), P.buf("c_obT%d" % i, dma=True)) for i in range(2)])
            mT = sb("c_mT", [128, 8, 512], BF16, st)
            B_mT = P.buf("c_mT")
            obw = Pools([(sb("c_obw%d" % i, [128, 512], F32, st), P.buf("c_obw%d" % i, dma=True)) for i in range(2)])
            ofw = Pools([(sb("c_ofw%d" % i, [128, 512], F32, st), P.buf("c_ofw%d" % i, dma=True)) for i in range(2)])
            osbr = Pools([(sb("c_osb%d" % i, [128, 512], F32, st), P.buf("c_osb%d" % i)) for i in range(2)])
            onbr = Pools([(sb("c_onb%d" % i, [128, 512], BF16, st), P.buf("c_onb%d" % i)) for i in range(2)])
            gstr = Pools([(sb("c_gst%d" % i, [128, 16], F32, st), P.buf("c_gst%d" % i)) for i in range(2)])
            gtmp = Pools([(sb("c_gt%d" % i, [128, 512], F32, st), P.buf("c_gt%d" % i)) for i in range(4)])
            x1st = Pools([(sb("c_x1%d" % i, [128, D], F32, st), P.buf("c_x1%d" % i, dma=True)) for i in range(2)])
            junk = sb("c_junk", [128, 512], BF16, st)
            B_junk = P.buf("c_junk")

            class FS:
                def __init__(self):
                    self.pend = None

                def step(self, t, s4):
                    rb = 1024 + t * 512 + s4 * 128
                    xt, B_x = fr.xs.get()
                    P.dma("sp", xt[:], xw[rb:rb + 128, :], [], [B_x], B_x)
                    a = (fr.run_a(xt, B_x), t, s4)
                    self.flush()
                    self.pend = a

                def flush(self):
                    if self.pend is not None:
                        pa_, pt, ps4 = self.pend
                        hTt, B_hTt = hTr[pt % 2]
                        fr.run_b(pa_, hTt[:, :, ps4 * 128:(ps4 + 1) * 128], B_hTt, False)
                        self.pend = None

            fs = FS()
            for s4 in range(4):
                fs.step(0, s4)
            fs.flush()
            for tile in range(8):
                hTt, B_hTt = hTr[tile % 2]
                t0 = tile * 512
                obT, B_obT = obTr.get()
                P.dma("sp", obT[:], ob_s.rearrange("(c p) t -> p c t", p=128)[:, :, t0:t0 + 512], [], [B_obT], B_obT)
                for h in range(4):
                    pg, PBg = PSR.get()
                    proj_fm(pg, PBg, W, B_W, h * 128, hTt, B_hTt)
                    g1, B1 = gtmp.get()
                    bg = cv(C_BIN + 16 + h)
                    P.op("act", [PBg, B_colv], [B1], lambda e, pg=pg, g1=g1, bg=bg: e.activation(
                        out=g1[:], in_=pg[:], func=AF.Sigmoid, bias=bg), tb='s')
                    P.op("dve", [PBg, B1, B_colv], [B_sgT], lambda e, pg=pg, g1=g1, bg=bg, h=h: e.scalar_tensor_tensor(
                        out=sgT[:, h, :], in0=pg[:], scalar=bg, in1=g1[:], op0=ALU.add, op1=ALU.mult))
                for s4 in range(4):
                    tt = t0 + s4 * 128
                    ow, B_ow = obw.get()
                    P.dma("sp", ow[:], obw_s[OWN - 128 - tt: OWN - tt, :], [], [B_ow], B_ow)
                    of, B_of = ofw.get()
                    P.dma("sp", of[:], ofw_s[tt:tt + 128, :], [], [B_of], B_of)
                    po, PBo = PSR.get()
                    P.op("pe", [B_ow, B_cst], [PBo], lambda e, po=po, ow=ow: e.matmul(po[:], J_f, ow[:], start=True, stop=True), n=2048)
                    osb, B_osb = osbr.get()
                    P.op("dve", [PBo, B_of], [B_osb], lambda e, po=po, of=of, osb=osb: e.tensor_tensor(
                        out=osb[:], in0=po[:], in1=of[:], op=ALU.add))
                    P.op("pool", [B_osb], [B_junk], lambda e, osb=osb: e.tensor_tensor(out=junk[:], in0=osb[:], in1=osb[:], op=ALU.mult))
                    gst, B_gst = gstr.get()
                    P.op("dve", [B_junk], [B_gst], lambda e, gst=gst: e.reduce_sum(
                        out=gst[:, 0:4], in_=junk[:].rearrange("p (h d) -> p h d", h=4), axis=mybir.AxisListType.X), n=512)
                    P.op("dve", [B_gst], [B_gst], lambda e, gst=gst: e.tensor_scalar(
                        out=gst[:, 4:8], in0=gst[:, 0:4], scalar1=1.0 / 128, scalar2=EPS, op0=ALU.mult, op1=ALU.add), n=4)
                    P.op("act", [B_gst], [B_gst], lambda e, gst=gst: e.activation(out=gst[:, 8:12], in_=gst[:, 4:8], func=AF.Ln), n=4, tb='e')
                    P.op("act", [B_gst], [B_gst], lambda e, gst=gst: e.activation(
                        out=gst[:, 12:16], in_=gst[:, 8:12], func=AF.Exp, scale=-0.5), n=4, tb='e')
                    onb, B_onb = onbr.get()
                    for h in range(4):
                        eng = "act" if h % 2 == 0 else "pool"
                        if eng == "act":
                            P.op("act", [B_osb, B_gst], [B_onb], lambda e, h=h, osb=osb, onb=onb, gst=gst: e.activation(
                                out=onb[:, h * 128:(h + 1) * 128], in_=osb[:, h * 128:(h + 1) * 128], func=AF.Copy,
                                scale=gst[:, 12 + h:13 + h]), n=128)
                        else:
                            P.op("pool", [B_osb, B_gst], [B_onb], lambda e, h=h, osb=osb, onb=onb, gst=gst: e.tensor_scalar(
                                out=onb[:, h * 128:(h + 1) * 128], in0=osb[:, h * 128:(h + 1) * 128],
                                scalar1=gst[:, 12 + h:13 + h], scalar2=0.0, op0=ALU.mult, op1=ALU.add), n=128)
                    ptp, PBt = PSR.get()
                    for h in range(4):
                        P.op("pe", [B_onb, B_cstb], [PBt], lambda e, ptp=ptp, h=h, onb=onb: e.matmul(
                            ptp[:, h * 128:(h + 1) * 128], onb[:, h * 128:(h + 1) * 128], ident_b, start=True, stop=True), n=128)
                    for h in range(4):
                        P.op("dve", [PBt, B_colv, B_sgT], [B_oaT], lambda e, ptp=ptp, h=h, s4=s4: e.scalar_tensor_tensor(
                            out=oaT[:, h, s4 * 128:(s4 + 1) * 128], in0=ptp[:, h * 128:(h + 1) * 128],
                            scalar=cv(C_GN + h), in1=sgT[:, h, s4 * 128:(s4 + 1) * 128], op0=ALU.mult, op1=ALU.mult), n=128)
                for cc in range(8):
                    pga, PBga = PSR.get()
                    proj_fm(pga, PBga, W, B_W, 512 + cc * 128, hTt, B_hTt)
                    pgb, PBgb = PSR.get()
                    proj_fm(pgb, PBgb, W, B_W, 1536 + cc * 128, hTt, B_hTt)
                    pa, PBa = PSR.get()
                    proj_fm(pa, PBa, Wa, B_W, cc * 128, oaT, B_oaT, nkc=4)
                    pb_, PBb = PSR.get()
                    proj_fm(pb_, PBb, Wb, B_W, cc * 128, obT, B_obT, nkc=4)
                    ga, B_ga = gtmp.get()
                    gb, B_gb = gtmp.get()
                    P.op("act", [PBga, B_colv], [B_ga], lambda e, pga=pga, ga=ga, cc=cc: e.activation(
                        out=ga[:], in_=pga[:], func=AF.Sigmoid, bias=cv(C_BIN + 32 + cc)), tb='s')
                    P.op("act", [PBgb, B_colv], [B_gb], lambda e, pgb=pgb, gb=gb, cc=cc: e.activation(
                        out=gb[:], in_=pgb[:], func=AF.Sigmoid, bias=cv(C_BIN + 40 + cc)), tb='s')
                    P.op("dve", [PBa, B_ga], [B_ga], lambda e, pa=pa, ga=ga: e.tensor_tensor(
                        out=ga[:], in0=pa[:], in1=ga[:], op=ALU.mult))
                    P.op("dve", [PBb, B_gb], [B_gb], lambda e, pb_=pb_, gb=gb: e.tensor_tensor(
                        out=gb[:], in0=pb_[:], in1=gb[:], op=ALU.mult))
                    P.op("pool", [B_ga, B_gb], [B_mT], lambda e, ga=ga, gb=gb, cc=cc: e.tensor_tensor(
                        out=mT[:, cc, :], in0=ga[:], in1=gb[:], op=ALU.add))
                    if tile + 1 < 8 and cc % 2 == 1:
                        fs.step(tile + 1, cc // 2)
                fs.flush()
                for s4 in range(4):
                    xo, B_xo = x1st.get()
                    rb = 1024 + t0 + s4 * 128
                    P.dma("sp", xo[:], xw[rb:rb + 128, :], [], [B_xo], B_xo)
                    for half in range(2):
                        px, PBx = PSR.get()
                        proj_tm(px[:], PBx, lambda kc, s4=s4: mT[:, kc, s4 * 128:(s4 + 1) * 128], B_mT, Wo, B_W, half * 512, 512)
                        P.op("dve", [PBx, B_xo], [B_xo], lambda e, px=px, xo=xo, half=half: e.tensor_tensor(
                            out=xo[:, half * 512:(half + 1) * 512], in0=px[:], in1=xo[:, half * 512:(half + 1) * 512],
                            op=ALU.add))
                    tt = t0 + s4 * 128
                    P.dma("pool", x1_s[tt:tt + 128, :], xo[:], [B_xo], [], B_xo)
            P.barrier()

    if upto >= 4:
        with contextlib.ExitStack() as st:
            Wi = sb("d_Wi", [128, 8, 2 * FFN], BF16, st)
            Wf = sb("d_Wf", [128, NJ, D], BF16, st)
            B_W = P.buf("d_W")
            with contextlib.ExitStack() as s1:
                stage = mk_stage(s1, 3)
                load_weight(s1, Wi, B_W, w_fi, 8, 2 * FFN, stage=stage)
                gate2_bc = sb("gate2_bc", [128, D], F32, s1)
                B_g2 = P.buf("g2bc", dma=True)
                P.dma("sp", gate2_bc[:], g_s[1], [], [B_g2], B_g2)
                load_weight(s1, Wf, B_W, w_fo, NJ, D, gate_bc=gate2_bc, B_gate=B_g2, stage=stage)
                P.barrier()
            fngb = sb("d_fng", [128, D], F32, st)
            B_fng = P.buf("d_fng", dma=True)
            P.dma("sp", fngb[:], fng.partition_broadcast(128), [], [B_fng], B_fng)
            fr = Front(st, 2, C_G2, C_SH2, "d", nxn=1)
            hTr = [(sb("d_hT%d" % i, [128, 8, 512], BF16, st), (P.buf("d_hT%d" % i), P.buf("d_hTb%d" % i))) for i in range(2)]
            yT = sb("d_yT", [128, NJ, 512], BF16, st)
            B_yT = P.buf("d_yT")
            gtmp = Pools([(sb("d_gt%d" % i, [128, 512], F32, st), P.buf("d_gt%d" % i)) for i in range(2)])
            res = Pools([(sb("d_res%d" % i, [128, D], F32, st), P.buf("d_res%d" % i, dma=True)) for i in range(2)])

            def d_front(tile):
                hTt, B_hTt = hTr[tile % 2]

                def d_ld(s4):
                    t0 = tile * 512 + s4 * 128
                    xt, B_x = fr.xs.get()
                    P.dma("sp", xt[:], x1_s[t0:t0 + 128, :], [], [B_x], B_x)
                    return xt, B_x
                for s4 in range(4):
                    xt, B_x = d_ld(s4)
                    fr.run(xt, B_x, hTt[:, :, s4 * 128:(s4 + 1) * 128], B_hTt, False)

            d_front(0)
            for tile in range(8):
                hTt, B_hTt = hTr[tile % 2]
                for j in range(NJ):
                    pg, PBg = PSR.get()
                    proj_fm(pg, PBg, Wi, B_W, j * 128, hTt, B_hTt)
                    pu, PBu = PSR.get()
                    proj_fm(pu, PBu, Wi, B_W, FFN + j * 128, hTt, B_hTt)
                    g1, B1 = gtmp.get()
                    P.op("act", [PBg], [B1], lambda e, pg=pg, g1=g1: e.activation(out=g1[:], in_=pg[:], func=AF.Sigmoid), tb='s')
                    P.op("dve", [PBg, B1], [B1], lambda e, pg=pg, g1=g1: e.tensor_tensor(
                        out=g1[:], in0=pg[:], in1=g1[:], op=ALU.mult))
                    P.op("dve", [PBu, B1], [B_yT], lambda e, pu=pu, g1=g1, j=j: e.tensor_tensor(
                        out=yT[:, j, :], in0=pu[:], in1=g1[:], op=ALU.mult))
                if tile + 1 < 8:
                    d_front(tile + 1)
                for s4 in range(4):
                    xo, B_xo = res.get()
                    t0 = tile * 512 + s4 * 128
                    P.dma("sp", xo[:], x1_s[t0:t0 + 128, :], [], [B_xo], B_xo)
                    for half in range(2):
                        px, PBx = PSR.get()
                        proj_tm(px[:], PBx, lambda kc, s4=s4: yT[:, kc, s4 * 128:(s4 + 1) * 128], B_yT, Wf, B_W, half * 512, 512, nkc=NJ)
                        P.op("dve", [PBx, B_xo], [B_xo], lambda e, px=px, xo=xo, half=half: e.tensor_tensor(
                            out=xo[:, half * 512:(half + 1) * 512], in0=px[:], in1=xo[:, half * 512:(half + 1) * 512],
                            op=ALU.add))
                    s4t, B_s = fr.stats(xo, B_xo)
                    P.op("act", [B_xo, B_s], [B_xo], lambda e, xo=xo, s4t=s4t: e.activation(
                        out=xo[:], in_=xo[:], func=AF.Copy, scale=s4t[:, 3:4]), n=1024)
                    P.op("pool", [B_xo, B_fng], [B_xo], lambda e, xo=xo: e.tensor_tensor(
                        out=xo[:], in0=xo[:], in1=fngb[:], op=ALU.mult), c=1.9)
                    P.dma("pool", y[t0:t0 + 128, :], xo[:], [B_xo], [], B_xo)
            P.barrier()
    P.flush()
    P.es.close()
    return nc, P


def _col(v, n):
    return np.ascontiguousarray(np.asarray(v, np.float32).reshape(n, 128).T)


def _consts():
    c = np.zeros((128, 1408), np.float32)
    i = np.arange(128)
    c[:, 0:128] = np.eye(128, dtype=np.float32)
    c[:, 128:256] = np.eye(128, dtype=np.float32)[::-1]
    s = i[:, None]
    t = i[None, :]
    c[:, 256:384] = ((s // 64 == t // 64) & (t >= s)).astype(np.float32)
    c[:, 384:512] = (t <= s).astype(np.float32)
    c[:, 512:640] = (t >= s).astype(np.float32)
    cm = np.ones(512, np.float32)
    cm[::64] = 0.0
    c[:, 640:1152] = cm[None, :]
    c[:, 1152:1280] = 1.0
    sw = np.where((i % 64) < 32, i + 32, i - 32)
    c[sw, 1280 + i] = 1.0
    return c


def _core_geom(c):
    if c < 4:
        return 0, c // 2, (c % 2) * OWN, 8192
    return 1, 0, (c - 4) * OWN, 16384


_NC_CACHE = {}


def kernel(x_prompt, x_sample, c_prompt, c_sample, w_ada, b_ada, norm1_g, w_in, b_in, lb_logits,
           hg_norm_g, w_branch_a, w_branch_b, w_out, norm2_g, w_ffn_in, w_ffn_out, final_norm_g,
           _debug=False):
    f32 = lambda a: np.ascontiguousarray(np.asarray(a, dtype=np.float32))
    x_prompt, x_sample = f32(x_prompt), f32(x_sample)
    c_prompt, c_sample = f32(c_prompt), f32(c_sample)
    w_in0 = f32(w_in)[0]
    b_in0 = f32(b_in)[0]
    perm = np.arange(1024).reshape(2, 8, 2, 32)[:, :, ::-1, :].reshape(-1)
    w_qksw = np.ascontiguousarray(w_in0[:, 2560:3584][:, perm])
    b_sw = b_in0[2560:3584][perm]
    lbl = np.concatenate([_col(f32(lb_logits)[l, d], 4) for l in range(2) for d in range(2)], axis=1)
    shared = {
        "w_ada": f32(w_ada)[0], "b_ada": f32(b_ada)[0], "n1g": _col(f32(norm1_g)[0], 8),
        "n2g": _col(f32(norm2_g)[0], 8), "w_in": w_in0, "w_qksw": w_qksw, "bin_col": _col(b_in0, 48),
        "bsw_col": _col(b_sw, 8), "b_in_row": b_in0, "lbl": np.ascontiguousarray(lbl),
        "gng": _col(f32(hg_norm_g)[0], 4), "w_bra": f32(w_branch_a)[0], "w_brb": f32(w_branch_b)[0],
        "w_out": f32(w_out)[0], "w_fi": f32(w_ffn_in)[0], "w_fo": f32(w_ffn_out)[0],
        "fng": f32(final_norm_g), "consts": _consts(),
    }
    half = 32
    inv = (np.float32(ROPE_THETA) ** (-np.arange(half, dtype=np.float32) / np.float32(half))).astype(np.float32)
    pidx = np.arange(128)
    fidx = pidx % 32
    sgn = np.where((pidx % 64) < 32, -1.0, 1.0).astype(np.float32)
    vts = att_vtiles()
    in_maps = []
    for c in range(NCORES):
        grp, b, own0, L = _core_geom(c)
        xs = x_prompt[b] if grp == 0 else x_sample[0]
        cvec = c_prompt[b] if grp == 0 else c_sample[0]
        w0 = own0 - HALO
        xw = np.zeros((WIN, D), np.float32)
        lo, hi = max(0, w0), min(L, w0 + WIN)
        xw[lo - w0:hi - w0] = xs[lo:hi]
        pos = (w0 + np.arange(WIN)).astype(np.float32)
        ang = (pos[None, :] * inv[fidx][:, None]).astype(np.float32)
        cosT = np.cos(ang).astype(np.float32)
        sinT = (np.sin(ang).astype(np.float32) * sgn[:, None]).astype(np.float32)
        valid = ((w0 + np.arange(WIN) >= 0) & (w0 + np.arange(WIN) < L)).astype(np.float32)
        vmA = np.zeros((128, 64), np.float32)
        for hf in range(2):
            for i in range(32):
                vmA[:, hf * 32 + i] = valid[2048 * hf + 128 * i + np.arange(128)]
        vmB = np.zeros((128, 36), np.float32)
        top = 1024 + OWN + HH
        for i in range(36):
            rb = top - 128 * (i + 1)
            vmB[:, i] = valid[rb + 127 - np.arange(128)]
        vmC = np.zeros((128, 36), np.float32)
        for i in range(36):
            rb = (1024 - HH) + 128 * i
            vmC[:, i] = valid[rb + np.arange(128)]
        m = dict(shared)
        m.update({"xw": xw, "ccol": _col(cvec, 8), "cosT": cosT, "sinT": sinT, "vmA": vmA, "vmB": vmB, "vmC": vmC})
        in_maps.append(m)
    import os as _os
    upto = int(_os.environ.get("K_UPTO", "4")) if _debug else 4
    key = (bool(_debug), upto)
    if key not in _NC_CACHE:
        _NC_CACHE[key] = build_program(debug=key[0], upto=upto)[0]
    nc = _NC_CACHE[key]
    res = run_bass_kernel_spmd(nc, in_maps, core_ids=list(range(NCORES)))
    outs = res.results
    y_prompt = np.zeros((2, 8192, D), np.float32)
    y_sample = np.zeros((1, 16384, D), np.float32)
    for c in range(NCORES):
        grp, b, own0, L = _core_geom(c)
        yc = np.asarray(outs[c]["y"], np.float32)
        if grp == 0:
            y_prompt[b, own0:own0 + OWN] = yc
        else:
            y_sample[0, own0:own0 + OWN] = yc
    if _debug:
        return (y_prompt, y_sample), outs
    return (y_prompt, y_sample)
```

```python
import contextlib
import numpy as np
import concourse.bass as bass
import concourse.mybir as mybir
from concourse.alu_op_type import AluOpType as ALU
from concourse.bass_utils import run_bass_kernel_spmd

F32 = mybir.dt.float32
BF16 = mybir.dt.bfloat16
AF = mybir.ActivationFunctionType

D = 1024
NCORES = 8
OWN = 4096
HALO = 1024
WIN = OWN + 2 * HALO
HH = 512
FFN = 2816
NJ = FFN // 128
EPS = 1e-6
ROPE_THETA = 10000.0

PATTERNS = (1, 4, 16)


def att_vtiles():
    out = []
    for r in PATTERNS:
        jq0 = 1024 // r
        nb = (2048 // r) // 128
        for rho in range(r):
            for m in range(nb + 1):
                start = rho + r * (jq0 - 64 + 128 * m)
                out.append((r, rho, m, start, nb))
    return out


class Buf:
    __slots__ = ("name", "w", "r", "dsem", "ssem")

    def __init__(self, name):
        self.name = name
        self.w = {}
        self.r = {}
        self.dsem = None
        self.ssem = None


class Prog:
    ENG = ("pe", "act", "dve", "pool", "sp")

    def __init__(self, nc):
        self.nc = nc
        self.es = contextlib.ExitStack()
        self.eng = {"pe": nc.tensor, "act": nc.scalar, "dve": nc.vector, "pool": nc.gpsimd, "sp": nc.sync}
        self.sems = {}
        self.cnt = {}
        for e in self.ENG:
            self.sems[e] = self.es.enter_context(nc.semaphore("sem_" + e))
            self.cnt[e] = 0
        self.ndma = 12
        for i in range(self.ndma):
            k = "d%d" % i
            self.sems[k] = self.es.enter_context(nc.semaphore("sem_" + k))
            self.cnt[k] = 0
        self.nsw = 6
        for i in range(self.nsw):
            k = "w%d" % i
            self.sems[k] = self.es.enter_context(nc.semaphore("sem_" + k))
            self.cnt[k] = 0
        self.dma_next = 0
        self.sw_next = 0
        self.seen = {e: {} for e in self.ENG}
        self.nins = 0
        self.defer = True
        self.q = []
        self.window = 120
        self.act_tbl = None
        self.est_time = 0.0

    def buf(self, name, dma=False):
        b = Buf(name)
        if dma:
            assert self.dma_next < self.ndma, "out of dma semaphores"
            b.dsem = "d%d" % self.dma_next
            self.dma_next += 1
        return b

    def _deps(self, e, reads, writes):
        deps = {}
        for b in reads:
            for k, v in b.w.items():
                if deps.get(k, 0) < v:
                    deps[k] = v
        for b in writes:
            for k, v in b.w.items():
                if deps.get(k, 0) < v:
                    deps[k] = v
            for k, v in b.r.items():
                if deps.get(k, 0) < v:
                    deps[k] = v
        seen = self.seen[e]
        for k, v in deps.items():
            if k == e and e == "pe":
                continue
            if (k[0] == "d" and k != "dve") or k[0] == "w":
                v = self.cnt[k]
            if seen.get(k, 0) < v:
                self.eng[e].wait_ge(self.sems[k], v)
                seen[k] = v

    DEFAULT_COST = {"pe": 0.22, "act": 0.55, "dve": 0.55, "pool": 1.2, "sp": 0.15}

    @staticmethod
    def _flat(bl):
        out = []
        for b in bl:
            if isinstance(b, (tuple, list)):
                out.extend(b)
            else:
                out.append(b)
        return out

    def op(self, e, reads, writes, ins_fn, c=None, n=512, tb=None):
        reads, writes = self._flat(reads), self._flat(writes)
        if self.defer:
            if c is None:
                if e == "pe":
                    c = 0.07 + n * 0.0004
                elif e == "act":
                    c = 0.22 + n / 1400.0
                elif e == "dve":
                    c = 0.12 + n * 0.0011
                else:
                    c = 0.15 + n * 0.0022
            self.q.append((0, e, list(reads), list(writes), ins_fn, c, tb))
            return
        self._op_now(e, reads, writes, ins_fn)

    def dma(self, q, out, in_, reads, writes, semb, c=4.0):
        reads, writes = self._flat(reads), self._flat(writes)
        if self.defer:
            self.q.append((1, q, list(reads), list(writes), (out, in_, semb), c))
            return
        self._dma_now(q, out, in_, reads, writes, semb)

    def semkey(self, q, semb):
        if q == "sp":
            assert semb.dsem is not None
            return semb.dsem
        if semb.ssem is None:
            assert self.sw_next < self.nsw, "out of sw dma semaphores"
            semb.ssem = "w%d" % self.sw_next
            self.sw_next += 1
        return semb.ssem

    def _op_now(self, e, reads, writes, ins_fn):
        self._deps(e, reads, writes)
        ins = ins_fn(self.eng[e])
        self.cnt[e] += 1
        ins.then_inc(self.sems[e], 1)
        ev = self.cnt[e]
        for b in reads:
            b.r[e] = ev
        for b in writes:
            b.w = {e: ev}
            b.r = {}
        self.nins += 1

    def _dma_now(self, q, out, in_, reads, writes, semb):
        self._deps(q, reads, writes)
        k = self.semkey(q, semb)
        self.cnt[k] += 16
        self.eng[q].dma_start(out=out, in_=in_).then_inc(self.sems[k], 16)
        ev = self.cnt[k]
        for b in reads:
            b.r[k] = ev
        for b in writes:
            b.w = {k: ev}
            b.r = {}
        self.nins += 1

    def flush(self):
        ops = self.q
        self.q = []
        n = len(ops)
        if n == 0:
            return
        last_w = {}
        readers = {}
        deps = [None] * n
        for i, o in enumerate(ops):
            d = set()
            reads, writes = o[2], o[3]
            if o[0] == 1:
                kk_ = ("sem", self.semkey(o[1], o[4][2]))
                reads = reads + [kk_]
                writes = writes + [kk_]
            for b in reads:
                w = last_w.get(id(b) if not isinstance(b, tuple) else b)
                if w is not None:
                    d.add(w)
            for b in writes:
                kb = id(b) if not isinstance(b, tuple) else b
                w = last_w.get(kb)
                if w is not None:
                    d.add(w)
                for r in readers.get(kb, ()):
                    d.add(r)
            for b in reads:
                kb = id(b) if not isinstance(b, tuple) else b
                readers.setdefault(kb, []).append(i)
            for b in writes:
                kb = id(b) if not isinstance(b, tuple) else b
                last_w[kb] = i
                readers[kb] = []
            d.discard(i)
            deps[i] = d
        queues = {}
        for i, o in enumerate(ops):
            queues.setdefault(o[1], []).append(i)
        heads = {e: 0 for e in queues}
        done = [False] * n
        finish = [0.0] * n
        eng_free = {e: 0.0 for e in queues}
        W = self.window
        LAT = 0.08
        TBL = 2.0
        nsched = 0
        while nsched < n:
            best = None
            for e, ql in queues.items():
                h = heads[e]
                while h < len(ql) and done[ql[h]]:
                    h += 1
                heads[e] = h
                cnt = 0
                j = h
                ef = eng_free[e]
                while j < len(ql) and cnt < W:
                    i = ql[j]
                    j += 1
                    if done[i]:
                        continue
                    cnt += 1
                    rt = 0.0
                    ok = True
                    for dd in deps[i]:
                        if not done[dd]:
                            ok = False
                            break
                        f = finish[dd]
                        if f > rt:
                            rt = f
                    if not ok:
                        continue
                    st = rt + LAT if rt + LAT > ef else ef
                    if e == "act":
                        tb_ = ops[i][6] if ops[i][0] == 0 else None
                        if tb_ is not None and tb_ != self.act_tbl:
                            st += TBL
                    if best is None or st < best[0] or (st == best[0] and i < best[1]):
                        best = (st, i, e)
                    if st <= ef:
                        break
            st, i, e = best
            o = ops[i]
            if o[0] == 1:
                eng_free[e] = st + 0.15
                finish[i] = st + o[5]
            else:
                eng_free[e] = st + o[5]
                finish[i] = st + o[5]
                if e == "act" and o[6] is not None:
                    self.act_tbl = o[6]
            done[i] = True
            nsched += 1
            if o[0] == 1:
                self._dma_now(o[1], o[4][0], o[4][1], o[2], o[3], o[4][2])
            else:
                self._op_now(o[1], o[2], o[3], o[4])
        self.est_time += max(finish) if n else 0.0

    def barrier(self):
        self.flush()
        for e in self.ENG:
            seen = self.seen[e]
            for k, v in self.cnt.items():
                if v > 0 and seen.get(k, 0) < v:
                    self.eng[e].wait_ge(self.sems[k], v)
                    seen[k] = v
        self.dma_next = 0
        self.sw_next = 0


class Pools:
    def __init__(self, items):
        self.items = items
        self.i = 0

    def get(self):
        it = self.items[self.i % len(self.items)]
        self.i += 1
        return it


def build_program(debug=False, upto=4):
    nc = bass.Bass("TRN2", target_bir_lowering=False)
    P = Prog(nc)
    es = P.es

    def din(name, shape, dt=F32):
        return nc.dram_tensor(name, list(shape), dt, kind="ExternalInput").ap()

    xw = din("xw", [WIN, D])
    ccol = din("ccol", [128, 8])
    w_ada = din("w_ada", [D, 6 * D])
    b_ada = din("b_ada", [6 * D])
    n1g = din("n1g", [128, 8])
    n2g = din("n2g", [128, 8])
    w_in = din("w_in", [D, 6144])
    w_qksw = din("w_qksw", [D, 1024])
    bin_col = din("bin_col", [128, 48])
    bsw_col = din("bsw_col", [128, 8])
    b_in_row = din("b_in_row", [6144])
    lbl = din("lbl", [128, 16])
    gng = din("gng", [128, 4])
    w_bra = din("w_bra", [512, D])
    w_brb = din("w_brb", [512, D])
    w_out = din("w_out", [D, D])
    w_fi = din("w_fi", [D, 2 * FFN])
    w_fo = din("w_fo", [FFN, D])
    fng = din("fng", [D])
    cosT = din("cosT", [128, WIN])
    sinT = din("sinT", [128, WIN])
    vmA = din("vmA", [128, 64])
    vmB = din("vmB", [128, 36])
    vmC = din("vmC", [128, 36])
    consts = din("consts", [128, 1408])

    okind = "ExternalOutput"
    y = nc.dram_tensor("y", [OWN, D], F32, kind=okind).ap()
    skind = okind if debug else "Internal"
    ob_s = nc.dram_tensor("ob_s", [512, OWN], BF16, kind=skind).ap()
    obw_s = nc.dram_tensor("obw_s", [OWN, 512], F32, kind=skind).ap()
    ofw_s = nc.dram_tensor("ofw_s", [OWN, 512], F32, kind=skind).ap()
    v_s = nc.dram_tensor("v_s", [4096, 8, 65], BF16, kind="Internal").ap()
    B_vs = P.buf("v_s")
    x1_s = nc.dram_tensor("x1_s", [OWN, D], F32, kind=skind).ap()

    uid = [0]

    def sb(name, shape, dt, stack=None):
        uid[0] += 1
        return (stack or es).enter_context(nc.sbuf_tensor("%s_u%d" % (name, uid[0]), list(shape), dt))

    def ps(name, stack=None):
        return (stack or es).enter_context(nc.psum_tensor(name, [128, 512], F32))

    cst = sb("cst", [128, 1408], F32)
    cstb = sb("cstb", [128, 1408], BF16)
    B_cst = P.buf("cst", dma=True)
    B_cstb = P.buf("cstb")
    ident_f = cst[:, 0:128]
    J_f = cst[:, 128:256]
    ident_b = cstb[:, 0:128]
    J_b = cstb[:, 128:256]
    tri_b = cstb[:, 256:384]
    mA_b = cstb[:, 384:512]
    mB_b = cstb[:, 512:640]
    cmask_f = cst[:, 640:1152]
    Rsw_b = cstb[:, 1280:1408]

    colv = sb("colv", [128, 160], F32)
    B_colv = P.buf("colv")
    C_G1, C_SH1, C_G2, C_SH2 = 0, 8, 16, 24
    C_LB, C_OML = 32, 40
    C_GN = 48
    C_BIN = 52
    C_BSW = 100
    C_TMP = 108
    g_s = nc.dram_tensor("g_s", [2, 128, D], F32, kind="Internal").ap()

    psb = [ps("psb%d" % i) for i in range(8)]
    PSR = Pools([(psb[i], P.buf("ps%d" % i)) for i in range(8)])

    def cv(c, n=1):
        return colv[:, c:c + n]

    def load_weight(st, dst, B_dst, src_rows_ap, nkc, ncols, gate_bc=None, B_gate=None, stage=None):
        v = src_rows_ap.rearrange("(kc p) n -> p kc n", p=128)
        cw = max(64, min(ncols, stage.cols // nkc))
        i = 0
        for c0 in range(0, ncols, cw):
            c1 = min(ncols, c0 + cw)
            t, B = stage.get()
            P.dma("sp", t[:, 0:nkc * (c1 - c0)].rearrange("p (k n) -> p k n", k=nkc), v[:, :, c0:c1], [], [B], B)
            tv = t[:, 0:nkc * (c1 - c0)].rearrange("p (k n) -> p k n", k=nkc)
            if gate_bc is None:
                h0 = nkc // 2
                for (eng, ka, kb) in (("pool", 0, max(1, nkc // 4)), ("dve", max(1, nkc // 4), max(2, (5 * nkc) // 8)),
                                      ("act", max(2, (5 * nkc) // 8), nkc)):
                    if kb <= ka:
                        continue
                    if eng == "act":
                        P.op("act", [B], [P.buf('wcast')], lambda e, tv=tv, c0=c0, c1=c1, ka=ka, kb=kb: e.activation(
                            out=dst[:, ka:kb, c0:c1], in_=tv[:, ka:kb, :], func=AF.Copy))
                    else:
                        P.op(eng, [B], [P.buf('wcast')], lambda e, tv=tv, c0=c0, c1=c1, ka=ka, kb=kb: e.tensor_copy(
                            out=dst[:, ka:kb, c0:c1], in_=tv[:, ka:kb, :]))
            else:
                for kc in range(nkc):
                    eng = "dve" if kc % 3 != 2 else "pool"
                    P.op(eng, [B, B_gate], [P.buf('wcast')], lambda e, tv=tv, c0=c0, c1=c1, kc=kc: e.tensor_tensor(
                        out=dst[:, kc, c0:c1], in0=tv[:, kc, :], in1=gate_bc[:, c0:c1], op=ALU.mult))
            i += 1

    def mk_stage(st, n=2, cols=4096):
        pl = Pools([(sb("wstg%d" % i, [128, cols], F32, st), P.buf("wstg%d" % i, dma=True)) for i in range(n)])
        pl.cols = cols
        return pl

    pa_st = contextlib.ExitStack()
    if upto >= 1:
        Wq = sb("a_Wq", [128, 8, 512], BF16, pa_st)
        Wk = sb("a_Wk", [128, 8, 512], BF16, pa_st)
        Wv = sb("a_Wv", [128, 8, 512], BF16, pa_st)
        B_W = P.buf("a_W")

    with contextlib.ExitStack() as st:
        small = sb("p0small", [128, 64], F32, st)
        B_small = P.buf("p0small", dma=True)
        P.dma("sp", cst[:], consts, [], [B_cst], B_cst)
        P.op("dve", [B_cst], [B_cstb], lambda e: e.tensor_copy(out=cstb[:], in_=cst[:]))
        P.dma("sp", small[:, 0:8], ccol, [], [B_small], B_small)
        P.dma("sp", small[:, 24:40], lbl, [], [B_small], B_small)
        P.dma("sp", colv[:, C_GN:C_GN + 4], gng, [], [B_colv], B_small)
        P.dma("sp", colv[:, C_BIN:C_BIN + 48], bin_col, [], [B_colv], B_small)
        P.dma("sp", colv[:, C_BSW:C_BSW + 8], bsw_col, [], [B_colv], B_small)
        P.op("dve", [B_small], [B_small], lambda e: e.tensor_tensor(
            out=small[:, 40:48], in0=small[:, 24:32], in1=small[:, 32:40], op=ALU.subtract))
        P.op("act", [B_small], [B_colv], lambda e: e.activation(
            out=cv(C_LB, 8), in_=small[:, 40:48], func=AF.Sigmoid), tb='s')
        P.op("dve", [B_colv], [B_colv], lambda e: e.tensor_scalar(
            out=cv(C_OML, 8), in0=cv(C_LB, 8), scalar1=-1.0, scalar2=1.0, op0=ALU.mult, op1=ALU.add))
        scb = sb("scb", [128, 8, 128], F32, st)
        B_scb = P.buf("scb")
        P.op("act", [B_small], [B_small], lambda e: e.activation(
            out=small[:, 48:56], in_=small[:, 0:8], func=AF.Silu), tb='s')
        for kc in range(8):
            src = bass.AP(small.tensor if hasattr(small, "tensor") else small, 48 + kc, [[64, 128], [0, 128]])
            P.op("dve", [B_small], [B_scb], lambda e, kc=kc, src=src: e.tensor_copy(out=scb[:, kc, :], in_=src))
        badab = sb("badab", [128, 6 * D], F32, st)
        B_badab = P.buf("badab", dma=True)
        P.dma("sp", badab[:], b_ada.partition_broadcast(128), [], [B_badab], B_badab)
        modbc = sb("modbc", [128, 6 * D], F32, st)
        B_mod = P.buf("modbc")
        stg = [(sb("p0stg%d" % i, [128, 8, 512], F32, st), P.buf("p0stg%d" % i, dma=True)) for i in range(2)]
        wa_v = w_ada.rearrange("(kc p) n -> p kc n", p=128)
        for blk in range(12):
            t, B = stg[blk % 2]
            P.dma("sp", t[:], wa_v[:, :, blk * 512:(blk + 1) * 512], [], [B], B)
            pt, PB_ = PSR.get()
            for kc in range(8):
                P.op("pe", [B, B_scb], [PB_], lambda e, kc=kc, t=t, pt=pt: e.matmul(
                    pt[:], scb[:, kc, :], t[:, kc, :], start=(kc == 0), stop=(kc == 7)))
            P.op("dve", [PB_, B_badab], [B_mod], lambda e, blk=blk, pt=pt: e.tensor_tensor(
                out=modbc[:, blk * 512:(blk + 1) * 512], in0=pt[:], in1=badab[:, blk * 512:(blk + 1) * 512],
                op=ALU.add))
        pt, PB_ = PSR.get()
        for vi, base in enumerate((0, 1024, 3072, 4096)):
            for kc in range(8):
                P.op("pe", [B_mod, B_cst], [PB_], lambda e, vi=vi, base=base, kc=kc, pt=pt: e.matmul(
                    pt[:, vi * 8 + kc: vi * 8 + kc + 1], modbc[:, base + kc * 128: base + (kc + 1) * 128],
                    ident_f[:, 0:1], start=True, stop=True))
        P.op("dve", [PB_], [B_small], lambda e, pt=pt: e.tensor_copy(out=small[:, 0:32], in_=pt[:, 0:32]))
        sm2 = sb("p0sm2", [128, 16], F32, st)
        B_sm2 = P.buf("p0sm2", dma=True)
        P.dma("sp", sm2[:, 0:8], n1g, [], [B_sm2], B_sm2)
        P.dma("sp", sm2[:, 8:16], n2g, [], [B_sm2], B_sm2)
        P.op("dve", [B_small, B_sm2], [B_colv], lambda e: e.scalar_tensor_tensor(
            out=cv(C_G1, 8), in0=small[:, 8:16], scalar=1.0, in1=sm2[:, 0:8], op0=ALU.add, op1=ALU.mult))
        P.op("dve", [B_small, B_sm2], [B_colv], lambda e: e.scalar_tensor_tensor(
            out=cv(C_G2, 8), in0=small[:, 24:32], scalar=1.0, in1=sm2[:, 8:16], op0=ALU.add, op1=ALU.mult))
        P.op("dve", [B_small], [B_colv], lambda e: e.tensor_copy(out=cv(C_SH1, 8), in_=small[:, 0:8]))
        P.op("dve", [B_small], [B_colv], lambda e: e.tensor_copy(out=cv(C_SH2, 8), in_=small[:, 16:24]))
        B_gs = P.buf("p0gs", dma=True)
        P.dma("pool", g_s[0], modbc[:, 2048:3072], [B_mod], [], B_gs)
        P.dma("pool", g_s[1], modbc[:, 5120:6144], [B_mod], [], B_gs)
        if upto >= 1:
            stage = mk_stage(st, 3, 4096)
            load_weight(st, Wq, B_W, w_in[:, 2560:3072], 8, 512, stage=stage)
            load_weight(st, Wk, B_W, w_in[:, 3072:3584], 8, 512, stage=stage)
            load_weight(st, Wv, B_W, w_in[:, 3584:4096], 8, 512, stage=stage)
        P.barrier()

    class Front:
        def __init__(self, st, nxs, cG, cSH, tag, nxn=2):
            self.xs = Pools([(sb("xs%s%d" % (tag, i), [128, D], F32, st), P.buf("xs%s%d" % (tag, i), dma=True))
                             for i in range(nxs)])
            self.xn = Pools([(sb("xn%s%d" % (tag, i), [128, D], BF16, st), P.buf("xn%s%d" % (tag, i)))
                             for i in range(nxn)])
            self.junk = sb("junk" + tag, [128, D], BF16, st)
            self.B_junk = P.buf("junk" + tag)
            self.st4 = Pools([(sb("fst%s%d" % (tag, i), [128, 4], F32, st), P.buf("fst%s%d" % (tag, i)))
                              for i in range(2)])
            self.cG, self.cSH = cG, cSH
            self.pool_xn = False

        def stats(self, xt, B_x):
            s4, B_s = self.st4.get()
            junk, B_junk = self.junk, self.B_junk
            P.op("pool", [B_x], [B_junk], lambda e: e.tensor_tensor(out=junk[:], in0=xt[:], in1=xt[:], op=ALU.mult), c=1.9)
            P.op("dve", [B_junk], [B_s], lambda e: e.reduce_sum(out=s4[:, 0:1], in_=junk[:], axis=mybir.AxisListType.X), n=1024)
            P.op("dve", [B_s], [B_s], lambda e: e.tensor_scalar(
                out=s4[:, 1:2], in0=s4[:, 0:1], scalar1=1.0 / D, scalar2=EPS, op0=ALU.mult, op1=ALU.add), n=1)
            P.op("act", [B_s], [B_s], lambda e: e.activation(out=s4[:, 2:3], in_=s4[:, 1:2], func=AF.Ln), n=1, tb='e')
            P.op("act", [B_s], [B_s], lambda e: e.activation(out=s4[:, 3:4], in_=s4[:, 2:3], func=AF.Exp, scale=-0.5), n=1, tb='e')
            return s4, B_s

        def run(self, xt, B_x, hT_view, B_hT, flip):
            self.run_b(self.run_a(xt, B_x), hT_view, B_hT, flip)

        def run_a(self, xt, B_x):
            s4, B_s = self.stats(xt, B_x)
            xn, B_xn = self.xn.get()
            if self.pool_xn:
                P.op("pool", [B_x, B_s], [B_xn], lambda e: e.tensor_scalar(
                    out=xn[:], in0=xt[:], scalar1=s4[:, 3:4], scalar2=0.0, op0=ALU.mult, op1=ALU.add), c=1.1)
            else:
                P.op("act", [B_x, B_s], [B_xn], lambda e: e.activation(
                    out=xn[:], in_=xt[:], func=AF.Copy, scale=s4[:, 3:4]), n=1024)
            return xn, B_xn

        def run_b(self, a, hT_view, B_hT, flip):
            xn, B_xn = a
            mat = J_b if flip else ident_b
            for half in range(2):
                pt, PB_ = PSR.get()
                for k4 in range(4):
                    kc = half * 4 + k4
                    P.op("pe", [B_xn, B_cstb], [PB_], lambda e, kc=kc, k4=k4, pt=pt: e.matmul(
                        pt[:, k4 * 128:(k4 + 1) * 128], xn[:, kc * 128:(kc + 1) * 128], mat, start=True, stop=True), n=128)
                for k4 in range(4):
                    kc = half * 4 + k4
                    if half == 0:
                        P.op("dve", [PB_, B_colv], [B_hT[0]], lambda e, kc=kc, k4=k4, pt=pt: e.tensor_scalar(
                            out=hT_view[:, kc, :], in0=pt[:, k4 * 128:(k4 + 1) * 128],
                            scalar1=cv(self.cG + kc), scalar2=cv(self.cSH + kc), op0=ALU.mult, op1=ALU.add), n=128)
                    else:
                        P.op("act", [PB_, B_colv], [B_hT[1]], lambda e, kc=kc, k4=k4, pt=pt: e.activation(
                            out=hT_view[:, kc, :], in_=pt[:, k4 * 128:(k4 + 1) * 128], func=AF.Identity,
                            scale=cv(self.cG + kc), bias=cv(self.cSH + kc)), n=128)

    def front_tile(fr, loader, hTt, B_hTt, flip):
        pend = None
        for s4 in range(5):
            a = None
            if s4 < 4:
                xt, B_x = loader(s4)
                a = (fr.run_a(xt, B_x), s4)
            if pend is not None:
                pa_, ps_ = pend
                fr.run_b(pa_, hTt[:, :, ps_ * 128:(ps_ + 1) * 128], B_hTt, flip)
            pend = a

    def proj_fm(pt, PB_, W, B_W, col0, hT, B_hT, nkc=8, ncol=128):
        for kc in range(nkc):
            P.op("pe", [B_W, B_hT], [PB_], lambda e, kc=kc: e.matmul(
                pt[0:ncol, 0:hT.shape[2]], W[:, kc, col0:col0 + ncol], hT[:, kc, :], start=(kc == 0), stop=(kc == nkc - 1)))

    def proj_tm(pt_view, PB_, lhs_fn, B_h, W, B_W, col0, ncol, nkc=8):
        for kc in range(nkc):
            P.op("pe", [B_W, B_h], [PB_], lambda e, kc=kc: e.matmul(
                pt_view, lhs_fn(kc), W[:, kc, col0:col0 + ncol], start=(kc == 0), stop=(kc == nkc - 1)))

    NVT = len(att_vtiles())

    for hf in (range(2) if upto >= 1 else []):
        with contextlib.ExitStack() as st:
            kT = sb("a_kT", [128, 4, 4096], BF16, st)
            B_kT = P.buf("a_kT")
            qT = sb("a_qT", [128, 4, 2048], BF16, st)
            B_qT = P.buf("a_qT")
            r0 = 2048 * hf
            with contextlib.ExitStack() as s1:
                fr = Front(s1, 3, C_G1, C_SH1, "a")
                bvb = sb("a_bvb", [128, 512], F32, s1)
                B_bvb = P.buf("a_bvb", dma=True)
                P.dma("sp", bvb[:], b_in_row[3584:4096].partition_broadcast(128), [], [B_bvb], B_bvb)
                vm = sb("a_vm", [128, 32], F32, s1)
                B_vm = P.buf("a_vm", dma=True)
                P.dma("sp", vm[:], vmA[:, hf * 32:(hf + 1) * 32], [], [B_vm], B_vm)
                hTr = [(sb("a_hT%d" % i, [128, 8, 512], BF16, s1), (P.buf("a_hT%d" % i), P.buf("a_hTb%d" % i))) for i in range(2)]
                csr = Pools([(sb("a_cs%d" % i, [128, 2, 512], F32, s1), P.buf("a_cs%d" % i, dma=True)) for i in range(2)])
                tmp = Pools([(sb("a_tmp%d" % i, [128, 512], F32, s1), P.buf("a_tmp%d" % i)) for i in range(6)])
                vrow = Pools([(sb("a_vrow%d" % i, [128, 8, 65], BF16, s1), P.buf("a_vrow%d" % i, dma=True)) for i in range(3)])
                kbp = Pools([(sb("a_kb%d" % i, [128, 512], BF16, s1), P.buf("a_kb%d" % i)) for i in range(4)])
                for tile in range(8):
                    hTt, B_hTt = hTr[tile % 2]

                    def a_ld(s4, tile=tile):
                        xt, B_x = fr.xs.get()
                        rb = r0 + tile * 512 + s4 * 128
                        P.dma("sp", xt[:], xw[rb:rb + 128, :], [], [B_x], B_x)
                        return xt, B_x
                    front_tile(fr, a_ld, hTt, B_hTt, False)
                    cst_, B_cs = csr.get()
                    P.dma("sp", cst_[:, 0, :], cosT[:, r0 + tile * 512: r0 + (tile + 1) * 512], [], [B_cs], B_cs)
                    P.dma("sp", cst_[:, 1, :], sinT[:, r0 + tile * 512: r0 + (tile + 1) * 512], [], [B_cs], B_cs)
                    cs_v, sn_v = cst_[:, 0, :], cst_[:, 1, :]
                    jobs = [(Wk, kT, B_kT, tile * 512, 4, False)]
                    if 2 <= tile < 6:
                        jobs.append((Wq, qT, B_qT, (tile - 2) * 512, 0, True))
                    for (W1, dst, B_dst, d0, bofs, isq) in jobs:
                        for cc in range(4):
                            p1, PB1 = PSR.get()
                            proj_fm(p1, PB1, W1, B_W, cc * 128, hTt, B_hTt)
                            kb, B_kb = kbp.get()
                            bc1 = cv(C_BIN + 20 + bofs + cc)
                            P.op("act", [PB1, B_colv], [B_kb], lambda e, p1=p1, kb=kb, bc1=bc1: e.activation(
                                out=kb[:], in_=p1[:], func=AF.Identity, bias=bc1))
                            p2, PB2 = PSR.get()
                            P.op("pe", [B_kb, B_cstb], [PB2], lambda e, p2=p2, kb=kb: e.matmul(
                                p2[:], Rsw_b, kb[:], start=True, stop=True))
                            t1, B1 = tmp.get()
                            t2, B2 = tmp.get()
                            P.op("dve", [B_kb, B_cs], [B1], lambda e, kb=kb, t1=t1, cs_v=cs_v: e.tensor_tensor(
                                out=t1[:], in0=kb[:], in1=cs_v, op=ALU.mult))
                            P.op("dve", [PB2, B_cs], [B2], lambda e, p2=p2, t2=t2, sn_v=sn_v: e.tensor_tensor(
                                out=t2[:], in0=p2[:], in1=sn_v, op=ALU.mult))
                            if isq:
                                P.op("pool", [B1, B2], [B1], lambda e, t1=t1, t2=t2: e.tensor_tensor(
                                    out=t1[:], in0=t1[:], in1=t2[:], op=ALU.add))
                                P.op("act", [B1], [B_dst], lambda e, t1=t1, dst=dst, cc=cc, d0=d0: e.activation(
                                    out=dst[:, cc, d0:d0 + 512], in_=t1[:], func=AF.Copy, scale=0.125))
                            else:
                                P.op("pool", [B1, B2], [B_dst], lambda e, t1=t1, t2=t2, dst=dst, cc=cc, d0=d0: e.tensor_tensor(
                                    out=dst[:, cc, d0:d0 + 512], in0=t1[:], in1=t2[:], op=ALU.add))
                    for s4 in range(4):
                        pv, PBv = PSR.get()
                        proj_tm(pv[:], PBv, lambda kc, s4=s4, hTt=hTt: hTt[:, kc, s4 * 128:(s4 + 1) * 128], B_hTt, Wv, B_W, 0, 512)
                        vt, B_vt = tmp.get()
                        P.op("dve", [PBv, B_bvb], [B_vt], lambda e, pv=pv, vt=vt: e.tensor_tensor(
                            out=vt[:], in0=pv[:], in1=bvb[:], op=ALU.add))
                        vr, B_vr = vrow.get()
                        vcol = tile * 4 + s4
                        P.op("act", [B_vt, B_vm], [B_vr], lambda e, vt=vt, vr=vr, vcol=vcol: e.activation(
                            out=vr[:, :, 0:64], in_=vt[:].rearrange("p (h d) -> p h d", d=64), func=AF.Copy,
                            scale=vm[:, vcol:vcol + 1]))
                        vsrc = bass.AP(vm.tensor if hasattr(vm, "tensor") else vm, vcol, [[32, 128], [0, 8], [1, 1]])
                        P.op("pool", [B_vm], [B_vr], lambda e, vr=vr, vsrc=vsrc: e.tensor_copy(
                            out=vr[:, :, 64:65], in_=vsrc), n=8)
                        tk = tile * 512 + s4 * 128
                        P.dma("pool", v_s[tk:tk + 128, :, :], vr[:], [B_vr], [], B_vr)
                P.barrier()
            with contextlib.ExitStack() as s2:
                mask4 = sb("a_mask4", [128, 512], BF16, s2)
                B_m4 = P.buf("a_mask4")
                for i in range(4):
                    src = mA_b if i % 2 == 0 else mB_b
                    P.op("dve", [B_cstb], [B_m4], lambda e, i=i, src=src: e.tensor_copy(
                        out=mask4[:, i * 128:(i + 1) * 128], in_=src), n=128)
                acc = sb("a_acc", [65, 8, 2048], F32, s2)
                B_acc = [P.buf("a_acc%d" % i) for i in range(2)]
                vext = Pools([(sb("a_vext%d" % i, [128, 8, 65], BF16, s2), P.buf("a_vext%d" % i, dma=True)) for i in range(6)])
                pt_ = Pools([(sb("a_P%d" % i, [128, 512], BF16, s2), P.buf("a_P%d" % i)) for i in range(10)])
                obst = Pools([(sb("a_obst%d" % i, [64, 512], BF16, s2), P.buf("a_obst%d" % i, dma=True)) for i in range(2)])
                vts = att_vtiles()
                prev = None
                pend2 = None
                for vi, (r, rho, m, start, nb) in enumerate(vts):
                    ve, B_ve = vext.get()
                    P.dma("sp", ve[:], v_s[start:start + 127 * r + 1:r, :, :], [], [B_ve], B_ve, c=2.5)
                    if m == 0:
                        prev = (ve, B_ve)
                        continue
                    bidx = m - 1
                    veA, B_veA = prev
                    veB, B_veB = ve, B_ve
                    prev = (ve, B_ve)
                    jq0 = 1024 // r
                    qst = rho + r * (jq0 + 128 * bidx)
                    qsl = slice(qst - 1024, qst - 1024 + 127 * r + 1, r)
                    kA = slice(qst - 64 * r, qst - 64 * r + 127 * r + 1, r)
                    kB = slice(qst + 64 * r, qst + 64 * r + 127 * r + 1, r)
                    Ps = {}
                    for hg in range(2):
                        pscs = [PSR.get(), PSR.get()]
                        for hh2 in range(2):
                            for ti, ksl in enumerate((kA, kB)):
                                for pair in range(2):
                                    psc, PBs = pscs[pair]
                                    h = hg * 4 + hh2 * 2 + pair
                                    ch, pb = h // 2, (h % 2) * 64
                                    c0 = hh2 * 256 + ti * 128
                                    P.op("pe", [B_kT, B_qT], [PBs], lambda e, psc=psc, c0=c0, ch=ch, pb=pb, ksl=ksl, qsl=qsl: e.matmul(
                                        psc[:, c0:c0 + 128], kT[pb:pb + 64, ch, ksl], qT[pb:pb + 64, ch, qsl],
                                        start=True, stop=True), n=300)
                        for pair in range(2):
                            psc, PBs = pscs[pair]
                            pp, B_pp = pt_.get()
                            P.op("act", [PBs], [B_pp], lambda e, psc=psc, pp=pp: e.activation(
                                out=pp[:], in_=psc[:], func=AF.Exp), tb='e')
                            P.op("dve", [B_pp, B_m4], [B_pp], lambda e, pp=pp: e.tensor_tensor(
                                out=pp[:], in0=pp[:], in1=mask4[:], op=ALU.mult))
                            for hh2 in range(2):
                                Ps[hg * 4 + hh2 * 2 + pair] = (pp, B_pp, hh2 * 256)

                    def stage2(Ps=Ps, veA=veA, B_veA=B_veA, veB=veB, B_veB=B_veB, qsl=qsl, r=r):
                        for hg in range(2):
                            po, PBo = PSR.get()
                            for hh in range(4):
                                h = hg * 4 + hh
                                pp, B_pp, c0 = Ps[h]
                                P.op("pe", [B_pp, B_veA], [PBo], lambda e, po=po, hh=hh, h=h, pp=pp, c0=c0: e.matmul(
                                    po[0:65, hh * 128:(hh + 1) * 128], veA[:, h, :], pp[:, c0:c0 + 128], start=True, stop=False), n=128)
                                P.op("pe", [B_pp, B_veB], [PBo], lambda e, po=po, hh=hh, h=h, pp=pp, c0=c0: e.matmul(
                                    po[0:65, hh * 128:(hh + 1) * 128], veB[:, h, :], pp[:, c0 + 128:c0 + 256], start=False, stop=True), n=128)
                            accv = acc[:, hg * 4:(hg + 1) * 4, qsl]
                            pov = po[0:65, :].rearrange("p (h q) -> p h q", h=4)
                            if r == 1:
                                P.op("dve", [PBo], [B_acc[hg]], lambda e, accv=accv, pov=pov: e.tensor_copy(out=accv, in_=pov))
                            else:
                                P.op("dve", [PBo, B_acc[hg]], [B_acc[hg]], lambda e, accv=accv, pov=pov: e.tensor_tensor(
                                    out=accv, in0=accv, in1=pov, op=ALU.add))
                    if pend2 is not None:
                        pend2()
                    pend2 = stage2
                if pend2 is not None:
                    pend2()
                    pend2 = None
                for hg in range(2):
                    hs = slice(hg * 4, (hg + 1) * 4)
                    P.op("act", [B_acc[hg]], [B_acc[hg]], lambda e, hs=hs: e.activation(
                        out=acc[64:65, hs, :], in_=acc[64:65, hs, :], func=AF.Ln), n=8192, tb='e')
                    P.op("act", [B_acc[hg]], [B_acc[hg]], lambda e, hs=hs: e.activation(
                        out=acc[64:65, hs, :], in_=acc[64:65, hs, :], func=AF.Exp, scale=-1.0), n=8192, tb='e')
                    for hh in range(4):
                        h = hg * 4 + hh
                        for q4 in range(4):
                            pd, PBd = PSR.get()
                            P.op("pe", [B_acc[hg], B_cst], [PBd], lambda e, pd=pd, h=h, q4=q4: e.matmul(
                                pd[0:64, :], cst[64:65, 1152:1216], acc[64:65, h, q4 * 512:(q4 + 1) * 512],
                                start=True, stop=True))
                            ot, B_ot = obst.get()
                            P.op("dve", [PBd, B_acc[hg]], [B_ot], lambda e, pd=pd, ot=ot, h=h, q4=q4: e.tensor_tensor(
                                out=ot[:], in0=acc[0:64, h, q4 * 512:(q4 + 1) * 512], in1=pd[0:64, :], op=ALU.mult))
                            c0 = hf * 2048 + q4 * 512
                            P.dma("pool", ob_s[h * 64:(h + 1) * 64, c0:c0 + 512], ot[:], [B_ot], [], B_ot)
                P.barrier()

    pa_st.close()

    class Scan:
        def __init__(self, st, tag):
            f = lambda n, dt=F32, sh=(128, 512), k=1: Pools([(sb("%s_%s%d" % (tag, n, i), list(sh), dt, st), P.buf("%s_%s%d" % (tag, n, i)))
                                                              for i in range(k)])
            self.t_sg, self.t_f, self.t_g, self.t_b = f("sg", k=4), f("f", k=2), f("g", k=2), f("b", k=2)
            self.t_eb, self.t_enb, self.t_kk, self.t_qs = f("eb", k=2), f("enb", k=2), f("kk", k=2), f("qs", k=2)
            NS = 2
            self.qT = [[sb("%s_qT%d_%d" % (tag, z, h), [128, 512], BF16, st) for h in range(4)] for z in range(NS)]
            self.kTt = [[sb("%s_kT%d_%d" % (tag, z, h), [128, 512], BF16, st) for h in range(4)] for z in range(NS)]
            self.ebk = [[sb("%s_ebk%d_%d" % (tag, z, h), [128, 8], F32, st) for h in range(4)] for z in range(NS)]
            self.B_q = [[P.buf("q") for h in range(4)] for z in range(NS)]
            self.B_k = [[P.buf("k") for h in range(4)] for z in range(NS)]
            self.B_e = [[P.buf("e") for h in range(4)] for z in range(NS)]
            self.vtok = [sb("%s_vtok%d" % (tag, z), [128, 4, 512], BF16, st) for z in range(NS)]
            self.ktok = [sb("%s_ktok%d" % (tag, z), [128, 4, 512], BF16, st) for z in range(NS)]
            self.B_vtok = [[P.buf("vtok") for i in range(4)] for z in range(NS)]
            self.B_ktok = [[P.buf("ktok") for i in range(4)] for z in range(NS)]
            self.vtmp = f("vtmp", k=2)
            self.A = f("A", BF16, k=3)
            self.S = [sb("%s_S%d" % (tag, h), [128, 128], F32, st) for h in range(4)]
            self.S1 = [sb("%s_S1%d" % (tag, h), [128, 128], F32, st) for h in range(4)]
            self.Sb = [sb("%s_Sb%d" % (tag, h), [128, 128], BF16, st) for h in range(4)]
            self.B_S = [P.buf("S") for h in range(4)]
            self.B_S1 = [P.buf("S1") for h in range(4)]
            self.B_Sb = [P.buf("Sb") for h in range(4)]
            for h in range(4):
                P.op("dve", [], [self.B_S[h]], lambda e, h=h: e.memset(self.S[h][:], 0.0))
                P.op("pool", [], [self.B_Sb[h]], lambda e, h=h: e.memset(self.Sb[h][:], 0.0))
            self.bvb = sb(tag + "_bvb", [128, 512], F32, st)
            self.B_bvb = P.buf(tag + "_bvb", dma=True)
            P.dma("sp", self.bvb[:], b_in_row[1536:2048].partition_broadcast(128), [], [self.B_bvb], self.B_bvb)

        def prepH(self, z, h, hTt, B_hTt, W, B_W, d):
            pq, PBq = PSR.get()
            proj_fm(pq, PBq, W, B_W, h * 128, hTt, B_hTt)
            yield
            pf, PBf = PSR.get()
            proj_fm(pf, PBf, W, B_W, 512 + h * 128, hTt, B_hTt)
            sg, B_sg = self.t_sg.get()
            bq = cv(C_BIN + 0 + h)
            bf = cv(C_BIN + 4 * (1 + d) + h)
            P.op("act", [PBq, B_colv], [B_sg], lambda e: e.activation(out=sg[:], in_=pq[:], func=AF.Sigmoid, bias=bq), tb='s')
            sf, B_sf = self.t_sg.get()
            P.op("act", [PBf, B_colv], [B_sf], lambda e: e.activation(out=sf[:], in_=pf[:], func=AF.Sigmoid, bias=bf), tb='s')
            yield
            qs, B_qs = self.t_qs.get()
            P.op("dve", [PBq, B_sg, B_colv], [B_qs], lambda e: e.scalar_tensor_tensor(
                out=qs[:], in0=pq[:], scalar=bq, in1=sg[:], op0=ALU.add, op1=ALU.mult))
            ff, B_ff = self.t_f.get()
            P.op("dve", [B_sf, B_colv], [B_ff], lambda e: e.tensor_scalar(
                out=ff[:], in0=sf[:], scalar1=cv(C_OML + d * 4 + h), scalar2=cv(C_LB + d * 4 + h),
                op0=ALU.mult, op1=ALU.add))
            yield
            g, B_g = self.t_g.get()
            P.op("act", [B_ff], [B_g], lambda e: e.activation(out=g[:], in_=ff[:], func=AF.Ln), tb='e')
            kk, B_kk = self.t_kk.get()
            P.op("pool", [B_ff], [B_kk], lambda e: e.tensor_scalar(
                out=kk[:], in0=ff[:], scalar1=-1.0, scalar2=1.0, op0=ALU.mult, op1=ALU.add), c=0.62)
            yield
            b, B_b = self.t_b.get()
            P.op("dve", [B_g, B_cst], [B_b], lambda e: e.tensor_tensor_scan(
                out=b[:], data0=cmask_f, data1=g[:], initial=0.0, op0=ALU.mult, op1=ALU.add), n=1024)
            yield
            eb, B_eb = self.t_eb.get()
            P.op("act", [B_b], [B_eb], lambda e: e.activation(out=eb[:], in_=b[:], func=AF.Exp), tb='e')
            enb, B_enb = self.t_enb.get()
            P.op("act", [B_b], [B_enb], lambda e: e.activation(out=enb[:], in_=b[:], func=AF.Exp, scale=-1.0), tb='e')
            yield
            P.op("pool", [B_qs, B_eb], [self.B_q[z][h]], lambda e: e.tensor_tensor(
                out=self.qT[z][h][:], in0=qs[:], in1=eb[:], op=ALU.mult))
            P.op("pool", [B_kk, B_enb], [self.B_k[z][h]], lambda e: e.tensor_tensor(
                out=self.kTt[z][h][:], in0=kk[:], in1=enb[:], op=ALU.mult))
            P.op("pool", [B_eb], [self.B_e[z][h]], lambda e: e.tensor_copy(out=self.ebk[z][h][:], in_=eb[:, 63:512:64]), n=8)

        def prepV(self, z, sub, hTt, B_hTt, W, B_W, vm_t, B_vm, vmcol):
            pv, PBv = PSR.get()
            proj_tm(pv[:], PBv, lambda kc: hTt[:, kc, sub * 128:(sub + 1) * 128], B_hTt, W, B_W, 1024, 512)
            vt, B_vt = self.vtmp.get()
            P.op("dve", [PBv, self.B_bvb], [B_vt], lambda e: e.tensor_tensor(out=vt[:], in0=pv[:], in1=self.bvb[:], op=ALU.add))
            P.op("act", [B_vt, B_vm], [self.B_vtok[z][sub]], lambda e: e.activation(
                out=self.vtok[z][:, sub, :], in_=vt[:], func=AF.Copy, scale=vm_t[:, vmcol:vmcol + 1]))

        def prepK(self, z):
            for sub in range(4):
                pk, PBk = PSR.get()
                for h in range(4):
                    P.op("pe", [self.B_k[z][h], B_cstb], [PBk], lambda e, h=h, pk=pk, sub=sub: e.matmul(
                        pk[:, h * 128:(h + 1) * 128], self.kTt[z][h][:, sub * 128:(sub + 1) * 128], ident_b,
                        start=True, stop=True), n=128)
                P.op("act", [PBk], [self.B_ktok[z][sub]], lambda e, pk=pk, sub=sub: e.activation(
                    out=self.ktok[z][:, sub, :], in_=pk[:], func=AF.Copy))

        def scan_sub(self, z, sub, res):
            qT, kTt, ebk, vtok, ktok = self.qT[z], self.kTt[z], self.ebk[z], self.vtok[z], self.ktok[z]
            B_q, B_k, B_e, B_vtok, B_ktok = self.B_q[z], self.B_k[z], self.B_e[z], self.B_vtok[z], self.B_ktok[z]
            psc, PBs = PSR.get()
            for h in range(4):
                P.op("pe", [B_k[h], B_q[h]], [PBs], lambda e, h=h: e.matmul(
                    psc[:, h * 128:(h + 1) * 128], kTt[h][:, sub * 128:(sub + 1) * 128],
                    qT[h][:, sub * 128:(sub + 1) * 128], start=True, stop=True), n=128)
            pus = [PSR.get(), PSR.get()]
            for h in range(4):
                for c in range(2):
                    pu, PBu = pus[c]
                    rows = slice(c * 64, (c + 1) * 64)
                    P.op("pe", [B_ktok[sub], B_vtok[sub]], [PBu], lambda e, pu=pu, h=h, rows=rows: e.matmul(
                        pu[:, h * 128:(h + 1) * 128], ktok[rows, sub, h * 128:(h + 1) * 128],
                        vtok[rows, sub, h * 128:(h + 1) * 128], start=True, stop=True), n=128)
            yield
            A, B_A = self.A.get()
            tri4 = bass.AP(cstb.tensor if hasattr(cstb, "tensor") else cstb, 256, [[1408, 128], [0, 4], [1, 128]])
            P.op("dve", [PBs, B_cstb], [B_A], lambda e: e.tensor_tensor(
                out=A[:].rearrange("p (h t) -> p h t", h=4), in0=psc[:].rearrange("p (h t) -> p h t", h=4),
                in1=tri4, op=ALU.mult))
            yield
            po, PBo = PSR.get()
            res.append((po, PBo))
            for h in range(4):
                P.op("pe", [B_A, B_vtok[sub]], [PBo], lambda e, h=h: e.matmul(
                    po[:, h * 128:(h + 1) * 128], A[:, h * 128:(h + 1) * 128], vtok[:, sub, h * 128:(h + 1) * 128],
                    start=(h == 0), stop=False, skip_group_check=True), n=128)
            for c in range(2):
                pu, PBu = pus[c]
                rows = slice(c * 64, (c + 1) * 64)
                toks = slice(sub * 128 + c * 64, sub * 128 + (c + 1) * 64)
                for h in range(4):
                    last = (c == 1 and h == 3)
                    P.op("pe", [B_q[h], self.B_Sb[h]], [PBo], lambda e, h=h, rows=rows, toks=toks, last=last: e.matmul(
                        po[rows, h * 128:(h + 1) * 128], qT[h][:, toks], self.Sb[h][:],
                        start=False, stop=last, skip_group_check=True), n=128)
                ci = sub * 2 + c
                for h in range(4):
                    e_ap = ebk[h][:, ci:ci + 1]
                    P.op("pool", [self.B_S[h], B_e[h]], [self.B_S1[h]], lambda e, h=h, e_ap=e_ap: e.tensor_scalar(
                        out=self.S1[h][:], in0=self.S[h][:], scalar1=e_ap, scalar2=0.0, op0=ALU.mult, op1=ALU.add), c=0.34)
                yield
                for h in range(4):
                    e_ap = ebk[h][:, ci:ci + 1]
                    P.op("dve", [PBu, self.B_S1[h], B_e[h]], [self.B_Sb[h]], lambda e, pu=pu, h=h, e_ap=e_ap: e.scalar_tensor_tensor(
                        out=self.Sb[h][:], in0=pu[:, h * 128:(h + 1) * 128], scalar=e_ap, in1=self.S1[h][:],
                        op0=ALU.mult, op1=ALU.add), n=128)
                    P.op("dve", [PBu, self.B_S1[h], B_e[h]], [self.B_S[h]], lambda e, pu=pu, h=h, e_ap=e_ap: e.scalar_tensor_tensor(
                        out=self.S[h][:], in0=pu[:, h * 128:(h + 1) * 128], scalar=e_ap, in1=self.S1[h][:],
                        op0=ALU.mult, op1=ALU.add), n=128)
                yield

    scan_W = {}
    sc_st = contextlib.ExitStack()
    if upto >= 2:
        for d_ in (1, 0):
            scan_W[d_] = (sb("s_W%d" % d_, [128, 8, 1536], BF16, sc_st), P.buf("s_W%d" % d_))
        with contextlib.ExitStack() as s1:
            stage = mk_stage(s1, 4)
            for d_ in (1, 0):
                W_, B_W_ = scan_W[d_]
                fcol = 1024 if d_ == 1 else 512
                load_weight(s1, W_[:, :, 0:512], B_W_, w_in[:, 0:512], 8, 512, stage=stage)
                load_weight(s1, W_[:, :, 512:1024], B_W_, w_in[:, fcol:fcol + 512], 8, 512, stage=stage)
                load_weight(s1, W_[:, :, 1024:1536], B_W_, w_in[:, 1536:2048], 8, 512, stage=stage)
            P.barrier()

    def scan_phase(d):
        with contextlib.ExitStack() as st:
            W, B_W = scan_W[d]
            fr = Front(st, 3, C_G1, C_SH1, "s%d" % d)
            fr.pool_xn = True
            sc = Scan(st, "s%d" % d)
            vm = sb("s_vm%d" % d, [128, 36], F32, st)
            B_vm = P.buf("s_vm", dma=True)
            P.dma("sp", vm[:], vmB if d == 1 else vmC, [], [B_vm], B_vm)
            hTr = [(sb("s_hT%d_%d" % (d, i), [128, 8, 512], BF16, st), (P.buf("s_hT%d" % i), P.buf("s_hTb%d" % i))) for i in range(2)]
            ost = Pools([(sb("s_ost%d_%d" % (d, i), [128, 512], F32, st), P.buf("s_ost%d" % i, dma=True)) for i in range(2)])
            top = 1024 + OWN + HH
            base = 1024 - HH
            flip = (d == 1)
            o_s = obw_s if d == 1 else ofw_s
            NT = 9

            class FrontStream:
                def __init__(self):
                    self.pend = None

                def step(self, t, s4):
                    hTt, B_hTt = hTr[t % 2]
                    i = t * 4 + s4
                    rb = (top - 128 * (i + 1)) if flip else (base + 128 * i)
                    xt, B_x = fr.xs.get()
                    P.dma("sp", xt[:], xw[rb:rb + 128, :], [], [B_x], B_x)
                    a = (fr.run_a(xt, B_x), t, s4)
                    self.flush()
                    self.pend = a

                def flush(self):
                    if self.pend is not None:
                        pa_, pt, ps4 = self.pend
                        hTt, B_hTt = hTr[pt % 2]
                        fr.run_b(pa_, hTt[:, :, ps4 * 128:(ps4 + 1) * 128], B_hTt, flip)
                        self.pend = None

            fs = FrontStream()

            def drain(*gens):
                gens = [g for g in gens if g is not None]
                while gens:
                    for g in list(gens):
                        try:
                            next(g)
                        except StopIteration:
                            gens.remove(g)

            def gen_front(t, s4):
                fs.step(t, s4)
                yield

            def gen_prepV(z, s4, hn, B_hn, col):
                yield
                yield
                sc.prepV(z, s4, hn, B_hn, W, B_W, vm, B_vm, col)
                yield

            def gen_out(res, i):
                po, PBo = res[0]
                if i >= 4:
                    ot, B_ot = ost.get()
                    P.op("act", [PBo], [B_ot], lambda e: e.activation(
                        out=ot[:], in_=po[:], func=AF.Copy, scale=float(128 ** -0.5)))
                    row = (i - 4) * 128
                    P.dma("pool", o_s[row:row + 128, :], ot[:], [B_ot], [], B_ot)

            for s4 in range(4):
                fs.step(0, s4)
            fs.flush()
            for s4 in range(4):
                drain(sc.prepH(0, s4, hTr[0][0], hTr[0][1], W, B_W, d))
                sc.prepV(0, s4, hTr[0][0], hTr[0][1], W, B_W, vm, B_vm, s4)
            for s4 in range(4):
                fs.step(1, s4)
            fs.flush()
            for t in range(NT):
                z = t % 2
                sc.prepK(z)
                for s4 in range(4):
                    res = []
                    g_f = gen_front(t + 2, s4) if t + 2 < NT else None
                    g_p = g_v = None
                    if t + 1 < NT:
                        hn, B_hn = hTr[(t + 1) % 2]
                        g_p = sc.prepH(1 - z, s4, hn, B_hn, W, B_W, d)
                        g_v = gen_prepV(1 - z, s4, hn, B_hn, (t + 1) * 4 + s4)
                    drain(g_f, g_p, sc.scan_sub(z, s4, res), g_v)
                    gen_out(res, t * 4 + s4)
                fs.flush()
            P.barrier()

    if upto >= 2:
        scan_phase(1)
    if upto >= 3:
        scan_phase(0)
    sc_st.close()

    if upto >= 3:
        with contextlib.ExitStack() as st:
            W = sb("c_W", [128, 8, 2560], BF16, st)
            Wa = sb("c_Wa", [128, 4, D], BF16, st)
            Wb = sb("c_Wb", [128, 4, D], BF16, st)
            Wo = sb("c_Wo", [128, 8, D], BF16, st)
            B_W = P.buf("c_W")
            with contextlib.ExitStack() as s1:
                stage = mk_stage(s1, 4)
                load_weight(s1, W[:, :, 0:512], B_W, w_in[:, 2048:2560], 8, 512, stage=stage)
                load_weight(s1, W[:, :, 512:2560], B_W, w_in[:, 4096:6144], 8, 2048, stage=stage)
                load_weight(s1, Wa, B_W, w_bra, 4, D, stage=stage)
                load_weight(s1, Wb, B_W, w_brb, 4, D, stage=stage)
                gate1_bc = sb("gate1_bc", [128, D], F32, s1)
                B_g1 = P.buf("g1bc", dma=True)
                P.dma("sp", gate1_bc[:], g_s[0], [], [B_g1], B_g1)
                load_weight(s1, Wo, B_W, w_out, 8, D, gate_bc=gate1_bc, B_gate=B_g1, stage=stage)
                P.barrier()
            fr = Front(st, 3, C_G1, C_SH1, "c")
            hTr = [(sb("c_hT%d" % i, [128, 8, 512], BF16, st), (P.buf("c_hT%d" % i), P.buf("c_hTb%d" % i))) for i in range(2)]
            sgT = sb("c_sgT", [128, 4, 512], BF16, st)
            B_sgT = P.buf("c_sgT")
            oaT = sb("c_oaT", [128, 4, 512], BF16, st)
            B_oaT = P.buf("c_oaT")
            obTr = Pools([(sb("c_obT%d" % i, [128, 4, 512], BF16, st), P.buf("c_obT%d" % i, dma=True)) for i in range(2)])
            mT = sb("c_mT", [128, 8, 512], BF16, st)
            B_mT = P.buf("c_mT")
            obw = Pools([(sb("c_obw%d" % i, [128, 512], F32, st), P.buf("c_obw%d" % i, dma=True)) for i in range(2)])
            ofw = Pools([(sb("c_ofw%d" % i, [128, 512], F32, st), P.buf("c_ofw%d" % i, dma=True)) for i in range(2)])
            osbr = Pools([(sb("c_osb%d" % i, [128, 512], F32, st), P.buf("c_osb%d" % i)) for i in range(2)])
            onbr = Pools([(sb("c_onb%d" % i, [128, 512], BF16, st), P.buf("c_onb%d" % i)) for i in range(2)])
            gstr = Pools([(sb("c_gst%d" % i, [128, 16], F32, st), P.buf("c_gst%d" % i)) for i in range(2)])
            gtmp = Pools([(sb("c_gt%d" % i, [128, 512], F32, st), P.buf("c_gt%d" % i)) for i in range(4)])
            x1st = Pools([(sb("c_x1%d" % i, [128, D], F32, st), P.buf("c_x1%d" % i, dma=True)) for i in range(2)])
            junk = sb("c_junk", [128, 512], BF16, st)
            B_junk = P.buf("c_junk")

            class FS:
                def __init__(self):
                    self.pend = None

                def step(self, t, s4):
                    rb = 1024 + t * 512 + s4 * 128
                    xt, B_x = fr.xs.get()
                    P.dma("sp", xt[:], xw[rb:rb + 128, :], [], [B_x], B_x)
                    a = (fr.run_a(xt, B_x), t, s4)
                    self.flush()
                    self.pend = a

                def flush(self):
                    if self.pend is not None:
                        pa_, pt, ps4 = self.pend
                        hTt, B_hTt = hTr[pt % 2]
                        fr.run_b(pa_, hTt[:, :, ps4 * 128:(ps4 + 1) * 128], B_hTt, False)
                        self.pend = None

            fs = FS()
            for s4 in range(4):
                fs.step(0, s4)
            fs.flush()
            for tile in range(8):
                hTt, B_hTt = hTr[tile % 2]
                t0 = tile * 512
                obT, B_obT = obTr.get()
                P.dma("sp", obT[:], ob_s.rearrange("(c p) t -> p c t", p=128)[:, :, t0:t0 + 512], [], [B_obT], B_obT)
                for h in range(4):
                    pg, PBg = PSR.get()
                    proj_fm(pg, PBg, W, B_W, h * 128, hTt, B_hTt)
                    g1, B1 = gtmp.get()
                    bg = cv(C_BIN + 16 + h)
                    P.op("act", [PBg, B_colv], [B1], lambda e, pg=pg, g1=g1, bg=bg: e.activation(
                        out=g1[:], in_=pg[:], func=AF.Sigmoid, bias=bg), tb='s')
                    P.op("dve", [PBg, B1, B_colv], [B_sgT], lambda e, pg=pg, g1=g1, bg=bg, h=h: e.scalar_tensor_tensor(
                        out=sgT[:, h, :], in0=pg[:], scalar=bg, in1=g1[:], op0=ALU.add, op1=ALU.mult))
                for s4 in range(4):
                    tt = t0 + s4 * 128
                    ow, B_ow = obw.get()
                    P.dma("sp", ow[:], obw_s[OWN - 128 - tt: OWN - tt, :], [], [B_ow], B_ow)
                    of, B_of = ofw.get()
                    P.dma("sp", of[:], ofw_s[tt:tt + 128, :], [], [B_of], B_of)
                    po, PBo = PSR.get()
                    P.op("pe", [B_ow, B_cst], [PBo], lambda e, po=po, ow=ow: e.matmul(po[:], J_f, ow[:], start=True, stop=True), n=2048)
                    osb, B_osb = osbr.get()
                    P.op("dve", [PBo, B_of], [B_osb], lambda e, po=po, of=of, osb=osb: e.tensor_tensor(
                        out=osb[:], in0=po[:], in1=of[:], op=ALU.add))
                    P.op("pool", [B_osb], [B_junk], lambda e, osb=osb: e.tensor_tensor(out=junk[:], in0=osb[:], in1=osb[:], op=ALU.mult))
                    gst, B_gst = gstr.get()
                    P.op("dve", [B_junk], [B_gst], lambda e, gst=gst: e.reduce_sum(
                        out=gst[:, 0:4], in_=junk[:].rearrange("p (h d) -> p h d", h=4), axis=mybir.AxisListType.X), n=512)
                    P.op("dve", [B_gst], [B_gst], lambda e, gst=gst: e.tensor_scalar(
                        out=gst[:, 4:8], in0=gst[:, 0:4], scalar1=1.0 / 128, scalar2=EPS, op0=ALU.mult, op1=ALU.add), n=4)
                    P.op("act", [B_gst], [B_gst], lambda e, gst=gst: e.activation(out=gst[:, 8:12], in_=gst[:, 4:8], func=AF.Ln), n=4, tb='e')
                    P.op("act", [B_gst], [B_gst], lambda e, gst=gst: e.activation(
                        out=gst[:, 12:16], in_=gst[:, 8:12], func=AF.Exp, scale=-0.5), n=4, tb='e')
                    onb, B_onb = onbr.get()
                    for h in range(4):
                        eng = "act" if h % 2 == 0 else "pool"
                        if eng == "act":
                            P.op("act", [B_osb, B_gst], [B_onb], lambda e, h=h, osb=osb, onb=onb, gst=gst: e.activation(
                                out=onb[:, h * 128:(h + 1) * 128], in_=osb[:, h * 128:(h + 1) * 128], func=AF.Copy,
                                scale=gst[:, 12 + h:13 + h]), n=128)
                        else:
                            P.op("pool", [B_osb, B_gst], [B_onb], lambda e, h=h, osb=osb, onb=onb, gst=gst: e.tensor_scalar(
                                out=onb[:, h * 128:(h + 1) * 128], in0=osb[:, h * 128:(h + 1) * 128],
                                scalar1=gst[:, 12 + h:13 + h], scalar2=0.0, op0=ALU.mult, op1=ALU.add), n=128)
                    ptp, PBt = PSR.get()
                    for h in range(4):
                        P.op("pe", [B_onb, B_cstb], [PBt], lambda e, ptp=ptp, h=h, onb=onb: e.matmul(
                            ptp[:, h * 128:(h + 1) * 128], onb[:, h * 128:(h + 1) * 128], ident_b, start=True, stop=True), n=128)
                    for h in range(4):
                        P.op("dve", [PBt, B_colv, B_sgT], [B_oaT], lambda e, ptp=ptp, h=h, s4=s4: e.scalar_tensor_tensor(
                            out=oaT[:, h, s4 * 128:(s4 + 1) * 128], in0=ptp[:, h * 128:(h + 1) * 128],
                            scalar=cv(C_GN + h), in1=sgT[:, h, s4 * 128:(s4 + 1) * 128], op0=ALU.mult, op1=ALU.mult), n=128)
                for cc in range(8):
                    pga, PBga = PSR.get()
                    proj_fm(pga, PBga, W, B_W, 512 + cc * 128, hTt, B_hTt)
                    pgb, PBgb = PSR.get()
                    proj_fm(pgb, PBgb, W, B_W, 1536 + cc * 128, hTt, B_hTt)
                    pa, PBa = PSR.get()
                    proj_fm(pa, PBa, Wa, B_W, cc * 128, oaT, B_oaT, nkc=4)
                    pb_, PBb = PSR.get()
                    proj_fm(pb_, PBb, Wb, B_W, cc * 128, obT, B_obT, nkc=4)
                    ga, B_ga = gtmp.get()
                    gb, B_gb = gtmp.get()
                    P.op("act", [PBga, B_colv], [B_ga], lambda e, pga=pga, ga=ga, cc=cc: e.activation(
                        out=ga[:], in_=pga[:], func=AF.Sigmoid, bias=cv(C_BIN + 32 + cc)), tb='s')
                    P.op("act", [PBgb, B_colv], [B_gb], lambda e, pgb=pgb, gb=gb, cc=cc: e.activation(
                        out=gb[:], in_=pgb[:], func=AF.Sigmoid, bias=cv(C_BIN + 40 + cc)), tb='s')
                    P.op("dve", [PBa, B_ga], [B_ga], lambda e, pa=pa, ga=ga: e.tensor_tensor(
                        out=ga[:], in0=pa[:], in1=ga[:], op=ALU.mult))
                    P.op("dve", [PBb, B_gb], [B_gb], lambda e, pb_=pb_, gb=gb: e.tensor_tensor(
                        out=gb[:], in0=pb_[:], in1=gb[:], op=ALU.mult))
                    P.op("pool", [B_ga, B_gb], [B_mT], lambda e, ga=ga, gb=gb, cc=cc: e.tensor_tensor(
                        out=mT[:, cc, :], in0=ga[:], in1=gb[:], op=ALU.add))
                    if tile + 1 < 8 and cc % 2 == 1:
                        fs.step(tile + 1, cc // 2)
                fs.flush()
                for s4 in range(4):
                    xo, B_xo = x1st.get()
                    rb = 1024 + t0 + s4 * 128
                    P.dma("sp", xo[:], xw[rb:rb + 128, :], [], [B_xo], B_xo)
                    for half in range(2):
                        px, PBx = PSR.get()
                        proj_tm(px[:], PBx, lambda kc, s4=s4: mT[:, kc, s4 * 128:(s4 + 1) * 128], B_mT, Wo, B_W, half * 512, 512)
                        P.op("dve", [PBx, B_xo], [B_xo], lambda e, px=px, xo=xo, half=half: e.tensor_tensor(
                            out=xo[:, half * 512:(half + 1) * 512], in0=px[:], in1=xo[:, half * 512:(half + 1) * 512],
                            op=ALU.add))
                    tt = t0 + s4 * 128
                    P.dma("pool", x1_s[tt:tt + 128, :], xo[:], [B_xo], [], B_xo)
            P.barrier()

    if upto >= 4:
        with contextlib.ExitStack() as st:
            Wi = sb("d_Wi", [128, 8, 2 * FFN], BF16, st)
            Wf = sb("d_Wf", [128, NJ, D], BF16, st)
            B_W = P.buf("d_W")
            with contextlib.ExitStack() as s1:
                stage = mk_stage(s1, 3)
                load_weight(s1, Wi, B_W, w_fi, 8, 2 * FFN, stage=stage)
                gate2_bc = sb("gate2_bc", [128, D], F32, s1)
                B_g2 = P.buf("g2bc", dma=True)
                P.dma("sp", gate2_bc[:], g_s[1], [], [B_g2], B_g2)
                load_weight(s1, Wf, B_W, w_fo, NJ, D, gate_bc=gate2_bc, B_gate=B_g2, stage=stage)
                P.barrier()
            fngb = sb("d_fng", [128, D], F32, st)
            B_fng = P.buf("d_fng", dma=True)
            P.dma("sp", fngb[:], fng.partition_broadcast(128), [], [B_fng], B_fng)
            fr = Front(st, 2, C_G2, C_SH2, "d", nxn=1)
            hTr = [(sb("d_hT%d" % i, [128, 8, 512], BF16, st), (P.buf("d_hT%d" % i), P.buf("d_hTb%d" % i))) for i in range(2)]
            yT = sb("d_yT", [128, NJ, 512], BF16, st)
            B_yT = P.buf("d_yT")
            gtmp = Pools([(sb("d_gt%d" % i, [128, 512], F32, st), P.buf("d_gt%d" % i)) for i in range(2)])
            res = Pools([(sb("d_res%d" % i, [128, D], F32, st), P.buf("d_res%d" % i, dma=True)) for i in range(2)])

            def d_front(tile):
                hTt, B_hTt = hTr[tile % 2]

                def d_ld(s4):
                    t0 = tile * 512 + s4 * 128
                    xt, B_x = fr.xs.get()
                    P.dma("sp", xt[:], x1_s[t0:t0 + 128, :], [], [B_x], B_x)
                    return xt, B_x
                for s4 in range(4):
                    xt, B_x = d_ld(s4)
                    fr.run(xt, B_x, hTt[:, :, s4 * 128:(s4 + 1) * 128], B_hTt, False)

            d_front(0)
            for tile in range(8):
                hTt, B_hTt = hTr[tile % 2]
                for j in range(NJ):
                    pg, PBg = PSR.get()
                    proj_fm(pg, PBg, Wi, B_W, j * 128, hTt, B_hTt)
                    pu, PBu = PSR.get()
                    proj_fm(pu, PBu, Wi, B_W, FFN + j * 128, hTt, B_hTt)
                    g1, B1 = gtmp.get()
                    P.op("act", [PBg], [B1], lambda e, pg=pg, g1=g1: e.activation(out=g1[:], in_=pg[:], func=AF.Sigmoid), tb='s')
                    P.op("dve", [PBg, B1], [B1], lambda e, pg=pg, g1=g1: e.tensor_tensor(
                        out=g1[:], in0=pg[:], in1=g1[:], op=ALU.mult))
                    P.op("dve", [PBu, B1], [B_yT], lambda e, pu=pu, g1=g1, j=j: e.tensor_tensor(
                        out=yT[:, j, :], in0=pu[:], in1=g1[:], op=ALU.mult))
                if tile + 1 < 8:
                    d_front(tile + 1)
                for s4 in range(4):
                    xo, B_xo = res.get()
                    t0 = tile * 512 + s4 * 128
                    P.dma("sp", xo[:], x1_s[t0:t0 + 128, :], [], [B_xo], B_xo)
                    for half in range(2):
                        px, PBx = PSR.get()
                        proj_tm(px[:], PBx, lambda kc, s4=s4: yT[:, kc, s4 * 128:(s4 + 1) * 128], B_yT, Wf, B_W, half * 512, 512, nkc=NJ)
                        P.op("dve", [PBx, B_xo], [B_xo], lambda e, px=px, xo=xo, half=half: e.tensor_tensor(
                            out=xo[:, half * 512:(half + 1) * 512], in0=px[:], in1=xo[:, half * 512:(half + 1) * 512],
                            op=ALU.add))
                    s4t, B_s = fr.stats(xo, B_xo)
                    P.op("act", [B_xo, B_s], [B_xo], lambda e, xo=xo, s4t=s4t: e.activation(
                        out=xo[:], in_=xo[:], func=AF.Copy, scale=s4t[:, 3:4]), n=1024)
                    P.op("pool", [B_xo, B_fng], [B_xo], lambda e, xo=xo: e.tensor_tensor(
                        out=xo[:], in0=xo[:], in1=fngb[:], op=ALU.mult), c=1.9)
                    P.dma("pool", y[t0:t0 + 128, :], xo[:], [B_xo], [], B_xo)
            P.barrier()
    P.flush()
    P.es.close()
    return nc, P


def _col(v, n):
    return np.ascontiguousarray(np.asarray(v, np.float32).reshape(n, 128).T)


def _consts():
    c = np.zeros((128, 1408), np.float32)
    i = np.arange(128)
    c[:, 0:128] = np.eye(128, dtype=np.float32)
    c[:, 128:256] = np.eye(128, dtype=np.float32)[::-1]
    s = i[:, None]
    t = i[None, :]
    c[:, 256:384] = ((s // 64 == t // 64) & (t >= s)).astype(np.float32)
    c[:, 384:512] = (t <= s).astype(np.float32)
    c[:, 512:640] = (t >= s).astype(np.float32)
    cm = np.ones(512, np.float32)
    cm[::64] = 0.0
    c[:, 640:1152] = cm[None, :]
    c[:, 1152:1280] = 1.0
    sw = np.where((i % 64) < 32, i + 32, i - 32)
    c[sw, 1280 + i] = 1.0
    return c


def _core_geom(c):
    if c < 4:
        return 0, c // 2, (c % 2) * OWN, 8192
    return 1, 0, (c - 4) * OWN, 16384


_NC_CACHE = {}


def kernel(x_prompt, x_sample, c_prompt, c_sample, w_ada, b_ada, norm1_g, w_in, b_in, lb_logits,
           hg_norm_g, w_branch_a, w_branch_b, w_out, norm2_g, w_ffn_in, w_ffn_out, final_norm_g,
           _debug=False):
    f32 = lambda a: np.ascontiguousarray(np.asarray(a, dtype=np.float32))
    x_prompt, x_sample = f32(x_prompt), f32(x_sample)
    c_prompt, c_sample = f32(c_prompt), f32(c_sample)
    w_in0 = f32(w_in)[0]
    b_in0 = f32(b_in)[0]
    perm = np.arange(1024).reshape(2, 8, 2, 32)[:, :, ::-1, :].reshape(-1)
    w_qksw = np.ascontiguousarray(w_in0[:, 2560:3584][:, perm])
    b_sw = b_in0[2560:3584][perm]
    lbl = np.concatenate([_col(f32(lb_logits)[l, d], 4) for l in range(2) for d in range(2)], axis=1)
    shared = {
        "w_ada": f32(w_ada)[0], "b_ada": f32(b_ada)[0], "n1g": _col(f32(norm1_g)[0], 8),
        "n2g": _col(f32(norm2_g)[0], 8), "w_in": w_in0, "w_qksw": w_qksw, "bin_col": _col(b_in0, 48),
        "bsw_col": _col(b_sw, 8), "b_in_row": b_in0, "lbl": np.ascontiguousarray(lbl),
        "gng": _col(f32(hg_norm_g)[0], 4), "w_bra": f32(w_branch_a)[0], "w_brb": f32(w_branch_b)[0],
        "w_out": f32(w_out)[0], "w_fi": f32(w_ffn_in)[0], "w_fo": f32(w_ffn_out)[0],
        "fng": f32(final_norm_g), "consts": _consts(),
    }
    half = 32
    inv = (np.float32(ROPE_THETA) ** (-np.arange(half, dtype=np.float32) / np.float32(half))).astype(np.float32)
    pidx = np.arange(128)
    fidx = pidx % 32
    sgn = np.where((pidx % 64) < 32, -1.0, 1.0).astype(np.float32)
    vts = att_vtiles()
    in_maps = []
    for c in range(NCORES):
        grp, b, own0, L = _core_geom(c)
        xs = x_prompt[b] if grp == 0 else x_sample[0]
        cvec = c_prompt[b] if grp == 0 else c_sample[0]
        w0 = own0 - HALO
        xw = np.zeros((WIN, D), np.float32)
        lo, hi = max(0, w0), min(L, w0 + WIN)
        xw[lo - w0:hi - w0] = xs[lo:hi]
        pos = (w0 + np.arange(WIN)).astype(np.float32)
        ang = (pos[None, :] * inv[fidx][:, None]).astype(np.float32)
        cosT = np.cos(ang).astype(np.float32)
        sinT = (np.sin(ang).astype(np.float32) * sgn[:, None]).astype(np.float32)
        valid = ((w0 + np.arange(WIN) >= 0) & (w0 + np.arange(WIN) < L)).astype(np.float32)
        vmA = np.zeros((128, 64), np.float32)
        for hf in range(2):
            for i in range(32):
                vmA[:, hf * 32 + i] = valid[2048 * hf + 128 * i + np.arange(128)]
        vmB = np.zeros((128, 36), np.float32)
        top = 1024 + OWN + HH
        for i in range(36):
            rb = top - 128 * (i + 1)
            vmB[:, i] = valid[rb + 127 - np.arange(128)]
        vmC = np.zeros((128, 36), np.float32)
        for i in range(36):
            rb = (1024 - HH) + 128 * i
            vmC[:, i] = valid[rb + np.arange(128)]
        m = dict(shared)
        m.update({"xw": xw, "ccol": _col(cvec, 8), "cosT": cosT, "sinT": sinT, "vmA": vmA, "vmB": vmB, "vmC": vmC})
        in_maps.append(m)
    import os as _os
    upto = int(_os.environ.get("K_UPTO", "4")) if _debug else 4
    key = (bool(_debug), upto)
    if key not in _NC_CACHE:
        _NC_CACHE[key] = build_program(debug=key[0], upto=upto)[0]
    nc = _NC_CACHE[key]
    res = run_bass_kernel_spmd(nc, in_maps, core_ids=list(range(NCORES)))
    outs = res.results
    y_prompt = np.zeros((2, 8192, D), np.float32)
    y_sample = np.zeros((1, 16384, D), np.float32)
    for c in range(NCORES):
        grp, b, own0, L = _core_geom(c)
        yc = np.asarray(outs[c]["y"], np.float32)
        if grp == 0:
            y_prompt[b, own0:own0 + OWN] = yc
        else:
            y_sample[0, own0:own0 + OWN] = yc
    if _debug:
        return (y_prompt, y_sample), outs
    return (y_prompt, y_sample)
```
